# Optimizing a Trainium2 kernel written in Bass

```python
import jax, jax.numpy as jnp
from jax import lax
import numpy as np


D_MODEL = 2048
BATCH = 2
SEQ = 4096
DEPTH = 2

HEAD_DIM = 128
N_HEADS = 4
BRANCH_WIDTH = N_HEADS * HEAD_DIM
N_BRANCHES = 4
MOBA_BLOCK = 256
MOBA_TOPK = 3
MOBA_Q_CHUNK = 64
FOX_Q_BLOCK = 128
HGRN_CHUNK = 64
POOL_WINDOWS = (2, 4, 8, 16)
POOL_GROUP_DIM = BRANCH_WIDTH // 4
ROPE_THETA = 10000.0
MLP_HIDDEN = 4 * D_MODEL
N_MEM = 256
N_XA_HEADS = 4
XA_HEAD_DIM = 128
NORM_EPS = 1e-6
MASK_VALUE = -1e30
IN_SIZES = (BRANCH_WIDTH,) * 3 + (BRANCH_WIDTH,) * 3 + (N_HEADS,) + (BRANCH_WIDTH,) * 4 + (BRANCH_WIDTH,) + (N_BRANCHES * D_MODEL,)
D_IN = sum(IN_SIZES)

kernel_name = 'hybrid_gated_moba_fox_hgrn2_pool'


def rms_norm(x, g):
    xf = x.astype(jnp.float32)
    y = xf * lax.rsqrt(jnp.mean(xf * xf, axis=-1, keepdims=True) + NORM_EPS)
    return (y * g.astype(jnp.float32)).astype(x.dtype)


def split_heads(y, n):
    B, T, W = y.shape
    return y.reshape(B, T, n, W // n).transpose(0, 2, 1, 3)


def merge_heads(y):
    B, H, T, D = y.shape
    return y.transpose(0, 2, 1, 3).reshape(B, T, H * D)


def rope(x):
    T, D = x.shape[2], x.shape[3]
    half = D // 2
    inv_freq = ROPE_THETA ** (-jnp.arange(half, dtype=jnp.float32) * 2.0 / D)
    ang = jnp.arange(T, dtype=jnp.float32)[:, None] * inv_freq[None, :]
    cos, sin = jnp.cos(ang), jnp.sin(ang)
    xf = x.astype(jnp.float32)
    x1, x2 = xf[..., :half], xf[..., half:]
    return jnp.concatenate([x1 * cos - x2 * sin, x2 * cos + x1 * sin], axis=-1).astype(x.dtype)


def moba_attention(q, k, v):
    B, H, T, D = q.shape
    q, k = rope(q), rope(k)
    nb = -(-T // MOBA_BLOCK)
    pad = nb * MOBA_BLOCK - T
    kp = jnp.pad(k, ((0, 0), (0, 0), (0, pad), (0, 0)))
    vp = jnp.pad(v, ((0, 0), (0, 0), (0, pad), (0, 0)))
    kb = kp.reshape(B, H, nb, MOBA_BLOCK, D)
    vb = vp.reshape(B, H, nb, MOBA_BLOCK, D)
    kbar = jnp.mean(kb.astype(jnp.float32), axis=3)
    gate = jnp.einsum('bhtd,bhnd->bhtn', q.astype(jnp.float32), kbar)
    qblk = jnp.arange(T) // MOBA_BLOCK
    past = jnp.arange(nb)[None, :] < qblk[:, None]
    gate = jnp.where(past, gate, MASK_VALUE)
    n_sel = min(MOBA_TOPK, nb)
    _, idx = lax.top_k(gate, n_sel)
    scale = D ** -0.5
    gather = jax.vmap(jax.vmap(lambda a, i: a[i]))
    n_chunks = T // MOBA_Q_CHUNK

    def chunk(c):
        t0 = c * MOBA_Q_CHUNK
        blk = t0 // MOBA_BLOCK
        qc = lax.dynamic_slice_in_dim(q, t0, MOBA_Q_CHUNK, axis=2)
        ic = lax.dynamic_slice_in_dim(idx, t0, MOBA_Q_CHUNK, axis=2)
        ks = gather(kb, ic)
        vs = gather(vb, ic)
        s_sel = jnp.einsum('bhqd,bhqnkd->bhqnk', qc, ks).astype(jnp.float32) * scale
        valid = jnp.arange(n_sel) < blk
        s_sel = jnp.where(valid[:, None], s_sel, MASK_VALUE)
        ko = lax.dynamic_slice_in_dim(kp, blk * MOBA_BLOCK, MOBA_BLOCK, axis=2)
        vo = lax.dynamic_slice_in_dim(vp, blk * MOBA_BLOCK, MOBA_BLOCK, axis=2)
        s_own = jnp.einsum('bhqd,bhkd->bhqk', qc, ko).astype(jnp.float32) * scale
        causal = (blk * MOBA_BLOCK + jnp.arange(MOBA_BLOCK))[None, :] <= (t0 + jnp.arange(MOBA_Q_CHUNK))[:, None]
        s_own = jnp.where(causal, s_own, MASK_VALUE)
        logits = jnp.concatenate([s_sel.reshape(B, H, MOBA_Q_CHUNK, n_sel * MOBA_BLOCK), s_own], axis=-1)
        p = jax.nn.softmax(logits, axis=-1).astype(v.dtype)
        p_sel = p[..., :n_sel * MOBA_BLOCK].reshape(B, H, MOBA_Q_CHUNK, n_sel, MOBA_BLOCK)
        p_own = p[..., n_sel * MOBA_BLOCK:]
        return jnp.einsum('bhqnk,bhqnkd->bhqd', p_sel, vs) + jnp.einsum('bhqk,bhkd->bhqd', p_own, vo)

    out = lax.map(chunk, jnp.arange(n_chunks))
    return out.transpose(1, 2, 0, 3, 4).reshape(B, H, T, D)


def forgetting_attention(q, k, v, f_logit):
    B, H, T, D = q.shape
    c = jnp.cumsum(jax.nn.log_sigmoid(f_logit.astype(jnp.float32)), axis=-1)
    scale = D ** -0.5

    def block(i):
        t0 = i * FOX_Q_BLOCK
        qb = lax.dynamic_slice_in_dim(q, t0, FOX_Q_BLOCK, axis=2)
        cb = lax.dynamic_slice_in_dim(c, t0, FOX_Q_BLOCK, axis=2)
        s = jnp.einsum('bhqd,bhkd->bhqk', qb, k).astype(jnp.float32) * scale
        s = s + cb[..., :, None] - c[:, :, None, :]
        mask = jnp.arange(T)[None, :] <= (t0 + jnp.arange(FOX_Q_BLOCK))[:, None]
        p = jax.nn.softmax(jnp.where(mask, s, MASK_VALUE), axis=-1).astype(v.dtype)
        return jnp.einsum('bhqk,bhkd->bhqd', p, v)

    out = lax.map(block, jnp.arange(T // FOX_Q_BLOCK))
    return out.transpose(1, 2, 0, 3, 4).reshape(B, H, T, D)


def hgrn2(q, f_logit, v, g, lb, norm_g):
    B, H, T, D = q.shape
    Dv = v.shape[-1]
    dt = q.dtype
    z = f_logit.astype(jnp.float32)
    lbh = lb.astype(jnp.float32).reshape(1, H, 1, D)
    f = lbh + (1.0 - lbh) * jax.nn.sigmoid(z)
    log_f = jnp.log(f)
    k = (1.0 - lbh) * jax.nn.sigmoid(-z)
    n = T // HGRN_CHUNK
    C = HGRN_CHUNK

    def chunks(a):
        return a.astype(jnp.float32).reshape(B, H, n, C, a.shape[-1]).transpose(2, 0, 1, 3, 4)

    causal = jnp.tril(jnp.ones((C, C), dtype=bool))

    def step(S, xs):
        qc, kc, vc, lf = xs
        b = jnp.cumsum(lf, axis=2)
        b_last = b[:, :, -1:, :]
        inter = jnp.einsum('bhcd,bhde->bhce', qc * jnp.exp(b), S)
        diff = b[:, :, :, None, :] - b[:, :, None, :, :]
        decay = jnp.exp(jnp.where(causal[:, :, None], diff, MASK_VALUE))
        att = jnp.einsum('bhtd,bhsd,bhtsd->bhts', qc, kc, decay)
        intra = jnp.einsum('bhts,bhse->bhte', att, vc)
        S = jnp.exp(b_last[:, :, 0, :])[..., None] * S + jnp.einsum('bhsd,bhse->bhde', kc * jnp.exp(b_last - b), vc)
        return S, inter + intra

    S0 = jnp.zeros((B, H, D, Dv), jnp.float32)
    _, o = lax.scan(step, S0, (chunks(q), chunks(k), chunks(v), chunks(log_f)))
    o = o.transpose(1, 2, 0, 3, 4).reshape(B, H, T, Dv)
    o = o * lax.rsqrt(jnp.mean(o * o, axis=-1, keepdims=True) + NORM_EPS) * norm_g.astype(jnp.float32).reshape(1, H, 1, Dv)
    o = o.transpose(0, 2, 1, 3).reshape(B, T, H * Dv)
    return (o * jax.nn.silu(g.astype(jnp.float32))).astype(dt)


def multiscale_pool(y, w_pool, scale):
    B, T, _ = y.shape
    G = len(POOL_WINDOWS)
    C = POOL_GROUP_DIM
    yg = y.astype(jnp.float32).reshape(B, T, G, C)
    cs = jnp.concatenate([jnp.zeros((B, 1, G, C), jnp.float32), jnp.cumsum(yg, axis=1)], axis=1)
    end = jnp.arange(1, T + 1)
    start = jnp.maximum(end[:, None] - jnp.array(POOL_WINDOWS, dtype=jnp.int32)[None, :], 0)
    lo = cs[:, start, jnp.arange(G)[None, :], :]
    mean = (cs[:, 1:] - lo) / (end[:, None] - start).astype(jnp.float32)[None, :, :, None]
    pooled = (mean - yg).astype(y.dtype)
    out = jnp.einsum('btgc,gce->btge', pooled, w_pool).reshape(B, T, G * C)
    return out * scale


def hybrid_mixer(h, w_in, f_bias, lb, hgrn_norm, pool_w, pool_scale, w_branch, w_out):
    B, T, _ = h.shape
    z = h @ w_in
    (mq, mk, mv, fq, fk, fv, ff, hq, hf, hi, hg, pin, gl) = jnp.split(z, np.cumsum(IN_SIZES)[:-1].tolist(), axis=-1)
    y_a = merge_heads(moba_attention(split_heads(mq, N_HEADS), split_heads(mk, N_HEADS), split_heads(mv, N_HEADS)))
    f_logit = (ff + f_bias).transpose(0, 2, 1)
    y_b = merge_heads(forgetting_attention(split_heads(fq, N_HEADS), split_heads(fk, N_HEADS), split_heads(fv, N_HEADS), f_logit))
    y_c = hgrn2(split_heads(hq, N_HEADS), split_heads(hf, N_HEADS), split_heads(hi, N_HEADS), hg, lb, hgrn_norm)
    y_d = multiscale_pool(pin, pool_w, pool_scale)
    branches = jnp.stack([y_a, y_b, y_c, y_d], axis=2)
    proj = jnp.einsum('btnc,ncd->btnd', branches, w_branch)
    gates = jax.nn.sigmoid(gl.reshape(B, T, N_BRANCHES, D_MODEL))
    return jnp.sum(gates * proj, axis=2) @ w_out


def memory_cross_attention(h, m, wq, wkv, wo):
    q = split_heads(h @ wq, N_XA_HEADS)
    k, v = jnp.split(m @ wkv, 2, axis=-1)
    k, v = split_heads(k, N_XA_HEADS), split_heads(v, N_XA_HEADS)
    s = jnp.einsum('bhtd,bhmd->bhtm', q, k).astype(jnp.float32) * XA_HEAD_DIM ** -0.5
    p = jax.nn.softmax(s, axis=-1).astype(v.dtype)
    return merge_heads(jnp.einsum('bhtm,bhmd->bhtd', p, v)) @ wo


def squared_relu_mlp(h, w_up, w_down):
    return jnp.square(jax.nn.relu(h @ w_up)) @ w_down


def setup_inputs(seed: int = 0) -> dict:
    key = jax.random.key(seed)
    ks = jax.random.split(key, 24)
    f32 = jnp.float32
    L = DEPTH
    XW = N_XA_HEADS * XA_HEAD_DIM

    def w(k, shape, fan_in):
        return jax.random.normal(k, shape, f32) * fan_in ** -0.5

    def gain(k, shape):
        return 1.0 + 0.02 * jax.random.normal(k, shape, f32)

    return {
        'x': jax.random.normal(ks[0], (BATCH, SEQ, D_MODEL), f32),
        'mem': jax.random.normal(ks[1], (BATCH, N_MEM, D_MODEL), f32),
        'hgrn_lb_logits': 0.5 * jax.random.normal(ks[2], (L, BRANCH_WIDTH), f32),
        'mix_norm_pre': gain(ks[3], (L, D_MODEL)),
        'mix_norm_post': gain(ks[4], (L, D_MODEL)),
        'w_in': w(ks[5], (L, D_MODEL, D_IN), D_MODEL),
        'fox_f_bias': 2.0 + 0.5 * jax.random.normal(ks[6], (L, N_HEADS), f32),
        'hgrn_out_norm': gain(ks[7], (L, BRANCH_WIDTH)),
        'pool_w': w(ks[8], (L, len(POOL_WINDOWS), POOL_GROUP_DIM, POOL_GROUP_DIM), POOL_GROUP_DIM),
        'pool_scale': gain(ks[9], (L, BRANCH_WIDTH)),
        'w_branch': w(ks[10], (L, N_BRANCHES, BRANCH_WIDTH, D_MODEL), BRANCH_WIDTH),
        'w_mix_out': w(ks[11], (L, D_MODEL, D_MODEL), D_MODEL),
        'xa_norm_pre': gain(ks[12], (L, D_MODEL)),
        'xa_norm_mem': gain(ks[13], (L, D_MODEL)),
        'xa_norm_post': gain(ks[14], (L, D_MODEL)),
        'xa_wq': w(ks[15], (L, D_MODEL, XW), D_MODEL),
        'xa_wkv': w(ks[16], (L, D_MODEL, 2 * XW), D_MODEL),
        'xa_wo': w(ks[17], (L, XW, D_MODEL), XW),
        'mlp_norm_pre': gain(ks[18], (L, D_MODEL)),
        'mlp_norm_post': gain(ks[19], (L, D_MODEL)),
        'mlp_w_up': w(ks[20], (L, D_MODEL, MLP_HIDDEN), D_MODEL),
        'mlp_w_down': w(ks[21], (L, MLP_HIDDEN, D_MODEL), MLP_HIDDEN),
    }


def reference(x, mem, hgrn_lb_logits, mix_norm_pre, mix_norm_post, w_in, fox_f_bias, hgrn_out_norm, pool_w, pool_scale, w_branch, w_mix_out, xa_norm_pre, xa_norm_mem, xa_norm_post, xa_wq, xa_wkv, xa_wo, mlp_norm_pre, mlp_norm_post, mlp_w_up, mlp_w_down):
    P = jax.nn.softmax(hgrn_lb_logits.astype(jnp.float32), axis=0)
    lower_bounds = jnp.cumsum(P, axis=0) - P[0:1]
    h = x
    for l in range(DEPTH):
        a = hybrid_mixer(rms_norm(h, mix_norm_pre[l]), w_in[l], fox_f_bias[l], lower_bounds[l], hgrn_out_norm[l], pool_w[l], pool_scale[l], w_branch[l], w_mix_out[l])
        h = h + rms_norm(a, mix_norm_post[l])
        c = memory_cross_attention(rms_norm(h, xa_norm_pre[l]), rms_norm(mem, xa_norm_mem[l]), xa_wq[l], xa_wkv[l], xa_wo[l])
        h = h + rms_norm(c, xa_norm_post[l])
        m = squared_relu_mlp(rms_norm(h, mlp_norm_pre[l]), mlp_w_up[l], mlp_w_down[l])
        h = h + rms_norm(m, mlp_norm_post[l])
    return h
```

```python
import contextlib
import numpy as np
import ml_dtypes
import concourse.bass as bass
import concourse.mybir as mybir
from concourse.bass_utils import run_bass_kernel_spmd

F32 = mybir.dt.float32
BF16 = mybir.dt.bfloat16
AF = mybir.ActivationFunctionType
ALU = mybir.AluOpType
AX = mybir.AxisListType
NPBF = ml_dtypes.bfloat16

D_MODEL = 2048
NCH = 16
BATCH = 2
SEQ = 4096
TOK = 1024
HD = 128
NEG = -30000.0
EPS = 1e-6
SCALE = HD ** -0.5
POOL_WINDOWS = (2, 4, 8, 16)
DEBUG_STOP = 0
ROPE_ADD_ENG = "dve"

COMPUTE = ("pe", "act", "dve", "pool")
N_DMA_SEMS = 12


class Buf:
    __slots__ = ("name", "w", "r", "excl")

    def __init__(self, name="", excl=False):
        self.name = name
        self.w = None
        self.r = []
        self.excl = excl


class Op:
    __slots__ = ("eng", "fn", "waits", "sig", "idx", "dma", "dma_ev")

    def __init__(self, eng, fn, sig, dma):
        self.eng = eng
        self.fn = fn
        self.waits = []
        self.sig = sig
        self.dma = dma
        self.dma_ev = None


class Sched:
    def __init__(self, nc):
        self.nc = nc
        self.ops = {e: [] for e in COMPUTE + ("sp",)}
        self.dma_count = {e: 0 for e in ("sp", "act", "pool")}
        self.out_events = []
        self.fence_waits = {e: [] for e in COMPUTE + ("sp",)}
        self.n_cc = 0

    def collective(self, fn, reads, writes):
        op = Op("pool", fn, False, False)
        op.dma = "cc"
        lst = self.ops["pool"]
        op.idx = len(lst)
        lst.append(op)
        ev = ("x", self.n_cc)
        op.dma_ev = ev
        self.n_cc += 1
        if self.fence_waits["pool"]:
            op.waits.extend(self.fence_waits["pool"])
            self.fence_waits["pool"] = []
        for b in reads:
            if b.w is not None:
                op.waits.append(b.w)
        for b in writes:
            if b.w is not None:
                op.waits.append(b.w)
            op.waits.extend(b.r)
        for d in op.waits:
            if d[0] == "c":
                self.ops[d[1]][d[2]].sig = True
        for b in reads:
            b.r.append(ev)
        for b in writes:
            b.w = ev
            b.r = []
        return ev

    def fence(self):
        evs = []
        for e in COMPUTE + ("sp",):
            lst = self.ops[e]
            for op in reversed(lst):
                if not op.dma:
                    evs.append(("c", e, op.idx))
                    op.sig = True
                    break
        for q, n in self.dma_count.items():
            for i in range(max(0, n - N_DMA_SEMS), n):
                evs.append(("d", q, i))
        for i in range(self.n_cc):
            evs.append(("x", i))
        for e in self.fence_waits:
            self.fence_waits[e] = list(evs)

    def _add(self, eng, fn, reads, writes, sig=False, dma=False):
        op = Op(eng, fn, sig, dma)
        lst = self.ops[eng]
        op.idx = len(lst)
        lst.append(op)
        if dma:
            n = self.dma_count[eng]
            self.dma_count[eng] += 1
            ev = ("d", eng, n)
            op.dma_ev = ev
            if n >= N_DMA_SEMS:
                op.waits.append(("d", eng, n - N_DMA_SEMS))
        else:
            ev = ("c", eng, op.idx)
        if self.fence_waits[eng]:
            op.waits.extend(d for d in self.fence_waits[eng] if d != ev)
            self.fence_waits[eng] = []
        excl_reads = [b for b in reads if b.excl]
        if excl_reads:
            reads = [b for b in reads if not b.excl]
            writes = list(writes) + excl_reads
        for b in reads:
            if b.w is not None and b.w != ev:
                op.waits.append(b.w)
        for b in writes:
            if b.w is not None and b.w != ev:
                op.waits.append(b.w)
            for d in b.r:
                if d != ev:
                    op.waits.append(d)
        for d in op.waits:
            if d[0] == "c":
                self.ops[d[1]][d[2]].sig = True
        for b in reads:
            b.r.append(ev)
        for b in writes:
            b.w = ev
            b.r = []
        return ev

    def pe(self, fn, reads, writes, sig=False):
        return self._add("pe", fn, reads, writes, sig)

    def act(self, fn, reads, writes):
        return self._add("act", fn, reads, writes)

    def dve(self, fn, reads, writes):
        return self._add("dve", fn, reads, writes)

    def pool(self, fn, reads, writes):
        return self._add("pool", fn, reads, writes)

    def dma(self, q, fn, reads, writes, is_output=False):
        ev = self._add(q, fn, reads, writes, dma=True)
        if is_output:
            self.out_events.append(ev)
        return ev

    def emit(self):
        nc = self.nc
        with contextlib.ExitStack() as st:
            csem = {e: st.enter_context(nc.semaphore("c_" + e)) for e in COMPUTE}
            dsem = {q: [st.enter_context(nc.semaphore(f"d_{q}{i}")) for i in range(N_DMA_SEMS)]
                    for q in ("sp", "act", "pool")}
            xsem = [st.enter_context(nc.semaphore(f"x_{i}")) for i in range(self.n_cc)]
            block = st.enter_context(nc.Block())
            sigcount = {}
            for e in COMPUTE:
                lst = self.ops[e]
                last = None
                for op in lst:
                    if not op.dma:
                        last = op
                if last is not None:
                    last.sig = True
                c = 0
                arr = []
                for op in lst:
                    if (not op.dma) and op.sig:
                        c += 1
                    arr.append(c)
                sigcount[e] = arr
            self.stats = {e: (len(self.ops[e]), sigcount[e][-1] if sigcount[e] else 0) for e in COMPUTE}
            self.stats["dma"] = dict(self.dma_count)

            def resolve(ev):
                if ev[0] == "c":
                    _, e, i = ev
                    op = self.ops[e][i]
                    v = sigcount[e][i]
                    assert op.sig
                    return ("c", e), csem[e], v
                if ev[0] == "x":
                    return ev, xsem[ev[1]], 1
                _, q, n = ev
                return ("d", q, n % N_DMA_SEMS), dsem[q][n % N_DMA_SEMS], 16 * (n // N_DMA_SEMS + 1)

            def run_engine(ename, eng):
                known = {}
                for op in self.ops[ename]:
                    need = {}
                    for ev in op.waits:
                        key, sem, v = resolve(ev)
                        if known.get(key, 0) >= v:
                            continue
                        if need.get(key, (None, 0))[1] < v:
                            need[key] = (sem, v)
                    for key, (sem, v) in need.items():
                        eng.wait_ge(sem, v)
                        known[key] = v
                    ins = op.fn(eng)
                    if op.dma == "cc":
                        ins.then_inc(xsem[op.dma_ev[1]])
                    elif op.dma:
                        _, q, n = op.dma_ev
                        ins.then_inc(dsem[q][n % N_DMA_SEMS], 16)
                    elif op.sig:
                        ins.then_inc(csem[ename], 1)
                if ename == "sp":
                    for ev in self.out_events:
                        key, sem, v = resolve(ev)
                        if known.get(key, 0) >= v:
                            continue
                        eng.wait_ge(sem, v)
                        known[key] = v

            @block.sync
            def _(eng):
                if getattr(self, "sp_init", None) is not None:
                    self.sp_init(eng)
                run_engine("sp", eng)

            @block.tensor
            def _(eng):
                run_engine("pe", eng)

            @block.scalar
            def _(eng):
                run_engine("act", eng)

            @block.vector
            def _(eng):
                run_engine("dve", eng)

            @block.gpsimd
            def _(eng):
                run_engine("pool", eng)


class T:
    def __init__(self, t, nsub=1, name="", psum=False):
        self.t = t
        if psum:
            self.b = [Buf(name, excl=True)] * nsub
        else:
            self.b = [Buf(f"{name}{i}") for i in range(nsub)]

    def __getitem__(self, k):
        return self.t[k]

    @property
    def all(self):
        return list(self.b)


class Ctx:
    def __init__(self):
        self.nc = bass.Bass("TRN2", target_bir_lowering=False)
        self.S = Sched(self.nc)
        self._n = 0
        self.scope = None

    def sb(self, shape, dt, nsub=1, name=None):
        self._n += 1
        name = (name or "sb") + f"_{self._n}"
        if self.scope is not None:
            return T(self.scope.enter_context(self.nc.sbuf_tensor(name, list(shape), dt)), nsub, name)
        return T(self.nc.alloc_sbuf_tensor(name, list(shape), dt), nsub, name)

    def ps(self, shape, dt=F32, nsub=1, name=None):
        self._n += 1
        name = name or f"ps{self._n}"
        return T(self.nc.alloc_psum_tensor(name, list(shape), dt), nsub, name, psum=True)

    def din(self, name, shape, dt):
        return self.nc.dram_tensor(name, list(shape), dt, kind="ExternalInput").ap()

    def dout(self, name, shape, dt):
        return self.nc.dram_tensor(name, list(shape), dt, kind="ExternalOutput").ap()

    def load(self, dst_ap, src_ap, wbufs, q="sp", rbufs=()):
        return self.S.dma(q, lambda e: e.dma_start(out=dst_ap, in_=src_ap), list(rbufs), list(wbufs))

    def store(self, dst_ap, src_ap, rbufs, q="sp", is_output=True):
        return self.S.dma(q, lambda e: e.dma_start(out=dst_ap, in_=src_ap), list(rbufs), [], is_output=is_output)

    def mm(self, out_ap, lhsT, rhs, start, stop, reads, writes, sig=None):
        if sig is None:
            sig = False
        return self.S.pe(lambda e: e.matmul(out_ap, lhsT, rhs, start=start, stop=stop), reads, writes, sig=sig)

    def transpose(self, out_ap, in_ap, ident_ap, reads, writes):
        return self.S.pe(lambda e: e.transpose(out_ap, in_ap, ident_ap), reads, writes)

    def activation(self, out_ap, in_ap, func, reads, writes, bias=None, scale=None):
        kw = {}
        if bias is not None:
            kw["bias"] = bias
        if scale is not None:
            kw["scale"] = scale
        return self.S.act(lambda e: e.activation(out=out_ap, in_=in_ap, func=func, **kw), reads, writes)

    def tt(self, out_ap, in0, in1, op, reads, writes, eng="dve"):
        f = lambda e: e.tensor_tensor(out=out_ap, in0=in0, in1=in1, op=op)
        return (self.S.dve if eng == "dve" else self.S.pool)(f, reads, writes)

    def ts(self, out_ap, in0, s1, s2, op0, op1, reads, writes, eng="dve"):
        if op1 is None:
            f = lambda e: e.tensor_scalar(out=out_ap, in0=in0, scalar1=s1, scalar2=None, op0=op0)
        else:
            f = lambda e: e.tensor_scalar(out=out_ap, in0=in0, scalar1=s1, scalar2=s2, op0=op0, op1=op1)
        return (self.S.dve if eng == "dve" else self.S.pool)(f, reads, writes)

    def stt(self, out_ap, in0, scalar, in1, op0, op1, reads, writes):
        return self.S.dve(lambda e: e.scalar_tensor_tensor(out=out_ap, in0=in0, scalar=scalar, in1=in1,
                                                            op0=op0, op1=op1), reads, writes)

    def copy(self, out_ap, in_ap, reads, writes, eng="dve"):
        if eng == "act":
            return self.S.act(lambda e: e.copy(out=out_ap, in_=in_ap), reads, writes)
        f = lambda e: e.tensor_copy(out=out_ap, in_=in_ap)
        return (self.S.dve if eng == "dve" else self.S.pool)(f, reads, writes)

    def memset(self, ap, val, writes, eng="pool"):
        f = lambda e: e.memset(ap, val)
        return (self.S.dve if eng == "dve" else self.S.pool)(f, [], writes)


class Common:
    def __init__(self, cx, consts_ap):
        self.cx = cx
        self.cb = cx.sb([128, 256], BF16, name="cbf")
        cx.load(self.cb[:, :], consts_ap, self.cb.all)
        self.ident = self.cb[:, 0:128]
        self.ones = self.cb[:, 128:256]
        self.banks = [cx.ps([128, 512], F32, nsub=4, name=f"bank{i}") for i in range(7)]
        self.bankb = cx.ps([128, 1024], BF16, nsub=8, name="bankb")


def rms_stats(cx, cm, src_fn, src_bufs, nfeat_chunks, ncols, sq_pool, bank, rstd, denom):
    for c in range(nfeat_chunks):
        sq = sq_pool[c % len(sq_pool)]
        cx.activation(sq[:, 0:ncols], src_fn(c), AF.Square, src_bufs(c), sq.all)
        cx.mm(bank[:, 0:ncols], cm.ones, sq[:, 0:ncols], c == 0, c == nfeat_chunks - 1,
              [cm.cb.b[0]] + sq.all, bank.all)
    cx.activation(rstd[:, 0:ncols], bank[:, 0:ncols], AF.Ln, bank.all + cm.epst.all, rstd.all, bias=cm.eps_ap,
                  scale=1.0 / denom)
    cx.activation(rstd[:, 0:ncols], rstd[:, 0:ncols], AF.Exp, rstd.all, rstd.all, scale=-0.5)


def add_eps(cx, cm):
    cm.epst = cx.sb([128, 1], F32, name="epst")
    cx.memset(cm.epst[:, :], EPS, cm.epst.all)
    cm.eps_ap = cm.epst[:, 0:1]


def phase_pre(cx, cm, xT, gpre, hn_out_fn, is_output):
    g = cx.sb([128, NCH], F32, name="g")
    cx.load(g[:, :], gpre, g.all)
    xv = xT.rearrange("(c p) t -> p c t", p=128)
    xs = [cx.sb([128, NCH, 512], F32, nsub=NCH, name=f"xs{i}") for i in range(2)]
    hs = [cx.sb([128, NCH, 512], BF16, nsub=1, name=f"hs{i}") for i in range(2)]
    sqp = [cx.sb([128, 512], BF16, name=f"sq{i}") for i in range(3)]
    rstd = cx.sb([128, 512], F32, name="rstd")
    for n in range(TOK // 512):
        x = xs[n % 2]
        h = hs[n % 2]
        cx.load(x[:, :, :], xv[:, :, n * 512:(n + 1) * 512], x.all)
        rms_stats(cx, cm, lambda c: x[:, c, :], lambda c: [x.b[c]], NCH, 512, sqp, cm.banks[n % 2], rstd, D_MODEL)
        for c in range(NCH):
            cx.stt(h[:, c, :], x[:, c, :], g[:, c:c + 1], rstd[:, :], ALU.mult, ALU.mult,
                   [x.b[c]] + g.all + rstd.all, h.all)
        for k in range(4):
            cx.store(hn_out_fn(n, k), h[:, 4 * k:4 * k + 4, :], h.all, is_output=is_output)


def build_pre():
    cx = Ctx()
    xT = cx.din("xT", [D_MODEL, TOK], F32)
    gpre = cx.din("gpre", [128, NCH], F32)
    consts = cx.din("consts", [128, 256], BF16)
    hn_out = cx.dout("hn_out", [D_MODEL, TOK], BF16)
    cm = Common(cx, consts)
    add_eps(cx, cm)
    ov = hn_out.rearrange("(c p) t -> p c t", p=128)
    phase_pre(cx, cm, xT, gpre, lambda n, k: ov[:, 4 * k:4 * k + 4, n * 512:(n + 1) * 512], True)
    cx.S.emit()
    return cx.nc


W_OFF = {"moba": (0, 384), "fox": (384, 384), "hgrn": (768, 512), "pool": (1280, 128)}
W_MIXCOLS = 1408
SM_WFF, SM_G0, SM_G1, SM_HNORM, SM_PSCALE, SM_FB = 0, 16, 17, 18, 19, 20
N_SMALL = 24


class MixEnv:
    pass


def project(cx, cm, env, w, fm_blocks, tm, row=None, pre_chunk=None):
    nb = 0
    for n in range(SEQ // 512):
        hs = env.hs[n % 2]
        for k in range(4):
            cx.load(hs[:, 4 * k:4 * k + 4, :], env.hn_chunk(n, k), hs.all)
        if pre_chunk is not None:
            pre_chunk(n)
        for (c0, handler) in fm_blocks:
            bank = cm.banks[nb % 4]
            nb += 1
            for c in range(NCH):
                cx.mm(bank[:, :], w[:, c, c0:c0 + 128], hs[:, c, :], c == 0, c == NCH - 1,
                      w.all + hs.all, bank.all)
            handler(n, bank)
        if tm is not None:
            c0, ncols, handler = tm
            for tl in range(4):
                bank = cm.banks[nb % 4]
                nb += 1
                for c in range(NCH):
                    cx.mm(bank[:, 0:ncols], hs[:, c, tl * 128:(tl + 1) * 128], w[:, c, c0:c0 + ncols],
                          c == 0, c == NCH - 1, w.all + hs.all, bank.all)
                handler(n * 4 + tl, bank)
        if row is not None:
            lfn, rreads, handler = row
            bank = cm.banks[nb % 4]
            nb += 1
            for c in range(NCH):
                cx.mm(bank[0:1, :], lfn(c), hs[:, c, :], c == 0, c == NCH - 1, rreads + hs.all, bank.all)
            handler(n, bank)


def load_w(cx, env, name, eng="pool"):
    c0, nc_ = W_OFF[name]
    w = cx.sb([128, NCH, nc_], BF16, name="w_" + name)
    for c in range(0, NCH, 4):
        cx.load(w[:, c:c + 4, :], env.wmix[:, c:c + 4, c0:c0 + nc_], w.all, q=eng)
    return w


def softmax_finish(cx, env, obank, dbank, ncols, out_ap_dram, k):
    rc = env.rc[k % 2]
    yt = env.yt[k % 2]
    cx.S.dve(lambda e: e.reciprocal(out=rc[:, 0:ncols], in_=dbank[:, 0:ncols]), dbank.all, rc.all)
    cx.tt(yt[:, 0:ncols], obank[:, 0:ncols], rc[:, 0:ncols], ALU.mult, obank.all + rc.all, yt.all)
    cx.store(out_ap_dram, yt[:, 0:ncols], yt.all, is_output=env.y_is_output)


def mix_fox(cx, cm, env):
    S = cx.S
    w = load_w(cx, env, "fox")
    fQ = cx.sb([128, SEQ], BF16, name="fQ")
    fK = cx.sb([128, SEQ], BF16, name="fK")
    fV = cx.sb([128, 32, 128], BF16, name="fV")
    wff = cx.sb([128, NCH], BF16, name="wff")
    cx.copy(wff[:, :], env.small[:, SM_WFF:SM_WFF + NCH], env.small.all, wff.all)
    nfb = cx.sb([128, 1], F32, name="nfb")
    cx.ts(nfb[:, :], env.small[:, SM_FB:SM_FB + 1], -1.0, None, ALU.mult, None, env.small.all, nfb.all)
    sprow = cx.sb([1, SEQ], F32, name="sprow")
    cprow = cx.sb([1, SEQ], F32, name="cprow")
    nrh = cx.sb([1, SEQ], BF16, name="nrh")
    nrl = cx.sb([1, SEQ], BF16, name="nrl")
    rowtmp = cx.sb([1, 512], F32, name="rowtmp")

    def h_q(n, bank):
        S.act(lambda e: e.mul(out=fQ[:, n * 512:(n + 1) * 512], in_=bank[:, :], mul=SCALE), bank.all, fQ.all)

    def h_k(n, bank):
        cx.copy(fK[:, n * 512:(n + 1) * 512], bank[:, :], bank.all, fK.all)

    def h_v(tl, bank):
        cx.copy(fV[:, tl, :], bank[:, 0:128], bank.all, fV.all, eng="act" if tl % 2 else "dve")

    def h_row(n, bank):
        cx.activation(rowtmp[0:1, :], bank[0:1, :], AF.Exp, bank.all + nfb.all, rowtmp.all, bias=nfb[0:1, 0:1], scale=-1.0)
        cx.activation(sprow[0:1, n * 512:(n + 1) * 512], rowtmp[0:1, :], AF.Ln, rowtmp.all + cm.c32.all, sprow.all,
                      bias=cm.one_ap[0:1, 0:1])

    project(cx, cm, env, w, [(0, h_q), (128, h_k)], (256, 128, h_v),
            row=(lambda c: wff[:, c:c + 1], wff.all, h_row))

    cx.memset(nrh[0:1, :], 1.0, nrh.all, eng="dve")
    S.dve(lambda e: e.tensor_tensor_scan(out=cprow[0:1, :], data0=nrh[0:1, :], data1=sprow[0:1, :], initial=0.0,
                                         op0=ALU.mult, op1=ALU.add), nrh.all + sprow.all, cprow.all)
    sm = cm.banks[6]
    cpv = cprow[0:1, :].rearrange("o (a b) -> o a b", b=512)
    cx.mm(sm[:, 0:8], cm.ones32[0:1, 0:128], cpv[:, :, 0], True, True, cm.c32.all + cprow.all, sm.all)
    for kt in range(32):
        cx.mm(sm[:, 8 + kt:9 + kt], cprow[0:1, kt * 128:(kt + 1) * 128], cm.ones32[0:1, 0:1], True, True,
              cm.c32.all + cprow.all, sm.all)
    rbcp = cx.sb([128, 40], F32, name="rbcp")
    cx.copy(rbcp[:, :], sm[:, 0:40], sm.all, rbcp.all)
    for qc in range(8):
        sl = slice(qc * 512, (qc + 1) * 512)
        cx.ts(sprow[0:1, sl], cprow[0:1, sl], cprow[0:1, qc * 512:qc * 512 + 1], -1.0, ALU.subtract, ALU.mult,
              cprow.all, sprow.all)
    cx.copy(nrh[0:1, :], sprow[0:1, :], sprow.all, nrh.all)
    cx.tt(nrl[0:1, :], sprow[0:1, :], nrh[0:1, :], ALU.subtract, sprow.all + nrh.all, nrl.all)

    biasq = [cx.sb([128, 32], F32, name=f"biasq{i}") for i in range(2)]
    pTs = [cx.sb([128, 512], BF16, name=f"fpT{i}") for i in range(3)]
    it = 0
    for qc in range(8):
        sl = slice(qc * 512, (qc + 1) * 512)
        bq = biasq[qc % 2]
        cx.ts(bq[:, :], rbcp[:, 8:40], rbcp[:, qc:qc + 1], None, ALU.subtract, None, rbcp.all, bq.all)
        obank = cm.banks[2 + qc % 2]
        dbank = cm.banks[4 + qc % 2]
        nkt = 4 * (qc + 1)
        for kt in range(nkt):
            sbank = cm.banks[kt % 2]
            a = kt - 4 * qc
            cx.mm(sbank[:, :], fK[:, kt * 128:(kt + 1) * 128], fQ[:, sl], True, False, fK.all + fQ.all, sbank.all)
            cx.mm(sbank[:, :], cm.ones[0:1, 0:128], nrh[0:1, sl], False, False, nrh.all, sbank.all)
            cx.mm(sbank[:, :], cm.ones[0:1, 0:128], nrl[0:1, sl], False, a < 0, nrl.all, sbank.all)
            if a >= 0:
                cx.mm(sbank[:, :], cm.ident, env.cmask[:, a, :], False, True, env.cmask.all, sbank.all)
            pT = pTs[it % 3]
            it += 1
            cx.activation(pT[:, :], sbank[:, :], AF.Exp, sbank.all + bq.all, pT.all, bias=bq[:, kt:kt + 1])
            cx.mm(obank[:, :], fV[:, kt, :], pT[:, :], kt == 0, kt == nkt - 1, fV.all + pT.all, obank.all)
            cx.mm(dbank[:, :], cm.ones, pT[:, :], kt == 0, kt == nkt - 1, pT.all, dbank.all)
        softmax_finish(cx, env, obank, dbank, 512, env.y_out(1, qc * 512, 512), qc)


def mix_moba(cx, cm, env):
    S = cx.S
    w = load_w(cx, env, "moba")
    mQ = cx.sb([128, SEQ], BF16, name="mQ")
    mK = cx.sb([128, SEQ], BF16, name="mK")
    mV = cx.sb([128, 32, 128], BF16, name="mV")
    perm = cx.sb([128, 128], BF16, name="perm")
    cx.load(perm[:, :], env.perm, perm.all)
    rC = [cx.sb([128, 512], F32, name=f"rC{i}") for i in range(2)]
    rS = [cx.sb([128, 512], F32, name=f"rS{i}") for i in range(2)]
    xb = [cx.sb([128, 512], BF16, name=f"xb{i}") for i in range(2)]
    t1 = [cx.sb([128, 512], F32, name=f"t1{i}") for i in range(2)]
    t2 = [cx.sb([128, 512], F32, name=f"t2{i}") for i in range(2)]
    cnt = [0]

    def pre_chunk(n):
        cx.load(rC[n % 2][:, :], env.ropeC[:, n * 512:(n + 1) * 512], rC[n % 2].all)
        cx.load(rS[n % 2][:, :], env.ropeS[:, n * 512:(n + 1) * 512], rS[n % 2].all)

    def rope_handler(dst, sc):
        def h(n, bank):
            k = cnt[0] % 2
            cnt[0] += 1
            sl = slice(n * 512, (n + 1) * 512)
            if DEBUG_STOP == 10:
                cx.copy(dst[:, sl], bank[:, :], bank.all, dst.all)
                return
            cx.copy(xb[k][:, :], bank[:, :], bank.all, xb[k].all, eng="act")
            swb = cm.banks[4 + k]
            cx.mm(swb[:, :], perm[:, :], xb[k][:, :], True, True, perm.all + xb[k].all, swb.all)
            if DEBUG_STOP == 11:
                cx.copy(dst[:, sl], swb[:, :], swb.all, dst.all)
                return
            if DEBUG_STOP == 13:
                cx.stt(t1[k][:, :], bank[:, :], sc, t2[k][:, :], ALU.mult, ALU.mult, bank.all + t2[k].all, t1[k].all)
            else:
                cx.stt(t1[k][:, :], bank[:, :], sc, rC[n % 2][:, :], ALU.mult, ALU.mult, bank.all + rC[n % 2].all, t1[k].all)
            if DEBUG_STOP in (12, 13):
                cx.copy(dst[:, sl], t1[k][:, :], t1[k].all, dst.all)
                return
            cx.stt(t2[k][:, :], swb[:, :], sc, rS[n % 2][:, :], ALU.mult, ALU.mult, swb.all + rS[n % 2].all, t2[k].all)
            cx.tt(dst[:, sl], t1[k][:, :], t2[k][:, :], ALU.add, t1[k].all + t2[k].all, dst.all, eng=ROPE_ADD_ENG)
        return h

    def h_v(tl, bank):
        cx.copy(mV[:, tl, :], bank[:, 0:128], bank.all, mV.all, eng="act")

    project(cx, cm, env, w, [(0, rope_handler(mQ, SCALE)), (128, rope_handler(mK, 1.0))], (256, 128, h_v),
            pre_chunk=pre_chunk)

    if DEBUG_STOP in (1, 10, 11, 12, 13):
        return
    kb32 = cx.sb([128, 16], F32, name="kb32")
    kbT = cx.sb([128, 16], BF16, name="kbT")
    S.dve(lambda e: e.tensor_reduce(out=kb32[:, :], in_=mK[:, :].rearrange("p (j k) -> p j k", k=256),
                                    axis=AX.X, op=ALU.add), mK.all, kb32.all)
    cx.copy(kbT[:, :], kb32[:, :], kb32.all, kbT.all)
    gb = cm.banks[6]
    for qt in range(32):
        cx.mm(gb[:, qt * 16:(qt + 1) * 16], mQ[:, qt * 128:(qt + 1) * 128], kbT[:, :], True, True,
              mQ.all + kbT.all, gb.all)
    g_sb = cx.sb([128, 32, 16], F32, name="g_sb")
    cx.copy(g_sb[:, :, :], gb[:, :].rearrange("p (a b) -> p a b", b=16), gb.all, g_sb.all)
    if DEBUG_STOP == 2:
        return
    S.pool(lambda e: e.affine_select(out=g_sb[:, :, :], in_=g_sb[:, :, :], pattern=[[1, 16], [0, 2], [-1, 16]],
                                     compare_op=ALU.is_ge, fill=-1e30, base=-1, channel_multiplier=0),
           g_sb.all, g_sb.all)
    if DEBUG_STOP == 3:
        return
    m8 = cx.sb([128, 32, 8], F32, name="m8")
    for qt in range(32):
        S.dve(lambda e, qt=qt: e.max(out=m8[:, qt, :], in_=g_sb[:, qt, :]), g_sb.all, m8.all)
    thr = cx.sb([128, 32, 1], F32, name="thr")
    cx.ts(thr[:, :, :], m8[:, :, 2:3], -1e29, None, ALU.max, None, m8.all, thr.all)
    nm = cx.sb([128, 32, 16], F32, name="nm")
    cx.tt(nm[:, :, :], g_sb[:, :, :], thr[:, :, :].to_broadcast([128, 32, 16]), ALU.is_lt, g_sb.all + thr.all, nm.all)
    cx.ts(nm[:, :, :], nm[:, :, :], NEG, None, ALU.mult, None, nm.all, nm.all)
    if DEBUG_STOP == 4:
        return
    nmT = cx.sb([16, SEQ], BF16, name="nmT")
    for grp in range(8):
        tb = cm.banks[grp % 2]
        for i in range(4):
            qt = grp * 4 + i
            cx.transpose(tb[0:16, i * 128:(i + 1) * 128], nm[:, qt, :], cm.ident32, nm.all + cm.c32.all, tb.all)
        cx.copy(nmT[0:16, grp * 512:(grp + 1) * 512], tb[0:16, :], tb.all, nmT.all, eng="act" if grp % 2 else "dve")

    if DEBUG_STOP == 5:
        return
    pTs = [cx.sb([128, 256], BF16, name=f"mpT{i}") for i in range(3)]
    it = 0
    for qb in range(16):
        sl = slice(qb * 256, (qb + 1) * 256)
        obank = cm.banks[2 + qb % 2]
        dbank = cm.banks[4 + qb % 2]
        nkt = 2 * qb + 2
        for kt in range(nkt):
            sbank = cm.banks[kt % 2]
            cx.mm(sbank[:, 0:256], mK[:, kt * 128:(kt + 1) * 128], mQ[:, sl], True, False, mK.all + mQ.all, sbank.all)
            if kt < 2 * qb:
                cx.mm(sbank[:, 0:256], env.esel[0:16, kt // 2, :], nmT[0:16, sl], False, True,
                      env.esel.all + nmT.all, sbank.all)
            else:
                cx.mm(sbank[:, 0:256], cm.ident, env.cmask[:, kt - 2 * qb, 0:256], False, True, env.cmask.all, sbank.all)
            pT = pTs[it % 3]
            it += 1
            cx.activation(pT[:, :], sbank[:, 0:256], AF.Exp, sbank.all, pT.all)
            cx.mm(obank[:, 0:256], mV[:, kt, :], pT[:, :], kt == 0, kt == nkt - 1, mV.all + pT.all, obank.all)
            cx.mm(dbank[:, 0:256], cm.ones, pT[:, :], kt == 0, kt == nkt - 1, pT.all, dbank.all)
        softmax_finish(cx, env, obank, dbank, 256, env.y_out(0, qb * 256, 256), qb)


def mix_hgrn(cx, cm, env, layer):
    S = cx.S
    w = load_w(cx, env, "hgrn")
    hq = cx.sb([128, SEQ], F32, name="hq")
    lf = cx.sb([128, SEQ], F32, name="lf")
    hk = cx.sb([128, SEQ], BF16, name="hk")
    sg = cx.sb([128, SEQ], BF16, name="sg")
    hv = cx.sb([128, 32, 128], BF16, name="hv")
    lb = cx.sb([128, 2], F32, name="lb")
    if layer == 0:
        cx.memset(lb[:, 0:1], 0.0, lb.all, eng="dve")
        cx.memset(lb[:, 1:2], 1.0, lb.all, eng="dve")
    else:
        ee = cx.sb([128, 4], F32, name="ee")
        cx.activation(ee[:, 0:2], env.small[:, SM_G0:SM_G0 + 2], AF.Exp, env.small.all, ee.all)
        cx.tt(ee[:, 2:3], ee[:, 0:1], ee[:, 1:2], ALU.add, ee.all, ee.all)
        S.dve(lambda e: e.reciprocal(out=ee[:, 3:4], in_=ee[:, 2:3]), ee.all, ee.all)
        cx.tt(lb[:, 0:1], ee[:, 1:2], ee[:, 3:4], ALU.mult, ee.all, lb.all)
        cx.ts(lb[:, 1:2], lb[:, 0:1], -1.0, 1.0, ALU.mult, ALU.add, lb.all, lb.all)
    sgm = [cx.sb([128, 512], F32, name=f"sgm{i}") for i in range(2)]
    ff_ = [cx.sb([128, 512], F32, name=f"ff{i}") for i in range(2)]

    def h_q(n, bank):
        cx.copy(hq[:, n * 512:(n + 1) * 512], bank[:, :], bank.all, hq.all)

    def h_f(n, bank):
        sl = slice(n * 512, (n + 1) * 512)
        a, f = sgm[n % 2], ff_[n % 2]
        cx.activation(a[:, :], bank[:, :], AF.Sigmoid, bank.all, a.all)
        cx.ts(f[:, :], a[:, :], lb[:, 1:2], lb[:, 0:1], ALU.mult, ALU.add, a.all + lb.all, f.all)
        cx.activation(lf[:, sl], f[:, :], AF.Ln, f.all, lf.all)
        cx.ts(hk[:, sl], f[:, :], -1.0, 1.0, ALU.mult, ALU.add, f.all, hk.all, eng="pool")

    def h_g(n, bank):
        sl = slice(n * 512, (n + 1) * 512)
        a = sgm[n % 2]
        cx.activation(a[:, :], bank[:, :], AF.Sigmoid, bank.all, a.all)
        cx.tt(sg[:, sl], bank[:, :], a[:, :], ALU.mult, bank.all + a.all, sg.all)

    def h_v(tl, bank):
        cx.copy(hv[:, tl, :], bank[:, 0:128], bank.all, hv.all, eng="act" if tl % 2 else "dve")

    project(cx, cm, env, w, [(0, h_q), (128, h_f), (256, h_g)], (384, 128, h_v))

    rm = cx.sb([128, SEQ], BF16, name="rm")
    cx.memset(rm[:, :], 1.0, rm.all)
    cx.memset(rm[:, :].rearrange("p (a b) -> p a b", b=64)[:, :, 0:1], 0.0, rm.all)
    bb = cx.sb([128, SEQ], F32, name="bb")
    S.dve(lambda e: e.tensor_tensor_scan(out=bb[:, :], data0=rm[:, :], data1=lf[:, :], initial=0.0,
                                         op0=ALU.mult, op1=ALU.add), rm.all + lf.all, bb.all)
    bv = bb[:, :].rearrange("p (a b) -> p a b", b=64)
    sm = cx.sb([128, 5, 64], F32, name="hsm")
    cx.copy(sm[:, 0, :], bv[:, :, 31], bb.all, sm.all)
    cx.activation(sm[:, 4, :], bv[:, :, 31], AF.Exp, bb.all, sm.all)
    cx.activation(sm[:, 1, :], bv[:, :, 63], AF.Exp, bb.all, sm.all)
    cx.tt(sm[:, 3, :], bv[:, :, 63], sm[:, 0, :], ALU.subtract, bb.all + sm.all, sm.all)
    cx.activation(sm[:, 2, :], sm[:, 3, :], AF.Exp, sm.all, sm.all)
    cx.tt(bv, bv, sm[:, 0, :].unsqueeze(2).to_broadcast([128, 64, 64]), ALU.subtract, bb.all + sm.all, bb.all)
    E = lf
    qt_ = cx.sb([128, SEQ], BF16, name="qtil")
    kt_ = cx.sb([128, SEQ], BF16, name="ktil")
    kh_ = cx.sb([128, SEQ], BF16, name="khat")
    cx.activation(E[:, :], bb[:, :], AF.Exp, bb.all, E.all)
    cx.tt(qt_[:, :], hq[:, :], E[:, :], ALU.mult, hq.all + E.all, qt_.all)
    cx.activation(E[:, :], bb[:, :], AF.Exp, bb.all, E.all, scale=-1.0)
    cx.tt(kt_[:, :], hk[:, :], E[:, :], ALU.mult, hk.all + E.all, kt_.all)
    cx.tt(kh_[:, :].rearrange("p (a b) -> p a b", b=64), kt_[:, :].rearrange("p (a b) -> p a b", b=64),
          sm[:, 2, :].unsqueeze(2).to_broadcast([128, 64, 64]), ALU.mult, kt_.all + sm.all, kh_.all, eng="pool")
    khtm = cx.sb([128, 32, 128], BF16, name="khtm")
    for grp in range(4):
        for i in range(8):
            tl = grp * 8 + i
            cx.transpose(cm.bankb[:, i * 128:(i + 1) * 128], kh_[:, tl * 128:(tl + 1) * 128], cm.ident,
                         kh_.all, cm.bankb.all)
        cx.copy(khtm[:, grp * 8:(grp + 1) * 8, :], cm.bankb[:, :].rearrange("p (a b) -> p a b", b=128),
                cm.bankb.all, khtm.all, eng="act" if grp % 2 else "dve")
    attT = cx.sb([128, 32, 128], BF16, nsub=8, name="attT")
    for grp in range(8):
        ab = cm.banks[grp % 2]
        for i in range(4):
            tl = grp * 4 + i
            ts_ = slice(tl * 128, (tl + 1) * 128)
            cx.mm(ab[:, i * 128:(i + 1) * 128], kt_[:, ts_], qt_[:, ts_], True, True, kt_.all + qt_.all, ab.all)
        cx.tt(attT[:, grp * 4:(grp + 1) * 4, :], ab[:, :].rearrange("p (a b) -> p a b", b=128),
              env.hmask[:, :].unsqueeze(1).to_broadcast([128, 4, 128]), ALU.mult, ab.all + env.hmask.all, [attT.b[grp]])
    S32 = cx.sb([128, 2, 128], F32, nsub=2, name="S32")
    Sb = cx.sb([128, 4, 128], BF16, nsub=4, name="Sb")
    cx.memset(S32[:, 0, :], 0.0, [S32.b[0]], eng="dve")
    cx.memset(Sb[:, 0, :], 0.0, [Sb.b[0]], eng="dve")
    oT = [cx.sb([128, 512], F32, name=f"oT{i}") for i in range(2)]
    sq = [cx.sb([128, 512], BF16, name=f"osq{i}") for i in range(2)]
    rs = [cx.sb([128, 512], F32, name=f"ors{i}") for i in range(2)]
    yo = [cx.sb([128, 512], BF16, name=f"oyo{i}") for i in range(2)]
    for tl in range(32):
        ob = cm.banks[2 + (tl // 4) % 2]
        for half in range(2):
            c = 2 * tl + half
            ps = slice(half * 64, half * 64 + 64)
            mslot = 0
            mb = cm.banks[4 + c % 2]
            cx.mm(mb[:, mslot * 128:(mslot + 1) * 128], khtm[ps, tl, :], hv[ps, tl, :], True, True,
                  khtm.all + hv.all, [mb.b[mslot]])
            oc = slice((tl % 4) * 128 + half * 64, (tl % 4) * 128 + half * 64 + 64)
            tsl = slice(c * 64, c * 64 + 64)
            cx.mm(ob[:, oc], hv[:, tl, :], attT[:, tl, half * 64:half * 64 + 64], True, False,
                  hv.all + [attT.b[tl // 4]], [ob.b[tl % 4]])
            cx.mm(ob[:, oc], Sb[:, c % 4, :], qt_[:, tsl], False, True, [Sb.b[c % 4]] + qt_.all, [ob.b[tl % 4]])
            cx.stt(S32[:, (c + 1) % 2, :], S32[:, c % 2, :], sm[:, 1, c:c + 1], mb[:, mslot * 128:(mslot + 1) * 128],
                   ALU.mult, ALU.add, [S32.b[c % 2], mb.b[mslot]] + sm.all, [S32.b[(c + 1) % 2]])
            if c + 1 < 64:
                cx.S.act(lambda e, c=c: e.mul(out=Sb[:, (c + 1) % 4, :], in_=S32[:, (c + 1) % 2, :], mul=sm[:, 4, c + 1:c + 2]),
                         [S32.b[(c + 1) % 2]] + sm.all, [Sb.b[(c + 1) % 4]])
        if tl % 4 == 3:
            n = tl // 4
            k = n % 2
            sl = slice(n * 512, (n + 1) * 512)
            cx.copy(oT[k][:, :], ob[:, :], ob.all, oT[k].all)
            cx.activation(sq[k][:, :], ob[:, :], AF.Square, ob.all, sq[k].all)
            nb_ = cm.banks[6]
            cx.mm(nb_[:, :], cm.ones, sq[k][:, :], True, True, sq[k].all, nb_.all)
            cx.activation(rs[k][:, :], nb_[:, :], AF.Ln, nb_.all + cm.epst.all, rs[k].all, bias=cm.eps_ap, scale=1.0 / HD)
            cx.activation(rs[k][:, :], rs[k][:, :], AF.Exp, rs[k].all, rs[k].all, scale=-0.5)
            cx.stt(oT[k][:, :], oT[k][:, :], env.small[:, SM_HNORM:SM_HNORM + 1], rs[k][:, :], ALU.mult, ALU.mult,
                   oT[k].all + rs[k].all + env.small.all, oT[k].all)
            cx.tt(yo[k][:, :], oT[k][:, :], sg[:, sl], ALU.mult, oT[k].all + sg.all, yo[k].all, eng="pool")
            cx.store(env.y_out(2, n * 512, 512), yo[k][:, :], yo[k].all, is_output=env.y_is_output)


def mix_pool(cx, cm, env):
    w = load_w(cx, env, "pool")
    pin = cx.sb([128, 32, 128], BF16, nsub=32, name="pin")
    pM = cx.sb([128, 3, 128], BF16, name="pM")
    cx.load(pM[:, :, :], env.poolM, pM.all)
    pw = cx.sb([128, 128], BF16, name="pw")
    cx.load(pw[:, :], env.poolw, pw.all, q="pool")

    def h_p(tl, bank):
        cx.copy(pin[:, tl, :], bank[:, 0:128], bank.all, [pin.b[tl]], eng="act" if tl % 2 else "dve")

    project(cx, cm, env, w, [], (0, 128, h_p))
    pt = [cx.sb([128, 512], BF16, name=f"ppt{i}") for i in range(2)]
    yo = [cx.sb([128, 512], BF16, name=f"pyo{i}") for i in range(2)]
    for n in range(8):
        pb = cm.banks[n % 2]
        for i in range(4):
            tl = n * 4 + i
            cs = slice(i * 128, (i + 1) * 128)
            cx.mm(pb[:, cs], pin[:, tl, :], pM[:, 0 if tl == 0 else 1, :], True, tl == 0, [pin.b[tl]] + pM.all, [pb.b[i]])
            if tl > 0:
                cx.mm(pb[:, cs], pin[:, tl - 1, :], pM[:, 2, :], False, True, [pin.b[tl - 1]] + pM.all, [pb.b[i]])
        k = n % 2
        cx.copy(pt[k][:, :], pb[:, :], pb.all, pt[k].all, eng="act")
        ob = cm.banks[2 + n % 2]
        cx.mm(ob[:, :], pw[:, :], pt[k][:, :], True, True, pw.all + pt[k].all, ob.all)
        cx.ts(yo[k][:, :], ob[:, :], env.small[:, SM_PSCALE:SM_PSCALE + 1], None, ALU.mult, None,
              ob.all + env.small.all, yo[k].all)
        cx.store(env.y_out(3, n * 512, 512), yo[k][:, :], yo[k].all, is_output=env.y_is_output)


def setup_common32(cx, cm, c32_ap):
    cm.c32 = cx.sb([128, 256], F32, name="c32")
    cx.load(cm.c32[:, :], c32_ap, cm.c32.all)
    cm.ident32 = cm.c32[:, 0:128]
    cm.ones32 = cm.c32[:, 128:256]
    cm.one_ap = cm.c32[:, 128:129]


def mix_globals(cx, env, cmask_d, esel_d, hmask_d):
    env.cmask = cx.sb([128, 4, 512], BF16, name="cmask")
    cx.load(env.cmask[:, :, :], cmask_d, env.cmask.all)
    env.esel = cx.sb([16, 16, 128], BF16, name="esel")
    cx.load(env.esel[:, :, :], esel_d, env.esel.all)
    env.hmask = cx.sb([128, 128], BF16, name="hmask")
    cx.load(env.hmask[:, :], hmask_d, env.hmask.all)


def phase_mix(cx, cm, env, layer, small_d, which=("fox", "moba", "hgrn", "pool")):
    with contextlib.ExitStack() as outer:
        cx.scope = outer
        env.small = cx.sb([128, N_SMALL], F32, name="small")
        cx.load(env.small[:, :], small_d, env.small.all)
        env.hs = [cx.sb([128, NCH, 512], BF16, name=f"hs{i}") for i in range(2)]
        env.rc = [cx.sb([128, 512], F32, name=f"rc{i}") for i in range(2)]
        env.yt = [cx.sb([128, 512], BF16, name=f"yt{i}") for i in range(2)]
        fns = {"fox": lambda: mix_fox(cx, cm, env), "moba": lambda: mix_moba(cx, cm, env),
               "hgrn": lambda: mix_hgrn(cx, cm, env, layer), "pool": lambda: mix_pool(cx, cm, env)}
        for name in which:
            with contextlib.ExitStack() as sc:
                cx.scope = sc
                fns[name]()
                cx.S.fence()
            cx.scope = outer
    cx.scope = None


def build_mix(layer, which=("fox", "moba", "hgrn", "pool")):
    cx = Ctx()
    env = MixEnv()
    env.hnT = cx.din("hnT", [D_MODEL, SEQ], BF16)
    env.wmix = cx.din("wmix", [128, NCH, W_MIXCOLS], F32)
    small_d = cx.din("small", [128, N_SMALL], F32)
    env.poolw = cx.din("poolw", [128, 128], F32)
    consts = cx.din("consts", [128, 256], BF16)
    c32 = cx.din("c32", [128, 256], F32)
    cmask_d = cx.din("cmask", [128, 4, 512], BF16)
    env.perm = cx.din("perm", [128, 128], BF16)
    env.ropeC = cx.din("ropeC", [128, SEQ], F32)
    env.ropeS = cx.din("ropeS", [128, SEQ], F32)
    esel_d = cx.din("esel", [16, 16, 128], BF16)
    hmask_d = cx.din("hmask", [128, 128], BF16)
    env.poolM = cx.din("poolM", [128, 3, 128], BF16)
    env.yT = cx.dout("yT", [4, 128, SEQ], BF16)
    hview_ = env.hnT.rearrange("(c p) t -> p c t", p=128)
    env.hn_chunk = lambda n, k: hview_[:, 4 * k:4 * k + 4, n * 512:(n + 1) * 512]
    env.y_out = lambda bi, t0, ncols: env.yT[bi, :, t0:t0 + ncols]
    env.y_is_output = True
    cm = Common(cx, consts)
    add_eps(cx, cm)
    setup_common32(cx, cm, c32)
    mix_globals(cx, env, cmask_d, esel_d, hmask_d)
    phase_mix(cx, cm, env, layer, small_d, which)
    cx.S.emit()
    return cx.nc


CC_GROUPS = [[0, 1, 2, 3], [4, 5, 6, 7]]


def _core_quarter(cx, e):
    if getattr(cx, "_q", None) is None:
        cx._q = e.snap(e.partition_id() % 4, min_val=0, max_val=3)
    return cx._q


def build_fused(depth):
    cx = Ctx()
    nc = cx.nc
    S = cx.S
    consts = cx.din("consts", [128, 256], BF16)
    c32 = cx.din("c32", [128, 256], F32)
    xT = cx.din("xT", [D_MODEL, TOK], F32)
    memT = cx.din("memT", [D_MODEL, NMEM], F32)
    gpre = cx.din("gpre", [128, NCH], F32)
    cmask_d = cx.din("cmask", [128, 4, 512], BF16)
    esel_d = cx.din("esel", [16, 16, 128], BF16)
    hmask_d = cx.din("hmask", [128, 128], BF16)
    env = MixEnv()
    env.perm = cx.din("perm", [128, 128], BF16)
    env.ropeC = cx.din("ropeC", [128, SEQ], F32)
    env.ropeS = cx.din("ropeS", [128, SEQ], F32)
    env.poolM = cx.din("poolM", [128, 3, 128], BF16)
    wmix_d = [cx.din(f"wmix_{l}", [128, NCH, W_MIXCOLS], F32) for l in range(depth)]
    small_d = [cx.din(f"small_{l}", [128, N_SMALL], F32) for l in range(depth)]
    poolw_d = [cx.din(f"poolw_{l}", [128, 128], F32) for l in range(depth)]
    ios = []
    for l in range(depth):
        io = TokIO()
        tok_weight_inputs(cx, io, f"_{l}")
        io.memT = memT
        ios.append(io)
    outT = cx.dout("outT", [D_MODEL, TOK], F32)
    hn_own = [nc.dram_tensor(f"hn_own{k}", [512, TOK], BF16, kind="Internal").ap() for k in range(4)]
    hn_all = [nc.dram_tensor(f"hn_all{k}", [4 * 512, TOK], BF16, kind="Internal").ap() for k in range(4)]
    y_own = [nc.dram_tensor(f"y_own{n}", [512, TOK], BF16, kind="Internal").ap() for n in range(4)]
    y_all = [nc.dram_tensor(f"y_all{n}", [4 * 512, TOK], BF16, kind="Internal").ap() for n in range(4)]
    h_res = nc.dram_tensor("h_res", [D_MODEL, TOK], F32, kind="Internal").ap()

    cm = Common(cx, consts)
    add_eps(cx, cm)
    setup_common32(cx, cm, c32)
    mix_globals(cx, env, cmask_d, esel_d, hmask_d)
    S.sp_init = lambda e: _core_quarter(cx, e)

    with contextlib.ExitStack() as sc:
        cx.scope = sc
        hn_own_v = [a.rearrange("(c p) t -> p c t", p=128) for a in hn_own]
        phase_pre(cx, cm, xT, gpre, lambda n, k: hn_own_v[k][:, :, n * 512:(n + 1) * 512], False)
        S.fence()
    cx.scope = None

    hn_all_v = [a.rearrange("(r c p) t -> r p c t", r=4, p=128) for a in hn_all]
    y_own_v = [a.rearrange("(tq d) t -> tq d t", tq=4) for a in y_own]
    y_all_v = [a.rearrange("(hh tq d) t -> tq d hh t", hh=4, tq=4) for a in y_all]
    env.hn_chunk = lambda n, k: hn_all_v[k][n // 2][:, :, (n % 2) * 512:(n % 2 + 1) * 512]
    env.y_out = lambda bi, t0, ncols: y_own_v[bi][t0 // TOK][:, t0 % TOK:t0 % TOK + ncols]
    env.y_is_output = False
    h_res_v = h_res.rearrange("(c p) t -> p c t", p=128)
    x_v = xT.rearrange("(c p) t -> p c t", p=128)
    out_v = outT.rearrange("(c p) t -> p c t", p=128)

    for l in range(depth):
        last = l == depth - 1
        for k in range(4):
            S.collective(lambda e, k=k: e.collective_compute("AllGather", ALU.bypass, replica_groups=CC_GROUPS,
                                                             ins=[hn_own[k].opt()], outs=[hn_all[k].opt()]), [], [])
        S.fence()
        env.wmix = wmix_d[l]
        env.poolw = poolw_d[l]
        phase_mix(cx, cm, env, l, small_d[l])
        for n in range(4):
            S.collective(lambda e, n=n: e.collective_compute("AllGather", ALU.bypass, replica_groups=CC_GROUPS,
                                                             ins=[y_own[n].opt()], outs=[y_all[n].opt()]), [], [])
        S.fence()
        io = ios[l]
        src_v = x_v if l == 0 else h_res_v
        dst_v = out_v if last else h_res_v
        io.hT_in = lambda hf, c0, c1, src_v=src_v: src_v[:, c0:c1, hf * HALF:(hf + 1) * HALF]
        io.hn_in = lambda hf, k: hn_own_v[k][:, :, hf * HALF:(hf + 1) * HALF]

        def ybr_load(dst, hf, wb):
            for n in range(4):
                def f(e, n=n):
                    q = _core_quarter(cx, e)
                    src = y_all_v[n][bass.ds(q, 1)][0][:, :, hf * HALF:(hf + 1) * HALF]
                    return e.dma_start(out=dst[:, n * 4:(n + 1) * 4, :], in_=src)
                S.dma("sp", f, [], list(wb))
        io.ybr_load = ybr_load
        io.hT_out = lambda hf, c0, c1, dst_v=dst_v: dst_v[:, c0:c1, hf * HALF:(hf + 1) * HALF]
        io.hn_out = lambda hf, k: hn_own_v[k][:, :, hf * HALF:(hf + 1) * HALF]
        io.h_is_output = last
        io.hn_is_output = False
        io.write_hn_when_last = False
        with contextlib.ExitStack() as sc:
            cx.scope = sc
            phase_tok(cx, cm, io, l, last)
            S.fence()
        cx.scope = None
    S.emit()
    return nc


IN_OFF = {"mq": 0, "mk": 512, "mv": 1024, "fq": 1536, "fk": 2048, "fv": 2560, "ff": 3072,
          "hq": 3076, "hf": 3588, "hi": 4100, "hg": 4612, "pin": 5124, "gl": 5636}
_HC = {}


def host_consts():
    if _HC:
        return _HC
    f32 = np.float32
    c = np.zeros((128, 256), f32)
    c[:, :128] = np.eye(128)
    c[:, 128:] = 1
    _HC["c32"] = c
    _HC["consts"] = c.astype(NPBF)
    k = np.arange(128)[:, None, None]
    a = np.arange(4)[None, :, None]
    q = np.arange(512)[None, None, :]
    _HC["cmask"] = np.where(q >= 128 * a + k, 0.0, NEG).astype(NPBF)
    perm = np.zeros((128, 128), f32)
    d = np.arange(128)
    perm[(d + 64) % 128, d] = 1
    _HC["perm"] = perm.astype(NPBF)
    inv_freq = (f32(10000.0) ** (-np.arange(64, dtype=f32) * f32(2.0) / f32(128))).astype(f32)
    ang = (np.arange(SEQ, dtype=f32)[None, :] * inv_freq[:, None]).astype(f32)
    cos, sin = np.cos(ang).astype(f32), np.sin(ang).astype(f32)
    _HC["ropeC"] = np.ascontiguousarray(np.concatenate([cos, cos], 0))
    _HC["ropeS"] = np.ascontiguousarray(np.concatenate([-sin, sin], 0))
    es = np.zeros((16, 16, 128), f32)
    for j in range(16):
        es[j, j, :] = 1
    _HC["esel"] = es.astype(NPBF)
    s = np.arange(128)[:, None]
    t = np.arange(128)[None, :]
    _HC["hmask"] = ((s // 64 == t // 64) & (s <= t)).astype(f32).astype(NPBF)
    for h, w in enumerate(POOL_WINDOWS):
        M = np.zeros((128, 3, 128), f32)
        eye = (s == t).astype(f32)
        band = ((s <= t) & (s > t - w)).astype(f32)
        M[:, 0, :] = band / np.minimum(w, t + 1).astype(f32) - eye
        M[:, 1, :] = band / f32(w) - eye
        M[:, 2, :] = ((s - 128) > (t - w)).astype(f32) / f32(w)
        _HC[f"poolM{h}"] = M.astype(NPBF)
    return _HC


def fm_layout(w):
    K, N = w.shape
    return np.ascontiguousarray(w.reshape(K // 128, 128, N).transpose(1, 0, 2))


def mix_inputs(inp, l, c, hnT_b):
    b, h = c // 4, c % 4
    hc = host_consts()
    w_in = inp["w_in"][l]
    hs = slice(h * 128, (h + 1) * 128)

    def col(name):
        return w_in[:, IN_OFF[name] + h * 128: IN_OFF[name] + (h + 1) * 128]
    wm = np.concatenate([col(n) for n in ("mq", "mk", "mv", "fq", "fk", "fv", "hq", "hf", "hg", "hi", "pin")], axis=1)
    small = np.zeros((128, N_SMALL), np.float32)
    small[:, SM_WFF:SM_WFF + NCH] = w_in[:, IN_OFF["ff"] + h].reshape(NCH, 128).T
    small[:, SM_G0] = inp["hgrn_lb_logits"][0, hs]
    small[:, SM_G1] = inp["hgrn_lb_logits"][1, hs]
    small[:, SM_HNORM] = inp["hgrn_out_norm"][l, hs]
    small[:, SM_PSCALE] = inp["pool_scale"][l, hs]
    small[:, SM_FB] = inp["fox_f_bias"][l, h]
    return {"hnT": hnT_b, "wmix": fm_layout(wm), "small": small,
            "poolw": np.ascontiguousarray(inp["pool_w"][l, h]),
            "consts": hc["consts"], "c32": hc["c32"], "cmask": hc["cmask"], "perm": hc["perm"],
            "ropeC": hc["ropeC"], "ropeS": hc["ropeS"], "esel": hc["esel"], "hmask": hc["hmask"],
            "poolM": hc[f"poolM{h}"]}


GV_MIXPOST, GV_XAPRE, GV_XAMEM, GV_XAPOST, GV_MLPPRE, GV_MLPPOST, GV_NEXT = range(7)
HALF = 512
NMEM = 256
RING_N = 6
RING_ELEMS = 4096


class Ring:
    def __init__(self, cx, n=RING_N, elems=RING_ELEMS):
        self.cx = cx
        self.bufs = [cx.sb([128, elems], BF16, name=f"ring{i}") for i in range(n)]
        self.i = 0

    def load(self, src_ap, a, b):
        t = self.bufs[self.i % len(self.bufs)]
        self.i += 1
        view = t[:, 0:a * b].rearrange("p (a b) -> p a b", b=b)
        self.cx.load(view, src_ap, t.all, q="pool")
        return view, t.all


class TokIO:
    pass


def tok_io_external(cx):
    io = TokIO()
    hT_d = cx.din("hT", [D_MODEL, TOK], F32)
    hnT_d = cx.din("hnT", [D_MODEL, TOK], BF16)
    ybr_d = cx.din("ybr", [D_MODEL, TOK], BF16)
    tok_weight_inputs(cx, io, "")
    io.memT = cx.din("memT", [D_MODEL, NMEM], F32)
    hT_o = cx.dout("hT_out", [D_MODEL, TOK], F32)
    hn_o = cx.dout("hn_next", [D_MODEL, TOK], BF16)
    hview = hT_d.rearrange("(c p) t -> p c t", p=128)
    hnview = hnT_d.rearrange("(c p) t -> p c t", p=128)
    ybview = ybr_d.rearrange("(c p) t -> p c t", p=128)
    hoview = hT_o.rearrange("(c p) t -> p c t", p=128)
    hnoview = hn_o.rearrange("(c p) t -> p c t", p=128)
    io.hT_in = lambda hf, c0, c1: hview[:, c0:c1, hf * HALF:(hf + 1) * HALF]
    io.hn_in = lambda hf, k: hnview[:, 4 * k:4 * k + 4, hf * HALF:(hf + 1) * HALF]
    io.ybr_load = lambda dst, hf, wb: cx.load(dst, ybview[:, :, hf * HALF:(hf + 1) * HALF], wb)
    io.hT_out = lambda hf, c0, c1: hoview[:, c0:c1, hf * HALF:(hf + 1) * HALF]
    io.hn_out = lambda hf, k: hnoview[:, 4 * k:4 * k + 4, hf * HALF:(hf + 1) * HALF]
    io.h_is_output = True
    io.hn_is_output = True
    io.write_hn_when_last = True
    return io


def tok_weight_inputs(cx, io, sfx):
    io.wg_d = cx.din("wg" + sfx, [64, 128, NCH, 128], F32)
    io.wb_d = cx.din("wb" + sfx, [64, 128, 4, 128], F32)
    io.wo_d = cx.din("wo" + sfx, [16, 128, NCH, 128], F32)
    io.xq_d = cx.din("xq" + sfx, [4, 128, NCH, 128], F32)
    io.xk_d = cx.din("xk" + sfx, [4, 128, NCH, 128], F32)
    io.xv_d = cx.din("xv" + sfx, [2, 128, 8, 512], F32)
    io.xo_d = cx.din("xo" + sfx, [16, 128, 4, 128], F32)
    io.wup_d = cx.din("wup" + sfx, [64, 128, NCH, 128], F32)
    io.wdn_d = cx.din("wdn" + sfx, [32, 128, 32, 128], F32)
    io.gv_d = cx.din("gv" + sfx, [128, 7 * NCH], F32)


def build_tok(layer, last):
    cx = Ctx()
    io = tok_io_external(cx)
    consts = cx.din("consts", [128, 256], BF16)
    cm = Common(cx, consts)
    add_eps(cx, cm)
    phase_tok(cx, cm, io, layer, last)
    cx.S.emit()
    return cx.nc


def phase_tok(cx, cm, io, layer, last):
    S = cx.S
    wg_d, wb_d, wo_d, xq_d, xk_d, xv_d, xo_d, wup_d, wdn_d = (io.wg_d, io.wb_d, io.wo_d, io.xq_d, io.xk_d, io.xv_d,
                                                              io.xo_d, io.wup_d, io.wdn_d)
    memT_d, gv_d = io.memT, io.gv_d
    gv = cx.sb([128, 7 * NCH], F32, name="gv")
    cx.load(gv[:, :], gv_d, gv.all)

    def gcol(which, c):
        return gv[:, which * NCH + c: which * NCH + c + 1]

    ring = Ring(cx)
    hT = cx.sb([128, NCH, HALF], F32, nsub=NCH, name="hT")
    hnb = cx.sb([128, NCH, HALF], BF16, name="hnb")
    RA = cx.sb([128, 32, HALF], BF16, name="RA")
    ybr = RA[:, 0:NCH, :]
    sT = RA[:, NCH:2 * NCH, :]
    aT = cx.sb([128, NCH, HALF], F32, nsub=NCH, name="aT")
    sqp = [cx.sb([128, HALF], BF16, name=f"sq{i}") for i in range(3)]
    rstd = cx.sb([128, HALF], F32, name="rstd")
    tmpA = [cx.sb([128, HALF], F32, name=f"tmpA{i}") for i in range(2)]
    tmpB = [cx.sb([128, HALF], F32, name=f"tmpB{i}") for i in range(2)]
    acc = cx.sb([128, HALF], F32, name="acc")
    kx = cx.sb([128, 4, NMEM], BF16, name="kx")
    vx = cx.sb([128, 2, 512], BF16, name="vx")
    qx = cx.sb([128, 4, HALF], BF16, name="qx")
    ox = cx.sb([128, 4, HALF], BF16, name="ox")
    pTs = [cx.sb([128, HALF], BF16, name=f"xpT{i}") for i in range(2)]
    rc = cx.sb([128, HALF], F32, name="xrc")
    B = cm.banks

    def norm_add(gw):
        rms_stats(cx, cm, lambda c: aT[:, c, :], lambda c: [aT.b[c]], NCH, HALF, sqp, B[6], rstd, D_MODEL)
        for c in range(NCH):
            t = tmpA[c % 2]
            cx.stt(t[:, :], aT[:, c, :], gcol(gw, c), rstd[:, :], ALU.mult, ALU.mult,
                   [aT.b[c]] + gv.all + rstd.all, t.all)
            cx.tt(hT[:, c, :], hT[:, c, :], t[:, :], ALU.add, [hT.b[c]] + t.all, [hT.b[c]], eng="pool")

    def norm_to(gw, dst, dst_bufs):
        rms_stats(cx, cm, lambda c: hT[:, c, :], lambda c: [hT.b[c]], NCH, HALF, sqp, B[6], rstd, D_MODEL)
        for c in range(NCH):
            cx.stt(dst[:, c, :], hT[:, c, :], gcol(gw, c), rstd[:, :], ALU.mult, ALU.mult,
                   [hT.b[c]] + gv.all + rstd.all, dst_bufs)

    memf = aT
    cx.load(memf[:, :, 0:NMEM], memT_d.rearrange("(c p) t -> p c t", p=128), memf.all)
    rms_stats(cx, cm, lambda c: memf[:, c, 0:NMEM], lambda c: [memf.b[c]], NCH, NMEM, sqp, B[6], rstd, D_MODEL)
    memn = hnb
    for c in range(NCH):
        cx.stt(memn[:, c, 0:NMEM], memf[:, c, 0:NMEM], gcol(GV_XAMEM, c), rstd[:, 0:NMEM], ALU.mult, ALU.mult,
               [memf.b[c]] + gv.all + rstd.all, memn.all)
    for hd in range(4):
        wv_, wb_ = ring.load(xk_d[hd], NCH, 128)
        bk = B[hd % 2]
        for c in range(NCH):
            cx.mm(bk[:, 0:NMEM], wv_[:, c, :], memn[:, c, 0:NMEM], c == 0, c == NCH - 1, wb_ + memn.all, bk.all)
        cx.copy(kx[:, hd, :], bk[:, 0:NMEM], bk.all, kx.all)
    wv0, wb0 = ring.load(xv_d[0], 8, 512)
    wv1, wb1 = ring.load(xv_d[1], 8, 512)
    for mt in range(2):
        bk = B[2 + mt]
        for c in range(NCH):
            wv_, wb_ = (wv0, wb0) if c < 8 else (wv1, wb1)
            cx.mm(bk[:, :], memn[:, c, mt * 128:(mt + 1) * 128], wv_[:, c % 8, :], c == 0, c == NCH - 1,
                  wb_ + memn.all, bk.all)
        cx.copy(vx[:, mt, :], bk[:, :], bk.all, vx.all, eng="act")

    for hf in range(TOK // HALF):
        tsl = slice(hf * HALF, (hf + 1) * HALF)
        for c in range(0, NCH, 4):
            cx.load(hT[:, c:c + 4, :], io.hT_in(hf, c, c + 4), hT.b[c:c + 4])
        for k in range(4):
            cx.load(hnb[:, 4 * k:4 * k + 4, :], io.hn_in(hf, k), hnb.all)
        io.ybr_load(ybr, hf, RA.all)
        it = 0
        for j in range(NCH):
            for n in range(4):
                wg_, wgb = ring.load(wg_d[j * 4 + n], NCH, 128)
                wb_, wbb = ring.load(wb_d[j * 4 + n], 4, 128)
                bg, bp = B[it % 2], B[2 + it % 2]
                it += 1
                for c in range(NCH):
                    cx.mm(bg[:, :], wg_[:, c, :], hnb[:, c, :], c == 0, c == NCH - 1, wgb + hnb.all, bg.all)
                for hh in range(4):
                    cx.mm(bp[:, :], wb_[:, hh, :], ybr[:, n * 4 + hh, :], hh == 0, hh == 3, wbb + RA.all, bp.all)
                sg_ = tmpA[it % 2]
                cx.activation(sg_[:, :], bg[:, :], AF.Sigmoid, bg.all, sg_.all)
                if n == 0:
                    cx.tt(acc[:, :], bp[:, :], sg_[:, :], ALU.mult, bp.all + sg_.all, acc.all)
                else:
                    t2 = tmpB[it % 2]
                    cx.tt(t2[:, :], bp[:, :], sg_[:, :], ALU.mult, bp.all + sg_.all, t2.all)
                    if n < 3:
                        cx.tt(acc[:, :], acc[:, :], t2[:, :], ALU.add, acc.all + t2.all, acc.all, eng="pool")
                    else:
                        cx.tt(sT[:, j, :], acc[:, :], t2[:, :], ALU.add, acc.all + t2.all, RA.all, eng="pool")
        for j in range(NCH):
            w_, wbf = ring.load(wo_d[j], NCH, 128)
            bk = B[j % 2]
            for c in range(NCH):
                cx.mm(bk[:, :], w_[:, c, :], sT[:, c, :], c == 0, c == NCH - 1, wbf + RA.all, bk.all)
            cx.copy(aT[:, j, :], bk[:, :], bk.all, [aT.b[j]], eng="act" if j % 2 else "dve")
        norm_add(GV_MIXPOST)
        norm_to(GV_XAPRE, hnb, hnb.all)
        for hd in range(4):
            w_, wbf = ring.load(xq_d[hd], NCH, 128)
            bk = B[hd % 2]
            for c in range(NCH):
                cx.mm(bk[:, :], w_[:, c, :], hnb[:, c, :], c == 0, c == NCH - 1, wbf + hnb.all, bk.all)
            S.act(lambda e, hd=hd, bk=bk: e.mul(out=qx[:, hd, :], in_=bk[:, :], mul=SCALE), bk.all, qx.all)
        for hd in range(4):
            ob, db = B[2 + hd % 2], B[4 + hd % 2]
            for mt in range(2):
                sb_ = B[mt]
                cx.mm(sb_[:, :], kx[:, hd, mt * 128:(mt + 1) * 128], qx[:, hd, :], True, True, kx.all + qx.all, sb_.all)
                pT = pTs[mt]
                cx.activation(pT[:, :], sb_[:, :], AF.Exp, sb_.all, pT.all)
                cx.mm(ob[:, :], vx[:, mt, hd * 128:(hd + 1) * 128], pT[:, :], mt == 0, mt == 1, vx.all + pT.all, ob.all)
                cx.mm(db[:, :], cm.ones, pT[:, :], mt == 0, mt == 1, pT.all, db.all)
            S.dve(lambda e, db=db: e.reciprocal(out=rc[:, :], in_=db[:, :]), db.all, rc.all)
            cx.tt(ox[:, hd, :], ob[:, :], rc[:, :], ALU.mult, ob.all + rc.all, ox.all)
        for j in range(NCH):
            w_, wbf = ring.load(xo_d[j], 4, 128)
            bk = B[j % 2]
            for hd in range(4):
                cx.mm(bk[:, :], w_[:, hd, :], ox[:, hd, :], hd == 0, hd == 3, wbf + ox.all, bk.all)
            cx.copy(aT[:, j, :], bk[:, :], bk.all, [aT.b[j]], eng="act" if j % 2 else "dve")
        norm_add(GV_XAPOST)
        norm_to(GV_MLPPRE, hnb, hnb.all)
        uT = RA
        for fh in range(2):
            for fb in range(32):
                w_, wbf = ring.load(wup_d[fh * 32 + fb], NCH, 128)
                bk = B[fb % 4]
                for c in range(NCH):
                    cx.mm(bk[:, :], w_[:, c, :], hnb[:, c, :], c == 0, c == NCH - 1, wbf + hnb.all, bk.all)
                r_ = tmpA[fb % 2]
                cx.activation(r_[:, :], bk[:, :], AF.Relu, bk.all, r_.all)
                cx.tt(uT[:, fb, :], r_[:, :], r_[:, :], ALU.mult, r_.all, RA.all, eng="dve" if fb % 2 else "pool")
            for j in range(NCH):
                w_, wbf = ring.load(wdn_d[j * 2 + fh], 32, 128)
                bk = B[4 + j % 2]
                for fb in range(32):
                    cx.mm(bk[:, :], w_[:, fb, :], uT[:, fb, :], fb == 0, fb == 31, wbf + RA.all, bk.all)
                if fh == 0:
                    cx.copy(aT[:, j, :], bk[:, :], bk.all, [aT.b[j]], eng="act")
                else:
                    cx.tt(aT[:, j, :], bk[:, :], aT[:, j, :], ALU.add, bk.all + [aT.b[j]], [aT.b[j]])
        norm_add(GV_MLPPOST)
        for c in range(0, NCH, 4):
            cx.store(io.hT_out(hf, c, c + 4), hT[:, c:c + 4, :], hT.b[c:c + 4], is_output=io.h_is_output)
        if not last:
            norm_to(GV_NEXT, hnb, hnb.all)
            for k in range(4):
                cx.store(io.hn_out(hf, k), hnb[:, 4 * k:4 * k + 4, :], hnb.all, is_output=io.hn_is_output)
        elif io.write_hn_when_last:
            for k in range(4):
                cx.store(io.hn_out(hf, k), hnb[:, 4 * k:4 * k + 4, :], hnb.all, is_output=io.hn_is_output)


def tok_weights(inp, l):
    f32 = np.float32
    wg = inp["w_in"][l][:, IN_OFF["gl"]:]
    wg = wg.reshape(NCH, 128, 4, NCH, 128).transpose(3, 2, 1, 0, 4)
    wg = np.ascontiguousarray(wg).reshape(64, 128, NCH, 128)
    wb = inp["w_branch"][l].reshape(4, 4, 128, NCH, 128).transpose(3, 0, 2, 1, 4)
    wb = np.ascontiguousarray(wb).reshape(64, 128, 4, 128)

    def tiles(w, kc):
        K_, N_ = w.shape
        return np.ascontiguousarray(w.reshape(kc, 128, N_ // 128, 128).transpose(2, 1, 0, 3))
    wo = tiles(inp["w_mix_out"][l], NCH)
    xq = tiles(inp["xa_wq"][l], NCH)
    xk = tiles(inp["xa_wkv"][l][:, 0:512], NCH)
    wv = inp["xa_wkv"][l][:, 512:1024].reshape(2, 8, 128, 512).transpose(0, 2, 1, 3)
    xv = np.ascontiguousarray(wv)
    xo = tiles(inp["xa_wo"][l], 4)
    wup = tiles(inp["mlp_w_up"][l], NCH)
    wd = inp["mlp_w_down"][l].reshape(2, 32, 128, NCH, 128).transpose(3, 0, 2, 1, 4)
    wdn = np.ascontiguousarray(wd).reshape(32, 128, 32, 128)
    gv = np.zeros((128, 7 * NCH), f32)
    names = ["mix_norm_post", "xa_norm_pre", "xa_norm_mem", "xa_norm_post", "mlp_norm_pre", "mlp_norm_post"]
    for i, nme in enumerate(names):
        gv[:, i * NCH:(i + 1) * NCH] = inp[nme][l].reshape(NCH, 128).T
    if l + 1 < inp["mix_norm_pre"].shape[0]:
        gv[:, GV_NEXT * NCH:(GV_NEXT + 1) * NCH] = inp["mix_norm_pre"][l + 1].reshape(NCH, 128).T
    return {"wg": wg, "wb": wb, "wo": wo, "xq": xq, "xk": xk, "xv": xv, "xo": xo, "wup": wup, "wdn": wdn,
            "gv": gv, "consts": host_consts()["consts"]}


_PROGS = {}


def _prog(key, builder):
    if key not in _PROGS:
        _PROGS[key] = builder()
    return _PROGS[key]


def kernel_unfused(**inp):
    inp = {k: np.asarray(v) for k, v in inp.items()}
    depth = inp["w_in"].shape[0]
    cores = list(range(8))
    hc = host_consts()
    x = inp["x"]
    tsl = [slice((c % 4) * TOK, (c % 4 + 1) * TOK) for c in cores]
    hT = [np.ascontiguousarray(x[c // 4, tsl[c]].T) for c in cores]
    memT = [np.ascontiguousarray(inp["mem"][b].T) for b in range(BATCH)]
    g0 = np.ascontiguousarray(inp["mix_norm_pre"][0].reshape(NCH, 128).T)
    res = run_bass_kernel_spmd(_prog("pre", build_pre),
                               [{"xT": hT[c], "gpre": g0, "consts": hc["consts"]} for c in cores], core_ids=cores)
    hn = [np.asarray(res.results[c]["hn_out"]) for c in cores]
    for l in range(depth):
        hn_full = [np.ascontiguousarray(np.concatenate([hn[b * 4 + q] for q in range(4)], axis=1)) for b in range(BATCH)]
        res = run_bass_kernel_spmd(_prog(("mix", l), lambda: build_mix(l)),
                                   [mix_inputs(inp, l, c, hn_full[c // 4]) for c in cores], core_ids=cores)
        yT = [np.asarray(res.results[c]["yT"]) for c in cores]
        tw = tok_weights(inp, l)
        maps = []
        for c in cores:
            b = c // 4
            yb = np.stack([yT[b * 4 + h][:, :, tsl[c]] for h in range(4)], axis=1)
            m = dict(tw)
            m["hT"] = hT[c]
            m["hnT"] = hn[c]
            m["ybr"] = np.ascontiguousarray(yb.reshape(D_MODEL, TOK))
            m["memT"] = memT[b]
            maps.append(m)
        res = run_bass_kernel_spmd(_prog(("tok", l), lambda: build_tok(l, l == depth - 1)), maps, core_ids=cores)
        hT = [np.asarray(res.results[c]["hT_out"]) for c in cores]
        hn = [np.asarray(res.results[c]["hn_next"]) for c in cores]
    out = np.empty_like(x)
    for c in cores:
        out[c // 4, tsl[c]] = hT[c].T
    return out


def fused_inputs(inp, c):
    depth = inp["w_in"].shape[0]
    b, q = c // 4, c % 4
    hc = host_consts()
    m = {"consts": hc["consts"], "c32": hc["c32"], "cmask": hc["cmask"], "esel": hc["esel"], "hmask": hc["hmask"],
         "perm": hc["perm"], "ropeC": hc["ropeC"], "ropeS": hc["ropeS"], "poolM": hc[f"poolM{q}"],
         "xT": np.ascontiguousarray(inp["x"][b, q * TOK:(q + 1) * TOK].T),
         "memT": np.ascontiguousarray(inp["mem"][b].T),
         "gpre": np.ascontiguousarray(inp["mix_norm_pre"][0].reshape(NCH, 128).T)}
    for l in range(depth):
        mi = mix_inputs(inp, l, c, None)
        m[f"wmix_{l}"] = mi["wmix"]
        m[f"small_{l}"] = mi["small"]
        m[f"poolw_{l}"] = mi["poolw"]
    return m


def kernel(**inp):
    inp = {k: np.asarray(v) for k, v in inp.items()}
    depth = inp["w_in"].shape[0]
    cores = list(range(8))
    nc = _prog(("fused", depth), lambda: build_fused(depth))
    tws = []
    for l in range(depth):
        tw = tok_weights(inp, l)
        tw.pop("consts")
        tws.append({f"{k}_{l}": v for k, v in tw.items()})
    maps = []
    for c in cores:
        m = fused_inputs(inp, c)
        for tw in tws:
            m.update(tw)
        maps.append(m)
    res = run_bass_kernel_spmd(nc, maps, core_ids=cores)
    x = inp["x"]
    out = np.empty_like(x)
    for c in cores:
        out[c // 4, (c % 4) * TOK:(c % 4 + 1) * TOK] = np.asarray(res.results[c]["outT"]).T
    return out
```

```python
import contextlib
import numpy as np
import ml_dtypes
import concourse.bass as bass
import concourse.mybir as mybir
from concourse.bass_utils import run_bass_kernel_spmd

F32 = mybir.dt.float32
BF16 = mybir.dt.bfloat16
AF = mybir.ActivationFunctionType
ALU = mybir.AluOpType
AX = mybir.AxisListType
NPBF = ml_dtypes.bfloat16

D_MODEL = 2048
NCH = 16
BATCH = 2
SEQ = 4096
TOK = 1024
HD = 128
NEG = -30000.0
EPS = 1e-6
SCALE = HD ** -0.5
POOL_WINDOWS = (2, 4, 8, 16)
DEBUG_STOP = 0
ROPE_ADD_ENG = "dve"

COMPUTE = ("pe", "act", "dve", "pool")
N_DMA_SEMS = 12


class Buf:
    __slots__ = ("name", "w", "r", "excl")

    def __init__(self, name="", excl=False):
        self.name = name
        self.w = None
        self.r = []
        self.excl = excl


class Op:
    __slots__ = ("eng", "fn", "waits", "sig", "idx", "dma", "dma_ev")

    def __init__(self, eng, fn, sig, dma):
        self.eng = eng
        self.fn = fn
        self.waits = []
        self.sig = sig
        self.dma = dma
        self.dma_ev = None


class Sched:
    def __init__(self, nc):
        self.nc = nc
        self.ops = {e: [] for e in COMPUTE + ("sp",)}
        self.dma_count = {e: 0 for e in ("sp", "act", "pool")}
        self.out_events = []
        self.fence_waits = {e: [] for e in COMPUTE + ("sp",)}
        self.n_cc = 0

    def collective(self, fn, reads, writes):
        op = Op("pool", fn, False, False)
        op.dma = "cc"
        lst = self.ops["pool"]
        op.idx = len(lst)
        lst.append(op)
        ev = ("x", self.n_cc)
        op.dma_ev = ev
        self.n_cc += 1
        if self.fence_waits["pool"]:
            op.waits.extend(self.fence_waits["pool"])
            self.fence_waits["pool"] = []
        for b in reads:
            if b.w is not None:
                op.waits.append(b.w)
        for b in writes:
            if b.w is not None:
                op.waits.append(b.w)
            op.waits.extend(b.r)
        for d in op.waits:
            if d[0] == "c":
                self.ops[d[1]][d[2]].sig = True
        for b in reads:
            b.r.append(ev)
        for b in writes:
            b.w = ev
            b.r = []
        return ev

    def fence(self):
        evs = []
        for e in COMPUTE + ("sp",):
            lst = self.ops[e]
            for op in reversed(lst):
                if not op.dma:
                    evs.append(("c", e, op.idx))
                    op.sig = True
                    break
        for q, n in self.dma_count.items():
            for i in range(max(0, n - N_DMA_SEMS), n):
                evs.append(("d", q, i))
        for i in range(self.n_cc):
            evs.append(("x", i))
        for e in self.fence_waits:
            self.fence_waits[e] = list(evs)

    def _add(self, eng, fn, reads, writes, sig=False, dma=False):
        op = Op(eng, fn, sig, dma)
        lst = self.ops[eng]
        op.idx = len(lst)
        lst.append(op)
        if dma:
            n = self.dma_count[eng]
            self.dma_count[eng] += 1
            ev = ("d", eng, n)
            op.dma_ev = ev
            if n >= N_DMA_SEMS:
                op.waits.append(("d", eng, n - N_DMA_SEMS))
        else:
            ev = ("c", eng, op.idx)
        if self.fence_waits[eng]:
            op.waits.extend(d for d in self.fence_waits[eng] if d != ev)
            self.fence_waits[eng] = []
        excl_reads = [b for b in reads if b.excl]
        if excl_reads:
            reads = [b for b in reads if not b.excl]
            writes = list(writes) + excl_reads
        for b in reads:
            if b.w is not None and b.w != ev:
                op.waits.append(b.w)
        for b in writes:
            if b.w is not None and b.w != ev:
                op.waits.append(b.w)
            for d in b.r:
                if d != ev:
                    op.waits.append(d)
        if eng == "pe" and not dma:
            op.waits = [d for d in op.waits if not (d[0] == "c" and d[1] == "pe")]
        for d in op.waits:
            if d[0] == "c":
                self.ops[d[1]][d[2]].sig = True
        for b in reads:
            if not dma:
                b.r = [d for d in b.r if not (d[0] == "c" and d[1] == eng)]
            b.r.append(ev)
        for b in writes:
            b.w = ev
            b.r = []
        return ev

    def pe(self, fn, reads, writes, sig=False):
        return self._add("pe", fn, reads, writes, sig)

    def act(self, fn, reads, writes):
        return self._add("act", fn, reads, writes)

    def dve(self, fn, reads, writes):
        return self._add("dve", fn, reads, writes)

    def pool(self, fn, reads, writes):
        return self._add("pool", fn, reads, writes)

    def dma(self, q, fn, reads, writes, is_output=False):
        ev = self._add(q, fn, reads, writes, dma=True)
        if is_output:
            self.out_events.append(ev)
        return ev

    def emit(self):
        nc = self.nc
        with contextlib.ExitStack() as st:
            csem = {e: st.enter_context(nc.semaphore("c_" + e)) for e in COMPUTE}
            dsem = {q: [st.enter_context(nc.semaphore(f"d_{q}{i}")) for i in range(N_DMA_SEMS)]
                    for q in ("sp", "act", "pool")}
            xsem = [st.enter_context(nc.semaphore(f"x_{i}")) for i in range(self.n_cc)]
            block = st.enter_context(nc.Block())
            sigcount = {}
            for e in COMPUTE:
                lst = self.ops[e]
                last = None
                for op in lst:
                    if not op.dma:
                        last = op
                if last is not None:
                    last.sig = True
                c = 0
                arr = []
                for op in lst:
                    if (not op.dma) and op.sig:
                        c += 1
                    arr.append(c)
                sigcount[e] = arr
            self.stats = {e: (len(self.ops[e]), sigcount[e][-1] if sigcount[e] else 0) for e in COMPUTE}
            self.stats["dma"] = dict(self.dma_count)

            def resolve(ev):
                if ev[0] == "c":
                    _, e, i = ev
                    op = self.ops[e][i]
                    v = sigcount[e][i]
                    assert op.sig
                    return ("c", e), csem[e], v
                if ev[0] == "x":
                    return ev, xsem[ev[1]], 1
                _, q, n = ev
                return ("d", q, n % N_DMA_SEMS), dsem[q][n % N_DMA_SEMS], 16 * (n // N_DMA_SEMS + 1)

            def run_engine(ename, eng):
                known = {}
                for op in self.ops[ename]:
                    need = {}
                    for ev in op.waits:
                        key, sem, v = resolve(ev)
                        if known.get(key, 0) >= v:
                            continue
                        if need.get(key, (None, 0))[1] < v:
                            need[key] = (sem, v)
                    for key, (sem, v) in need.items():
                        eng.wait_ge(sem, v)
                        known[key] = v
                    ins = op.fn(eng)
                    if op.dma == "cc":
                        ins.then_inc(xsem[op.dma_ev[1]])
                    elif op.dma:
                        _, q, n = op.dma_ev
                        ins.then_inc(dsem[q][n % N_DMA_SEMS], 16)
                    elif op.sig:
                        ins.then_inc(csem[ename], 1)
                if ename == "sp":
                    for ev in self.out_events:
                        key, sem, v = resolve(ev)
                        if known.get(key, 0) >= v:
                            continue
                        eng.wait_ge(sem, v)
                        known[key] = v

            @block.sync
            def _(eng):
                if getattr(self, "sp_init", None) is not None:
                    self.sp_init(eng)
                run_engine("sp", eng)

            @block.tensor
            def _(eng):
                run_engine("pe", eng)

            @block.scalar
            def _(eng):
                run_engine("act", eng)

            @block.vector
            def _(eng):
                run_engine("dve", eng)

            @block.gpsimd
            def _(eng):
                run_engine("pool", eng)


class T:
    def __init__(self, t, nsub=1, name="", psum=False):
        self.t = t
        if psum:
            self.b = [Buf(name, excl=True)] * nsub
        else:
            self.b = [Buf(f"{name}{i}") for i in range(nsub)]

    def __getitem__(self, k):
        return self.t[k]

    @property
    def all(self):
        return list(self.b)


class Ctx:
    def __init__(self):
        self.nc = bass.Bass("TRN2", target_bir_lowering=False)
        self.S = Sched(self.nc)
        self._n = 0
        self.scope = None

    def sb(self, shape, dt, nsub=1, name=None):
        self._n += 1
        name = (name or "sb") + f"_{self._n}"
        if self.scope is not None:
            return T(self.scope.enter_context(self.nc.sbuf_tensor(name, list(shape), dt)), nsub, name)
        return T(self.nc.alloc_sbuf_tensor(name, list(shape), dt), nsub, name)

    def ps(self, shape, dt=F32, nsub=1, name=None):
        self._n += 1
        name = name or f"ps{self._n}"
        return T(self.nc.alloc_psum_tensor(name, list(shape), dt), nsub, name, psum=True)

    def din(self, name, shape, dt):
        return self.nc.dram_tensor(name, list(shape), dt, kind="ExternalInput").ap()

    def dout(self, name, shape, dt):
        return self.nc.dram_tensor(name, list(shape), dt, kind="ExternalOutput").ap()

    def load(self, dst_ap, src_ap, wbufs, q="sp", rbufs=()):
        return self.S.dma(q, lambda e: e.dma_start(out=dst_ap, in_=src_ap), list(rbufs), list(wbufs))

    def store(self, dst_ap, src_ap, rbufs, q="sp", is_output=True):
        return self.S.dma(q, lambda e: e.dma_start(out=dst_ap, in_=src_ap), list(rbufs), [], is_output=is_output)

    def mm(self, out_ap, lhsT, rhs, start, stop, reads, writes, sig=None):
        if sig is None:
            sig = False
        return self.S.pe(lambda e: e.matmul(out_ap, lhsT, rhs, start=start, stop=stop), reads, writes, sig=sig)

    def transpose(self, out_ap, in_ap, ident_ap, reads, writes):
        return self.S.pe(lambda e: e.transpose(out_ap, in_ap, ident_ap), reads, writes)

    def activation(self, out_ap, in_ap, func, reads, writes, bias=None, scale=None):
        kw = {}
        if bias is not None:
            kw["bias"] = bias
        if scale is not None:
            kw["scale"] = scale
        return self.S.act(lambda e: e.activation(out=out_ap, in_=in_ap, func=func, **kw), reads, writes)

    def tt(self, out_ap, in0, in1, op, reads, writes, eng="dve"):
        f = lambda e: e.tensor_tensor(out=out_ap, in0=in0, in1=in1, op=op)
        return (self.S.dve if eng == "dve" else self.S.pool)(f, reads, writes)

    def ts(self, out_ap, in0, s1, s2, op0, op1, reads, writes, eng="dve"):
        if op1 is None:
            f = lambda e: e.tensor_scalar(out=out_ap, in0=in0, scalar1=s1, scalar2=None, op0=op0)
        else:
            f = lambda e: e.tensor_scalar(out=out_ap, in0=in0, scalar1=s1, scalar2=s2, op0=op0, op1=op1)
        return (self.S.dve if eng == "dve" else self.S.pool)(f, reads, writes)

    def stt(self, out_ap, in0, scalar, in1, op0, op1, reads, writes):
        return self.S.dve(lambda e: e.scalar_tensor_tensor(out=out_ap, in0=in0, scalar=scalar, in1=in1,
                                                            op0=op0, op1=op1), reads, writes)

    def copy(self, out_ap, in_ap, reads, writes, eng="dve"):
        if eng == "act":
            return self.S.act(lambda e: e.copy(out=out_ap, in_=in_ap), reads, writes)
        f = lambda e: e.tensor_copy(out=out_ap, in_=in_ap)
        return (self.S.dve if eng == "dve" else self.S.pool)(f, reads, writes)

    def memset(self, ap, val, writes, eng="pool"):
        f = lambda e: e.memset(ap, val)
        return (self.S.dve if eng == "dve" else self.S.pool)(f, [], writes)


class Common:
    def __init__(self, cx, consts_ap):
        self.cx = cx
        self.cb = cx.sb([128, 256], BF16, name="cbf")
        cx.load(self.cb[:, :], consts_ap, self.cb.all)
        self.ident = self.cb[:, 0:128]
        self.ones = self.cb[:, 128:256]
        self.banks = [cx.ps([128, 512], F32, nsub=4, name=f"bank{i}") for i in range(7)]
        self.bankb = cx.ps([128, 1024], BF16, nsub=8, name="bankb")


def rms_stats(cx, cm, src_fn, src_bufs, nfeat_chunks, ncols, sq_pool, bank, rstd, denom):
    for c in range(nfeat_chunks):
        sq = sq_pool[c % len(sq_pool)]
        cx.activation(sq[:, 0:ncols], src_fn(c), AF.Square, src_bufs(c), sq.all)
        cx.mm(bank[:, 0:ncols], cm.ones, sq[:, 0:ncols], c == 0, c == nfeat_chunks - 1,
              [cm.cb.b[0]] + sq.all, bank.all)
    cx.activation(rstd[:, 0:ncols], bank[:, 0:ncols], AF.Ln, bank.all + cm.epst.all, rstd.all, bias=cm.eps_ap,
                  scale=1.0 / denom)
    cx.activation(rstd[:, 0:ncols], rstd[:, 0:ncols], AF.Exp, rstd.all, rstd.all, scale=-0.5)


def add_eps(cx, cm):
    cm.epst = cx.sb([128, 1], F32, name="epst")
    cx.memset(cm.epst[:, :], EPS, cm.epst.all)
    cm.eps_ap = cm.epst[:, 0:1]


def phase_pre(cx, cm, xT, gpre, hn_out_fn, is_output):
    g = cx.sb([128, NCH], F32, name="g")
    cx.load(g[:, :], gpre, g.all)
    xv = xT.rearrange("(c p) t -> p c t", p=128)
    xs = [cx.sb([128, NCH, 512], F32, nsub=NCH, name=f"xs{i}") for i in range(2)]
    hs = [cx.sb([128, NCH, 512], BF16, nsub=1, name=f"hs{i}") for i in range(2)]
    sqp = [cx.sb([128, 512], BF16, name=f"sq{i}") for i in range(3)]
    rstd = cx.sb([128, 512], F32, name="rstd")
    for n in range(TOK // 512):
        x = xs[n % 2]
        h = hs[n % 2]
        cx.load(x[:, :, :], xv[:, :, n * 512:(n + 1) * 512], x.all)
        rms_stats(cx, cm, lambda c: x[:, c, :], lambda c: [x.b[c]], NCH, 512, sqp, cm.banks[n % 2], rstd, D_MODEL)
        for c in range(NCH):
            cx.stt(h[:, c, :], x[:, c, :], g[:, c:c + 1], rstd[:, :], ALU.mult, ALU.mult,
                   [x.b[c]] + g.all + rstd.all, h.all)
        for k in range(4):
            cx.store(hn_out_fn(n, k), h[:, 4 * k:4 * k + 4, :], h.all, is_output=is_output)


def build_pre():
    cx = Ctx()
    xT = cx.din("xT", [D_MODEL, TOK], F32)
    gpre = cx.din("gpre", [128, NCH], F32)
    consts = cx.din("consts", [128, 256], BF16)
    hn_out = cx.dout("hn_out", [D_MODEL, TOK], BF16)
    cm = Common(cx, consts)
    add_eps(cx, cm)
    ov = hn_out.rearrange("(c p) t -> p c t", p=128)
    phase_pre(cx, cm, xT, gpre, lambda n, k: ov[:, 4 * k:4 * k + 4, n * 512:(n + 1) * 512], True)
    cx.S.emit()
    return cx.nc


W_OFF = {"moba": (0, 384), "fox": (384, 384), "hgrn": (768, 512), "pool": (1280, 128)}
W_MIXCOLS = 1408
SM_WFF, SM_G0, SM_G1, SM_HNORM, SM_PSCALE, SM_FB = 0, 16, 17, 18, 19, 20
N_SMALL = 24


class MixEnv:
    pass


def project(cx, cm, env, w, fm_blocks, tm, row=None, pre_chunk=None):
    nb = 0
    for n in range(SEQ // 512):
        hs = env.hs[n % 2]
        for k in range(4):
            cx.load(hs[:, 4 * k:4 * k + 4, :], env.hn_chunk(n, k), hs.all)
        if pre_chunk is not None:
            pre_chunk(n)
        for (c0, handler) in fm_blocks:
            bank = cm.banks[nb % 4]
            nb += 1
            for c in range(NCH):
                cx.mm(bank[:, :], w[:, c, c0:c0 + 128], hs[:, c, :], c == 0, c == NCH - 1,
                      w.all + hs.all, bank.all)
            handler(n, bank)
        if tm is not None:
            c0, ncols, handler = tm
            for tl in range(4):
                bank = cm.banks[nb % 4]
                nb += 1
                for c in range(NCH):
                    cx.mm(bank[:, 0:ncols], hs[:, c, tl * 128:(tl + 1) * 128], w[:, c, c0:c0 + ncols],
                          c == 0, c == NCH - 1, w.all + hs.all, bank.all)
                handler(n * 4 + tl, bank)
        if row is not None:
            lfn, rreads, handler = row
            bank = cm.banks[nb % 4]
            nb += 1
            for c in range(NCH):
                cx.mm(bank[0:1, :], lfn(c), hs[:, c, :], c == 0, c == NCH - 1, rreads + hs.all, bank.all)
            handler(n, bank)


def load_w(cx, env, name, eng="pool"):
    c0, nc_ = W_OFF[name]
    w = cx.sb([128, NCH, nc_], BF16, name="w_" + name)
    for c in range(0, NCH, 4):
        cx.load(w[:, c:c + 4, :], env.wmix[:, c:c + 4, c0:c0 + nc_], w.all, q=eng)
    return w


def softmax_finish(cx, env, obank, dbank, ncols, out_ap_dram, k):
    rc = env.rc[k % 2]
    yt = env.yt[k % 2]
    cx.S.dve(lambda e: e.reciprocal(out=rc[:, 0:ncols], in_=dbank[:, 0:ncols]), dbank.all, rc.all)
    cx.tt(yt[:, 0:ncols], obank[:, 0:ncols], rc[:, 0:ncols], ALU.mult, obank.all + rc.all, yt.all)
    cx.store(out_ap_dram, yt[:, 0:ncols], yt.all, is_output=env.y_is_output)


def mix_fox(cx, cm, env):
    S = cx.S
    w = load_w(cx, env, "fox")
    fQ = cx.sb([128, SEQ], BF16, name="fQ")
    fK = cx.sb([128, SEQ], BF16, name="fK")
    fV = cx.sb([128, 32, 128], BF16, name="fV")
    wff = cx.sb([128, NCH], BF16, name="wff")
    cx.copy(wff[:, :], env.small[:, SM_WFF:SM_WFF + NCH], env.small.all, wff.all)
    nfb = cx.sb([128, 1], F32, name="nfb")
    cx.ts(nfb[:, :], env.small[:, SM_FB:SM_FB + 1], -1.0, None, ALU.mult, None, env.small.all, nfb.all)
    sprow = cx.sb([1, SEQ], F32, name="sprow")
    cprow = cx.sb([1, SEQ], F32, name="cprow")
    nrh = cx.sb([1, SEQ], BF16, name="nrh")
    nrl = cx.sb([1, SEQ], BF16, name="nrl")
    rowtmp = cx.sb([1, 512], F32, name="rowtmp")

    def h_q(n, bank):
        S.act(lambda e: e.mul(out=fQ[:, n * 512:(n + 1) * 512], in_=bank[:, :], mul=SCALE), bank.all, fQ.all)

    def h_k(n, bank):
        cx.copy(fK[:, n * 512:(n + 1) * 512], bank[:, :], bank.all, fK.all)

    def h_v(tl, bank):
        cx.copy(fV[:, tl, :], bank[:, 0:128], bank.all, fV.all, eng="act" if tl % 2 else "dve")

    def h_row(n, bank):
        cx.activation(rowtmp[0:1, :], bank[0:1, :], AF.Exp, bank.all + nfb.all, rowtmp.all, bias=nfb[0:1, 0:1], scale=-1.0)
        cx.activation(sprow[0:1, n * 512:(n + 1) * 512], rowtmp[0:1, :], AF.Ln, rowtmp.all + cm.c32.all, sprow.all,
                      bias=cm.one_ap[0:1, 0:1])

    project(cx, cm, env, w, [(0, h_q), (128, h_k)], (256, 128, h_v),
            row=(lambda c: wff[:, c:c + 1], wff.all, h_row))

    cx.memset(nrh[0:1, :], 1.0, nrh.all, eng="dve")
    S.dve(lambda e: e.tensor_tensor_scan(out=cprow[0:1, :], data0=nrh[0:1, :], data1=sprow[0:1, :], initial=0.0,
                                         op0=ALU.mult, op1=ALU.add), nrh.all + sprow.all, cprow.all)
    sm = cm.banks[6]
    cpv = cprow[0:1, :].rearrange("o (a b) -> o a b", b=512)
    cx.mm(sm[:, 0:8], cm.ones32[0:1, 0:128], cpv[:, :, 0], True, True, cm.c32.all + cprow.all, sm.all)
    for kt in range(32):
        cx.mm(sm[:, 8 + kt:9 + kt], cprow[0:1, kt * 128:(kt + 1) * 128], cm.ones32[0:1, 0:1], True, True,
              cm.c32.all + cprow.all, sm.all)
    rbcp = cx.sb([128, 40], F32, name="rbcp")
    cx.copy(rbcp[:, :], sm[:, 0:40], sm.all, rbcp.all)
    for qc in range(8):
        sl = slice(qc * 512, (qc + 1) * 512)
        cx.ts(sprow[0:1, sl], cprow[0:1, sl], cprow[0:1, qc * 512:qc * 512 + 1], -1.0, ALU.subtract, ALU.mult,
              cprow.all, sprow.all)
    cx.copy(nrh[0:1, :], sprow[0:1, :], sprow.all, nrh.all)
    cx.tt(nrl[0:1, :], sprow[0:1, :], nrh[0:1, :], ALU.subtract, sprow.all + nrh.all, nrl.all)

    biasq = [cx.sb([128, 32], F32, name=f"biasq{i}") for i in range(2)]
    pTs = [cx.sb([128, 512], BF16, name=f"fpT{i}") for i in range(3)]
    it = 0
    for qc in range(8):
        sl = slice(qc * 512, (qc + 1) * 512)
        bq = biasq[qc % 2]
        cx.ts(bq[:, :], rbcp[:, 8:40], rbcp[:, qc:qc + 1], None, ALU.subtract, None, rbcp.all, bq.all)
        obank = cm.banks[2 + qc % 2]
        dbank = cm.banks[4 + qc % 2]
        nkt = 4 * (qc + 1)
        for kt in range(nkt):
            sbank = cm.banks[kt % 2]
            a = kt - 4 * qc
            cx.mm(sbank[:, :], fK[:, kt * 128:(kt + 1) * 128], fQ[:, sl], True, False, fK.all + fQ.all, sbank.all)
            cx.mm(sbank[:, :], cm.ones[0:1, 0:128], nrh[0:1, sl], False, False, nrh.all, sbank.all)
            cx.mm(sbank[:, :], cm.ones[0:1, 0:128], nrl[0:1, sl], False, a < 0, nrl.all, sbank.all)
            if a >= 0:
                cx.mm(sbank[:, :], cm.ident, env.cmask[:, a, :], False, True, env.cmask.all, sbank.all)
            pT = pTs[it % 3]
            it += 1
            cx.activation(pT[:, :], sbank[:, :], AF.Exp, sbank.all + bq.all, pT.all, bias=bq[:, kt:kt + 1])
            cx.mm(obank[:, :], fV[:, kt, :], pT[:, :], kt == 0, kt == nkt - 1, fV.all + pT.all, obank.all)
            cx.mm(dbank[:, :], cm.ones, pT[:, :], kt == 0, kt == nkt - 1, pT.all, dbank.all)
        softmax_finish(cx, env, obank, dbank, 512, env.y_out(1, qc * 512, 512), qc)


def mix_moba(cx, cm, env):
    S = cx.S
    w = load_w(cx, env, "moba")
    mQ = cx.sb([128, SEQ], BF16, name="mQ")
    mK = cx.sb([128, SEQ], BF16, name="mK")
    mV = cx.sb([128, 32, 128], BF16, name="mV")
    perm = cx.sb([128, 128], BF16, name="perm")
    cx.load(perm[:, :], env.perm, perm.all)
    rC = [cx.sb([128, 512], F32, name=f"rC{i}") for i in range(2)]
    rS = [cx.sb([128, 512], F32, name=f"rS{i}") for i in range(2)]
    xb = [cx.sb([128, 512], BF16, name=f"xb{i}") for i in range(2)]
    t1 = [cx.sb([128, 512], F32, name=f"t1{i}") for i in range(2)]
    t2 = [cx.sb([128, 512], F32, name=f"t2{i}") for i in range(2)]
    cnt = [0]

    def pre_chunk(n):
        cx.load(rC[n % 2][:, :], env.ropeC[:, n * 512:(n + 1) * 512], rC[n % 2].all)
        cx.load(rS[n % 2][:, :], env.ropeS[:, n * 512:(n + 1) * 512], rS[n % 2].all)

    def rope_handler(dst, sc):
        def h(n, bank):
            k = cnt[0] % 2
            cnt[0] += 1
            sl = slice(n * 512, (n + 1) * 512)
            if DEBUG_STOP == 10:
                cx.copy(dst[:, sl], bank[:, :], bank.all, dst.all)
                return
            cx.copy(xb[k][:, :], bank[:, :], bank.all, xb[k].all, eng="act")
            swb = cm.banks[4 + k]
            cx.mm(swb[:, :], perm[:, :], xb[k][:, :], True, True, perm.all + xb[k].all, swb.all)
            if DEBUG_STOP == 11:
                cx.copy(dst[:, sl], swb[:, :], swb.all, dst.all)
                return
            if DEBUG_STOP == 13:
                cx.stt(t1[k][:, :], bank[:, :], sc, t2[k][:, :], ALU.mult, ALU.mult, bank.all + t2[k].all, t1[k].all)
            else:
                cx.stt(t1[k][:, :], bank[:, :], sc, rC[n % 2][:, :], ALU.mult, ALU.mult, bank.all + rC[n % 2].all, t1[k].all)
            if DEBUG_STOP in (12, 13):
                cx.copy(dst[:, sl], t1[k][:, :], t1[k].all, dst.all)
                return
            cx.stt(t2[k][:, :], swb[:, :], sc, rS[n % 2][:, :], ALU.mult, ALU.mult, swb.all + rS[n % 2].all, t2[k].all)
            cx.tt(dst[:, sl], t1[k][:, :], t2[k][:, :], ALU.add, t1[k].all + t2[k].all, dst.all, eng=ROPE_ADD_ENG)
        return h

    def h_v(tl, bank):
        cx.copy(mV[:, tl, :], bank[:, 0:128], bank.all, mV.all, eng="act")

    project(cx, cm, env, w, [(0, rope_handler(mQ, SCALE)), (128, rope_handler(mK, 1.0))], (256, 128, h_v),
            pre_chunk=pre_chunk)

    if DEBUG_STOP in (1, 10, 11, 12, 13):
        return
    kb32 = cx.sb([128, 16], F32, name="kb32")
    kbT = cx.sb([128, 16], BF16, name="kbT")
    S.dve(lambda e: e.tensor_reduce(out=kb32[:, :], in_=mK[:, :].rearrange("p (j k) -> p j k", k=256),
                                    axis=AX.X, op=ALU.add), mK.all, kb32.all)
    cx.copy(kbT[:, :], kb32[:, :], kb32.all, kbT.all)
    gb = cm.banks[6]
    for qt in range(32):
        cx.mm(gb[:, qt * 16:(qt + 1) * 16], mQ[:, qt * 128:(qt + 1) * 128], kbT[:, :], True, True,
              mQ.all + kbT.all, gb.all)
    g_sb = cx.sb([128, 32, 16], F32, name="g_sb")
    cx.copy(g_sb[:, :, :], gb[:, :].rearrange("p (a b) -> p a b", b=16), gb.all, g_sb.all)
    if DEBUG_STOP == 2:
        return
    S.pool(lambda e: e.affine_select(out=g_sb[:, :, :], in_=g_sb[:, :, :], pattern=[[1, 16], [0, 2], [-1, 16]],
                                     compare_op=ALU.is_ge, fill=-1e30, base=-1, channel_multiplier=0),
           g_sb.all, g_sb.all)
    if DEBUG_STOP == 3:
        return
    m8 = cx.sb([128, 32, 8], F32, name="m8")
    for qt in range(32):
        S.dve(lambda e, qt=qt: e.max(out=m8[:, qt, :], in_=g_sb[:, qt, :]), g_sb.all, m8.all)
    thr = cx.sb([128, 32, 1], F32, name="thr")
    cx.ts(thr[:, :, :], m8[:, :, 2:3], -1e29, None, ALU.max, None, m8.all, thr.all)
    nm = cx.sb([128, 32, 16], F32, name="nm")
    cx.tt(nm[:, :, :], g_sb[:, :, :], thr[:, :, :].to_broadcast([128, 32, 16]), ALU.is_lt, g_sb.all + thr.all, nm.all)
    cx.ts(nm[:, :, :], nm[:, :, :], NEG, None, ALU.mult, None, nm.all, nm.all)
    if DEBUG_STOP == 4:
        return
    nmT = cx.sb([16, SEQ], BF16, name="nmT")
    for grp in range(8):
        tb = cm.banks[grp % 2]
        for i in range(4):
            qt = grp * 4 + i
            cx.transpose(tb[0:16, i * 128:(i + 1) * 128], nm[:, qt, :], cm.ident32, nm.all + cm.c32.all, tb.all)
        cx.copy(nmT[0:16, grp * 512:(grp + 1) * 512], tb[0:16, :], tb.all, nmT.all, eng="act" if grp % 2 else "dve")

    if DEBUG_STOP == 5:
        return
    pTs = [cx.sb([128, 256], BF16, name=f"mpT{i}") for i in range(3)]
    it = 0
    for qb in range(16):
        sl = slice(qb * 256, (qb + 1) * 256)
        obank = cm.banks[2 + qb % 2]
        dbank = cm.banks[4 + qb % 2]
        nkt = 2 * qb + 2
        for kt in range(nkt):
            sbank = cm.banks[kt % 2]
            cx.mm(sbank[:, 0:256], mK[:, kt * 128:(kt + 1) * 128], mQ[:, sl], True, False, mK.all + mQ.all, sbank.all)
            if kt < 2 * qb:
                cx.mm(sbank[:, 0:256], env.esel[0:16, kt // 2, :], nmT[0:16, sl], False, True,
                      env.esel.all + nmT.all, sbank.all)
            else:
                cx.mm(sbank[:, 0:256], cm.ident, env.cmask[:, kt - 2 * qb, 0:256], False, True, env.cmask.all, sbank.all)
            pT = pTs[it % 3]
            it += 1
            cx.activation(pT[:, :], sbank[:, 0:256], AF.Exp, sbank.all, pT.all)
            cx.mm(obank[:, 0:256], mV[:, kt, :], pT[:, :], kt == 0, kt == nkt - 1, mV.all + pT.all, obank.all)
            cx.mm(dbank[:, 0:256], cm.ones, pT[:, :], kt == 0, kt == nkt - 1, pT.all, dbank.all)
        softmax_finish(cx, env, obank, dbank, 256, env.y_out(0, qb * 256, 256), qb)


def mix_hgrn(cx, cm, env, layer):
    S = cx.S
    w = load_w(cx, env, "hgrn")
    hq = cx.sb([128, SEQ], F32, name="hq")
    lf = cx.sb([128, SEQ], F32, name="lf")
    hk = cx.sb([128, SEQ], BF16, name="hk")
    sg = cx.sb([128, SEQ], BF16, name="sg")
    hv = cx.sb([128, 32, 128], BF16, name="hv")
    lb = cx.sb([128, 2], F32, name="lb")
    if layer == 0:
        cx.memset(lb[:, 0:1], 0.0, lb.all, eng="dve")
        cx.memset(lb[:, 1:2], 1.0, lb.all, eng="dve")
    else:
        ee = cx.sb([128, 4], F32, name="ee")
        cx.activation(ee[:, 0:2], env.small[:, SM_G0:SM_G0 + 2], AF.Exp, env.small.all, ee.all)
        cx.tt(ee[:, 2:3], ee[:, 0:1], ee[:, 1:2], ALU.add, ee.all, ee.all)
        S.dve(lambda e: e.reciprocal(out=ee[:, 3:4], in_=ee[:, 2:3]), ee.all, ee.all)
        cx.tt(lb[:, 0:1], ee[:, 1:2], ee[:, 3:4], ALU.mult, ee.all, lb.all)
        cx.ts(lb[:, 1:2], lb[:, 0:1], -1.0, 1.0, ALU.mult, ALU.add, lb.all, lb.all)
    sgm = [cx.sb([128, 512], F32, name=f"sgm{i}") for i in range(2)]
    ff_ = [cx.sb([128, 512], F32, name=f"ff{i}") for i in range(2)]

    def h_q(n, bank):
        cx.copy(hq[:, n * 512:(n + 1) * 512], bank[:, :], bank.all, hq.all)

    def h_f(n, bank):
        sl = slice(n * 512, (n + 1) * 512)
        a, f = sgm[n % 2], ff_[n % 2]
        cx.activation(a[:, :], bank[:, :], AF.Sigmoid, bank.all, a.all)
        cx.ts(f[:, :], a[:, :], lb[:, 1:2], lb[:, 0:1], ALU.mult, ALU.add, a.all + lb.all, f.all)
        cx.activation(lf[:, sl], f[:, :], AF.Ln, f.all, lf.all)
        cx.ts(hk[:, sl], f[:, :], -1.0, 1.0, ALU.mult, ALU.add, f.all, hk.all, eng="pool")

    def h_g(n, bank):
        sl = slice(n * 512, (n + 1) * 512)
        a = sgm[n % 2]
        cx.activation(a[:, :], bank[:, :], AF.Sigmoid, bank.all, a.all)
        cx.tt(sg[:, sl], bank[:, :], a[:, :], ALU.mult, bank.all + a.all, sg.all)

    def h_v(tl, bank):
        cx.copy(hv[:, tl, :], bank[:, 0:128], bank.all, hv.all, eng="act" if tl % 2 else "dve")

    project(cx, cm, env, w, [(0, h_q), (128, h_f), (256, h_g)], (384, 128, h_v))

    rm = cx.sb([128, SEQ], BF16, name="rm")
    cx.memset(rm[:, :], 1.0, rm.all)
    cx.memset(rm[:, :].rearrange("p (a b) -> p a b", b=64)[:, :, 0:1], 0.0, rm.all)
    bb = cx.sb([128, SEQ], F32, name="bb")
    S.dve(lambda e: e.tensor_tensor_scan(out=bb[:, :], data0=rm[:, :], data1=lf[:, :], initial=0.0,
                                         op0=ALU.mult, op1=ALU.add), rm.all + lf.all, bb.all)
    bv = bb[:, :].rearrange("p (a b) -> p a b", b=64)
    sm = cx.sb([128, 5, 64], F32, name="hsm")
    cx.copy(sm[:, 0, :], bv[:, :, 31], bb.all, sm.all)
    cx.activation(sm[:, 4, :], bv[:, :, 31], AF.Exp, bb.all, sm.all)
    cx.activation(sm[:, 1, :], bv[:, :, 63], AF.Exp, bb.all, sm.all)
    cx.tt(sm[:, 3, :], bv[:, :, 63], sm[:, 0, :], ALU.subtract, bb.all + sm.all, sm.all)
    cx.activation(sm[:, 2, :], sm[:, 3, :], AF.Exp, sm.all, sm.all)
    cx.tt(bv, bv, sm[:, 0, :].unsqueeze(2).to_broadcast([128, 64, 64]), ALU.subtract, bb.all + sm.all, bb.all)
    E = lf
    qt_ = cx.sb([128, SEQ], BF16, name="qtil")
    kt_ = cx.sb([128, SEQ], BF16, name="ktil")
    kh_ = cx.sb([128, SEQ], BF16, name="khat")
    cx.activation(E[:, :], bb[:, :], AF.Exp, bb.all, E.all)
    cx.tt(qt_[:, :], hq[:, :], E[:, :], ALU.mult, hq.all + E.all, qt_.all)
    cx.activation(E[:, :], bb[:, :], AF.Exp, bb.all, E.all, scale=-1.0)
    cx.tt(kt_[:, :], hk[:, :], E[:, :], ALU.mult, hk.all + E.all, kt_.all)
    cx.tt(kh_[:, :].rearrange("p (a b) -> p a b", b=64), kt_[:, :].rearrange("p (a b) -> p a b", b=64),
          sm[:, 2, :].unsqueeze(2).to_broadcast([128, 64, 64]), ALU.mult, kt_.all + sm.all, kh_.all, eng="pool")
    khtm = cx.sb([128, 32, 128], BF16, name="khtm")
    for grp in range(4):
        for i in range(8):
            tl = grp * 8 + i
            cx.transpose(cm.bankb[:, i * 128:(i + 1) * 128], kh_[:, tl * 128:(tl + 1) * 128], cm.ident,
                         kh_.all, cm.bankb.all)
        cx.copy(khtm[:, grp * 8:(grp + 1) * 8, :], cm.bankb[:, :].rearrange("p (a b) -> p a b", b=128),
                cm.bankb.all, khtm.all, eng="act" if grp % 2 else "dve")
    attT = cx.sb([128, 32, 128], BF16, nsub=8, name="attT")
    for grp in range(8):
        ab = cm.banks[grp % 2]
        for i in range(4):
            tl = grp * 4 + i
            ts_ = slice(tl * 128, (tl + 1) * 128)
            cx.mm(ab[:, i * 128:(i + 1) * 128], kt_[:, ts_], qt_[:, ts_], True, True, kt_.all + qt_.all, ab.all)
        cx.tt(attT[:, grp * 4:(grp + 1) * 4, :], ab[:, :].rearrange("p (a b) -> p a b", b=128),
              env.hmask[:, :].unsqueeze(1).to_broadcast([128, 4, 128]), ALU.mult, ab.all + env.hmask.all, [attT.b[grp]])
    S32 = cx.sb([128, 2, 128], F32, nsub=2, name="S32")
    Sb = cx.sb([128, 4, 128], BF16, nsub=4, name="Sb")
    cx.memset(S32[:, 0, :], 0.0, [S32.b[0]], eng="dve")
    cx.memset(Sb[:, 0, :], 0.0, [Sb.b[0]], eng="dve")
    oT = [cx.sb([128, 512], F32, name=f"oT{i}") for i in range(2)]
    sq = [cx.sb([128, 512], BF16, name=f"osq{i}") for i in range(2)]
    rs = [cx.sb([128, 512], F32, name=f"ors{i}") for i in range(2)]
    yo = [cx.sb([128, 512], BF16, name=f"oyo{i}") for i in range(2)]
    for tl in range(32):
        ob = cm.banks[2 + (tl // 4) % 2]
        for half in range(2):
            c = 2 * tl + half
            ps = slice(half * 64, half * 64 + 64)
            mslot = 0
            mb = cm.banks[4 + c % 2]
            cx.mm(mb[:, mslot * 128:(mslot + 1) * 128], khtm[ps, tl, :], hv[ps, tl, :], True, True,
                  khtm.all + hv.all, [mb.b[mslot]])
            oc = slice((tl % 4) * 128 + half * 64, (tl % 4) * 128 + half * 64 + 64)
            tsl = slice(c * 64, c * 64 + 64)
            cx.mm(ob[:, oc], hv[:, tl, :], attT[:, tl, half * 64:half * 64 + 64], True, False,
                  hv.all + [attT.b[tl // 4]], [ob.b[tl % 4]])
            cx.mm(ob[:, oc], Sb[:, c % 4, :], qt_[:, tsl], False, True, [Sb.b[c % 4]] + qt_.all, [ob.b[tl % 4]])
            cx.stt(S32[:, (c + 1) % 2, :], S32[:, c % 2, :], sm[:, 1, c:c + 1], mb[:, mslot * 128:(mslot + 1) * 128],
                   ALU.mult, ALU.add, [S32.b[c % 2], mb.b[mslot]] + sm.all, [S32.b[(c + 1) % 2]])
            if c + 1 < 64:
                cx.S.act(lambda e, c=c: e.mul(out=Sb[:, (c + 1) % 4, :], in_=S32[:, (c + 1) % 2, :], mul=sm[:, 4, c + 1:c + 2]),
                         [S32.b[(c + 1) % 2]] + sm.all, [Sb.b[(c + 1) % 4]])
        if tl % 4 == 3:
            n = tl // 4
            k = n % 2
            sl = slice(n * 512, (n + 1) * 512)
            cx.copy(oT[k][:, :], ob[:, :], ob.all, oT[k].all)
            cx.activation(sq[k][:, :], ob[:, :], AF.Square, ob.all, sq[k].all)
            nb_ = cm.banks[6]
            cx.mm(nb_[:, :], cm.ones, sq[k][:, :], True, True, sq[k].all, nb_.all)
            cx.activation(rs[k][:, :], nb_[:, :], AF.Ln, nb_.all + cm.epst.all, rs[k].all, bias=cm.eps_ap, scale=1.0 / HD)
            cx.activation(rs[k][:, :], rs[k][:, :], AF.Exp, rs[k].all, rs[k].all, scale=-0.5)
            cx.stt(oT[k][:, :], oT[k][:, :], env.small[:, SM_HNORM:SM_HNORM + 1], rs[k][:, :], ALU.mult, ALU.mult,
                   oT[k].all + rs[k].all + env.small.all, oT[k].all)
            cx.tt(yo[k][:, :], oT[k][:, :], sg[:, sl], ALU.mult, oT[k].all + sg.all, yo[k].all, eng="pool")
            cx.store(env.y_out(2, n * 512, 512), yo[k][:, :], yo[k].all, is_output=env.y_is_output)


def mix_pool(cx, cm, env):
    w = load_w(cx, env, "pool")
    pin = cx.sb([128, 32, 128], BF16, nsub=32, name="pin")
    pM = cx.sb([128, 3, 128], BF16, name="pM")
    cx.load(pM[:, :, :], env.poolM, pM.all)
    pw = cx.sb([128, 128], BF16, name="pw")
    cx.load(pw[:, :], env.poolw, pw.all, q="pool")

    def h_p(tl, bank):
        cx.copy(pin[:, tl, :], bank[:, 0:128], bank.all, [pin.b[tl]], eng="act" if tl % 2 else "dve")

    project(cx, cm, env, w, [], (0, 128, h_p))
    pt = [cx.sb([128, 512], BF16, name=f"ppt{i}") for i in range(2)]
    yo = [cx.sb([128, 512], BF16, name=f"pyo{i}") for i in range(2)]
    for n in range(8):
        pb = cm.banks[n % 2]
        for i in range(4):
            tl = n * 4 + i
            cs = slice(i * 128, (i + 1) * 128)
            cx.mm(pb[:, cs], pin[:, tl, :], pM[:, 0 if tl == 0 else 1, :], True, tl == 0, [pin.b[tl]] + pM.all, [pb.b[i]])
            if tl > 0:
                cx.mm(pb[:, cs], pin[:, tl - 1, :], pM[:, 2, :], False, True, [pin.b[tl - 1]] + pM.all, [pb.b[i]])
        k = n % 2
        cx.copy(pt[k][:, :], pb[:, :], pb.all, pt[k].all, eng="act")
        ob = cm.banks[2 + n % 2]
        cx.mm(ob[:, :], pw[:, :], pt[k][:, :], True, True, pw.all + pt[k].all, ob.all)
        cx.ts(yo[k][:, :], ob[:, :], env.small[:, SM_PSCALE:SM_PSCALE + 1], None, ALU.mult, None,
              ob.all + env.small.all, yo[k].all)
        cx.store(env.y_out(3, n * 512, 512), yo[k][:, :], yo[k].all, is_output=env.y_is_output)


def setup_common32(cx, cm, c32_ap):
    cm.c32 = cx.sb([128, 256], F32, name="c32")
    cx.load(cm.c32[:, :], c32_ap, cm.c32.all)
    cm.ident32 = cm.c32[:, 0:128]
    cm.ones32 = cm.c32[:, 128:256]
    cm.one_ap = cm.c32[:, 128:129]


def mix_globals(cx, env, cmask_d, esel_d, hmask_d):
    env.cmask = cx.sb([128, 4, 512], BF16, name="cmask")
    cx.load(env.cmask[:, :, :], cmask_d, env.cmask.all)
    env.esel = cx.sb([16, 16, 128], BF16, name="esel")
    cx.load(env.esel[:, :, :], esel_d, env.esel.all)
    env.hmask = cx.sb([128, 128], BF16, name="hmask")
    cx.load(env.hmask[:, :], hmask_d, env.hmask.all)


def phase_mix(cx, cm, env, layer, small_d, which=("fox", "moba", "hgrn", "pool")):
    with contextlib.ExitStack() as outer:
        cx.scope = outer
        env.small = cx.sb([128, N_SMALL], F32, name="small")
        cx.load(env.small[:, :], small_d, env.small.all)
        env.hs = [cx.sb([128, NCH, 512], BF16, name=f"hs{i}") for i in range(2)]
        env.rc = [cx.sb([128, 512], F32, name=f"rc{i}") for i in range(2)]
        env.yt = [cx.sb([128, 512], BF16, name=f"yt{i}") for i in range(2)]
        fns = {"fox": lambda: mix_fox(cx, cm, env), "moba": lambda: mix_moba(cx, cm, env),
               "hgrn": lambda: mix_hgrn(cx, cm, env, layer), "pool": lambda: mix_pool(cx, cm, env)}
        for name in which:
            with contextlib.ExitStack() as sc:
                cx.scope = sc
                fns[name]()
                cx.S.fence()
            cx.scope = outer
    cx.scope = None


def build_mix(layer, which=("fox", "moba", "hgrn", "pool")):
    cx = Ctx()
    env = MixEnv()
    env.hnT = cx.din("hnT", [D_MODEL, SEQ], BF16)
    env.wmix = cx.din("wmix", [128, NCH, W_MIXCOLS], F32)
    small_d = cx.din("small", [128, N_SMALL], F32)
    env.poolw = cx.din("poolw", [128, 128], F32)
    consts = cx.din("consts", [128, 256], BF16)
    c32 = cx.din("c32", [128, 256], F32)
    cmask_d = cx.din("cmask", [128, 4, 512], BF16)
    env.perm = cx.din("perm", [128, 128], BF16)
    env.ropeC = cx.din("ropeC", [128, SEQ], F32)
    env.ropeS = cx.din("ropeS", [128, SEQ], F32)
    esel_d = cx.din("esel", [16, 16, 128], BF16)
    hmask_d = cx.din("hmask", [128, 128], BF16)
    env.poolM = cx.din("poolM", [128, 3, 128], BF16)
    env.yT = cx.dout("yT", [4, 128, SEQ], BF16)
    hview_ = env.hnT.rearrange("(c p) t -> p c t", p=128)
    env.hn_chunk = lambda n, k: hview_[:, 4 * k:4 * k + 4, n * 512:(n + 1) * 512]
    env.y_out = lambda bi, t0, ncols: env.yT[bi, :, t0:t0 + ncols]
    env.y_is_output = True
    cm = Common(cx, consts)
    add_eps(cx, cm)
    setup_common32(cx, cm, c32)
    mix_globals(cx, env, cmask_d, esel_d, hmask_d)
    phase_mix(cx, cm, env, layer, small_d, which)
    cx.S.emit()
    return cx.nc


CC_GROUPS = [[0, 1, 2, 3], [4, 5, 6, 7]]


def _core_quarter(cx, e):
    if getattr(cx, "_q", None) is None:
        cx._q = e.snap(e.partition_id() % 4, min_val=0, max_val=3)
    return cx._q


def build_fused(depth):
    cx = Ctx()
    nc = cx.nc
    S = cx.S
    consts = cx.din("consts", [128, 256], BF16)
    c32 = cx.din("c32", [128, 256], F32)
    xT = cx.din("xT", [D_MODEL, TOK], F32)
    memT = cx.din("memT", [D_MODEL, NMEM], F32)
    gpre = cx.din("gpre", [128, NCH], F32)
    cmask_d = cx.din("cmask", [128, 4, 512], BF16)
    esel_d = cx.din("esel", [16, 16, 128], BF16)
    hmask_d = cx.din("hmask", [128, 128], BF16)
    env = MixEnv()
    env.perm = cx.din("perm", [128, 128], BF16)
    env.ropeC = cx.din("ropeC", [128, SEQ], F32)
    env.ropeS = cx.din("ropeS", [128, SEQ], F32)
    env.poolM = cx.din("poolM", [128, 3, 128], BF16)
    wmix_d = [cx.din(f"wmix_{l}", [128, NCH, W_MIXCOLS], F32) for l in range(depth)]
    small_d = [cx.din(f"small_{l}", [128, N_SMALL], F32) for l in range(depth)]
    poolw_d = [cx.din(f"poolw_{l}", [128, 128], F32) for l in range(depth)]
    ios = []
    for l in range(depth):
        io = TokIO()
        tok_weight_inputs(cx, io, f"_{l}")
        io.memT = memT
        ios.append(io)
    outT = cx.dout("outT", [D_MODEL, TOK], F32)
    hn_own = [nc.dram_tensor(f"hn_own{k}", [512, TOK], BF16, kind="Internal").ap() for k in range(4)]
    hn_all = [nc.dram_tensor(f"hn_all{k}", [4 * 512, TOK], BF16, kind="Internal").ap() for k in range(4)]
    y_own = [nc.dram_tensor(f"y_own{n}", [512, TOK], BF16, kind="Internal").ap() for n in range(4)]
    y_all = [nc.dram_tensor(f"y_all{n}", [4 * 512, TOK], BF16, kind="Internal").ap() for n in range(4)]
    h_res = nc.dram_tensor("h_res", [D_MODEL, TOK], F32, kind="Internal").ap()

    cm = Common(cx, consts)
    add_eps(cx, cm)
    setup_common32(cx, cm, c32)
    mix_globals(cx, env, cmask_d, esel_d, hmask_d)
    S.sp_init = lambda e: _core_quarter(cx, e)

    with contextlib.ExitStack() as sc:
        cx.scope = sc
        hn_own_v = [a.rearrange("(c p) t -> p c t", p=128) for a in hn_own]
        phase_pre(cx, cm, xT, gpre, lambda n, k: hn_own_v[k][:, :, n * 512:(n + 1) * 512], False)
        S.fence()
    cx.scope = None

    hn_all_v = [a.rearrange("(r c p) t -> r p c t", r=4, p=128) for a in hn_all]
    y_own_v = [a.rearrange("(tq d) t -> tq d t", tq=4) for a in y_own]
    y_all_v = [a.rearrange("(hh tq d) t -> tq d hh t", hh=4, tq=4) for a in y_all]
    env.hn_chunk = lambda n, k: hn_all_v[k][n // 2][:, :, (n % 2) * 512:(n % 2 + 1) * 512]
    env.y_out = lambda bi, t0, ncols: y_own_v[bi][t0 // TOK][:, t0 % TOK:t0 % TOK + ncols]
    env.y_is_output = False
    h_res_v = h_res.rearrange("(c p) t -> p c t", p=128)
    x_v = xT.rearrange("(c p) t -> p c t", p=128)
    out_v = outT.rearrange("(c p) t -> p c t", p=128)

    for l in range(depth):
        last = l == depth - 1
        for k in range(4):
            S.collective(lambda e, k=k: e.collective_compute("AllGather", ALU.bypass, replica_groups=CC_GROUPS,
                                                             ins=[hn_own[k].opt()], outs=[hn_all[k].opt()]), [], [])
        S.fence()
        env.wmix = wmix_d[l]
        env.poolw = poolw_d[l]
        phase_mix(cx, cm, env, l, small_d[l])
        for n in range(4):
            S.collective(lambda e, n=n: e.collective_compute("AllGather", ALU.bypass, replica_groups=CC_GROUPS,
                                                             ins=[y_own[n].opt()], outs=[y_all[n].opt()]), [], [])
        S.fence()
        io = ios[l]
        src_v = x_v if l == 0 else h_res_v
        dst_v = out_v if last else h_res_v
        io.hT_in = lambda hf, c0, c1, src_v=src_v: src_v[:, c0:c1, hf * HALF:(hf + 1) * HALF]
        io.hn_in = lambda hf, k: hn_own_v[k][:, :, hf * HALF:(hf + 1) * HALF]

        def ybr_load(dst, hf, wb):
            for n in range(4):
                def f(e, n=n):
                    q = _core_quarter(cx, e)
                    src = y_all_v[n][bass.ds(q, 1)][0][:, :, hf * HALF:(hf + 1) * HALF]
                    return e.dma_start(out=dst[:, n * 4:(n + 1) * 4, :], in_=src)
                S.dma("sp", f, [], list(wb))
        io.ybr_load = ybr_load
        io.hT_out = lambda hf, c0, c1, dst_v=dst_v: dst_v[:, c0:c1, hf * HALF:(hf + 1) * HALF]
        io.hn_out = lambda hf, k: hn_own_v[k][:, :, hf * HALF:(hf + 1) * HALF]
        io.h_is_output = last
        io.hn_is_output = False
        io.write_hn_when_last = False
        with contextlib.ExitStack() as sc:
            cx.scope = sc
            phase_tok(cx, cm, io, l, last)
            S.fence()
        cx.scope = None
    S.emit()
    return nc


IN_OFF = {"mq": 0, "mk": 512, "mv": 1024, "fq": 1536, "fk": 2048, "fv": 2560, "ff": 3072,
          "hq": 3076, "hf": 3588, "hi": 4100, "hg": 4612, "pin": 5124, "gl": 5636}
_HC = {}


def host_consts():
    if _HC:
        return _HC
    f32 = np.float32
    c = np.zeros((128, 256), f32)
    c[:, :128] = np.eye(128)
    c[:, 128:] = 1
    _HC["c32"] = c
    _HC["consts"] = c.astype(NPBF)
    k = np.arange(128)[:, None, None]
    a = np.arange(4)[None, :, None]
    q = np.arange(512)[None, None, :]
    _HC["cmask"] = np.where(q >= 128 * a + k, 0.0, NEG).astype(NPBF)
    perm = np.zeros((128, 128), f32)
    d = np.arange(128)
    perm[(d + 64) % 128, d] = 1
    _HC["perm"] = perm.astype(NPBF)
    inv_freq = (f32(10000.0) ** (-np.arange(64, dtype=f32) * f32(2.0) / f32(128))).astype(f32)
    ang = (np.arange(SEQ, dtype=f32)[None, :] * inv_freq[:, None]).astype(f32)
    cos, sin = np.cos(ang).astype(f32), np.sin(ang).astype(f32)
    _HC["ropeC"] = np.ascontiguousarray(np.concatenate([cos, cos], 0))
    _HC["ropeS"] = np.ascontiguousarray(np.concatenate([-sin, sin], 0))
    es = np.zeros((16, 16, 128), f32)
    for j in range(16):
        es[j, j, :] = 1
    _HC["esel"] = es.astype(NPBF)
    s = np.arange(128)[:, None]
    t = np.arange(128)[None, :]
    _HC["hmask"] = ((s // 64 == t // 64) & (s <= t)).astype(f32).astype(NPBF)
    for h, w in enumerate(POOL_WINDOWS):
        M = np.zeros((128, 3, 128), f32)
        eye = (s == t).astype(f32)
        band = ((s <= t) & (s > t - w)).astype(f32)
        M[:, 0, :] = band / np.minimum(w, t + 1).astype(f32) - eye
        M[:, 1, :] = band / f32(w) - eye
        M[:, 2, :] = ((s - 128) > (t - w)).astype(f32) / f32(w)
        _HC[f"poolM{h}"] = M.astype(NPBF)
    return _HC


def fm_layout(w):
    K, N = w.shape
    return np.ascontiguousarray(w.reshape(K // 128, 128, N).transpose(1, 0, 2))


def mix_inputs(inp, l, c, hnT_b):
    b, h = c // 4, c % 4
    hc = host_consts()
    w_in = inp["w_in"][l]
    hs = slice(h * 128, (h + 1) * 128)

    def col(name):
        return w_in[:, IN_OFF[name] + h * 128: IN_OFF[name] + (h + 1) * 128]
    wm = np.concatenate([col(n) for n in ("mq", "mk", "mv", "fq", "fk", "fv", "hq", "hf", "hg", "hi", "pin")], axis=1)
    small = np.zeros((128, N_SMALL), np.float32)
    small[:, SM_WFF:SM_WFF + NCH] = w_in[:, IN_OFF["ff"] + h].reshape(NCH, 128).T
    small[:, SM_G0] = inp["hgrn_lb_logits"][0, hs]
    small[:, SM_G1] = inp["hgrn_lb_logits"][1, hs]
    small[:, SM_HNORM] = inp["hgrn_out_norm"][l, hs]
    small[:, SM_PSCALE] = inp["pool_scale"][l, hs]
    small[:, SM_FB] = inp["fox_f_bias"][l, h]
    return {"hnT": hnT_b, "wmix": fm_layout(wm), "small": small,
            "poolw": np.ascontiguousarray(inp["pool_w"][l, h]),
            "consts": hc["consts"], "c32": hc["c32"], "cmask": hc["cmask"], "perm": hc["perm"],
            "ropeC": hc["ropeC"], "ropeS": hc["ropeS"], "esel": hc["esel"], "hmask": hc["hmask"],
            "poolM": hc[f"poolM{h}"]}


GV_MIXPOST, GV_XAPRE, GV_XAMEM, GV_XAPOST, GV_MLPPRE, GV_MLPPOST, GV_NEXT = range(7)
HALF = 512
NMEM = 256
RING_N = 6
RING_ELEMS = 4096


class Ring:
    def __init__(self, cx, n=RING_N, elems=RING_ELEMS):
        self.cx = cx
        self.bufs = [cx.sb([128, elems], BF16, name=f"ring{i}") for i in range(n)]
        self.i = 0

    def load(self, src_ap, a, b):
        t = self.bufs[self.i % len(self.bufs)]
        self.i += 1
        view = t[:, 0:a * b].rearrange("p (a b) -> p a b", b=b)
        self.cx.load(view, src_ap, t.all, q="pool")
        return view, t.all


class TokIO:
    pass


def tok_io_external(cx):
    io = TokIO()
    hT_d = cx.din("hT", [D_MODEL, TOK], F32)
    hnT_d = cx.din("hnT", [D_MODEL, TOK], BF16)
    ybr_d = cx.din("ybr", [D_MODEL, TOK], BF16)
    tok_weight_inputs(cx, io, "")
    io.memT = cx.din("memT", [D_MODEL, NMEM], F32)
    hT_o = cx.dout("hT_out", [D_MODEL, TOK], F32)
    hn_o = cx.dout("hn_next", [D_MODEL, TOK], BF16)
    hview = hT_d.rearrange("(c p) t -> p c t", p=128)
    hnview = hnT_d.rearrange("(c p) t -> p c t", p=128)
    ybview = ybr_d.rearrange("(c p) t -> p c t", p=128)
    hoview = hT_o.rearrange("(c p) t -> p c t", p=128)
    hnoview = hn_o.rearrange("(c p) t -> p c t", p=128)
    io.hT_in = lambda hf, c0, c1: hview[:, c0:c1, hf * HALF:(hf + 1) * HALF]
    io.hn_in = lambda hf, k: hnview[:, 4 * k:4 * k + 4, hf * HALF:(hf + 1) * HALF]
    io.ybr_load = lambda dst, hf, wb: cx.load(dst, ybview[:, :, hf * HALF:(hf + 1) * HALF], wb)
    io.hT_out = lambda hf, c0, c1: hoview[:, c0:c1, hf * HALF:(hf + 1) * HALF]
    io.hn_out = lambda hf, k: hnoview[:, 4 * k:4 * k + 4, hf * HALF:(hf + 1) * HALF]
    io.h_is_output = True
    io.hn_is_output = True
    io.write_hn_when_last = True
    return io


def tok_weight_inputs(cx, io, sfx):
    io.wg_d = cx.din("wg" + sfx, [64, 128, NCH, 128], F32)
    io.wb_d = cx.din("wb" + sfx, [64, 128, 4, 128], F32)
    io.wo_d = cx.din("wo" + sfx, [16, 128, NCH, 128], F32)
    io.xq_d = cx.din("xq" + sfx, [4, 128, NCH, 128], F32)
    io.xk_d = cx.din("xk" + sfx, [4, 128, NCH, 128], F32)
    io.xv_d = cx.din("xv" + sfx, [2, 128, 8, 512], F32)
    io.xo_d = cx.din("xo" + sfx, [16, 128, 4, 128], F32)
    io.wup_d = cx.din("wup" + sfx, [64, 128, NCH, 128], F32)
    io.wdn_d = cx.din("wdn" + sfx, [32, 128, 32, 128], F32)
    io.gv_d = cx.din("gv" + sfx, [128, 7 * NCH], F32)


def build_tok(layer, last):
    cx = Ctx()
    io = tok_io_external(cx)
    consts = cx.din("consts", [128, 256], BF16)
    cm = Common(cx, consts)
    add_eps(cx, cm)
    phase_tok(cx, cm, io, layer, last)
    cx.S.emit()
    return cx.nc


def phase_tok(cx, cm, io, layer, last):
    S = cx.S
    wg_d, wb_d, wo_d, xq_d, xk_d, xv_d, xo_d, wup_d, wdn_d = (io.wg_d, io.wb_d, io.wo_d, io.xq_d, io.xk_d, io.xv_d,
                                                              io.xo_d, io.wup_d, io.wdn_d)
    memT_d, gv_d = io.memT, io.gv_d
    gv = cx.sb([128, 7 * NCH], F32, name="gv")
    cx.load(gv[:, :], gv_d, gv.all)

    def gcol(which, c):
        return gv[:, which * NCH + c: which * NCH + c + 1]

    ring = Ring(cx)
    hT = cx.sb([128, NCH, HALF], F32, nsub=NCH, name="hT")
    hnb = cx.sb([128, NCH, HALF], BF16, name="hnb")
    RA = cx.sb([128, 32, HALF], BF16, name="RA")
    ybr = RA[:, 0:NCH, :]
    sT = RA[:, NCH:2 * NCH, :]
    aT = cx.sb([128, NCH, HALF], F32, nsub=NCH, name="aT")
    sqp = [cx.sb([128, HALF], BF16, name=f"sq{i}") for i in range(3)]
    rstd = cx.sb([128, HALF], F32, name="rstd")
    tmpA = [cx.sb([128, HALF], F32, name=f"tmpA{i}") for i in range(2)]
    tmpB = [cx.sb([128, HALF], F32, name=f"tmpB{i}") for i in range(2)]
    acc = cx.sb([128, HALF], F32, name="acc")
    kx = cx.sb([128, 4, NMEM], BF16, name="kx")
    vx = cx.sb([128, 2, 512], BF16, name="vx")
    qx = cx.sb([128, 4, HALF], BF16, name="qx")
    ox = cx.sb([128, 4, HALF], BF16, name="ox")
    pTs = [cx.sb([128, HALF], BF16, name=f"xpT{i}") for i in range(2)]
    rc = cx.sb([128, HALF], F32, name="xrc")
    B = cm.banks

    def norm_add(gw):
        rms_stats(cx, cm, lambda c: aT[:, c, :], lambda c: [aT.b[c]], NCH, HALF, sqp, B[6], rstd, D_MODEL)
        for c in range(NCH):
            t = tmpA[c % 2]
            cx.stt(t[:, :], aT[:, c, :], gcol(gw, c), rstd[:, :], ALU.mult, ALU.mult,
                   [aT.b[c]] + gv.all + rstd.all, t.all)
            cx.tt(hT[:, c, :], hT[:, c, :], t[:, :], ALU.add, [hT.b[c]] + t.all, [hT.b[c]], eng="pool")

    def norm_to(gw, dst, dst_bufs):
        rms_stats(cx, cm, lambda c: hT[:, c, :], lambda c: [hT.b[c]], NCH, HALF, sqp, B[6], rstd, D_MODEL)
        for c in range(NCH):
            cx.stt(dst[:, c, :], hT[:, c, :], gcol(gw, c), rstd[:, :], ALU.mult, ALU.mult,
                   [hT.b[c]] + gv.all + rstd.all, dst_bufs)

    memf = aT
    cx.load(memf[:, :, 0:NMEM], memT_d.rearrange("(c p) t -> p c t", p=128), memf.all)
    rms_stats(cx, cm, lambda c: memf[:, c, 0:NMEM], lambda c: [memf.b[c]], NCH, NMEM, sqp, B[6], rstd, D_MODEL)
    memn = hnb
    for c in range(NCH):
        cx.stt(memn[:, c, 0:NMEM], memf[:, c, 0:NMEM], gcol(GV_XAMEM, c), rstd[:, 0:NMEM], ALU.mult, ALU.mult,
               [memf.b[c]] + gv.all + rstd.all, memn.all)
    for hd in range(4):
        wv_, wb_ = ring.load(xk_d[hd], NCH, 128)
        bk = B[hd % 2]
        for c in range(NCH):
            cx.mm(bk[:, 0:NMEM], wv_[:, c, :], memn[:, c, 0:NMEM], c == 0, c == NCH - 1, wb_ + memn.all, bk.all)
        cx.copy(kx[:, hd, :], bk[:, 0:NMEM], bk.all, kx.all)
    wv0, wb0 = ring.load(xv_d[0], 8, 512)
    wv1, wb1 = ring.load(xv_d[1], 8, 512)
    for mt in range(2):
        bk = B[2 + mt]
        for c in range(NCH):
            wv_, wb_ = (wv0, wb0) if c < 8 else (wv1, wb1)
            cx.mm(bk[:, :], memn[:, c, mt * 128:(mt + 1) * 128], wv_[:, c % 8, :], c == 0, c == NCH - 1,
                  wb_ + memn.all, bk.all)
        cx.copy(vx[:, mt, :], bk[:, :], bk.all, vx.all, eng="act")

    for hf in range(TOK // HALF):
        tsl = slice(hf * HALF, (hf + 1) * HALF)
        for c in range(0, NCH, 4):
            cx.load(hT[:, c:c + 4, :], io.hT_in(hf, c, c + 4), hT.b[c:c + 4])
        for k in range(4):
            cx.load(hnb[:, 4 * k:4 * k + 4, :], io.hn_in(hf, k), hnb.all)
        io.ybr_load(ybr, hf, RA.all)
        it = 0
        for j in range(NCH):
            for n in range(4):
                wg_, wgb = ring.load(wg_d[j * 4 + n], NCH, 128)
                wb_, wbb = ring.load(wb_d[j * 4 + n], 4, 128)
                bg, bp = B[it % 2], B[2 + it % 2]
                it += 1
                for c in range(NCH):
                    cx.mm(bg[:, :], wg_[:, c, :], hnb[:, c, :], c == 0, c == NCH - 1, wgb + hnb.all, bg.all)
                for hh in range(4):
                    cx.mm(bp[:, :], wb_[:, hh, :], ybr[:, n * 4 + hh, :], hh == 0, hh == 3, wbb + RA.all, bp.all)
                sg_ = tmpA[it % 2]
                cx.activation(sg_[:, :], bg[:, :], AF.Sigmoid, bg.all, sg_.all)
                if n == 0:
                    cx.tt(acc[:, :], bp[:, :], sg_[:, :], ALU.mult, bp.all + sg_.all, acc.all)
                else:
                    t2 = tmpB[it % 2]
                    cx.tt(t2[:, :], bp[:, :], sg_[:, :], ALU.mult, bp.all + sg_.all, t2.all)
                    if n < 3:
                        cx.tt(acc[:, :], acc[:, :], t2[:, :], ALU.add, acc.all + t2.all, acc.all, eng="pool")
                    else:
                        cx.tt(sT[:, j, :], acc[:, :], t2[:, :], ALU.add, acc.all + t2.all, RA.all, eng="pool")
        for j in range(NCH):
            w_, wbf = ring.load(wo_d[j], NCH, 128)
            bk = B[j % 2]
            for c in range(NCH):
                cx.mm(bk[:, :], w_[:, c, :], sT[:, c, :], c == 0, c == NCH - 1, wbf + RA.all, bk.all)
            cx.copy(aT[:, j, :], bk[:, :], bk.all, [aT.b[j]], eng="act" if j % 2 else "dve")
        norm_add(GV_MIXPOST)
        norm_to(GV_XAPRE, hnb, hnb.all)
        for hd in range(4):
            w_, wbf = ring.load(xq_d[hd], NCH, 128)
            bk = B[hd % 2]
            for c in range(NCH):
                cx.mm(bk[:, :], w_[:, c, :], hnb[:, c, :], c == 0, c == NCH - 1, wbf + hnb.all, bk.all)
            S.act(lambda e, hd=hd, bk=bk: e.mul(out=qx[:, hd, :], in_=bk[:, :], mul=SCALE), bk.all, qx.all)
        for hd in range(4):
            ob, db = B[2 + hd % 2], B[4 + hd % 2]
            for mt in range(2):
                sb_ = B[mt]
                cx.mm(sb_[:, :], kx[:, hd, mt * 128:(mt + 1) * 128], qx[:, hd, :], True, True, kx.all + qx.all, sb_.all)
                pT = pTs[mt]
                cx.activation(pT[:, :], sb_[:, :], AF.Exp, sb_.all, pT.all)
                cx.mm(ob[:, :], vx[:, mt, hd * 128:(hd + 1) * 128], pT[:, :], mt == 0, mt == 1, vx.all + pT.all, ob.all)
                cx.mm(db[:, :], cm.ones, pT[:, :], mt == 0, mt == 1, pT.all, db.all)
            S.dve(lambda e, db=db: e.reciprocal(out=rc[:, :], in_=db[:, :]), db.all, rc.all)
            cx.tt(ox[:, hd, :], ob[:, :], rc[:, :], ALU.mult, ob.all + rc.all, ox.all)
        for j in range(NCH):
            w_, wbf = ring.load(xo_d[j], 4, 128)
            bk = B[j % 2]
            for hd in range(4):
                cx.mm(bk[:, :], w_[:, hd, :], ox[:, hd, :], hd == 0, hd == 3, wbf + ox.all, bk.all)
            cx.copy(aT[:, j, :], bk[:, :], bk.all, [aT.b[j]], eng="act" if j % 2 else "dve")
        norm_add(GV_XAPOST)
        norm_to(GV_MLPPRE, hnb, hnb.all)
        uT = RA
        for fh in range(2):
            for fb in range(32):
                w_, wbf = ring.load(wup_d[fh * 32 + fb], NCH, 128)
                bk = B[fb % 4]
                for c in range(NCH):
                    cx.mm(bk[:, :], w_[:, c, :], hnb[:, c, :], c == 0, c == NCH - 1, wbf + hnb.all, bk.all)
                r_ = tmpA[fb % 2]
                cx.activation(r_[:, :], bk[:, :], AF.Relu, bk.all, r_.all)
                cx.tt(uT[:, fb, :], r_[:, :], r_[:, :], ALU.mult, r_.all, RA.all, eng="dve" if fb % 2 else "pool")
            for j in range(NCH):
                w_, wbf = ring.load(wdn_d[j * 2 + fh], 32, 128)
                bk = B[4 + j % 2]
                for fb in range(32):
                    cx.mm(bk[:, :], w_[:, fb, :], uT[:, fb, :], fb == 0, fb == 31, wbf + RA.all, bk.all)
                if fh == 0:
                    cx.copy(aT[:, j, :], bk[:, :], bk.all, [aT.b[j]], eng="act")
                else:
                    cx.tt(aT[:, j, :], bk[:, :], aT[:, j, :], ALU.add, bk.all + [aT.b[j]], [aT.b[j]])
        norm_add(GV_MLPPOST)
        for c in range(0, NCH, 4):
            cx.store(io.hT_out(hf, c, c + 4), hT[:, c:c + 4, :], hT.b[c:c + 4], is_output=io.h_is_output)
        if not last:
            norm_to(GV_NEXT, hnb, hnb.all)
            for k in range(4):
                cx.store(io.hn_out(hf, k), hnb[:, 4 * k:4 * k + 4, :], hnb.all, is_output=io.hn_is_output)
        elif io.write_hn_when_last:
            for k in range(4):
                cx.store(io.hn_out(hf, k), hnb[:, 4 * k:4 * k + 4, :], hnb.all, is_output=io.hn_is_output)


def tok_weights(inp, l):
    f32 = np.float32
    wg = inp["w_in"][l][:, IN_OFF["gl"]:]
    wg = wg.reshape(NCH, 128, 4, NCH, 128).transpose(3, 2, 1, 0, 4)
    wg = np.ascontiguousarray(wg).reshape(64, 128, NCH, 128)
    wb = inp["w_branch"][l].reshape(4, 4, 128, NCH, 128).transpose(3, 0, 2, 1, 4)
    wb = np.ascontiguousarray(wb).reshape(64, 128, 4, 128)

    def tiles(w, kc):
        K_, N_ = w.shape
        return np.ascontiguousarray(w.reshape(kc, 128, N_ // 128, 128).transpose(2, 1, 0, 3))
    wo = tiles(inp["w_mix_out"][l], NCH)
    xq = tiles(inp["xa_wq"][l], NCH)
    xk = tiles(inp["xa_wkv"][l][:, 0:512], NCH)
    wv = inp["xa_wkv"][l][:, 512:1024].reshape(2, 8, 128, 512).transpose(0, 2, 1, 3)
    xv = np.ascontiguousarray(wv)
    xo = tiles(inp["xa_wo"][l], 4)
    wup = tiles(inp["mlp_w_up"][l], NCH)
    wd = inp["mlp_w_down"][l].reshape(2, 32, 128, NCH, 128).transpose(3, 0, 2, 1, 4)
    wdn = np.ascontiguousarray(wd).reshape(32, 128, 32, 128)
    gv = np.zeros((128, 7 * NCH), f32)
    names = ["mix_norm_post", "xa_norm_pre", "xa_norm_mem", "xa_norm_post", "mlp_norm_pre", "mlp_norm_post"]
    for i, nme in enumerate(names):
        gv[:, i * NCH:(i + 1) * NCH] = inp[nme][l].reshape(NCH, 128).T
    if l + 1 < inp["mix_norm_pre"].shape[0]:
        gv[:, GV_NEXT * NCH:(GV_NEXT + 1) * NCH] = inp["mix_norm_pre"][l + 1].reshape(NCH, 128).T
    return {"wg": wg, "wb": wb, "wo": wo, "xq": xq, "xk": xk, "xv": xv, "xo": xo, "wup": wup, "wdn": wdn,
            "gv": gv, "consts": host_consts()["consts"]}


_PROGS = {}


def _prog(key, builder):
    if key not in _PROGS:
        _PROGS[key] = builder()
    return _PROGS[key]


def kernel_unfused(**inp):
    inp = {k: np.asarray(v) for k, v in inp.items()}
    depth = inp["w_in"].shape[0]
    cores = list(range(8))
    hc = host_consts()
    x = inp["x"]
    tsl = [slice((c % 4) * TOK, (c % 4 + 1) * TOK) for c in cores]
    hT = [np.ascontiguousarray(x[c // 4, tsl[c]].T) for c in cores]
    memT = [np.ascontiguousarray(inp["mem"][b].T) for b in range(BATCH)]
    g0 = np.ascontiguousarray(inp["mix_norm_pre"][0].reshape(NCH, 128).T)
    res = run_bass_kernel_spmd(_prog("pre", build_pre),
                               [{"xT": hT[c], "gpre": g0, "consts": hc["consts"]} for c in cores], core_ids=cores)
    hn = [np.asarray(res.results[c]["hn_out"]) for c in cores]
    for l in range(depth):
        hn_full = [np.ascontiguousarray(np.concatenate([hn[b * 4 + q] for q in range(4)], axis=1)) for b in range(BATCH)]
        res = run_bass_kernel_spmd(_prog(("mix", l), lambda: build_mix(l)),
                                   [mix_inputs(inp, l, c, hn_full[c // 4]) for c in cores], core_ids=cores)
        yT = [np.asarray(res.results[c]["yT"]) for c in cores]
        tw = tok_weights(inp, l)
        maps = []
        for c in cores:
            b = c // 4
            yb = np.stack([yT[b * 4 + h][:, :, tsl[c]] for h in range(4)], axis=1)
            m = dict(tw)
            m["hT"] = hT[c]
            m["hnT"] = hn[c]
            m["ybr"] = np.ascontiguousarray(yb.reshape(D_MODEL, TOK))
            m["memT"] = memT[b]
            maps.append(m)
        res = run_bass_kernel_spmd(_prog(("tok", l), lambda: build_tok(l, l == depth - 1)), maps, core_ids=cores)
        hT = [np.asarray(res.results[c]["hT_out"]) for c in cores]
        hn = [np.asarray(res.results[c]["hn_next"]) for c in cores]
    out = np.empty_like(x)
    for c in cores:
        out[c // 4, tsl[c]] = hT[c].T
    return out


def fused_inputs(inp, c):
    depth = inp["w_in"].shape[0]
    b, q = c // 4, c % 4
    hc = host_consts()
    m = {"consts": hc["consts"], "c32": hc["c32"], "cmask": hc["cmask"], "esel": hc["esel"], "hmask": hc["hmask"],
         "perm": hc["perm"], "ropeC": hc["ropeC"], "ropeS": hc["ropeS"], "poolM": hc[f"poolM{q}"],
         "xT": np.ascontiguousarray(inp["x"][b, q * TOK:(q + 1) * TOK].T),
         "memT": np.ascontiguousarray(inp["mem"][b].T),
         "gpre": np.ascontiguousarray(inp["mix_norm_pre"][0].reshape(NCH, 128).T)}
    for l in range(depth):
        mi = mix_inputs(inp, l, c, None)
        m[f"wmix_{l}"] = mi["wmix"]
        m[f"small_{l}"] = mi["small"]
        m[f"poolw_{l}"] = mi["poolw"]
    return m


def kernel(**inp):
    inp = {k: np.asarray(v) for k, v in inp.items()}
    depth = inp["w_in"].shape[0]
    cores = list(range(8))
    nc = _prog(("fused", depth), lambda: build_fused(depth))
    tws = []
    for l in range(depth):
        tw = tok_weights(inp, l)
        tw.pop("consts")
        tws.append({f"{k}_{l}": v for k, v in tw.items()})
    maps = []
    for c in cores:
        m = fused_inputs(inp, c)
        for tw in tws:
            m.update(tw)
        maps.append(m)
    res = run_bass_kernel_spmd(nc, maps, core_ids=cores)
    x = inp["x"]
    out = np.empty_like(x)
    for c in cores:
        out[c // 4, (c % 4) * TOK:(c % 4 + 1) * TOK] = np.asarray(res.results[c]["outT"]).T
    return out
```

```python
import contextlib
import numpy as np
import ml_dtypes
import concourse.bass as bass
import concourse.mybir as mybir
from concourse.bass_utils import run_bass_kernel_spmd

F32 = mybir.dt.float32
BF16 = mybir.dt.bfloat16
AF = mybir.ActivationFunctionType
ALU = mybir.AluOpType
AX = mybir.AxisListType
NPBF = ml_dtypes.bfloat16

D_MODEL = 2048
NCH = 16
BATCH = 2
SEQ = 4096
TOK = 1024
HD = 128
NEG = -30000.0
EPS = 1e-6
SCALE = HD ** -0.5
POOL_WINDOWS = (2, 4, 8, 16)
DEBUG_STOP = 0
ROPE_ADD_ENG = "dve"

COMPUTE = ("pe", "act", "dve", "pool")
N_DMA_SEMS = 12


class Buf:
    __slots__ = ("name", "w", "r", "excl")

    def __init__(self, name="", excl=False):
        self.name = name
        self.w = None
        self.r = []
        self.excl = excl


class Op:
    __slots__ = ("eng", "fn", "waits", "sig", "idx", "dma", "dma_ev")

    def __init__(self, eng, fn, sig, dma):
        self.eng = eng
        self.fn = fn
        self.waits = []
        self.sig = sig
        self.dma = dma
        self.dma_ev = None


class Sched:
    def __init__(self, nc):
        self.nc = nc
        self.ops = {e: [] for e in COMPUTE + ("sp",)}
        self.dma_count = {e: 0 for e in ("sp", "act", "pool")}
        self.out_events = []
        self.fence_waits = {e: [] for e in COMPUTE + ("sp",)}
        self.n_cc = 0

    def collective(self, fn, reads, writes):
        op = Op("pool", fn, False, False)
        op.dma = "cc"
        lst = self.ops["pool"]
        op.idx = len(lst)
        lst.append(op)
        ev = ("x", self.n_cc)
        op.dma_ev = ev
        self.n_cc += 1
        if self.fence_waits["pool"]:
            op.waits.extend(self.fence_waits["pool"])
            self.fence_waits["pool"] = []
        for b in reads:
            if b.w is not None:
                op.waits.append(b.w)
        for b in writes:
            if b.w is not None:
                op.waits.append(b.w)
            op.waits.extend(b.r)
        for d in op.waits:
            if d[0] == "c":
                self.ops[d[1]][d[2]].sig = True
        for b in reads:
            b.r.append(ev)
        for b in writes:
            b.w = ev
            b.r = []
        return ev

    def fence(self):
        evs = []
        for e in COMPUTE + ("sp",):
            lst = self.ops[e]
            for op in reversed(lst):
                if not op.dma:
                    evs.append(("c", e, op.idx))
                    op.sig = True
                    break
        for q, n in self.dma_count.items():
            for i in range(max(0, n - N_DMA_SEMS), n):
                evs.append(("d", q, i))
        for i in range(self.n_cc):
            evs.append(("x", i))
        for e in self.fence_waits:
            self.fence_waits[e] = list(evs)

    def _add(self, eng, fn, reads, writes, sig=False, dma=False):
        op = Op(eng, fn, sig, dma)
        lst = self.ops[eng]
        op.idx = len(lst)
        lst.append(op)
        if dma:
            n = self.dma_count[eng]
            self.dma_count[eng] += 1
            ev = ("d", eng, n)
            op.dma_ev = ev
            if n >= N_DMA_SEMS:
                op.waits.append(("d", eng, n - N_DMA_SEMS))
        else:
            ev = ("c", eng, op.idx)
        if self.fence_waits[eng]:
            op.waits.extend(d for d in self.fence_waits[eng] if d != ev)
            self.fence_waits[eng] = []
        excl_reads = [b for b in reads if b.excl]
        if excl_reads:
            reads = [b for b in reads if not b.excl]
            writes = list(writes) + excl_reads
        for b in reads:
            if b.w is not None and b.w != ev:
                op.waits.append(b.w)
        for b in writes:
            if b.w is not None and b.w != ev:
                op.waits.append(b.w)
            for d in b.r:
                if d != ev:
                    op.waits.append(d)
        if eng == "pe" and not dma:
            op.waits = [d for d in op.waits if not (d[0] == "c" and d[1] == "pe")]
        for d in op.waits:
            if d[0] == "c":
                self.ops[d[1]][d[2]].sig = True
        for b in reads:
            if not dma:
                b.r = [d for d in b.r if not (d[0] == "c" and d[1] == eng)]
            b.r.append(ev)
        for b in writes:
            b.w = ev
            b.r = []
        return ev

    def pe(self, fn, reads, writes, sig=False):
        return self._add("pe", fn, reads, writes, sig)

    def act(self, fn, reads, writes):
        return self._add("act", fn, reads, writes)

    def dve(self, fn, reads, writes):
        return self._add("dve", fn, reads, writes)

    def pool(self, fn, reads, writes):
        return self._add("pool", fn, reads, writes)

    def dma(self, q, fn, reads, writes, is_output=False):
        ev = self._add(q, fn, reads, writes, dma=True)
        if is_output:
            self.out_events.append(ev)
        return ev

    def emit(self):
        nc = self.nc
        with contextlib.ExitStack() as st:
            csem = {e: st.enter_context(nc.semaphore("c_" + e)) for e in COMPUTE}
            dsem = {q: [st.enter_context(nc.semaphore(f"d_{q}{i}")) for i in range(N_DMA_SEMS)]
                    for q in ("sp", "act", "pool")}
            xsem = [st.enter_context(nc.semaphore(f"x_{i}")) for i in range(self.n_cc)]
            block = st.enter_context(nc.Block())
            sigcount = {}
            for e in COMPUTE:
                lst = self.ops[e]
                last = None
                for op in lst:
                    if not op.dma:
                        last = op
                if last is not None:
                    last.sig = True
                c = 0
                arr = []
                for op in lst:
                    if (not op.dma) and op.sig:
                        c += 1
                    arr.append(c)
                sigcount[e] = arr
            self.stats = {e: (len(self.ops[e]), sigcount[e][-1] if sigcount[e] else 0) for e in COMPUTE}
            self.stats["dma"] = dict(self.dma_count)

            def resolve(ev):
                if ev[0] == "c":
                    _, e, i = ev
                    op = self.ops[e][i]
                    v = sigcount[e][i]
                    assert op.sig
                    return ("c", e), csem[e], v
                if ev[0] == "x":
                    return ev, xsem[ev[1]], 1
                _, q, n = ev
                return ("d", q, n % N_DMA_SEMS), dsem[q][n % N_DMA_SEMS], 16 * (n // N_DMA_SEMS + 1)

            def run_engine(ename, eng):
                known = {}
                for op in self.ops[ename]:
                    need = {}
                    for ev in op.waits:
                        key, sem, v = resolve(ev)
                        if known.get(key, 0) >= v:
                            continue
                        if need.get(key, (None, 0))[1] < v:
                            need[key] = (sem, v)
                    for key, (sem, v) in need.items():
                        eng.wait_ge(sem, v)
                        known[key] = v
                    ins = op.fn(eng)
                    if op.dma == "cc":
                        ins.then_inc(xsem[op.dma_ev[1]])
                    elif op.dma:
                        _, q, n = op.dma_ev
                        ins.then_inc(dsem[q][n % N_DMA_SEMS], 16)
                    elif op.sig:
                        ins.then_inc(csem[ename], 1)
                if ename == "sp":
                    for ev in self.out_events:
                        key, sem, v = resolve(ev)
                        if known.get(key, 0) >= v:
                            continue
                        eng.wait_ge(sem, v)
                        known[key] = v

            @block.sync
            def _(eng):
                if getattr(self, "sp_init", None) is not None:
                    self.sp_init(eng)
                run_engine("sp", eng)

            @block.tensor
            def _(eng):
                run_engine("pe", eng)

            @block.scalar
            def _(eng):
                run_engine("act", eng)

            @block.vector
            def _(eng):
                run_engine("dve", eng)

            @block.gpsimd
            def _(eng):
                run_engine("pool", eng)


class T:
    def __init__(self, t, nsub=1, name="", psum=False):
        self.t = t
        if psum:
            self.b = [Buf(name, excl=True)] * nsub
        else:
            self.b = [Buf(f"{name}{i}") for i in range(nsub)]

    def __getitem__(self, k):
        return self.t[k]

    @property
    def all(self):
        return list(self.b)


class Ctx:
    def __init__(self):
        self.nc = bass.Bass("TRN2", target_bir_lowering=False)
        self.S = Sched(self.nc)
        self._n = 0
        self.scope = None

    def sb(self, shape, dt, nsub=1, name=None):
        self._n += 1
        name = (name or "sb") + f"_{self._n}"
        if self.scope is not None:
            return T(self.scope.enter_context(self.nc.sbuf_tensor(name, list(shape), dt)), nsub, name)
        return T(self.nc.alloc_sbuf_tensor(name, list(shape), dt), nsub, name)

    def ps(self, shape, dt=F32, nsub=1, name=None):
        self._n += 1
        name = name or f"ps{self._n}"
        return T(self.nc.alloc_psum_tensor(name, list(shape), dt), nsub, name, psum=True)

    def din(self, name, shape, dt):
        return self.nc.dram_tensor(name, list(shape), dt, kind="ExternalInput").ap()

    def dout(self, name, shape, dt):
        return self.nc.dram_tensor(name, list(shape), dt, kind="ExternalOutput").ap()

    def load(self, dst_ap, src_ap, wbufs, q="sp", rbufs=()):
        return self.S.dma(q, lambda e: e.dma_start(out=dst_ap, in_=src_ap), list(rbufs), list(wbufs))

    def store(self, dst_ap, src_ap, rbufs, q="sp", is_output=True):
        return self.S.dma(q, lambda e: e.dma_start(out=dst_ap, in_=src_ap), list(rbufs), [], is_output=is_output)

    def mm(self, out_ap, lhsT, rhs, start, stop, reads, writes, sig=None):
        if sig is None:
            sig = False
        return self.S.pe(lambda e: e.matmul(out_ap, lhsT, rhs, start=start, stop=stop), reads, writes, sig=sig)

    def transpose(self, out_ap, in_ap, ident_ap, reads, writes):
        return self.S.pe(lambda e: e.transpose(out_ap, in_ap, ident_ap), reads, writes)

    def activation(self, out_ap, in_ap, func, reads, writes, bias=None, scale=None):
        kw = {}
        if bias is not None:
            kw["bias"] = bias
        if scale is not None:
            kw["scale"] = scale
        return self.S.act(lambda e: e.activation(out=out_ap, in_=in_ap, func=func, **kw), reads, writes)

    def tt(self, out_ap, in0, in1, op, reads, writes, eng="dve"):
        f = lambda e: e.tensor_tensor(out=out_ap, in0=in0, in1=in1, op=op)
        return (self.S.dve if eng == "dve" else self.S.pool)(f, reads, writes)

    def ts(self, out_ap, in0, s1, s2, op0, op1, reads, writes, eng="dve"):
        if op1 is None:
            f = lambda e: e.tensor_scalar(out=out_ap, in0=in0, scalar1=s1, scalar2=None, op0=op0)
        else:
            f = lambda e: e.tensor_scalar(out=out_ap, in0=in0, scalar1=s1, scalar2=s2, op0=op0, op1=op1)
        return (self.S.dve if eng == "dve" else self.S.pool)(f, reads, writes)

    def stt(self, out_ap, in0, scalar, in1, op0, op1, reads, writes):
        return self.S.dve(lambda e: e.scalar_tensor_tensor(out=out_ap, in0=in0, scalar=scalar, in1=in1,
                                                            op0=op0, op1=op1), reads, writes)

    def copy(self, out_ap, in_ap, reads, writes, eng="dve"):
        if eng == "act":
            return self.S.act(lambda e: e.copy(out=out_ap, in_=in_ap), reads, writes)
        f = lambda e: e.tensor_copy(out=out_ap, in_=in_ap)
        return (self.S.dve if eng == "dve" else self.S.pool)(f, reads, writes)

    def memset(self, ap, val, writes, eng="pool"):
        f = lambda e: e.memset(ap, val)
        return (self.S.dve if eng == "dve" else self.S.pool)(f, [], writes)


class Common:
    def __init__(self, cx, consts_ap):
        self.cx = cx
        self.cb = cx.sb([128, 256], BF16, name="cbf")
        cx.load(self.cb[:, :], consts_ap, self.cb.all)
        self.ident = self.cb[:, 0:128]
        self.ones = self.cb[:, 128:256]
        self.banks = [cx.ps([128, 512], F32, nsub=4, name=f"bank{i}") for i in range(7)]
        self.bankb = cx.ps([128, 1024], BF16, nsub=8, name="bankb")


def rms_stats(cx, cm, src_fn, src_bufs, nfeat_chunks, ncols, sq_pool, bank, rstd, denom):
    for c in range(nfeat_chunks):
        sq = sq_pool[c % len(sq_pool)]
        cx.activation(sq[:, 0:ncols], src_fn(c), AF.Square, src_bufs(c), sq.all)
        cx.mm(bank[:, 0:ncols], cm.ones, sq[:, 0:ncols], c == 0, c == nfeat_chunks - 1,
              [cm.cb.b[0]] + sq.all, bank.all)
    cx.activation(rstd[:, 0:ncols], bank[:, 0:ncols], AF.Ln, bank.all + cm.epst.all, rstd.all, bias=cm.eps_ap,
                  scale=1.0 / denom)
    cx.activation(rstd[:, 0:ncols], rstd[:, 0:ncols], AF.Exp, rstd.all, rstd.all, scale=-0.5)


def add_eps(cx, cm):
    cm.epst = cx.sb([128, 1], F32, name="epst")
    cx.memset(cm.epst[:, :], EPS, cm.epst.all)
    cm.eps_ap = cm.epst[:, 0:1]


def phase_pre(cx, cm, xT, gpre, hn_out_fn, is_output):
    g = cx.sb([128, NCH], F32, name="g")
    cx.load(g[:, :], gpre, g.all)
    xv = xT.rearrange("(c p) t -> p c t", p=128)
    xs = [cx.sb([128, NCH, 512], F32, nsub=NCH, name=f"xs{i}") for i in range(2)]
    hs = [cx.sb([128, NCH, 512], BF16, nsub=1, name=f"hs{i}") for i in range(2)]
    sqp = [cx.sb([128, 512], BF16, name=f"sq{i}") for i in range(3)]
    rstd = cx.sb([128, 512], F32, name="rstd")
    for n in range(TOK // 512):
        x = xs[n % 2]
        h = hs[n % 2]
        cx.load(x[:, :, :], xv[:, :, n * 512:(n + 1) * 512], x.all)
        rms_stats(cx, cm, lambda c: x[:, c, :], lambda c: [x.b[c]], NCH, 512, sqp, cm.banks[n % 2], rstd, D_MODEL)
        for c in range(NCH):
            cx.stt(h[:, c, :], x[:, c, :], g[:, c:c + 1], rstd[:, :], ALU.mult, ALU.mult,
                   [x.b[c]] + g.all + rstd.all, h.all)
        for k in range(4):
            cx.store(hn_out_fn(n, k), h[:, 4 * k:4 * k + 4, :], h.all, is_output=is_output)


def build_pre():
    cx = Ctx()
    xT = cx.din("xT", [D_MODEL, TOK], F32)
    gpre = cx.din("gpre", [128, NCH], F32)
    consts = cx.din("consts", [128, 256], BF16)
    hn_out = cx.dout("hn_out", [D_MODEL, TOK], BF16)
    cm = Common(cx, consts)
    add_eps(cx, cm)
    ov = hn_out.rearrange("(c p) t -> p c t", p=128)
    phase_pre(cx, cm, xT, gpre, lambda n, k: ov[:, 4 * k:4 * k + 4, n * 512:(n + 1) * 512], True)
    cx.S.emit()
    return cx.nc


W_OFF = {"moba": (0, 384), "fox": (384, 384), "hgrn": (768, 512), "pool": (1280, 128)}
W_MIXCOLS = 1408
SM_WFF, SM_G0, SM_G1, SM_HNORM, SM_PSCALE, SM_FB = 0, 16, 17, 18, 19, 20
N_SMALL = 24


class MixEnv:
    pass


def project(cx, cm, env, w, fm_blocks, tm, row=None, pre_chunk=None):
    nb = 0
    for n in range(SEQ // 512):
        hs = env.hs[n % 2]
        for k in range(4):
            cx.load(hs[:, 4 * k:4 * k + 4, :], env.hn_chunk(n, k), hs.all)
        if pre_chunk is not None:
            pre_chunk(n)
        for (c0, handler) in fm_blocks:
            bank = cm.banks[nb % 4]
            nb += 1
            for c in range(NCH):
                cx.mm(bank[:, :], w[:, c, c0:c0 + 128], hs[:, c, :], c == 0, c == NCH - 1,
                      w.all + hs.all, bank.all)
            handler(n, bank)
        if tm is not None:
            c0, ncols, handler = tm
            for tl in range(4):
                bank = cm.banks[nb % 4]
                nb += 1
                for c in range(NCH):
                    cx.mm(bank[:, 0:ncols], hs[:, c, tl * 128:(tl + 1) * 128], w[:, c, c0:c0 + ncols],
                          c == 0, c == NCH - 1, w.all + hs.all, bank.all)
                handler(n * 4 + tl, bank)
        if row is not None:
            lfn, rreads, handler = row
            bank = cm.banks[nb % 4]
            nb += 1
            for c in range(NCH):
                cx.mm(bank[0:1, :], lfn(c), hs[:, c, :], c == 0, c == NCH - 1, rreads + hs.all, bank.all)
            handler(n, bank)


def load_w(cx, env, name, eng="pool"):
    c0, nc_ = W_OFF[name]
    w = cx.sb([128, NCH, nc_], BF16, name="w_" + name)
    for c in range(0, NCH, 4):
        cx.load(w[:, c:c + 4, :], env.wmix[:, c:c + 4, c0:c0 + nc_], w.all, q=eng)
    return w


def softmax_finish(cx, env, obank, dbank, ncols, out_ap_dram, k):
    rc = env.rc[k % 2]
    yt = env.yt[k % 2]
    cx.S.dve(lambda e: e.reciprocal(out=rc[:, 0:ncols], in_=dbank[:, 0:ncols]), dbank.all, rc.all)
    cx.tt(yt[:, 0:ncols], obank[:, 0:ncols], rc[:, 0:ncols], ALU.mult, obank.all + rc.all, yt.all)
    cx.store(out_ap_dram, yt[:, 0:ncols], yt.all, is_output=env.y_is_output)


def mix_fox(cx, cm, env):
    S = cx.S
    w = load_w(cx, env, "fox")
    fQ = cx.sb([128, SEQ], BF16, name="fQ")
    fK = cx.sb([128, SEQ], BF16, name="fK")
    fV = cx.sb([128, 32, 128], BF16, name="fV")
    wff = cx.sb([128, NCH], BF16, name="wff")
    cx.copy(wff[:, :], env.small[:, SM_WFF:SM_WFF + NCH], env.small.all, wff.all)
    nfb = cx.sb([128, 1], F32, name="nfb")
    cx.ts(nfb[:, :], env.small[:, SM_FB:SM_FB + 1], -1.0, None, ALU.mult, None, env.small.all, nfb.all)
    sprow = cx.sb([1, SEQ], F32, name="sprow")
    cprow = cx.sb([1, SEQ], F32, name="cprow")
    nrh = cx.sb([1, SEQ], BF16, name="nrh")
    nrl = cx.sb([1, SEQ], BF16, name="nrl")
    rowtmp = cx.sb([1, 512], F32, name="rowtmp")

    def h_q(n, bank):
        S.act(lambda e: e.mul(out=fQ[:, n * 512:(n + 1) * 512], in_=bank[:, :], mul=SCALE), bank.all, fQ.all)

    def h_k(n, bank):
        cx.copy(fK[:, n * 512:(n + 1) * 512], bank[:, :], bank.all, fK.all)

    def h_v(tl, bank):
        cx.copy(fV[:, tl, :], bank[:, 0:128], bank.all, fV.all, eng="act" if tl % 2 else "dve")

    def h_row(n, bank):
        cx.activation(rowtmp[0:1, :], bank[0:1, :], AF.Exp, bank.all + nfb.all, rowtmp.all, bias=nfb[0:1, 0:1], scale=-1.0)
        cx.activation(sprow[0:1, n * 512:(n + 1) * 512], rowtmp[0:1, :], AF.Ln, rowtmp.all + cm.c32.all, sprow.all,
                      bias=cm.one_ap[0:1, 0:1])

    project(cx, cm, env, w, [(0, h_q), (128, h_k)], (256, 128, h_v),
            row=(lambda c: wff[:, c:c + 1], wff.all, h_row))

    cx.memset(nrh[0:1, :], 1.0, nrh.all, eng="dve")
    S.dve(lambda e: e.tensor_tensor_scan(out=cprow[0:1, :], data0=nrh[0:1, :], data1=sprow[0:1, :], initial=0.0,
                                         op0=ALU.mult, op1=ALU.add), nrh.all + sprow.all, cprow.all)
    sm = cm.banks[6]
    cpv = cprow[0:1, :].rearrange("o (a b) -> o a b", b=512)
    cx.mm(sm[:, 0:8], cm.ones32[0:1, 0:128], cpv[:, :, 0], True, True, cm.c32.all + cprow.all, sm.all)
    for kt in range(32):
        cx.mm(sm[:, 8 + kt:9 + kt], cprow[0:1, kt * 128:(kt + 1) * 128], cm.ones32[0:1, 0:1], True, True,
              cm.c32.all + cprow.all, sm.all)
    rbcp = cx.sb([128, 40], F32, name="rbcp")
    cx.copy(rbcp[:, :], sm[:, 0:40], sm.all, rbcp.all)
    for qc in range(8):
        sl = slice(qc * 512, (qc + 1) * 512)
        cx.ts(sprow[0:1, sl], cprow[0:1, sl], cprow[0:1, qc * 512:qc * 512 + 1], -1.0, ALU.subtract, ALU.mult,
              cprow.all, sprow.all)
    cx.copy(nrh[0:1, :], sprow[0:1, :], sprow.all, nrh.all)
    cx.tt(nrl[0:1, :], sprow[0:1, :], nrh[0:1, :], ALU.subtract, sprow.all + nrh.all, nrl.all)

    biasq = [cx.sb([128, 32], F32, name=f"biasq{i}") for i in range(2)]
    pTs = [cx.sb([128, 512], BF16, name=f"fpT{i}") for i in range(3)]
    it = 0
    for qc in range(8):
        sl = slice(qc * 512, (qc + 1) * 512)
        bq = biasq[qc % 2]
        cx.ts(bq[:, :], rbcp[:, 8:40], rbcp[:, qc:qc + 1], None, ALU.subtract, None, rbcp.all, bq.all)
        obank = cm.banks[2 + qc % 2]
        dbank = cm.banks[4 + qc % 2]
        nkt = 4 * (qc + 1)
        for kt in range(nkt):
            sbank = cm.banks[kt % 2]
            a = kt - 4 * qc
            cx.mm(sbank[:, :], fK[:, kt * 128:(kt + 1) * 128], fQ[:, sl], True, False, fK.all + fQ.all, sbank.all)
            cx.mm(sbank[:, :], cm.ones[0:1, 0:128], nrh[0:1, sl], False, False, nrh.all, sbank.all)
            cx.mm(sbank[:, :], cm.ones[0:1, 0:128], nrl[0:1, sl], False, a < 0, nrl.all, sbank.all)
            if a >= 0:
                cx.mm(sbank[:, :], cm.ident, env.cmask[:, a, :], False, True, env.cmask.all, sbank.all)
            pT = pTs[it % 3]
            it += 1
            cx.activation(pT[:, :], sbank[:, :], AF.Exp, sbank.all + bq.all, pT.all, bias=bq[:, kt:kt + 1])
            cx.mm(obank[:, :], fV[:, kt, :], pT[:, :], kt == 0, kt == nkt - 1, fV.all + pT.all, obank.all)
            cx.mm(dbank[:, :], cm.ones, pT[:, :], kt == 0, kt == nkt - 1, pT.all, dbank.all)
        softmax_finish(cx, env, obank, dbank, 512, env.y_out(1, qc * 512, 512), qc)


def mix_moba(cx, cm, env):
    S = cx.S
    w = load_w(cx, env, "moba")
    mQ = cx.sb([128, SEQ], BF16, name="mQ")
    mK = cx.sb([128, SEQ], BF16, name="mK")
    mV = cx.sb([128, 32, 128], BF16, name="mV")
    perm = cx.sb([128, 128], BF16, name="perm")
    cx.load(perm[:, :], env.perm, perm.all)
    rC = [cx.sb([128, 512], F32, name=f"rC{i}") for i in range(2)]
    rS = [cx.sb([128, 512], F32, name=f"rS{i}") for i in range(2)]
    xb = [cx.sb([128, 512], BF16, name=f"xb{i}") for i in range(2)]
    t1 = [cx.sb([128, 512], F32, name=f"t1{i}") for i in range(2)]
    t2 = [cx.sb([128, 512], F32, name=f"t2{i}") for i in range(2)]
    cnt = [0]

    def pre_chunk(n):
        cx.load(rC[n % 2][:, :], env.ropeC[:, n * 512:(n + 1) * 512], rC[n % 2].all)
        cx.load(rS[n % 2][:, :], env.ropeS[:, n * 512:(n + 1) * 512], rS[n % 2].all)

    def rope_handler(dst, sc):
        def h(n, bank):
            k = cnt[0] % 2
            cnt[0] += 1
            sl = slice(n * 512, (n + 1) * 512)
            if DEBUG_STOP == 10:
                cx.copy(dst[:, sl], bank[:, :], bank.all, dst.all)
                return
            cx.copy(xb[k][:, :], bank[:, :], bank.all, xb[k].all, eng="act")
            swb = cm.banks[4 + k]
            cx.mm(swb[:, :], perm[:, :], xb[k][:, :], True, True, perm.all + xb[k].all, swb.all)
            if DEBUG_STOP == 11:
                cx.copy(dst[:, sl], swb[:, :], swb.all, dst.all)
                return
            if DEBUG_STOP == 13:
                cx.stt(t1[k][:, :], bank[:, :], sc, t2[k][:, :], ALU.mult, ALU.mult, bank.all + t2[k].all, t1[k].all)
            else:
                cx.stt(t1[k][:, :], bank[:, :], sc, rC[n % 2][:, :], ALU.mult, ALU.mult, bank.all + rC[n % 2].all, t1[k].all)
            if DEBUG_STOP in (12, 13):
                cx.copy(dst[:, sl], t1[k][:, :], t1[k].all, dst.all)
                return
            cx.stt(t2[k][:, :], swb[:, :], sc, rS[n % 2][:, :], ALU.mult, ALU.mult, swb.all + rS[n % 2].all, t2[k].all)
            cx.tt(dst[:, sl], t1[k][:, :], t2[k][:, :], ALU.add, t1[k].all + t2[k].all, dst.all, eng=ROPE_ADD_ENG)
        return h

    def h_v(tl, bank):
        cx.copy(mV[:, tl, :], bank[:, 0:128], bank.all, mV.all, eng="act")

    project(cx, cm, env, w, [(0, rope_handler(mQ, SCALE)), (128, rope_handler(mK, 1.0))], (256, 128, h_v),
            pre_chunk=pre_chunk)

    if DEBUG_STOP in (1, 10, 11, 12, 13):
        return
    kb32 = cx.sb([128, 16], F32, name="kb32")
    kbT = cx.sb([128, 16], BF16, name="kbT")
    S.dve(lambda e: e.tensor_reduce(out=kb32[:, :], in_=mK[:, :].rearrange("p (j k) -> p j k", k=256),
                                    axis=AX.X, op=ALU.add), mK.all, kb32.all)
    cx.copy(kbT[:, :], kb32[:, :], kb32.all, kbT.all)
    gb = cm.banks[6]
    for qt in range(32):
        cx.mm(gb[:, qt * 16:(qt + 1) * 16], mQ[:, qt * 128:(qt + 1) * 128], kbT[:, :], True, True,
              mQ.all + kbT.all, gb.all)
    g_sb = cx.sb([128, 32, 16], F32, name="g_sb")
    cx.copy(g_sb[:, :, :], gb[:, :].rearrange("p (a b) -> p a b", b=16), gb.all, g_sb.all)
    if DEBUG_STOP == 2:
        return
    S.pool(lambda e: e.affine_select(out=g_sb[:, :, :], in_=g_sb[:, :, :], pattern=[[1, 16], [0, 2], [-1, 16]],
                                     compare_op=ALU.is_ge, fill=-1e30, base=-1, channel_multiplier=0),
           g_sb.all, g_sb.all)
    if DEBUG_STOP == 3:
        return
    m8 = cx.sb([128, 32, 8], F32, name="m8")
    for qt in range(32):
        S.dve(lambda e, qt=qt: e.max(out=m8[:, qt, :], in_=g_sb[:, qt, :]), g_sb.all, m8.all)
    thr = cx.sb([128, 32, 1], F32, name="thr")
    cx.ts(thr[:, :, :], m8[:, :, 2:3], -1e29, None, ALU.max, None, m8.all, thr.all)
    nm = cx.sb([128, 32, 16], F32, name="nm")
    cx.tt(nm[:, :, :], g_sb[:, :, :], thr[:, :, :].to_broadcast([128, 32, 16]), ALU.is_lt, g_sb.all + thr.all, nm.all)
    cx.ts(nm[:, :, :], nm[:, :, :], NEG, None, ALU.mult, None, nm.all, nm.all)
    if DEBUG_STOP == 4:
        return
    nmT = cx.sb([16, SEQ], BF16, name="nmT")
    for grp in range(8):
        tb = cm.banks[grp % 2]
        for i in range(4):
            qt = grp * 4 + i
            cx.transpose(tb[0:16, i * 128:(i + 1) * 128], nm[:, qt, :], cm.ident32, nm.all + cm.c32.all, tb.all)
        cx.copy(nmT[0:16, grp * 512:(grp + 1) * 512], tb[0:16, :], tb.all, nmT.all, eng="act" if grp % 2 else "dve")

    if DEBUG_STOP == 5:
        return
    pTs = [cx.sb([128, 256], BF16, name=f"mpT{i}") for i in range(3)]
    it = 0
    for qb in range(16):
        sl = slice(qb * 256, (qb + 1) * 256)
        obank = cm.banks[2 + qb % 2]
        dbank = cm.banks[4 + qb % 2]
        nkt = 2 * qb + 2
        for kt in range(nkt):
            sbank = cm.banks[kt % 2]
            cx.mm(sbank[:, 0:256], mK[:, kt * 128:(kt + 1) * 128], mQ[:, sl], True, False, mK.all + mQ.all, sbank.all)
            if kt < 2 * qb:
                cx.mm(sbank[:, 0:256], env.esel[0:16, kt // 2, :], nmT[0:16, sl], False, True,
                      env.esel.all + nmT.all, sbank.all)
            else:
                cx.mm(sbank[:, 0:256], cm.ident, env.cmask[:, kt - 2 * qb, 0:256], False, True, env.cmask.all, sbank.all)
            pT = pTs[it % 3]
            it += 1
            cx.activation(pT[:, :], sbank[:, 0:256], AF.Exp, sbank.all, pT.all)
            cx.mm(obank[:, 0:256], mV[:, kt, :], pT[:, :], kt == 0, kt == nkt - 1, mV.all + pT.all, obank.all)
            cx.mm(dbank[:, 0:256], cm.ones, pT[:, :], kt == 0, kt == nkt - 1, pT.all, dbank.all)
        softmax_finish(cx, env, obank, dbank, 256, env.y_out(0, qb * 256, 256), qb)


def mix_hgrn(cx, cm, env, layer):
    S = cx.S
    w = load_w(cx, env, "hgrn")
    hq = cx.sb([128, SEQ], F32, name="hq")
    lf = cx.sb([128, SEQ], F32, name="lf")
    hk = cx.sb([128, SEQ], BF16, name="hk")
    sg = cx.sb([128, SEQ], BF16, name="sg")
    hv = cx.sb([128, 32, 128], BF16, name="hv")
    lb = cx.sb([128, 2], F32, name="lb")
    if layer == 0:
        cx.memset(lb[:, 0:1], 0.0, lb.all, eng="dve")
        cx.memset(lb[:, 1:2], 1.0, lb.all, eng="dve")
    else:
        ee = cx.sb([128, 4], F32, name="ee")
        cx.activation(ee[:, 0:2], env.small[:, SM_G0:SM_G0 + 2], AF.Exp, env.small.all, ee.all)
        cx.tt(ee[:, 2:3], ee[:, 0:1], ee[:, 1:2], ALU.add, ee.all, ee.all)
        S.dve(lambda e: e.reciprocal(out=ee[:, 3:4], in_=ee[:, 2:3]), ee.all, ee.all)
        cx.tt(lb[:, 0:1], ee[:, 1:2], ee[:, 3:4], ALU.mult, ee.all, lb.all)
        cx.ts(lb[:, 1:2], lb[:, 0:1], -1.0, 1.0, ALU.mult, ALU.add, lb.all, lb.all)
    sgm = [cx.sb([128, 512], F32, name=f"sgm{i}") for i in range(2)]
    ff_ = [cx.sb([128, 512], F32, name=f"ff{i}") for i in range(2)]

    def h_q(n, bank):
        cx.copy(hq[:, n * 512:(n + 1) * 512], bank[:, :], bank.all, hq.all)

    def h_f(n, bank):
        sl = slice(n * 512, (n + 1) * 512)
        a, f = sgm[n % 2], ff_[n % 2]
        cx.activation(a[:, :], bank[:, :], AF.Sigmoid, bank.all, a.all)
        cx.ts(f[:, :], a[:, :], lb[:, 1:2], lb[:, 0:1], ALU.mult, ALU.add, a.all + lb.all, f.all)
        cx.activation(lf[:, sl], f[:, :], AF.Ln, f.all, lf.all)
        cx.ts(hk[:, sl], f[:, :], -1.0, 1.0, ALU.mult, ALU.add, f.all, hk.all, eng="pool")

    def h_g(n, bank):
        sl = slice(n * 512, (n + 1) * 512)
        a = sgm[n % 2]
        cx.activation(a[:, :], bank[:, :], AF.Sigmoid, bank.all, a.all)
        cx.tt(sg[:, sl], bank[:, :], a[:, :], ALU.mult, bank.all + a.all, sg.all)

    def h_v(tl, bank):
        cx.copy(hv[:, tl, :], bank[:, 0:128], bank.all, hv.all, eng="act" if tl % 2 else "dve")

    project(cx, cm, env, w, [(0, h_q), (128, h_f), (256, h_g)], (384, 128, h_v))

    rm = cx.sb([128, SEQ], BF16, name="rm")
    cx.memset(rm[:, :], 1.0, rm.all)
    cx.memset(rm[:, :].rearrange("p (a b) -> p a b", b=64)[:, :, 0:1], 0.0, rm.all)
    bb = cx.sb([128, SEQ], F32, name="bb")
    S.dve(lambda e: e.tensor_tensor_scan(out=bb[:, :], data0=rm[:, :], data1=lf[:, :], initial=0.0,
                                         op0=ALU.mult, op1=ALU.add), rm.all + lf.all, bb.all)
    bv = bb[:, :].rearrange("p (a b) -> p a b", b=64)
    sm = cx.sb([128, 5, 64], F32, name="hsm")
    cx.copy(sm[:, 0, :], bv[:, :, 31], bb.all, sm.all)
    cx.activation(sm[:, 4, :], bv[:, :, 31], AF.Exp, bb.all, sm.all)
    cx.activation(sm[:, 1, :], bv[:, :, 63], AF.Exp, bb.all, sm.all)
    cx.tt(sm[:, 3, :], bv[:, :, 63], sm[:, 0, :], ALU.subtract, bb.all + sm.all, sm.all)
    cx.activation(sm[:, 2, :], sm[:, 3, :], AF.Exp, sm.all, sm.all)
    cx.tt(bv, bv, sm[:, 0, :].unsqueeze(2).to_broadcast([128, 64, 64]), ALU.subtract, bb.all + sm.all, bb.all)
    E = lf
    qt_ = cx.sb([128, SEQ], BF16, name="qtil")
    kt_ = cx.sb([128, SEQ], BF16, name="ktil")
    kh_ = cx.sb([128, SEQ], BF16, name="khat")
    cx.activation(E[:, :], bb[:, :], AF.Exp, bb.all, E.all)
    cx.tt(qt_[:, :], hq[:, :], E[:, :], ALU.mult, hq.all + E.all, qt_.all)
    cx.activation(E[:, :], bb[:, :], AF.Exp, bb.all, E.all, scale=-1.0)
    cx.tt(kt_[:, :], hk[:, :], E[:, :], ALU.mult, hk.all + E.all, kt_.all)
    cx.tt(kh_[:, :].rearrange("p (a b) -> p a b", b=64), kt_[:, :].rearrange("p (a b) -> p a b", b=64),
          sm[:, 2, :].unsqueeze(2).to_broadcast([128, 64, 64]), ALU.mult, kt_.all + sm.all, kh_.all, eng="pool")
    khtm = cx.sb([128, 32, 128], BF16, name="khtm")
    for grp in range(4):
        for i in range(8):
            tl = grp * 8 + i
            cx.transpose(cm.bankb[:, i * 128:(i + 1) * 128], kh_[:, tl * 128:(tl + 1) * 128], cm.ident,
                         kh_.all, cm.bankb.all)
        cx.copy(khtm[:, grp * 8:(grp + 1) * 8, :], cm.bankb[:, :].rearrange("p (a b) -> p a b", b=128),
                cm.bankb.all, khtm.all, eng="act" if grp % 2 else "dve")
    attT = cx.sb([128, 32, 128], BF16, nsub=8, name="attT")
    for grp in range(8):
        ab = cm.banks[grp % 2]
        for i in range(4):
            tl = grp * 4 + i
            ts_ = slice(tl * 128, (tl + 1) * 128)
            cx.mm(ab[:, i * 128:(i + 1) * 128], kt_[:, ts_], qt_[:, ts_], True, True, kt_.all + qt_.all, ab.all)
        cx.tt(attT[:, grp * 4:(grp + 1) * 4, :], ab[:, :].rearrange("p (a b) -> p a b", b=128),
              env.hmask[:, :].unsqueeze(1).to_broadcast([128, 4, 128]), ALU.mult, ab.all + env.hmask.all, [attT.b[grp]])
    S32 = cx.sb([128, 2, 128], F32, nsub=2, name="S32")
    Sb = cx.sb([128, 4, 128], BF16, nsub=4, name="Sb")
    cx.memset(S32[:, 0, :], 0.0, [S32.b[0]], eng="dve")
    cx.memset(Sb[:, 0, :], 0.0, [Sb.b[0]], eng="dve")
    oT = [cx.sb([128, 512], F32, name=f"oT{i}") for i in range(2)]
    sq = [cx.sb([128, 512], BF16, name=f"osq{i}") for i in range(2)]
    rs = [cx.sb([128, 512], F32, name=f"ors{i}") for i in range(2)]
    yo = [cx.sb([128, 512], BF16, name=f"oyo{i}") for i in range(2)]
    for tl in range(32):
        ob = cm.banks[2 + (tl // 4) % 2]
        for half in range(2):
            c = 2 * tl + half
            ps = slice(half * 64, half * 64 + 64)
            mslot = 0
            mb = cm.banks[4 + c % 2]
            cx.mm(mb[:, mslot * 128:(mslot + 1) * 128], khtm[ps, tl, :], hv[ps, tl, :], True, True,
                  khtm.all + hv.all, [mb.b[mslot]])
            oc = slice((tl % 4) * 128 + half * 64, (tl % 4) * 128 + half * 64 + 64)
            tsl = slice(c * 64, c * 64 + 64)
            cx.mm(ob[:, oc], hv[:, tl, :], attT[:, tl, half * 64:half * 64 + 64], True, False,
                  hv.all + [attT.b[tl // 4]], [ob.b[tl % 4]])
            cx.mm(ob[:, oc], Sb[:, c % 4, :], qt_[:, tsl], False, True, [Sb.b[c % 4]] + qt_.all, [ob.b[tl % 4]])
            cx.stt(S32[:, (c + 1) % 2, :], S32[:, c % 2, :], sm[:, 1, c:c + 1], mb[:, mslot * 128:(mslot + 1) * 128],
                   ALU.mult, ALU.add, [S32.b[c % 2], mb.b[mslot]] + sm.all, [S32.b[(c + 1) % 2]])
            if c + 1 < 64:
                cx.S.act(lambda e, c=c: e.mul(out=Sb[:, (c + 1) % 4, :], in_=S32[:, (c + 1) % 2, :], mul=sm[:, 4, c + 1:c + 2]),
                         [S32.b[(c + 1) % 2]] + sm.all, [Sb.b[(c + 1) % 4]])
        if tl % 4 == 3:
            n = tl // 4
            k = n % 2
            sl = slice(n * 512, (n + 1) * 512)
            cx.copy(oT[k][:, :], ob[:, :], ob.all, oT[k].all)
            cx.activation(sq[k][:, :], ob[:, :], AF.Square, ob.all, sq[k].all)
            nb_ = cm.banks[6]
            cx.mm(nb_[:, :], cm.ones, sq[k][:, :], True, True, sq[k].all, nb_.all)
            cx.activation(rs[k][:, :], nb_[:, :], AF.Ln, nb_.all + cm.epst.all, rs[k].all, bias=cm.eps_ap, scale=1.0 / HD)
            cx.activation(rs[k][:, :], rs[k][:, :], AF.Exp, rs[k].all, rs[k].all, scale=-0.5)
            cx.stt(oT[k][:, :], oT[k][:, :], env.small[:, SM_HNORM:SM_HNORM + 1], rs[k][:, :], ALU.mult, ALU.mult,
                   oT[k].all + rs[k].all + env.small.all, oT[k].all)
            cx.tt(yo[k][:, :], oT[k][:, :], sg[:, sl], ALU.mult, oT[k].all + sg.all, yo[k].all, eng="pool")
            cx.store(env.y_out(2, n * 512, 512), yo[k][:, :], yo[k].all, is_output=env.y_is_output)


def mix_pool(cx, cm, env):
    w = load_w(cx, env, "pool")
    pin = cx.sb([128, 32, 128], BF16, nsub=32, name="pin")
    pM = cx.sb([128, 3, 128], BF16, name="pM")
    cx.load(pM[:, :, :], env.poolM, pM.all)
    pw = cx.sb([128, 128], BF16, name="pw")
    cx.load(pw[:, :], env.poolw, pw.all, q="pool")

    def h_p(tl, bank):
        cx.copy(pin[:, tl, :], bank[:, 0:128], bank.all, [pin.b[tl]], eng="act" if tl % 2 else "dve")

    project(cx, cm, env, w, [], (0, 128, h_p))
    pt = [cx.sb([128, 512], BF16, name=f"ppt{i}") for i in range(2)]
    yo = [cx.sb([128, 512], BF16, name=f"pyo{i}") for i in range(2)]
    for n in range(8):
        pb = cm.banks[n % 2]
        for i in range(4):
            tl = n * 4 + i
            cs = slice(i * 128, (i + 1) * 128)
            cx.mm(pb[:, cs], pin[:, tl, :], pM[:, 0 if tl == 0 else 1, :], True, tl == 0, [pin.b[tl]] + pM.all, [pb.b[i]])
            if tl > 0:
                cx.mm(pb[:, cs], pin[:, tl - 1, :], pM[:, 2, :], False, True, [pin.b[tl - 1]] + pM.all, [pb.b[i]])
        k = n % 2
        cx.copy(pt[k][:, :], pb[:, :], pb.all, pt[k].all, eng="act")
        ob = cm.banks[2 + n % 2]
        cx.mm(ob[:, :], pw[:, :], pt[k][:, :], True, True, pw.all + pt[k].all, ob.all)
        cx.ts(yo[k][:, :], ob[:, :], env.small[:, SM_PSCALE:SM_PSCALE + 1], None, ALU.mult, None,
              ob.all + env.small.all, yo[k].all)
        cx.store(env.y_out(3, n * 512, 512), yo[k][:, :], yo[k].all, is_output=env.y_is_output)


def setup_common32(cx, cm, c32_ap):
    cm.c32 = cx.sb([128, 256], F32, name="c32")
    cx.load(cm.c32[:, :], c32_ap, cm.c32.all)
    cm.ident32 = cm.c32[:, 0:128]
    cm.ones32 = cm.c32[:, 128:256]
    cm.one_ap = cm.c32[:, 128:129]


def mix_globals(cx, env, cmask_d, esel_d, hmask_d):
    env.cmask = cx.sb([128, 4, 512], BF16, name="cmask")
    cx.load(env.cmask[:, :, :], cmask_d, env.cmask.all)
    env.esel = cx.sb([16, 16, 128], BF16, name="esel")
    cx.load(env.esel[:, :, :], esel_d, env.esel.all)
    env.hmask = cx.sb([128, 128], BF16, name="hmask")
    cx.load(env.hmask[:, :], hmask_d, env.hmask.all)


def phase_mix(cx, cm, env, layer, small_d, which=("fox", "moba", "hgrn", "pool")):
    with contextlib.ExitStack() as outer:
        cx.scope = outer
        env.small = cx.sb([128, N_SMALL], F32, name="small")
        cx.load(env.small[:, :], small_d, env.small.all)
        env.hs = [cx.sb([128, NCH, 512], BF16, name=f"hs{i}") for i in range(2)]
        env.rc = [cx.sb([128, 512], F32, name=f"rc{i}") for i in range(2)]
        env.yt = [cx.sb([128, 512], BF16, name=f"yt{i}") for i in range(2)]
        fns = {"fox": lambda: mix_fox(cx, cm, env), "moba": lambda: mix_moba(cx, cm, env),
               "hgrn": lambda: mix_hgrn(cx, cm, env, layer), "pool": lambda: mix_pool(cx, cm, env)}
        for name in which:
            with contextlib.ExitStack() as sc:
                cx.scope = sc
                fns[name]()
                cx.S.fence()
            cx.scope = outer
    cx.scope = None


def build_mix(layer, which=("fox", "moba", "hgrn", "pool")):
    cx = Ctx()
    env = MixEnv()
    env.hnT = cx.din("hnT", [D_MODEL, SEQ], BF16)
    env.wmix = cx.din("wmix", [128, NCH, W_MIXCOLS], F32)
    small_d = cx.din("small", [128, N_SMALL], F32)
    env.poolw = cx.din("poolw", [128, 128], F32)
    consts = cx.din("consts", [128, 256], BF16)
    c32 = cx.din("c32", [128, 256], F32)
    cmask_d = cx.din("cmask", [128, 4, 512], BF16)
    env.perm = cx.din("perm", [128, 128], BF16)
    env.ropeC = cx.din("ropeC", [128, SEQ], F32)
    env.ropeS = cx.din("ropeS", [128, SEQ], F32)
    esel_d = cx.din("esel", [16, 16, 128], BF16)
    hmask_d = cx.din("hmask", [128, 128], BF16)
    env.poolM = cx.din("poolM", [128, 3, 128], BF16)
    env.yT = cx.dout("yT", [4, 128, SEQ], BF16)
    hview_ = env.hnT.rearrange("(c p) t -> p c t", p=128)
    env.hn_chunk = lambda n, k: hview_[:, 4 * k:4 * k + 4, n * 512:(n + 1) * 512]
    env.y_out = lambda bi, t0, ncols: env.yT[bi, :, t0:t0 + ncols]
    env.y_is_output = True
    cm = Common(cx, consts)
    add_eps(cx, cm)
    setup_common32(cx, cm, c32)
    mix_globals(cx, env, cmask_d, esel_d, hmask_d)
    phase_mix(cx, cm, env, layer, small_d, which)
    cx.S.emit()
    return cx.nc


CC_GROUPS = [[0, 1, 2, 3], [4, 5, 6, 7]]


def _core_quarter(cx, e):
    if getattr(cx, "_q", None) is None:
        cx._q = e.snap(e.partition_id() % 4, min_val=0, max_val=3)
    return cx._q


def build_fused(depth):
    cx = Ctx()
    nc = cx.nc
    S = cx.S
    consts = cx.din("consts", [128, 256], BF16)
    c32 = cx.din("c32", [128, 256], F32)
    xT = cx.din("xT", [D_MODEL, TOK], F32)
    memT = cx.din("memT", [D_MODEL, NMEM], F32)
    gpre = cx.din("gpre", [128, NCH], F32)
    cmask_d = cx.din("cmask", [128, 4, 512], BF16)
    esel_d = cx.din("esel", [16, 16, 128], BF16)
    hmask_d = cx.din("hmask", [128, 128], BF16)
    env = MixEnv()
    env.perm = cx.din("perm", [128, 128], BF16)
    env.ropeC = cx.din("ropeC", [128, SEQ], F32)
    env.ropeS = cx.din("ropeS", [128, SEQ], F32)
    env.poolM = cx.din("poolM", [128, 3, 128], BF16)
    wmix_d = [cx.din(f"wmix_{l}", [128, NCH, W_MIXCOLS], F32) for l in range(depth)]
    small_d = [cx.din(f"small_{l}", [128, N_SMALL], F32) for l in range(depth)]
    poolw_d = [cx.din(f"poolw_{l}", [128, 128], F32) for l in range(depth)]
    ios = []
    for l in range(depth):
        io = TokIO()
        tok_weight_inputs(cx, io, f"_{l}")
        io.memT = memT
        ios.append(io)
    outT = cx.dout("outT", [D_MODEL, TOK], F32)
    hn_own = [nc.dram_tensor(f"hn_own{k}", [512, TOK], BF16, kind="Internal").ap() for k in range(4)]
    hn_all = [nc.dram_tensor(f"hn_all{k}", [4 * 512, TOK], BF16, kind="Internal").ap() for k in range(4)]
    y_own = [nc.dram_tensor(f"y_own{n}", [512, TOK], BF16, kind="Internal").ap() for n in range(4)]
    y_all = [nc.dram_tensor(f"y_all{n}", [4 * 512, TOK], BF16, kind="Internal").ap() for n in range(4)]
    h_res = nc.dram_tensor("h_res", [D_MODEL, TOK], F32, kind="Internal").ap()

    cm = Common(cx, consts)
    add_eps(cx, cm)
    setup_common32(cx, cm, c32)
    mix_globals(cx, env, cmask_d, esel_d, hmask_d)
    S.sp_init = lambda e: _core_quarter(cx, e)

    with contextlib.ExitStack() as sc:
        cx.scope = sc
        hn_own_v = [a.rearrange("(c p) t -> p c t", p=128) for a in hn_own]
        phase_pre(cx, cm, xT, gpre, lambda n, k: hn_own_v[k][:, :, n * 512:(n + 1) * 512], False)
        S.fence()
    cx.scope = None

    hn_all_v = [a.rearrange("(r c p) t -> r p c t", r=4, p=128) for a in hn_all]
    y_own_v = [a.rearrange("(tq d) t -> tq d t", tq=4) for a in y_own]
    y_all_v = [a.rearrange("(hh tq d) t -> tq d hh t", hh=4, tq=4) for a in y_all]
    env.hn_chunk = lambda n, k: hn_all_v[k][n // 2][:, :, (n % 2) * 512:(n % 2 + 1) * 512]
    env.y_out = lambda bi, t0, ncols: y_own_v[bi][t0 // TOK][:, t0 % TOK:t0 % TOK + ncols]
    env.y_is_output = False
    h_res_v = h_res.rearrange("(c p) t -> p c t", p=128)
    x_v = xT.rearrange("(c p) t -> p c t", p=128)
    out_v = outT.rearrange("(c p) t -> p c t", p=128)

    for l in range(depth):
        last = l == depth - 1
        for k in range(4):
            S.collective(lambda e, k=k: e.collective_compute("AllGather", ALU.bypass, replica_groups=CC_GROUPS,
                                                             ins=[hn_own[k].opt()], outs=[hn_all[k].opt()]), [], [])
        S.fence()
        env.wmix = wmix_d[l]
        env.poolw = poolw_d[l]
        phase_mix(cx, cm, env, l, small_d[l])
        for n in range(4):
            S.collective(lambda e, n=n: e.collective_compute("AllGather", ALU.bypass, replica_groups=CC_GROUPS,
                                                             ins=[y_own[n].opt()], outs=[y_all[n].opt()]), [], [])
        S.fence()
        io = ios[l]
        src_v = x_v if l == 0 else h_res_v
        dst_v = out_v if last else h_res_v
        io.hT_in = lambda hf, c0, c1, src_v=src_v: src_v[:, c0:c1, hf * HALF:(hf + 1) * HALF]
        io.hn_in = lambda hf, k: hn_own_v[k][:, :, hf * HALF:(hf + 1) * HALF]

        def ybr_load(dst, hf, wb):
            for n in range(4):
                def f(e, n=n):
                    q = _core_quarter(cx, e)
                    src = y_all_v[n][bass.ds(q, 1)][0][:, :, hf * HALF:(hf + 1) * HALF]
                    return e.dma_start(out=dst[:, n * 4:(n + 1) * 4, :], in_=src)
                S.dma("sp", f, [], list(wb))
        io.ybr_load = ybr_load
        io.hT_out = lambda hf, c0, c1, dst_v=dst_v: dst_v[:, c0:c1, hf * HALF:(hf + 1) * HALF]
        io.hn_out = lambda hf, k: hn_own_v[k][:, :, hf * HALF:(hf + 1) * HALF]
        io.h_is_output = last
        io.hn_is_output = False
        io.write_hn_when_last = False
        with contextlib.ExitStack() as sc:
            cx.scope = sc
            phase_tok(cx, cm, io, l, last)
            S.fence()
        cx.scope = None
    S.emit()
    return nc


IN_OFF = {"mq": 0, "mk": 512, "mv": 1024, "fq": 1536, "fk": 2048, "fv": 2560, "ff": 3072,
          "hq": 3076, "hf": 3588, "hi": 4100, "hg": 4612, "pin": 5124, "gl": 5636}
_HC = {}


def host_consts():
    if _HC:
        return _HC
    f32 = np.float32
    c = np.zeros((128, 256), f32)
    c[:, :128] = np.eye(128)
    c[:, 128:] = 1
    _HC["c32"] = c
    _HC["consts"] = c.astype(NPBF)
    k = np.arange(128)[:, None, None]
    a = np.arange(4)[None, :, None]
    q = np.arange(512)[None, None, :]
    _HC["cmask"] = np.where(q >= 128 * a + k, 0.0, NEG).astype(NPBF)
    perm = np.zeros((128, 128), f32)
    d = np.arange(128)
    perm[(d + 64) % 128, d] = 1
    _HC["perm"] = perm.astype(NPBF)
    inv_freq = (f32(10000.0) ** (-np.arange(64, dtype=f32) * f32(2.0) / f32(128))).astype(f32)
    ang = (np.arange(SEQ, dtype=f32)[None, :] * inv_freq[:, None]).astype(f32)
    cos, sin = np.cos(ang).astype(f32), np.sin(ang).astype(f32)
    _HC["ropeC"] = np.ascontiguousarray(np.concatenate([cos, cos], 0))
    _HC["ropeS"] = np.ascontiguousarray(np.concatenate([-sin, sin], 0))
    es = np.zeros((16, 16, 128), f32)
    for j in range(16):
        es[j, j, :] = 1
    _HC["esel"] = es.astype(NPBF)
    s = np.arange(128)[:, None]
    t = np.arange(128)[None, :]
    _HC["hmask"] = ((s // 64 == t // 64) & (s <= t)).astype(f32).astype(NPBF)
    for h, w in enumerate(POOL_WINDOWS):
        M = np.zeros((128, 3, 128), f32)
        eye = (s == t).astype(f32)
        band = ((s <= t) & (s > t - w)).astype(f32)
        M[:, 0, :] = band / np.minimum(w, t + 1).astype(f32) - eye
        M[:, 1, :] = band / f32(w) - eye
        M[:, 2, :] = ((s - 128) > (t - w)).astype(f32) / f32(w)
        _HC[f"poolM{h}"] = M.astype(NPBF)
    return _HC


def fm_layout(w):
    K, N = w.shape
    return np.ascontiguousarray(w.reshape(K // 128, 128, N).transpose(1, 0, 2))


def mix_inputs(inp, l, c, hnT_b):
    b, h = c // 4, c % 4
    hc = host_consts()
    w_in = inp["w_in"][l]
    hs = slice(h * 128, (h + 1) * 128)

    def col(name):
        return w_in[:, IN_OFF[name] + h * 128: IN_OFF[name] + (h + 1) * 128]
    wm = np.concatenate([col(n) for n in ("mq", "mk", "mv", "fq", "fk", "fv", "hq", "hf", "hg", "hi", "pin")], axis=1)
    small = np.zeros((128, N_SMALL), np.float32)
    small[:, SM_WFF:SM_WFF + NCH] = w_in[:, IN_OFF["ff"] + h].reshape(NCH, 128).T
    small[:, SM_G0] = inp["hgrn_lb_logits"][0, hs]
    small[:, SM_G1] = inp["hgrn_lb_logits"][1, hs]
    small[:, SM_HNORM] = inp["hgrn_out_norm"][l, hs]
    small[:, SM_PSCALE] = inp["pool_scale"][l, hs]
    small[:, SM_FB] = inp["fox_f_bias"][l, h]
    return {"hnT": hnT_b, "wmix": fm_layout(wm), "small": small,
            "poolw": np.ascontiguousarray(inp["pool_w"][l, h]),
            "consts": hc["consts"], "c32": hc["c32"], "cmask": hc["cmask"], "perm": hc["perm"],
            "ropeC": hc["ropeC"], "ropeS": hc["ropeS"], "esel": hc["esel"], "hmask": hc["hmask"],
            "poolM": hc[f"poolM{h}"]}


GV_MIXPOST, GV_XAPRE, GV_XAMEM, GV_XAPOST, GV_MLPPRE, GV_MLPPOST, GV_NEXT = range(7)
HALF = 512
NMEM = 256
RING_N = 6
RING_ELEMS = 4096


class Ring:
    def __init__(self, cx, n=RING_N, elems=RING_ELEMS):
        self.cx = cx
        self.bufs = [cx.sb([128, elems], BF16, name=f"ring{i}") for i in range(n)]
        self.i = 0

    def load(self, src_ap, a, b):
        t = self.bufs[self.i % len(self.bufs)]
        self.i += 1
        view = t[:, 0:a * b].rearrange("p (a b) -> p a b", b=b)
        self.cx.load(view, src_ap, t.all, q="pool")
        return view, t.all


class TokIO:
    pass


def tok_io_external(cx):
    io = TokIO()
    hT_d = cx.din("hT", [D_MODEL, TOK], F32)
    hnT_d = cx.din("hnT", [D_MODEL, TOK], BF16)
    ybr_d = cx.din("ybr", [D_MODEL, TOK], BF16)
    tok_weight_inputs(cx, io, "")
    io.memT = cx.din("memT", [D_MODEL, NMEM], F32)
    hT_o = cx.dout("hT_out", [D_MODEL, TOK], F32)
    hn_o = cx.dout("hn_next", [D_MODEL, TOK], BF16)
    hview = hT_d.rearrange("(c p) t -> p c t", p=128)
    hnview = hnT_d.rearrange("(c p) t -> p c t", p=128)
    ybview = ybr_d.rearrange("(c p) t -> p c t", p=128)
    hoview = hT_o.rearrange("(c p) t -> p c t", p=128)
    hnoview = hn_o.rearrange("(c p) t -> p c t", p=128)
    io.hT_in = lambda hf, c0, c1: hview[:, c0:c1, hf * HALF:(hf + 1) * HALF]
    io.hn_in = lambda hf, k: hnview[:, 4 * k:4 * k + 4, hf * HALF:(hf + 1) * HALF]
    io.ybr_load = lambda dst, hf, wb: cx.load(dst, ybview[:, :, hf * HALF:(hf + 1) * HALF], wb)
    io.hT_out = lambda hf, c0, c1: hoview[:, c0:c1, hf * HALF:(hf + 1) * HALF]
    io.hn_out = lambda hf, k: hnoview[:, 4 * k:4 * k + 4, hf * HALF:(hf + 1) * HALF]
    io.h_is_output = True
    io.hn_is_output = True
    io.write_hn_when_last = True
    return io


def tok_weight_inputs(cx, io, sfx):
    io.wg_d = cx.din("wg" + sfx, [64, 128, NCH, 128], F32)
    io.wb_d = cx.din("wb" + sfx, [64, 128, 4, 128], F32)
    io.wo_d = cx.din("wo" + sfx, [16, 128, NCH, 128], F32)
    io.xq_d = cx.din("xq" + sfx, [4, 128, NCH, 128], F32)
    io.xk_d = cx.din("xk" + sfx, [4, 128, NCH, 128], F32)
    io.xv_d = cx.din("xv" + sfx, [2, 128, 8, 512], F32)
    io.xo_d = cx.din("xo" + sfx, [16, 128, 4, 128], F32)
    io.wup_d = cx.din("wup" + sfx, [64, 128, NCH, 128], F32)
    io.wdn_d = cx.din("wdn" + sfx, [32, 128, 32, 128], F32)
    io.gv_d = cx.din("gv" + sfx, [128, 7 * NCH], F32)


def build_tok(layer, last):
    cx = Ctx()
    io = tok_io_external(cx)
    consts = cx.din("consts", [128, 256], BF16)
    cm = Common(cx, consts)
    add_eps(cx, cm)
    phase_tok(cx, cm, io, layer, last)
    cx.S.emit()
    return cx.nc


def phase_tok(cx, cm, io, layer, last):
    S = cx.S
    wg_d, wb_d, wo_d, xq_d, xk_d, xv_d, xo_d, wup_d, wdn_d = (io.wg_d, io.wb_d, io.wo_d, io.xq_d, io.xk_d, io.xv_d,
                                                              io.xo_d, io.wup_d, io.wdn_d)
    memT_d, gv_d = io.memT, io.gv_d
    gv = cx.sb([128, 7 * NCH], F32, name="gv")
    cx.load(gv[:, :], gv_d, gv.all)

    def gcol(which, c):
        return gv[:, which * NCH + c: which * NCH + c + 1]

    ring = Ring(cx)
    hT = cx.sb([128, NCH, HALF], F32, nsub=NCH, name="hT")
    hnb = cx.sb([128, NCH, HALF], BF16, name="hnb")
    RA = cx.sb([128, 32, HALF], BF16, name="RA")
    ybr = RA[:, 0:NCH, :]
    sT = RA[:, NCH:2 * NCH, :]
    aT = cx.sb([128, NCH, HALF], F32, nsub=NCH, name="aT")
    sqp = [cx.sb([128, HALF], BF16, name=f"sq{i}") for i in range(3)]
    rstd = cx.sb([128, HALF], F32, name="rstd")
    tmpA = [cx.sb([128, HALF], F32, name=f"tmpA{i}") for i in range(2)]
    tmpB = [cx.sb([128, HALF], F32, name=f"tmpB{i}") for i in range(2)]
    acc = cx.sb([128, HALF], F32, name="acc")
    kx = cx.sb([128, 4, NMEM], BF16, name="kx")
    vx = cx.sb([128, 2, 512], BF16, name="vx")
    qx = cx.sb([128, 4, HALF], BF16, name="qx")
    ox = cx.sb([128, 4, HALF], BF16, name="ox")
    pTs = [cx.sb([128, HALF], BF16, name=f"xpT{i}") for i in range(2)]
    rc = cx.sb([128, HALF], F32, name="xrc")
    B = cm.banks

    def norm_add(gw):
        rms_stats(cx, cm, lambda c: aT[:, c, :], lambda c: [aT.b[c]], NCH, HALF, sqp, B[6], rstd, D_MODEL)
        for c in range(NCH):
            t = tmpA[c % 2]
            cx.stt(t[:, :], aT[:, c, :], gcol(gw, c), rstd[:, :], ALU.mult, ALU.mult,
                   [aT.b[c]] + gv.all + rstd.all, t.all)
            cx.tt(hT[:, c, :], hT[:, c, :], t[:, :], ALU.add, [hT.b[c]] + t.all, [hT.b[c]])

    def norm_to(gw, dst, dst_bufs):
        rms_stats(cx, cm, lambda c: hT[:, c, :], lambda c: [hT.b[c]], NCH, HALF, sqp, B[6], rstd, D_MODEL)
        for c in range(NCH):
            cx.stt(dst[:, c, :], hT[:, c, :], gcol(gw, c), rstd[:, :], ALU.mult, ALU.mult,
                   [hT.b[c]] + gv.all + rstd.all, dst_bufs)

    memf = aT
    cx.load(memf[:, :, 0:NMEM], memT_d.rearrange("(c p) t -> p c t", p=128), memf.all)
    rms_stats(cx, cm, lambda c: memf[:, c, 0:NMEM], lambda c: [memf.b[c]], NCH, NMEM, sqp, B[6], rstd, D_MODEL)
    memn = hnb
    for c in range(NCH):
        cx.stt(memn[:, c, 0:NMEM], memf[:, c, 0:NMEM], gcol(GV_XAMEM, c), rstd[:, 0:NMEM], ALU.mult, ALU.mult,
               [memf.b[c]] + gv.all + rstd.all, memn.all)
    for hd in range(4):
        wv_, wb_ = ring.load(xk_d[hd], NCH, 128)
        bk = B[hd % 2]
        for c in range(NCH):
            cx.mm(bk[:, 0:NMEM], wv_[:, c, :], memn[:, c, 0:NMEM], c == 0, c == NCH - 1, wb_ + memn.all, bk.all)
        cx.copy(kx[:, hd, :], bk[:, 0:NMEM], bk.all, kx.all)
    wv0, wb0 = ring.load(xv_d[0], 8, 512)
    wv1, wb1 = ring.load(xv_d[1], 8, 512)
    for mt in range(2):
        bk = B[2 + mt]
        for c in range(NCH):
            wv_, wb_ = (wv0, wb0) if c < 8 else (wv1, wb1)
            cx.mm(bk[:, :], memn[:, c, mt * 128:(mt + 1) * 128], wv_[:, c % 8, :], c == 0, c == NCH - 1,
                  wb_ + memn.all, bk.all)
        cx.copy(vx[:, mt, :], bk[:, :], bk.all, vx.all, eng="act")

    for hf in range(TOK // HALF):
        tsl = slice(hf * HALF, (hf + 1) * HALF)
        for c in range(0, NCH, 4):
            cx.load(hT[:, c:c + 4, :], io.hT_in(hf, c, c + 4), hT.b[c:c + 4])
        for k in range(4):
            cx.load(hnb[:, 4 * k:4 * k + 4, :], io.hn_in(hf, k), hnb.all)
        io.ybr_load(ybr, hf, RA.all)
        it = 0
        for j in range(NCH):
            for n in range(4):
                wg_, wgb = ring.load(wg_d[j * 4 + n], NCH, 128)
                wb_, wbb = ring.load(wb_d[j * 4 + n], 4, 128)
                bg, bp = B[it % 2], B[2 + it % 2]
                it += 1
                for c in range(NCH):
                    cx.mm(bg[:, :], wg_[:, c, :], hnb[:, c, :], c == 0, c == NCH - 1, wgb + hnb.all, bg.all)
                for hh in range(4):
                    cx.mm(bp[:, :], wb_[:, hh, :], ybr[:, n * 4 + hh, :], hh == 0, hh == 3, wbb + RA.all, bp.all)
                sg_ = tmpA[it % 2]
                cx.activation(sg_[:, :], bg[:, :], AF.Sigmoid, bg.all, sg_.all)
                if n == 0:
                    cx.tt(acc[:, :], bp[:, :], sg_[:, :], ALU.mult, bp.all + sg_.all, acc.all)
                else:
                    t2 = tmpB[it % 2]
                    cx.tt(t2[:, :], bp[:, :], sg_[:, :], ALU.mult, bp.all + sg_.all, t2.all)
                    if n < 3:
                        cx.tt(acc[:, :], acc[:, :], t2[:, :], ALU.add, acc.all + t2.all, acc.all)
                    else:
                        cx.tt(sT[:, j, :], acc[:, :], t2[:, :], ALU.add, acc.all + t2.all, RA.all)
        for j in range(NCH):
            w_, wbf = ring.load(wo_d[j], NCH, 128)
            bk = B[j % 2]
            for c in range(NCH):
                cx.mm(bk[:, :], w_[:, c, :], sT[:, c, :], c == 0, c == NCH - 1, wbf + RA.all, bk.all)
            cx.copy(aT[:, j, :], bk[:, :], bk.all, [aT.b[j]], eng="act" if j % 2 else "dve")
        norm_add(GV_MIXPOST)
        norm_to(GV_XAPRE, hnb, hnb.all)
        for hd in range(4):
            w_, wbf = ring.load(xq_d[hd], NCH, 128)
            bk = B[hd % 2]
            for c in range(NCH):
                cx.mm(bk[:, :], w_[:, c, :], hnb[:, c, :], c == 0, c == NCH - 1, wbf + hnb.all, bk.all)
            S.act(lambda e, hd=hd, bk=bk: e.mul(out=qx[:, hd, :], in_=bk[:, :], mul=SCALE), bk.all, qx.all)
        for hd in range(4):
            ob, db = B[2 + hd % 2], B[4 + hd % 2]
            for mt in range(2):
                sb_ = B[mt]
                cx.mm(sb_[:, :], kx[:, hd, mt * 128:(mt + 1) * 128], qx[:, hd, :], True, True, kx.all + qx.all, sb_.all)
                pT = pTs[mt]
                cx.activation(pT[:, :], sb_[:, :], AF.Exp, sb_.all, pT.all)
                cx.mm(ob[:, :], vx[:, mt, hd * 128:(hd + 1) * 128], pT[:, :], mt == 0, mt == 1, vx.all + pT.all, ob.all)
                cx.mm(db[:, :], cm.ones, pT[:, :], mt == 0, mt == 1, pT.all, db.all)
            S.dve(lambda e, db=db: e.reciprocal(out=rc[:, :], in_=db[:, :]), db.all, rc.all)
            cx.tt(ox[:, hd, :], ob[:, :], rc[:, :], ALU.mult, ob.all + rc.all, ox.all)
        for j in range(NCH):
            w_, wbf = ring.load(xo_d[j], 4, 128)
            bk = B[j % 2]
            for hd in range(4):
                cx.mm(bk[:, :], w_[:, hd, :], ox[:, hd, :], hd == 0, hd == 3, wbf + ox.all, bk.all)
            cx.copy(aT[:, j, :], bk[:, :], bk.all, [aT.b[j]], eng="act" if j % 2 else "dve")
        norm_add(GV_XAPOST)
        norm_to(GV_MLPPRE, hnb, hnb.all)
        uT = RA
        for fh in range(2):
            for fb in range(32):
                w_, wbf = ring.load(wup_d[fh * 32 + fb], NCH, 128)
                bk = B[fb % 4]
                for c in range(NCH):
                    cx.mm(bk[:, :], w_[:, c, :], hnb[:, c, :], c == 0, c == NCH - 1, wbf + hnb.all, bk.all)
                r_ = tmpA[fb % 2]
                cx.activation(r_[:, :], bk[:, :], AF.Relu, bk.all, r_.all)
                cx.tt(uT[:, fb, :], r_[:, :], r_[:, :], ALU.mult, r_.all, RA.all, eng="dve")
            for j in range(NCH):
                w_, wbf = ring.load(wdn_d[j * 2 + fh], 32, 128)
                bk = B[4 + j % 2]
                for fb in range(32):
                    cx.mm(bk[:, :], w_[:, fb, :], uT[:, fb, :], fb == 0, fb == 31, wbf + RA.all, bk.all)
                if fh == 0:
                    cx.copy(aT[:, j, :], bk[:, :], bk.all, [aT.b[j]], eng="act")
                else:
                    cx.tt(aT[:, j, :], bk[:, :], aT[:, j, :], ALU.add, bk.all + [aT.b[j]], [aT.b[j]])
        norm_add(GV_MLPPOST)
        for c in range(0, NCH, 4):
            cx.store(io.hT_out(hf, c, c + 4), hT[:, c:c + 4, :], hT.b[c:c + 4], is_output=io.h_is_output)
        if not last:
            norm_to(GV_NEXT, hnb, hnb.all)
            for k in range(4):
                cx.store(io.hn_out(hf, k), hnb[:, 4 * k:4 * k + 4, :], hnb.all, is_output=io.hn_is_output)
        elif io.write_hn_when_last:
            for k in range(4):
                cx.store(io.hn_out(hf, k), hnb[:, 4 * k:4 * k + 4, :], hnb.all, is_output=io.hn_is_output)


def tok_weights(inp, l):
    f32 = np.float32
    wg = inp["w_in"][l][:, IN_OFF["gl"]:]
    wg = wg.reshape(NCH, 128, 4, NCH, 128).transpose(3, 2, 1, 0, 4)
    wg = np.ascontiguousarray(wg).reshape(64, 128, NCH, 128)
    wb = inp["w_branch"][l].reshape(4, 4, 128, NCH, 128).transpose(3, 0, 2, 1, 4)
    wb = np.ascontiguousarray(wb).reshape(64, 128, 4, 128)

    def tiles(w, kc):
        K_, N_ = w.shape
        return np.ascontiguousarray(w.reshape(kc, 128, N_ // 128, 128).transpose(2, 1, 0, 3))
    wo = tiles(inp["w_mix_out"][l], NCH)
    xq = tiles(inp["xa_wq"][l], NCH)
    xk = tiles(inp["xa_wkv"][l][:, 0:512], NCH)
    wv = inp["xa_wkv"][l][:, 512:1024].reshape(2, 8, 128, 512).transpose(0, 2, 1, 3)
    xv = np.ascontiguousarray(wv)
    xo = tiles(inp["xa_wo"][l], 4)
    wup = tiles(inp["mlp_w_up"][l], NCH)
    wd = inp["mlp_w_down"][l].reshape(2, 32, 128, NCH, 128).transpose(3, 0, 2, 1, 4)
    wdn = np.ascontiguousarray(wd).reshape(32, 128, 32, 128)
    gv = np.zeros((128, 7 * NCH), f32)
    names = ["mix_norm_post", "xa_norm_pre", "xa_norm_mem", "xa_norm_post", "mlp_norm_pre", "mlp_norm_post"]
    for i, nme in enumerate(names):
        gv[:, i * NCH:(i + 1) * NCH] = inp[nme][l].reshape(NCH, 128).T
    if l + 1 < inp["mix_norm_pre"].shape[0]:
        gv[:, GV_NEXT * NCH:(GV_NEXT + 1) * NCH] = inp["mix_norm_pre"][l + 1].reshape(NCH, 128).T
    return {"wg": wg, "wb": wb, "wo": wo, "xq": xq, "xk": xk, "xv": xv, "xo": xo, "wup": wup, "wdn": wdn,
            "gv": gv, "consts": host_consts()["consts"]}


_PROGS = {}


def _prog(key, builder):
    if key not in _PROGS:
        _PROGS[key] = builder()
    return _PROGS[key]


def kernel_unfused(**inp):
    inp = {k: np.asarray(v) for k, v in inp.items()}
    depth = inp["w_in"].shape[0]
    cores = list(range(8))
    hc = host_consts()
    x = inp["x"]
    tsl = [slice((c % 4) * TOK, (c % 4 + 1) * TOK) for c in cores]
    hT = [np.ascontiguousarray(x[c // 4, tsl[c]].T) for c in cores]
    memT = [np.ascontiguousarray(inp["mem"][b].T) for b in range(BATCH)]
    g0 = np.ascontiguousarray(inp["mix_norm_pre"][0].reshape(NCH, 128).T)
    res = run_bass_kernel_spmd(_prog("pre", build_pre),
                               [{"xT": hT[c], "gpre": g0, "consts": hc["consts"]} for c in cores], core_ids=cores)
    hn = [np.asarray(res.results[c]["hn_out"]) for c in cores]
    for l in range(depth):
        hn_full = [np.ascontiguousarray(np.concatenate([hn[b * 4 + q] for q in range(4)], axis=1)) for b in range(BATCH)]
        res = run_bass_kernel_spmd(_prog(("mix", l), lambda: build_mix(l)),
                                   [mix_inputs(inp, l, c, hn_full[c // 4]) for c in cores], core_ids=cores)
        yT = [np.asarray(res.results[c]["yT"]) for c in cores]
        tw = tok_weights(inp, l)
        maps = []
        for c in cores:
            b = c // 4
            yb = np.stack([yT[b * 4 + h][:, :, tsl[c]] for h in range(4)], axis=1)
            m = dict(tw)
            m["hT"] = hT[c]
            m["hnT"] = hn[c]
            m["ybr"] = np.ascontiguousarray(yb.reshape(D_MODEL, TOK))
            m["memT"] = memT[b]
            maps.append(m)
        res = run_bass_kernel_spmd(_prog(("tok", l), lambda: build_tok(l, l == depth - 1)), maps, core_ids=cores)
        hT = [np.asarray(res.results[c]["hT_out"]) for c in cores]
        hn = [np.asarray(res.results[c]["hn_next"]) for c in cores]
    out = np.empty_like(x)
    for c in cores:
        out[c // 4, tsl[c]] = hT[c].T
    return out


def fused_inputs(inp, c):
    depth = inp["w_in"].shape[0]
    b, q = c // 4, c % 4
    hc = host_consts()
    m = {"consts": hc["consts"], "c32": hc["c32"], "cmask": hc["cmask"], "esel": hc["esel"], "hmask": hc["hmask"],
         "perm": hc["perm"], "ropeC": hc["ropeC"], "ropeS": hc["ropeS"], "poolM": hc[f"poolM{q}"],
         "xT": np.ascontiguousarray(inp["x"][b, q * TOK:(q + 1) * TOK].T),
         "memT": np.ascontiguousarray(inp["mem"][b].T),
         "gpre": np.ascontiguousarray(inp["mix_norm_pre"][0].reshape(NCH, 128).T)}
    for l in range(depth):
        mi = mix_inputs(inp, l, c, None)
        m[f"wmix_{l}"] = mi["wmix"]
        m[f"small_{l}"] = mi["small"]
        m[f"poolw_{l}"] = mi["poolw"]
    return m


def kernel(**inp):
    inp = {k: np.asarray(v) for k, v in inp.items()}
    depth = inp["w_in"].shape[0]
    cores = list(range(8))
    nc = _prog(("fused", depth), lambda: build_fused(depth))
    tws = []
    for l in range(depth):
        tw = tok_weights(inp, l)
        tw.pop("consts")
        tws.append({f"{k}_{l}": v for k, v in tw.items()})
    maps = []
    for c in cores:
        m = fused_inputs(inp, c)
        for tw in tws:
            m.update(tw)
        maps.append(m)
    res = run_bass_kernel_spmd(nc, maps, core_ids=cores)
    x = inp["x"]
    out = np.empty_like(x)
    for c in cores:
        out[c // 4, (c % 4) * TOK:(c % 4 + 1) * TOK] = np.asarray(res.results[c]["outT"]).T
    return out
```

```python
import contextlib
import numpy as np
import ml_dtypes
import concourse.bass as bass
import concourse.mybir as mybir
from concourse.bass_utils import run_bass_kernel_spmd

F32 = mybir.dt.float32
BF16 = mybir.dt.bfloat16
AF = mybir.ActivationFunctionType
ALU = mybir.AluOpType
AX = mybir.AxisListType
NPBF = ml_dtypes.bfloat16

D_MODEL = 2048
NCH = 16
BATCH = 2
SEQ = 4096
TOK = 1024
HD = 128
NEG = -30000.0
EPS = 1e-6
SCALE = HD ** -0.5
POOL_WINDOWS = (2, 4, 8, 16)
DEBUG_STOP = 0
ROPE_ADD_ENG = "dve"

COMPUTE = ("pe", "act", "dve", "pool")
N_DMA_SEMS = 12


class Buf:
    __slots__ = ("name", "w", "r", "excl")

    def __init__(self, name="", excl=False):
        self.name = name
        self.w = None
        self.r = []
        self.excl = excl


class Op:
    __slots__ = ("eng", "fn", "waits", "sig", "idx", "dma", "dma_ev")

    def __init__(self, eng, fn, sig, dma):
        self.eng = eng
        self.fn = fn
        self.waits = []
        self.sig = sig
        self.dma = dma
        self.dma_ev = None


class Sched:
    def __init__(self, nc):
        self.nc = nc
        self.ops = {e: [] for e in COMPUTE + ("sp",)}
        self.dma_count = {e: 0 for e in ("sp", "act", "pool")}
        self.out_events = []
        self.fence_waits = {e: [] for e in COMPUTE + ("sp",)}
        self.n_cc = 0

    def collective(self, fn, reads, writes):
        op = Op("pool", fn, False, False)
        op.dma = "cc"
        lst = self.ops["pool"]
        op.idx = len(lst)
        lst.append(op)
        ev = ("x", self.n_cc)
        op.dma_ev = ev
        self.n_cc += 1
        if self.fence_waits["pool"]:
            op.waits.extend(self.fence_waits["pool"])
            self.fence_waits["pool"] = []
        for b in reads:
            if b.w is not None:
                op.waits.append(b.w)
        for b in writes:
            if b.w is not None:
                op.waits.append(b.w)
            op.waits.extend(b.r)
        for d in op.waits:
            if d[0] == "c":
                self.ops[d[1]][d[2]].sig = True
        for b in reads:
            b.r.append(ev)
        for b in writes:
            b.w = ev
            b.r = []
        return ev

    def fence(self):
        evs = []
        for e in COMPUTE + ("sp",):
            lst = self.ops[e]
            for op in reversed(lst):
                if not op.dma:
                    evs.append(("c", e, op.idx))
                    op.sig = True
                    break
        for q, n in self.dma_count.items():
            for i in range(max(0, n - N_DMA_SEMS), n):
                evs.append(("d", q, i))
        for i in range(self.n_cc):
            evs.append(("x", i))
        for e in self.fence_waits:
            self.fence_waits[e] = list(evs)

    def _add(self, eng, fn, reads, writes, sig=False, dma=False):
        op = Op(eng, fn, sig, dma)
        lst = self.ops[eng]
        op.idx = len(lst)
        lst.append(op)
        if dma:
            n = self.dma_count[eng]
            self.dma_count[eng] += 1
            ev = ("d", eng, n)
            op.dma_ev = ev
            if n >= N_DMA_SEMS:
                op.waits.append(("d", eng, n - N_DMA_SEMS))
        else:
            ev = ("c", eng, op.idx)
        if self.fence_waits[eng]:
            op.waits.extend(d for d in self.fence_waits[eng] if d != ev)
            self.fence_waits[eng] = []
        excl_reads = [b for b in reads if b.excl]
        if excl_reads:
            reads = [b for b in reads if not b.excl]
            writes = list(writes) + excl_reads
        for b in reads:
            if b.w is not None and b.w != ev:
                op.waits.append(b.w)
        for b in writes:
            if b.w is not None and b.w != ev:
                op.waits.append(b.w)
            for d in b.r:
                if d != ev:
                    op.waits.append(d)
        if eng == "pe" and not dma:
            op.waits = [d for d in op.waits if not (d[0] == "c" and d[1] == "pe")]
        for d in op.waits:
            if d[0] == "c":
                self.ops[d[1]][d[2]].sig = True
        for b in reads:
            if not dma:
                b.r = [d for d in b.r if not (d[0] == "c" and d[1] == eng)]
            b.r.append(ev)
        for b in writes:
            b.w = ev
            b.r = []
        return ev

    def pe(self, fn, reads, writes, sig=False):
        return self._add("pe", fn, reads, writes, sig)

    def act(self, fn, reads, writes):
        return self._add("act", fn, reads, writes)

    def dve(self, fn, reads, writes):
        return self._add("dve", fn, reads, writes)

    def pool(self, fn, reads, writes):
        return self._add("pool", fn, reads, writes)

    def dma(self, q, fn, reads, writes, is_output=False):
        ev = self._add(q, fn, reads, writes, dma=True)
        if is_output:
            self.out_events.append(ev)
        return ev

    def emit(self):
        nc = self.nc
        with contextlib.ExitStack() as st:
            csem = {e: st.enter_context(nc.semaphore("c_" + e)) for e in COMPUTE}
            dsem = {q: [st.enter_context(nc.semaphore(f"d_{q}{i}")) for i in range(N_DMA_SEMS)]
                    for q in ("sp", "act", "pool")}
            xsem = [st.enter_context(nc.semaphore(f"x_{i}")) for i in range(self.n_cc)]
            block = st.enter_context(nc.Block())
            sigcount = {}
            for e in COMPUTE:
                lst = self.ops[e]
                last = None
                for op in lst:
                    if not op.dma:
                        last = op
                if last is not None:
                    last.sig = True
                c = 0
                arr = []
                for op in lst:
                    if (not op.dma) and op.sig:
                        c += 1
                    arr.append(c)
                sigcount[e] = arr
            self.stats = {e: (len(self.ops[e]), sigcount[e][-1] if sigcount[e] else 0) for e in COMPUTE}
            self.stats["dma"] = dict(self.dma_count)

            def resolve(ev):
                if ev[0] == "c":
                    _, e, i = ev
                    op = self.ops[e][i]
                    v = sigcount[e][i]
                    assert op.sig
                    return ("c", e), csem[e], v
                if ev[0] == "x":
                    return ev, xsem[ev[1]], 1
                _, q, n = ev
                return ("d", q, n % N_DMA_SEMS), dsem[q][n % N_DMA_SEMS], 16 * (n // N_DMA_SEMS + 1)

            def run_engine(ename, eng):
                known = {}
                for op in self.ops[ename]:
                    need = {}
                    for ev in op.waits:
                        key, sem, v = resolve(ev)
                        if known.get(key, 0) >= v:
                            continue
                        if need.get(key, (None, 0))[1] < v:
                            need[key] = (sem, v)
                    for key, (sem, v) in need.items():
                        eng.wait_ge(sem, v)
                        known[key] = v
                    ins = op.fn(eng)
                    if op.dma == "cc":
                        ins.then_inc(xsem[op.dma_ev[1]])
                    elif op.dma:
                        _, q, n = op.dma_ev
                        ins.then_inc(dsem[q][n % N_DMA_SEMS], 16)
                    elif op.sig:
                        ins.then_inc(csem[ename], 1)
                if ename == "sp":
                    for ev in self.out_events:
                        key, sem, v = resolve(ev)
                        if known.get(key, 0) >= v:
                            continue
                        eng.wait_ge(sem, v)
                        known[key] = v

            @block.sync
            def _(eng):
                if getattr(self, "sp_init", None) is not None:
                    self.sp_init(eng)
                run_engine("sp", eng)

            @block.tensor
            def _(eng):
                run_engine("pe", eng)

            @block.scalar
            def _(eng):
                run_engine("act", eng)

            @block.vector
            def _(eng):
                run_engine("dve", eng)

            @block.gpsimd
            def _(eng):
                run_engine("pool", eng)


class T:
    def __init__(self, t, nsub=1, name="", psum=False):
        self.t = t
        if psum:
            self.b = [Buf(name, excl=True)] * nsub
        else:
            self.b = [Buf(f"{name}{i}") for i in range(nsub)]

    def __getitem__(self, k):
        return self.t[k]

    @property
    def all(self):
        return list(self.b)


class Ctx:
    def __init__(self):
        self.nc = bass.Bass("TRN2", target_bir_lowering=False)
        self.S = Sched(self.nc)
        self._n = 0
        self.scope = None

    def sb(self, shape, dt, nsub=1, name=None):
        self._n += 1
        name = (name or "sb") + f"_{self._n}"
        if self.scope is not None:
            return T(self.scope.enter_context(self.nc.sbuf_tensor(name, list(shape), dt)), nsub, name)
        return T(self.nc.alloc_sbuf_tensor(name, list(shape), dt), nsub, name)

    def ps(self, shape, dt=F32, nsub=1, name=None):
        self._n += 1
        name = name or f"ps{self._n}"
        return T(self.nc.alloc_psum_tensor(name, list(shape), dt), nsub, name, psum=True)

    def din(self, name, shape, dt):
        return self.nc.dram_tensor(name, list(shape), dt, kind="ExternalInput").ap()

    def dout(self, name, shape, dt):
        return self.nc.dram_tensor(name, list(shape), dt, kind="ExternalOutput").ap()

    def load(self, dst_ap, src_ap, wbufs, q="sp", rbufs=()):
        return self.S.dma(q, lambda e: e.dma_start(out=dst_ap, in_=src_ap), list(rbufs), list(wbufs))

    def store(self, dst_ap, src_ap, rbufs, q="sp", is_output=True, wbufs=()):
        return self.S.dma(q, lambda e: e.dma_start(out=dst_ap, in_=src_ap), list(rbufs), list(wbufs), is_output=is_output)

    def mm(self, out_ap, lhsT, rhs, start, stop, reads, writes, sig=None):
        if sig is None:
            sig = False
        return self.S.pe(lambda e: e.matmul(out_ap, lhsT, rhs, start=start, stop=stop), reads, writes, sig=sig)

    def transpose(self, out_ap, in_ap, ident_ap, reads, writes):
        return self.S.pe(lambda e: e.transpose(out_ap, in_ap, ident_ap), reads, writes)

    def activation(self, out_ap, in_ap, func, reads, writes, bias=None, scale=None):
        kw = {}
        if bias is not None:
            kw["bias"] = bias
        if scale is not None:
            kw["scale"] = scale
        return self.S.act(lambda e: e.activation(out=out_ap, in_=in_ap, func=func, **kw), reads, writes)

    def tt(self, out_ap, in0, in1, op, reads, writes, eng="dve"):
        f = lambda e: e.tensor_tensor(out=out_ap, in0=in0, in1=in1, op=op)
        return (self.S.dve if eng == "dve" else self.S.pool)(f, reads, writes)

    def ts(self, out_ap, in0, s1, s2, op0, op1, reads, writes, eng="dve"):
        if op1 is None:
            f = lambda e: e.tensor_scalar(out=out_ap, in0=in0, scalar1=s1, scalar2=None, op0=op0)
        else:
            f = lambda e: e.tensor_scalar(out=out_ap, in0=in0, scalar1=s1, scalar2=s2, op0=op0, op1=op1)
        return (self.S.dve if eng == "dve" else self.S.pool)(f, reads, writes)

    def stt(self, out_ap, in0, scalar, in1, op0, op1, reads, writes):
        return self.S.dve(lambda e: e.scalar_tensor_tensor(out=out_ap, in0=in0, scalar=scalar, in1=in1,
                                                            op0=op0, op1=op1), reads, writes)

    def copy(self, out_ap, in_ap, reads, writes, eng="dve"):
        if eng == "act":
            return self.S.act(lambda e: e.copy(out=out_ap, in_=in_ap), reads, writes)
        f = lambda e: e.tensor_copy(out=out_ap, in_=in_ap)
        return (self.S.dve if eng == "dve" else self.S.pool)(f, reads, writes)

    def memset(self, ap, val, writes, eng="pool"):
        f = lambda e: e.memset(ap, val)
        return (self.S.dve if eng == "dve" else self.S.pool)(f, [], writes)


class Common:
    def __init__(self, cx, consts_ap):
        self.cx = cx
        self.cb = cx.sb([128, 256], BF16, name="cbf")
        cx.load(self.cb[:, :], consts_ap, self.cb.all)
        self.ident = self.cb[:, 0:128]
        self.ones = self.cb[:, 128:256]
        self.banks = [cx.ps([128, 512], F32, nsub=4, name=f"bank{i}") for i in range(7)]
        self.bankb = cx.ps([128, 1024], BF16, nsub=8, name="bankb")


def rms_stats(cx, cm, src_fn, src_bufs, nfeat_chunks, ncols, sq_pool, bank, rstd, denom):
    for c in range(nfeat_chunks):
        sq = sq_pool[c % len(sq_pool)]
        cx.activation(sq[:, 0:ncols], src_fn(c), AF.Square, src_bufs(c), sq.all)
        cx.mm(bank[:, 0:ncols], cm.ones, sq[:, 0:ncols], c == 0, c == nfeat_chunks - 1,
              [cm.cb.b[0]] + sq.all, bank.all)
    cx.activation(rstd[:, 0:ncols], bank[:, 0:ncols], AF.Ln, bank.all + cm.epst.all, rstd.all, bias=cm.eps_ap,
                  scale=1.0 / denom)
    cx.activation(rstd[:, 0:ncols], rstd[:, 0:ncols], AF.Exp, rstd.all, rstd.all, scale=-0.5)


def add_eps(cx, cm):
    cm.epst = cx.sb([128, 1], F32, name="epst")
    cx.memset(cm.epst[:, :], EPS, cm.epst.all)
    cm.eps_ap = cm.epst[:, 0:1]


def phase_pre(cx, cm, xT, gpre, hn_out_fn, is_output):
    g = cx.sb([128, NCH], F32, name="g")
    cx.load(g[:, :], gpre, g.all)
    xv = xT.rearrange("(c p) t -> p c t", p=128)
    xs = [cx.sb([128, NCH, 512], F32, nsub=NCH, name=f"xs{i}") for i in range(2)]
    hs = [cx.sb([128, NCH, 512], BF16, nsub=1, name=f"hs{i}") for i in range(2)]
    sqp = [cx.sb([128, 512], BF16, name=f"sq{i}") for i in range(3)]
    rstd = cx.sb([128, 512], F32, name="rstd")
    for n in range(TOK // 512):
        x = xs[n % 2]
        h = hs[n % 2]
        cx.load(x[:, :, :], xv[:, :, n * 512:(n + 1) * 512], x.all)
        rms_stats(cx, cm, lambda c: x[:, c, :], lambda c: [x.b[c]], NCH, 512, sqp, cm.banks[n % 2], rstd, D_MODEL)
        for c in range(NCH):
            cx.stt(h[:, c, :], x[:, c, :], g[:, c:c + 1], rstd[:, :], ALU.mult, ALU.mult,
                   [x.b[c]] + g.all + rstd.all, h.all)
        for j in range(2):
            hn_out_fn(n, j, h[:, :, j * 256:(j + 1) * 256], h.all)


def build_pre():
    cx = Ctx()
    xT = cx.din("xT", [D_MODEL, TOK], F32)
    gpre = cx.din("gpre", [128, NCH], F32)
    consts = cx.din("consts", [128, 256], BF16)
    hn_out = cx.dout("hn_out", [D_MODEL, TOK], BF16)
    cm = Common(cx, consts)
    add_eps(cx, cm)
    ov = hn_out.rearrange("(c p) t -> p c t", p=128)
    phase_pre(cx, cm, xT, gpre,
              lambda n, j, src, rb: cx.store(ov[:, :, n * 512 + j * 256:n * 512 + (j + 1) * 256], src, rb), True)
    cx.S.emit()
    return cx.nc


W_OFF = {"moba": (0, 384), "fox": (384, 384), "hgrn": (768, 512), "pool": (1280, 128)}
W_MIXCOLS = 1408
SM_WFF, SM_G0, SM_G1, SM_HNORM, SM_PSCALE, SM_FB = 0, 16, 17, 18, 19, 20
N_SMALL = 24


class MixEnv:
    pass


def project(cx, cm, env, w, fm_blocks, tm, row=None, pre_chunk=None):
    nb = 0
    for n in range(SEQ // 512):
        hs = env.hs[n % 2]
        for j in range(2):
            cx.load(hs[:, :, j * 256:(j + 1) * 256], env.hn_chunk(n, j), hs.all, rbufs=env.hn_rbufs(n, j))
        if pre_chunk is not None:
            pre_chunk(n)
        for (c0, handler) in fm_blocks:
            bank = cm.banks[nb % 4]
            nb += 1
            for c in range(NCH):
                cx.mm(bank[:, :], w[:, c, c0:c0 + 128], hs[:, c, :], c == 0, c == NCH - 1,
                      w.all + hs.all, bank.all)
            handler(n, bank)
        if tm is not None:
            c0, ncols, handler = tm
            for tl in range(4):
                bank = cm.banks[nb % 4]
                nb += 1
                for c in range(NCH):
                    cx.mm(bank[:, 0:ncols], hs[:, c, tl * 128:(tl + 1) * 128], w[:, c, c0:c0 + ncols],
                          c == 0, c == NCH - 1, w.all + hs.all, bank.all)
                handler(n * 4 + tl, bank)
        if row is not None:
            lfn, rreads, handler = row
            bank = cm.banks[nb % 4]
            nb += 1
            for c in range(NCH):
                cx.mm(bank[0:1, :], lfn(c), hs[:, c, :], c == 0, c == NCH - 1, rreads + hs.all, bank.all)
            handler(n, bank)


def load_w(cx, env, name, eng="pool"):
    c0, nc_ = W_OFF[name]
    w = cx.sb([128, NCH, nc_], BF16, name="w_" + name)
    for c in range(0, NCH, 4):
        cx.load(w[:, c:c + 4, :], env.wmix[:, c:c + 4, c0:c0 + nc_], w.all, q=eng)
    return w


def softmax_finish(cx, env, obank, dbank, ncols, out_ap_dram, k):
    rc = env.rc[k % 2]
    yt = env.yt[k % 2]
    cx.S.dve(lambda e: e.reciprocal(out=rc[:, 0:ncols], in_=dbank[:, 0:ncols]), dbank.all, rc.all)
    cx.tt(yt[:, 0:ncols], obank[:, 0:ncols], rc[:, 0:ncols], ALU.mult, obank.all + rc.all, yt.all)
    cx.store(out_ap_dram, yt[:, 0:ncols], yt.all, is_output=env.y_is_output)


def mix_fox(cx, cm, env):
    S = cx.S
    w = load_w(cx, env, "fox")
    fQ = cx.sb([128, SEQ], BF16, name="fQ")
    fK = cx.sb([128, SEQ], BF16, name="fK")
    fV = cx.sb([128, 32, 128], BF16, name="fV")
    wff = cx.sb([128, NCH], BF16, name="wff")
    cx.copy(wff[:, :], env.small[:, SM_WFF:SM_WFF + NCH], env.small.all, wff.all)
    nfb = cx.sb([128, 1], F32, name="nfb")
    cx.ts(nfb[:, :], env.small[:, SM_FB:SM_FB + 1], -1.0, None, ALU.mult, None, env.small.all, nfb.all)
    sprow = cx.sb([1, SEQ], F32, name="sprow")
    cprow = cx.sb([1, SEQ], F32, name="cprow")
    nrh = cx.sb([1, SEQ], BF16, name="nrh")
    nrl = cx.sb([1, SEQ], BF16, name="nrl")
    rowtmp = cx.sb([1, 512], F32, name="rowtmp")

    def h_q(n, bank):
        S.act(lambda e: e.mul(out=fQ[:, n * 512:(n + 1) * 512], in_=bank[:, :], mul=SCALE), bank.all, fQ.all)

    def h_k(n, bank):
        cx.copy(fK[:, n * 512:(n + 1) * 512], bank[:, :], bank.all, fK.all)

    def h_v(tl, bank):
        cx.copy(fV[:, tl, :], bank[:, 0:128], bank.all, fV.all, eng="act" if tl % 2 else "dve")

    def h_row(n, bank):
        cx.activation(rowtmp[0:1, :], bank[0:1, :], AF.Exp, bank.all + nfb.all, rowtmp.all, bias=nfb[0:1, 0:1], scale=-1.0)
        cx.activation(sprow[0:1, n * 512:(n + 1) * 512], rowtmp[0:1, :], AF.Ln, rowtmp.all + cm.c32.all, sprow.all,
                      bias=cm.one_ap[0:1, 0:1])

    project(cx, cm, env, w, [(0, h_q), (128, h_k)], (256, 128, h_v),
            row=(lambda c: wff[:, c:c + 1], wff.all, h_row))

    cx.memset(nrh[0:1, :], 1.0, nrh.all, eng="dve")
    S.dve(lambda e: e.tensor_tensor_scan(out=cprow[0:1, :], data0=nrh[0:1, :], data1=sprow[0:1, :], initial=0.0,
                                         op0=ALU.mult, op1=ALU.add), nrh.all + sprow.all, cprow.all)
    sm = cm.banks[6]
    cpv = cprow[0:1, :].rearrange("o (a b) -> o a b", b=512)
    cx.mm(sm[:, 0:8], cm.ones32[0:1, 0:128], cpv[:, :, 0], True, True, cm.c32.all + cprow.all, sm.all)
    for kt in range(32):
        cx.mm(sm[:, 8 + kt:9 + kt], cprow[0:1, kt * 128:(kt + 1) * 128], cm.ones32[0:1, 0:1], True, True,
              cm.c32.all + cprow.all, sm.all)
    rbcp = cx.sb([128, 40], F32, name="rbcp")
    cx.copy(rbcp[:, :], sm[:, 0:40], sm.all, rbcp.all)
    for qc in range(8):
        sl = slice(qc * 512, (qc + 1) * 512)
        cx.ts(sprow[0:1, sl], cprow[0:1, sl], cprow[0:1, qc * 512:qc * 512 + 1], -1.0, ALU.subtract, ALU.mult,
              cprow.all, sprow.all)
    cx.copy(nrh[0:1, :], sprow[0:1, :], sprow.all, nrh.all)
    cx.tt(nrl[0:1, :], sprow[0:1, :], nrh[0:1, :], ALU.subtract, sprow.all + nrh.all, nrl.all)

    biasq = [cx.sb([128, 32], F32, name=f"biasq{i}") for i in range(2)]
    pTs = [cx.sb([128, 512], BF16, name=f"fpT{i}") for i in range(3)]
    it = 0
    for qc in range(8):
        sl = slice(qc * 512, (qc + 1) * 512)
        bq = biasq[qc % 2]
        cx.ts(bq[:, :], rbcp[:, 8:40], rbcp[:, qc:qc + 1], None, ALU.subtract, None, rbcp.all, bq.all)
        obank = cm.banks[2 + qc % 2]
        dbank = cm.banks[4 + qc % 2]
        nkt = 4 * (qc + 1)

        def emit_S(kt):
            sbank = cm.banks[kt % 2]
            a = kt - 4 * qc
            cx.mm(sbank[:, :], fK[:, kt * 128:(kt + 1) * 128], fQ[:, sl], True, False, fK.all + fQ.all, sbank.all)
            cx.mm(sbank[:, :], cm.ones[0:1, 0:128], nrh[0:1, sl], False, False, nrh.all, sbank.all)
            cx.mm(sbank[:, :], cm.ones[0:1, 0:128], nrl[0:1, sl], False, a < 0, nrl.all, sbank.all)
            if a >= 0:
                cx.mm(sbank[:, :], cm.ident, env.cmask[:, a, :], False, True, env.cmask.all, sbank.all)
        emit_S(0)
        for kt in range(nkt):
            sbank = cm.banks[kt % 2]
            pT = pTs[it % 3]
            it += 1
            cx.activation(pT[:, :], sbank[:, :], AF.Exp, sbank.all + bq.all, pT.all, bias=bq[:, kt:kt + 1])
            if kt + 1 < nkt:
                emit_S(kt + 1)
            cx.mm(obank[:, :], fV[:, kt, :], pT[:, :], kt == 0, kt == nkt - 1, fV.all + pT.all, obank.all)
            cx.mm(dbank[:, :], cm.ones, pT[:, :], kt == 0, kt == nkt - 1, pT.all, dbank.all)
        softmax_finish(cx, env, obank, dbank, 512, env.y_out(1, qc * 512, 512), qc)


def mix_moba(cx, cm, env):
    S = cx.S
    w = load_w(cx, env, "moba")
    mQ = cx.sb([128, SEQ], BF16, name="mQ")
    mK = cx.sb([128, SEQ], BF16, name="mK")
    mV = cx.sb([128, 32, 128], BF16, name="mV")
    perm = cx.sb([128, 128], BF16, name="perm")
    cx.load(perm[:, :], env.perm, perm.all)
    rC = [cx.sb([128, 512], F32, name=f"rC{i}") for i in range(2)]
    rS = [cx.sb([128, 512], F32, name=f"rS{i}") for i in range(2)]
    xb = [cx.sb([128, 512], BF16, name=f"xb{i}") for i in range(2)]
    t1 = [cx.sb([128, 512], F32, name=f"t1{i}") for i in range(2)]
    t2 = [cx.sb([128, 512], F32, name=f"t2{i}") for i in range(2)]
    cnt = [0]

    def pre_chunk(n):
        cx.load(rC[n % 2][:, :], env.ropeC[:, n * 512:(n + 1) * 512], rC[n % 2].all)
        cx.load(rS[n % 2][:, :], env.ropeS[:, n * 512:(n + 1) * 512], rS[n % 2].all)

    def rope_handler(dst, sc):
        def h(n, bank):
            k = cnt[0] % 2
            cnt[0] += 1
            sl = slice(n * 512, (n + 1) * 512)
            if DEBUG_STOP == 10:
                cx.copy(dst[:, sl], bank[:, :], bank.all, dst.all)
                return
            cx.copy(xb[k][:, :], bank[:, :], bank.all, xb[k].all, eng="act")
            swb = cm.banks[4 + k]
            cx.mm(swb[:, :], perm[:, :], xb[k][:, :], True, True, perm.all + xb[k].all, swb.all)
            if DEBUG_STOP == 11:
                cx.copy(dst[:, sl], swb[:, :], swb.all, dst.all)
                return
            if DEBUG_STOP == 13:
                cx.stt(t1[k][:, :], bank[:, :], sc, t2[k][:, :], ALU.mult, ALU.mult, bank.all + t2[k].all, t1[k].all)
            else:
                cx.stt(t1[k][:, :], bank[:, :], sc, rC[n % 2][:, :], ALU.mult, ALU.mult, bank.all + rC[n % 2].all, t1[k].all)
            if DEBUG_STOP in (12, 13):
                cx.copy(dst[:, sl], t1[k][:, :], t1[k].all, dst.all)
                return
            cx.stt(t2[k][:, :], swb[:, :], sc, rS[n % 2][:, :], ALU.mult, ALU.mult, swb.all + rS[n % 2].all, t2[k].all)
            cx.tt(dst[:, sl], t1[k][:, :], t2[k][:, :], ALU.add, t1[k].all + t2[k].all, dst.all, eng=ROPE_ADD_ENG)
        return h

    def h_v(tl, bank):
        cx.copy(mV[:, tl, :], bank[:, 0:128], bank.all, mV.all, eng="act")

    project(cx, cm, env, w, [(0, rope_handler(mQ, SCALE)), (128, rope_handler(mK, 1.0))], (256, 128, h_v),
            pre_chunk=pre_chunk)

    if DEBUG_STOP in (1, 10, 11, 12, 13):
        return
    kb32 = cx.sb([128, 16], F32, name="kb32")
    kbT = cx.sb([128, 16], BF16, name="kbT")
    S.dve(lambda e: e.tensor_reduce(out=kb32[:, :], in_=mK[:, :].rearrange("p (j k) -> p j k", k=256),
                                    axis=AX.X, op=ALU.add), mK.all, kb32.all)
    cx.copy(kbT[:, :], kb32[:, :], kb32.all, kbT.all)
    gb = cm.banks[6]
    for qt in range(32):
        cx.mm(gb[:, qt * 16:(qt + 1) * 16], mQ[:, qt * 128:(qt + 1) * 128], kbT[:, :], True, True,
              mQ.all + kbT.all, gb.all)
    g_sb = cx.sb([128, 32, 16], F32, name="g_sb")
    cx.copy(g_sb[:, :, :], gb[:, :].rearrange("p (a b) -> p a b", b=16), gb.all, g_sb.all)
    if DEBUG_STOP == 2:
        return
    S.pool(lambda e: e.affine_select(out=g_sb[:, :, :], in_=g_sb[:, :, :], pattern=[[1, 16], [0, 2], [-1, 16]],
                                     compare_op=ALU.is_ge, fill=-1e30, base=-1, channel_multiplier=0),
           g_sb.all, g_sb.all)
    if DEBUG_STOP == 3:
        return
    m8 = cx.sb([128, 32, 8], F32, name="m8")
    for qt in range(32):
        S.dve(lambda e, qt=qt: e.max(out=m8[:, qt, :], in_=g_sb[:, qt, :]), g_sb.all, m8.all)
    thr = cx.sb([128, 32, 1], F32, name="thr")
    cx.ts(thr[:, :, :], m8[:, :, 2:3], -1e29, None, ALU.max, None, m8.all, thr.all)
    nm = cx.sb([128, 32, 16], F32, name="nm")
    cx.tt(nm[:, :, :], g_sb[:, :, :], thr[:, :, :].to_broadcast([128, 32, 16]), ALU.is_lt, g_sb.all + thr.all, nm.all)
    cx.ts(nm[:, :, :], nm[:, :, :], NEG, None, ALU.mult, None, nm.all, nm.all)
    if DEBUG_STOP == 4:
        return
    nmT = cx.sb([16, SEQ], BF16, name="nmT")
    for grp in range(8):
        tb = cm.banks[grp % 2]
        for i in range(4):
            qt = grp * 4 + i
            cx.transpose(tb[0:16, i * 128:(i + 1) * 128], nm[:, qt, :], cm.ident32, nm.all + cm.c32.all, tb.all)
        cx.copy(nmT[0:16, grp * 512:(grp + 1) * 512], tb[0:16, :], tb.all, nmT.all, eng="act" if grp % 2 else "dve")

    if DEBUG_STOP == 5:
        return
    pTs = [cx.sb([128, 256], BF16, name=f"mpT{i}") for i in range(3)]
    it = 0
    for qb in range(16):
        sl = slice(qb * 256, (qb + 1) * 256)
        obank = cm.banks[2 + qb % 2]
        dbank = cm.banks[4 + qb % 2]
        nkt = 2 * qb + 2

        def emit_S(kt):
            sbank = cm.banks[kt % 2]
            cx.mm(sbank[:, 0:256], mK[:, kt * 128:(kt + 1) * 128], mQ[:, sl], True, False, mK.all + mQ.all, sbank.all)
            if kt < 2 * qb:
                cx.mm(sbank[:, 0:256], env.esel[0:16, kt // 2, :], nmT[0:16, sl], False, True,
                      env.esel.all + nmT.all, sbank.all)
            else:
                cx.mm(sbank[:, 0:256], cm.ident, env.cmask[:, kt - 2 * qb, 0:256], False, True, env.cmask.all, sbank.all)
        emit_S(0)
        for kt in range(nkt):
            sbank = cm.banks[kt % 2]
            pT = pTs[it % 3]
            it += 1
            cx.activation(pT[:, :], sbank[:, 0:256], AF.Exp, sbank.all, pT.all)
            if kt + 1 < nkt:
                emit_S(kt + 1)
            cx.mm(obank[:, 0:256], mV[:, kt, :], pT[:, :], kt == 0, kt == nkt - 1, mV.all + pT.all, obank.all)
            cx.mm(dbank[:, 0:256], cm.ones, pT[:, :], kt == 0, kt == nkt - 1, pT.all, dbank.all)
        softmax_finish(cx, env, obank, dbank, 256, env.y_out(0, qb * 256, 256), qb)


def mix_hgrn(cx, cm, env, layer):
    S = cx.S
    w = load_w(cx, env, "hgrn")
    hq = cx.sb([128, SEQ], F32, name="hq")
    lf = cx.sb([128, SEQ], F32, name="lf")
    hk = cx.sb([128, SEQ], BF16, name="hk")
    sg = cx.sb([128, SEQ], BF16, name="sg")
    hv = cx.sb([128, 32, 128], BF16, name="hv")
    lb = cx.sb([128, 2], F32, name="lb")
    if layer == 0:
        cx.memset(lb[:, 0:1], 0.0, lb.all, eng="dve")
        cx.memset(lb[:, 1:2], 1.0, lb.all, eng="dve")
    else:
        ee = cx.sb([128, 4], F32, name="ee")
        cx.activation(ee[:, 0:2], env.small[:, SM_G0:SM_G0 + 2], AF.Exp, env.small.all, ee.all)
        cx.tt(ee[:, 2:3], ee[:, 0:1], ee[:, 1:2], ALU.add, ee.all, ee.all)
        S.dve(lambda e: e.reciprocal(out=ee[:, 3:4], in_=ee[:, 2:3]), ee.all, ee.all)
        cx.tt(lb[:, 0:1], ee[:, 1:2], ee[:, 3:4], ALU.mult, ee.all, lb.all)
        cx.ts(lb[:, 1:2], lb[:, 0:1], -1.0, 1.0, ALU.mult, ALU.add, lb.all, lb.all)
    sgm = [cx.sb([128, 512], F32, name=f"sgm{i}") for i in range(2)]
    ff_ = [cx.sb([128, 512], F32, name=f"ff{i}") for i in range(2)]

    def h_q(n, bank):
        cx.copy(hq[:, n * 512:(n + 1) * 512], bank[:, :], bank.all, hq.all)

    def h_f(n, bank):
        sl = slice(n * 512, (n + 1) * 512)
        a, f = sgm[n % 2], ff_[n % 2]
        cx.activation(a[:, :], bank[:, :], AF.Sigmoid, bank.all, a.all)
        cx.ts(f[:, :], a[:, :], lb[:, 1:2], lb[:, 0:1], ALU.mult, ALU.add, a.all + lb.all, f.all)
        cx.activation(lf[:, sl], f[:, :], AF.Ln, f.all, lf.all)
        cx.ts(hk[:, sl], f[:, :], -1.0, 1.0, ALU.mult, ALU.add, f.all, hk.all, eng="pool")

    def h_g(n, bank):
        sl = slice(n * 512, (n + 1) * 512)
        a = sgm[n % 2]
        cx.activation(a[:, :], bank[:, :], AF.Sigmoid, bank.all, a.all)
        cx.tt(sg[:, sl], bank[:, :], a[:, :], ALU.mult, bank.all + a.all, sg.all)

    def h_v(tl, bank):
        cx.copy(hv[:, tl, :], bank[:, 0:128], bank.all, hv.all, eng="act" if tl % 2 else "dve")

    project(cx, cm, env, w, [(0, h_q), (128, h_f), (256, h_g)], (384, 128, h_v))

    rm = cx.sb([128, SEQ], BF16, name="rm")
    cx.memset(rm[:, :], 1.0, rm.all)
    cx.memset(rm[:, :].rearrange("p (a b) -> p a b", b=64)[:, :, 0:1], 0.0, rm.all)
    bb = cx.sb([128, SEQ], F32, name="bb")
    S.dve(lambda e: e.tensor_tensor_scan(out=bb[:, :], data0=rm[:, :], data1=lf[:, :], initial=0.0,
                                         op0=ALU.mult, op1=ALU.add), rm.all + lf.all, bb.all)
    bv = bb[:, :].rearrange("p (a b) -> p a b", b=64)
    sm = cx.sb([128, 5, 64], F32, name="hsm")
    cx.copy(sm[:, 0, :], bv[:, :, 31], bb.all, sm.all)
    cx.activation(sm[:, 4, :], bv[:, :, 31], AF.Exp, bb.all, sm.all)
    cx.activation(sm[:, 1, :], bv[:, :, 63], AF.Exp, bb.all, sm.all)
    cx.tt(sm[:, 3, :], bv[:, :, 63], sm[:, 0, :], ALU.subtract, bb.all + sm.all, sm.all)
    cx.activation(sm[:, 2, :], sm[:, 3, :], AF.Exp, sm.all, sm.all)
    cx.tt(bv, bv, sm[:, 0, :].unsqueeze(2).to_broadcast([128, 64, 64]), ALU.subtract, bb.all + sm.all, bb.all)
    E = lf
    qt_ = cx.sb([128, SEQ], BF16, name="qtil")
    kt_ = cx.sb([128, SEQ], BF16, name="ktil")
    kh_ = cx.sb([128, SEQ], BF16, name="khat")
    cx.activation(E[:, :], bb[:, :], AF.Exp, bb.all, E.all)
    cx.tt(qt_[:, :], hq[:, :], E[:, :], ALU.mult, hq.all + E.all, qt_.all)
    cx.activation(E[:, :], bb[:, :], AF.Exp, bb.all, E.all, scale=-1.0)
    cx.tt(kt_[:, :], hk[:, :], E[:, :], ALU.mult, hk.all + E.all, kt_.all)
    cx.tt(kh_[:, :].rearrange("p (a b) -> p a b", b=64), kt_[:, :].rearrange("p (a b) -> p a b", b=64),
          sm[:, 2, :].unsqueeze(2).to_broadcast([128, 64, 64]), ALU.mult, kt_.all + sm.all, kh_.all, eng="pool")
    khtm = cx.sb([128, 32, 128], BF16, name="khtm")
    for grp in range(4):
        for i in range(8):
            tl = grp * 8 + i
            cx.transpose(cm.bankb[:, i * 128:(i + 1) * 128], kh_[:, tl * 128:(tl + 1) * 128], cm.ident,
                         kh_.all, cm.bankb.all)
        cx.copy(khtm[:, grp * 8:(grp + 1) * 8, :], cm.bankb[:, :].rearrange("p (a b) -> p a b", b=128),
                cm.bankb.all, khtm.all, eng="act" if grp % 2 else "dve")
    attT = cx.sb([128, 32, 128], BF16, nsub=8, name="attT")
    for grp in range(8):
        ab = cm.banks[grp % 2]
        for i in range(4):
            tl = grp * 4 + i
            ts_ = slice(tl * 128, (tl + 1) * 128)
            cx.mm(ab[:, i * 128:(i + 1) * 128], kt_[:, ts_], qt_[:, ts_], True, True, kt_.all + qt_.all, ab.all)
        cx.tt(attT[:, grp * 4:(grp + 1) * 4, :], ab[:, :].rearrange("p (a b) -> p a b", b=128),
              env.hmask[:, :].unsqueeze(1).to_broadcast([128, 4, 128]), ALU.mult, ab.all + env.hmask.all, [attT.b[grp]])
    S32 = cx.sb([128, 2, 128], F32, nsub=2, name="S32")
    Sb = cx.sb([128, 4, 128], BF16, nsub=4, name="Sb")
    cx.memset(S32[:, 0, :], 0.0, [S32.b[0]], eng="dve")
    cx.memset(Sb[:, 0, :], 0.0, [Sb.b[0]], eng="dve")
    oT = [cx.sb([128, 512], F32, name=f"oT{i}") for i in range(2)]
    sq = [cx.sb([128, 512], BF16, name=f"osq{i}") for i in range(2)]
    rs = [cx.sb([128, 512], F32, name=f"ors{i}") for i in range(2)]
    yo = [cx.sb([128, 512], BF16, name=f"oyo{i}") for i in range(2)]
    for tl in range(32):
        ob = cm.banks[2 + (tl // 4) % 2]
        for half in range(2):
            c = 2 * tl + half
            ps = slice(half * 64, half * 64 + 64)
            mslot = 0
            mb = cm.banks[4 + c % 2]
            cx.mm(mb[:, mslot * 128:(mslot + 1) * 128], khtm[ps, tl, :], hv[ps, tl, :], True, True,
                  khtm.all + hv.all, [mb.b[mslot]])
            oc = slice((tl % 4) * 128 + half * 64, (tl % 4) * 128 + half * 64 + 64)
            tsl = slice(c * 64, c * 64 + 64)
            cx.mm(ob[:, oc], hv[:, tl, :], attT[:, tl, half * 64:half * 64 + 64], True, False,
                  hv.all + [attT.b[tl // 4]], [ob.b[tl % 4]])
            cx.mm(ob[:, oc], Sb[:, c % 4, :], qt_[:, tsl], False, True, [Sb.b[c % 4]] + qt_.all, [ob.b[tl % 4]])
            cx.stt(S32[:, (c + 1) % 2, :], S32[:, c % 2, :], sm[:, 1, c:c + 1], mb[:, mslot * 128:(mslot + 1) * 128],
                   ALU.mult, ALU.add, [S32.b[c % 2], mb.b[mslot]] + sm.all, [S32.b[(c + 1) % 2]])
            if c + 1 < 64:
                cx.S.act(lambda e, c=c: e.mul(out=Sb[:, (c + 1) % 4, :], in_=S32[:, (c + 1) % 2, :], mul=sm[:, 4, c + 1:c + 2]),
                         [S32.b[(c + 1) % 2]] + sm.all, [Sb.b[(c + 1) % 4]])
        if tl % 4 == 3:
            n = tl // 4
            k = n % 2
            sl = slice(n * 512, (n + 1) * 512)
            cx.copy(oT[k][:, :], ob[:, :], ob.all, oT[k].all)
            cx.activation(sq[k][:, :], ob[:, :], AF.Square, ob.all, sq[k].all)
            nb_ = cm.banks[6]
            cx.mm(nb_[:, :], cm.ones, sq[k][:, :], True, True, sq[k].all, nb_.all)
            cx.activation(rs[k][:, :], nb_[:, :], AF.Ln, nb_.all + cm.epst.all, rs[k].all, bias=cm.eps_ap, scale=1.0 / HD)
            cx.activation(rs[k][:, :], rs[k][:, :], AF.Exp, rs[k].all, rs[k].all, scale=-0.5)
            cx.stt(oT[k][:, :], oT[k][:, :], env.small[:, SM_HNORM:SM_HNORM + 1], rs[k][:, :], ALU.mult, ALU.mult,
                   oT[k].all + rs[k].all + env.small.all, oT[k].all)
            cx.tt(yo[k][:, :], oT[k][:, :], sg[:, sl], ALU.mult, oT[k].all + sg.all, yo[k].all, eng="pool")
            cx.store(env.y_out(2, n * 512, 512), yo[k][:, :], yo[k].all, is_output=env.y_is_output)


def mix_pool(cx, cm, env):
    w = load_w(cx, env, "pool")
    pin = cx.sb([128, 32, 128], BF16, nsub=32, name="pin")
    pM = cx.sb([128, 3, 128], BF16, name="pM")
    cx.load(pM[:, :, :], env.poolM, pM.all)
    pw = cx.sb([128, 128], BF16, name="pw")
    cx.load(pw[:, :], env.poolw, pw.all, q="pool")

    def h_p(tl, bank):
        cx.copy(pin[:, tl, :], bank[:, 0:128], bank.all, [pin.b[tl]], eng="act" if tl % 2 else "dve")

    project(cx, cm, env, w, [], (0, 128, h_p))
    pt = [cx.sb([128, 512], BF16, name=f"ppt{i}") for i in range(2)]
    yo = [cx.sb([128, 512], BF16, name=f"pyo{i}") for i in range(2)]
    for n in range(8):
        pb = cm.banks[n % 2]
        for i in range(4):
            tl = n * 4 + i
            cs = slice(i * 128, (i + 1) * 128)
            cx.mm(pb[:, cs], pin[:, tl, :], pM[:, 0 if tl == 0 else 1, :], True, tl == 0, [pin.b[tl]] + pM.all, [pb.b[i]])
            if tl > 0:
                cx.mm(pb[:, cs], pin[:, tl - 1, :], pM[:, 2, :], False, True, [pin.b[tl - 1]] + pM.all, [pb.b[i]])
        k = n % 2
        cx.copy(pt[k][:, :], pb[:, :], pb.all, pt[k].all, eng="act")
        ob = cm.banks[2 + n % 2]
        cx.mm(ob[:, :], pw[:, :], pt[k][:, :], True, True, pw.all + pt[k].all, ob.all)
        cx.ts(yo[k][:, :], ob[:, :], env.small[:, SM_PSCALE:SM_PSCALE + 1], None, ALU.mult, None,
              ob.all + env.small.all, yo[k].all)
        cx.store(env.y_out(3, n * 512, 512), yo[k][:, :], yo[k].all, is_output=env.y_is_output)


def setup_common32(cx, cm, c32_ap):
    cm.c32 = cx.sb([128, 256], F32, name="c32")
    cx.load(cm.c32[:, :], c32_ap, cm.c32.all)
    cm.ident32 = cm.c32[:, 0:128]
    cm.ones32 = cm.c32[:, 128:256]
    cm.one_ap = cm.c32[:, 128:129]


def mix_globals(cx, env, cmask_d, esel_d, hmask_d):
    env.cmask = cx.sb([128, 4, 512], BF16, name="cmask")
    cx.load(env.cmask[:, :, :], cmask_d, env.cmask.all)
    env.esel = cx.sb([16, 16, 128], BF16, name="esel")
    cx.load(env.esel[:, :, :], esel_d, env.esel.all)
    env.hmask = cx.sb([128, 128], BF16, name="hmask")
    cx.load(env.hmask[:, :], hmask_d, env.hmask.all)


def phase_mix(cx, cm, env, layer, small_d, which=("fox", "moba", "hgrn", "pool")):
    with contextlib.ExitStack() as outer:
        cx.scope = outer
        env.small = cx.sb([128, N_SMALL], F32, name="small")
        cx.load(env.small[:, :], small_d, env.small.all)
        env.hs = [cx.sb([128, NCH, 512], BF16, name=f"hs{i}") for i in range(2)]
        env.rc = [cx.sb([128, 512], F32, name=f"rc{i}") for i in range(2)]
        env.yt = [cx.sb([128, 512], BF16, name=f"yt{i}") for i in range(2)]
        fns = {"fox": lambda: mix_fox(cx, cm, env), "moba": lambda: mix_moba(cx, cm, env),
               "hgrn": lambda: mix_hgrn(cx, cm, env, layer), "pool": lambda: mix_pool(cx, cm, env)}
        for name in which:
            with contextlib.ExitStack() as sc:
                cx.scope = sc
                fns[name]()
                cx.S.fence()
                if getattr(env, "after_mixer", None) is not None:
                    env.after_mixer(name)
            cx.scope = outer
    cx.scope = None


def build_mix(layer, which=("fox", "moba", "hgrn", "pool")):
    cx = Ctx()
    env = MixEnv()
    env.hnT = cx.din("hnT", [D_MODEL, SEQ], BF16)
    env.wmix = cx.din("wmix", [128, NCH, W_MIXCOLS], F32)
    small_d = cx.din("small", [128, N_SMALL], F32)
    env.poolw = cx.din("poolw", [128, 128], F32)
    consts = cx.din("consts", [128, 256], BF16)
    c32 = cx.din("c32", [128, 256], F32)
    cmask_d = cx.din("cmask", [128, 4, 512], BF16)
    env.perm = cx.din("perm", [128, 128], BF16)
    env.ropeC = cx.din("ropeC", [128, SEQ], F32)
    env.ropeS = cx.din("ropeS", [128, SEQ], F32)
    esel_d = cx.din("esel", [16, 16, 128], BF16)
    hmask_d = cx.din("hmask", [128, 128], BF16)
    env.poolM = cx.din("poolM", [128, 3, 128], BF16)
    env.yT = cx.dout("yT", [4, 128, SEQ], BF16)
    hview_ = env.hnT.rearrange("(c p) t -> p c t", p=128)
    env.hn_chunk = lambda n, j: hview_[:, :, n * 512 + j * 256:n * 512 + (j + 1) * 256]
    env.hn_rbufs = lambda n, j: []
    env.y_out = lambda bi, t0, ncols: env.yT[bi, :, t0:t0 + ncols]
    env.y_is_output = True
    cm = Common(cx, consts)
    add_eps(cx, cm)
    setup_common32(cx, cm, c32)
    mix_globals(cx, env, cmask_d, esel_d, hmask_d)
    phase_mix(cx, cm, env, layer, small_d, which)
    cx.S.emit()
    return cx.nc


CC_GROUPS = [[0, 1, 2, 3], [4, 5, 6, 7]]


def _core_quarter(cx, e):
    if getattr(cx, "_q", None) is None:
        cx._q = e.snap(e.partition_id() % 4, min_val=0, max_val=3)
    return cx._q


def build_fused(depth):
    cx = Ctx()
    nc = cx.nc
    S = cx.S
    consts = cx.din("consts", [128, 256], BF16)
    c32 = cx.din("c32", [128, 256], F32)
    xT = cx.din("xT", [D_MODEL, TOK], F32)
    memT = cx.din("memT", [D_MODEL, NMEM], F32)
    gpre = cx.din("gpre", [128, NCH], F32)
    cmask_d = cx.din("cmask", [128, 4, 512], BF16)
    esel_d = cx.din("esel", [16, 16, 128], BF16)
    hmask_d = cx.din("hmask", [128, 128], BF16)
    env = MixEnv()
    env.perm = cx.din("perm", [128, 128], BF16)
    env.ropeC = cx.din("ropeC", [128, SEQ], F32)
    env.ropeS = cx.din("ropeS", [128, SEQ], F32)
    env.poolM = cx.din("poolM", [128, 3, 128], BF16)
    wmix_d = [cx.din(f"wmix_{l}", [128, NCH, W_MIXCOLS], F32) for l in range(depth)]
    small_d = [cx.din(f"small_{l}", [128, N_SMALL], F32) for l in range(depth)]
    poolw_d = [cx.din(f"poolw_{l}", [128, 128], F32) for l in range(depth)]
    ios = []
    for l in range(depth):
        io = TokIO()
        tok_weight_inputs(cx, io, f"_{l}")
        io.memT = memT
        ios.append(io)
    outT = cx.dout("outT", [D_MODEL, TOK], F32)
    hn_own = [nc.dram_tensor(f"hn_own{k}", [D_MODEL, 256], BF16, kind="Internal").ap() for k in range(4)]
    hn_all = [nc.dram_tensor(f"hn_all{k}", [4 * D_MODEL, 256], BF16, kind="Internal").ap() for k in range(4)]
    hnp = [Buf(f"hnp{k}") for k in range(4)]
    hna = [Buf(f"hna{k}") for k in range(4)]

    def hn_store(piece, src, rb):
        cx.store(hn_own_v[piece], src, rb, is_output=False, wbufs=[hnp[piece]])

    def hn_gather(piece):
        S.collective(lambda e: e.collective_compute("AllGather", ALU.bypass, replica_groups=CC_GROUPS,
                                                    ins=[hn_own[piece].opt()], outs=[hn_all[piece].opt()]),
                     [hnp[piece]], [hna[piece]])
    y_own = [nc.dram_tensor(f"y_own{n}", [512, TOK], BF16, kind="Internal").ap() for n in range(4)]
    y_all = [nc.dram_tensor(f"y_all{n}", [4 * 512, TOK], BF16, kind="Internal").ap() for n in range(4)]
    h_res = nc.dram_tensor("h_res", [D_MODEL, TOK], F32, kind="Internal").ap()

    cm = Common(cx, consts)
    add_eps(cx, cm)
    setup_common32(cx, cm, c32)
    mix_globals(cx, env, cmask_d, esel_d, hmask_d)
    S.sp_init = lambda e: _core_quarter(cx, e)

    with contextlib.ExitStack() as sc:
        cx.scope = sc
        hn_own_v = [a.rearrange("(c p) t -> p c t", p=128) for a in hn_own]

        def pre_out(n, j, src, rb):
            hn_store(2 * n + j, src, rb)
            hn_gather(2 * n + j)
        phase_pre(cx, cm, xT, gpre, pre_out, False)
        S.fence()
    cx.scope = None

    hn_all_v = [a.rearrange("(r c p) t -> r p c t", r=4, p=128) for a in hn_all]
    y_own_v = [a.rearrange("(tq d) t -> tq d t", tq=4) for a in y_own]
    y_all_v = [a.rearrange("(hh tq d) t -> tq d hh t", hh=4, tq=4) for a in y_all]
    env.hn_chunk = lambda n, j: hn_all_v[2 * (n % 2) + j][n // 2]
    env.hn_rbufs = lambda n, j: [hna[2 * (n % 2) + j]]
    env.y_out = lambda bi, t0, ncols: y_own_v[bi][t0 // TOK][:, t0 % TOK:t0 % TOK + ncols]
    env.y_is_output = False
    h_res_v = h_res.rearrange("(c p) t -> p c t", p=128)
    x_v = xT.rearrange("(c p) t -> p c t", p=128)
    out_v = outT.rearrange("(c p) t -> p c t", p=128)

    for l in range(depth):
        last = l == depth - 1
        S.fence()
        env.wmix = wmix_d[l]
        env.poolw = poolw_d[l]
        def after_mixer(name):
            n = {"moba": 0, "fox": 1, "hgrn": 2, "pool": 3}[name]
            S.collective(lambda e, n=n: e.collective_compute("AllGather", ALU.bypass, replica_groups=CC_GROUPS,
                                                             ins=[y_own[n].opt()], outs=[y_all[n].opt()]), [], [])
        env.after_mixer = after_mixer
        phase_mix(cx, cm, env, l, small_d[l])
        S.fence()
        io = ios[l]
        src_v = x_v if l == 0 else h_res_v
        dst_v = out_v if last else h_res_v
        io.hT_in = lambda hf, c0, c1, src_v=src_v: src_v[:, c0:c1, hf * HALF:(hf + 1) * HALF]
        io.hn_load = lambda hf, j, dst, wb: cx.load(dst, hn_own_v[2 * hf + j], wb, rbufs=[hnp[2 * hf + j]])

        def ybr_load(dst, hf, wb):
            for n in range(4):
                def f(e, n=n):
                    q = _core_quarter(cx, e)
                    src = y_all_v[n][bass.ds(q, 1)][0][:, :, hf * HALF:(hf + 1) * HALF]
                    return e.dma_start(out=dst[:, n * 4:(n + 1) * 4, :], in_=src)
                S.dma("sp", f, [], list(wb))
        io.ybr_load = ybr_load
        io.hT_out = lambda hf, c0, c1, dst_v=dst_v: dst_v[:, c0:c1, hf * HALF:(hf + 1) * HALF]
        pending = []

        def hn_store_tok(hf, j, src, rb):
            hn_store(2 * hf + j, src, rb)
            if hf == 0:
                pending.append(2 * hf + j)
            else:
                hn_gather(2 * hf + j)

        def after_gates(hf):
            while pending:
                hn_gather(pending.pop(0))
        io.hn_store = hn_store_tok
        io.after_gates = after_gates
        io.h_is_output = last
        io.hn_is_output = False
        io.write_hn_when_last = False
        with contextlib.ExitStack() as sc:
            cx.scope = sc
            phase_tok(cx, cm, io, l, last)
            S.fence()
        cx.scope = None
    S.emit()
    return nc


IN_OFF = {"mq": 0, "mk": 512, "mv": 1024, "fq": 1536, "fk": 2048, "fv": 2560, "ff": 3072,
          "hq": 3076, "hf": 3588, "hi": 4100, "hg": 4612, "pin": 5124, "gl": 5636}
_HC = {}


def host_consts():
    if _HC:
        return _HC
    f32 = np.float32
    c = np.zeros((128, 256), f32)
    c[:, :128] = np.eye(128)
    c[:, 128:] = 1
    _HC["c32"] = c
    _HC["consts"] = c.astype(NPBF)
    k = np.arange(128)[:, None, None]
    a = np.arange(4)[None, :, None]
    q = np.arange(512)[None, None, :]
    _HC["cmask"] = np.where(q >= 128 * a + k, 0.0, NEG).astype(NPBF)
    perm = np.zeros((128, 128), f32)
    d = np.arange(128)
    perm[(d + 64) % 128, d] = 1
    _HC["perm"] = perm.astype(NPBF)
    inv_freq = (f32(10000.0) ** (-np.arange(64, dtype=f32) * f32(2.0) / f32(128))).astype(f32)
    ang = (np.arange(SEQ, dtype=f32)[None, :] * inv_freq[:, None]).astype(f32)
    cos, sin = np.cos(ang).astype(f32), np.sin(ang).astype(f32)
    _HC["ropeC"] = np.ascontiguousarray(np.concatenate([cos, cos], 0))
    _HC["ropeS"] = np.ascontiguousarray(np.concatenate([-sin, sin], 0))
    es = np.zeros((16, 16, 128), f32)
    for j in range(16):
        es[j, j, :] = 1
    _HC["esel"] = es.astype(NPBF)
    s = np.arange(128)[:, None]
    t = np.arange(128)[None, :]
    _HC["hmask"] = ((s // 64 == t // 64) & (s <= t)).astype(f32).astype(NPBF)
    for h, w in enumerate(POOL_WINDOWS):
        M = np.zeros((128, 3, 128), f32)
        eye = (s == t).astype(f32)
        band = ((s <= t) & (s > t - w)).astype(f32)
        M[:, 0, :] = band / np.minimum(w, t + 1).astype(f32) - eye
        M[:, 1, :] = band / f32(w) - eye
        M[:, 2, :] = ((s - 128) > (t - w)).astype(f32) / f32(w)
        _HC[f"poolM{h}"] = M.astype(NPBF)
    return _HC


def fm_layout(w):
    K, N = w.shape
    return np.ascontiguousarray(w.reshape(K // 128, 128, N).transpose(1, 0, 2))


def mix_inputs(inp, l, c, hnT_b):
    b, h = c // 4, c % 4
    hc = host_consts()
    w_in = inp["w_in"][l]
    hs = slice(h * 128, (h + 1) * 128)

    def col(name):
        return w_in[:, IN_OFF[name] + h * 128: IN_OFF[name] + (h + 1) * 128]
    wm = np.concatenate([col(n) for n in ("mq", "mk", "mv", "fq", "fk", "fv", "hq", "hf", "hg", "hi", "pin")], axis=1)
    small = np.zeros((128, N_SMALL), np.float32)
    small[:, SM_WFF:SM_WFF + NCH] = w_in[:, IN_OFF["ff"] + h].reshape(NCH, 128).T
    small[:, SM_G0] = inp["hgrn_lb_logits"][0, hs]
    small[:, SM_G1] = inp["hgrn_lb_logits"][1, hs]
    small[:, SM_HNORM] = inp["hgrn_out_norm"][l, hs]
    small[:, SM_PSCALE] = inp["pool_scale"][l, hs]
    small[:, SM_FB] = inp["fox_f_bias"][l, h]
    return {"hnT": hnT_b, "wmix": fm_layout(wm), "small": small,
            "poolw": np.ascontiguousarray(inp["pool_w"][l, h]),
            "consts": hc["consts"], "c32": hc["c32"], "cmask": hc["cmask"], "perm": hc["perm"],
            "ropeC": hc["ropeC"], "ropeS": hc["ropeS"], "esel": hc["esel"], "hmask": hc["hmask"],
            "poolM": hc[f"poolM{h}"]}


GV_MIXPOST, GV_XAPRE, GV_XAMEM, GV_XAPOST, GV_MLPPRE, GV_MLPPOST, GV_NEXT = range(7)
HALF = 512
NMEM = 256
RING_N = 6
RING_ELEMS = 4096


class Ring:
    def __init__(self, cx, n=RING_N, elems=RING_ELEMS):
        self.cx = cx
        self.bufs = [cx.sb([128, elems], BF16, name=f"ring{i}") for i in range(n)]
        self.i = 0

    def load(self, src_ap, a, b):
        t = self.bufs[self.i % len(self.bufs)]
        self.i += 1
        view = t[:, 0:a * b].rearrange("p (a b) -> p a b", b=b)
        self.cx.load(view, src_ap, t.all, q="pool")
        return view, t.all


class TokIO:
    pass


def tok_io_external(cx):
    io = TokIO()
    hT_d = cx.din("hT", [D_MODEL, TOK], F32)
    hnT_d = cx.din("hnT", [D_MODEL, TOK], BF16)
    ybr_d = cx.din("ybr", [D_MODEL, TOK], BF16)
    tok_weight_inputs(cx, io, "")
    io.memT = cx.din("memT", [D_MODEL, NMEM], F32)
    hT_o = cx.dout("hT_out", [D_MODEL, TOK], F32)
    hn_o = cx.dout("hn_next", [D_MODEL, TOK], BF16)
    hview = hT_d.rearrange("(c p) t -> p c t", p=128)
    hnview = hnT_d.rearrange("(c p) t -> p c t", p=128)
    ybview = ybr_d.rearrange("(c p) t -> p c t", p=128)
    hoview = hT_o.rearrange("(c p) t -> p c t", p=128)
    hnoview = hn_o.rearrange("(c p) t -> p c t", p=128)
    io.hT_in = lambda hf, c0, c1: hview[:, c0:c1, hf * HALF:(hf + 1) * HALF]
    io.hn_load = lambda hf, j, dst, wb: cx.load(dst, hnview[:, :, hf * HALF + j * 256:hf * HALF + (j + 1) * 256], wb)
    io.ybr_load = lambda dst, hf, wb: cx.load(dst, ybview[:, :, hf * HALF:(hf + 1) * HALF], wb)
    io.hT_out = lambda hf, c0, c1: hoview[:, c0:c1, hf * HALF:(hf + 1) * HALF]
    io.hn_store = lambda hf, j, src, rb: cx.store(hnoview[:, :, hf * HALF + j * 256:hf * HALF + (j + 1) * 256], src, rb)
    io.h_is_output = True
    io.hn_is_output = True
    io.write_hn_when_last = True
    return io


def tok_weight_inputs(cx, io, sfx):
    io.wg_d = cx.din("wg" + sfx, [64, 128, NCH, 128], F32)
    io.wb_d = cx.din("wb" + sfx, [64, 128, 4, 128], F32)
    io.wo_d = cx.din("wo" + sfx, [16, 128, NCH, 128], F32)
    io.xq_d = cx.din("xq" + sfx, [4, 128, NCH, 128], F32)
    io.xk_d = cx.din("xk" + sfx, [4, 128, NCH, 128], F32)
    io.xv_d = cx.din("xv" + sfx, [2, 128, 8, 512], F32)
    io.xo_d = cx.din("xo" + sfx, [16, 128, 4, 128], F32)
    io.wup_d = cx.din("wup" + sfx, [64, 128, NCH, 128], F32)
    io.wdn_d = cx.din("wdn" + sfx, [32, 128, 32, 128], F32)
    io.gv_d = cx.din("gv" + sfx, [128, 7 * NCH], F32)


def build_tok(layer, last):
    cx = Ctx()
    io = tok_io_external(cx)
    consts = cx.din("consts", [128, 256], BF16)
    cm = Common(cx, consts)
    add_eps(cx, cm)
    phase_tok(cx, cm, io, layer, last)
    cx.S.emit()
    return cx.nc


def phase_tok(cx, cm, io, layer, last):
    S = cx.S
    wg_d, wb_d, wo_d, xq_d, xk_d, xv_d, xo_d, wup_d, wdn_d = (io.wg_d, io.wb_d, io.wo_d, io.xq_d, io.xk_d, io.xv_d,
                                                              io.xo_d, io.wup_d, io.wdn_d)
    memT_d, gv_d = io.memT, io.gv_d
    gv = cx.sb([128, 7 * NCH], F32, name="gv")
    cx.load(gv[:, :], gv_d, gv.all)

    def gcol(which, c):
        return gv[:, which * NCH + c: which * NCH + c + 1]

    ring = Ring(cx)
    hT = cx.sb([128, NCH, HALF], F32, nsub=NCH, name="hT")
    hnb = cx.sb([128, NCH, HALF], BF16, name="hnb")
    RA = cx.sb([128, 32, HALF], BF16, name="RA")
    ybr = RA[:, 0:NCH, :]
    sT = RA[:, NCH:2 * NCH, :]
    aT = cx.sb([128, NCH, HALF], F32, nsub=NCH, name="aT")
    sqp = [cx.sb([128, HALF], BF16, name=f"sq{i}") for i in range(3)]
    rstd = cx.sb([128, HALF], F32, name="rstd")
    tmpA = [cx.sb([128, HALF], F32, name=f"tmpA{i}") for i in range(2)]
    tmpB = [cx.sb([128, HALF], F32, name=f"tmpB{i}") for i in range(2)]
    acc = cx.sb([128, HALF], F32, name="acc")
    kx = cx.sb([128, 4, NMEM], BF16, name="kx")
    vx = cx.sb([128, 2, 512], BF16, name="vx")
    qx = cx.sb([128, 4, HALF], BF16, name="qx")
    ox = cx.sb([128, 4, HALF], BF16, name="ox")
    pTs = [cx.sb([128, HALF], BF16, name=f"xpT{i}") for i in range(2)]
    rc = cx.sb([128, HALF], F32, name="xrc")
    B = cm.banks

    def norm_add(gw):
        rms_stats(cx, cm, lambda c: aT[:, c, :], lambda c: [aT.b[c]], NCH, HALF, sqp, B[6], rstd, D_MODEL)
        for c in range(NCH):
            t = tmpA[c % 2]
            cx.stt(t[:, :], aT[:, c, :], gcol(gw, c), rstd[:, :], ALU.mult, ALU.mult,
                   [aT.b[c]] + gv.all + rstd.all, t.all)
            cx.tt(hT[:, c, :], hT[:, c, :], t[:, :], ALU.add, [hT.b[c]] + t.all, [hT.b[c]])

    def norm_to(gw, dst, dst_bufs):
        rms_stats(cx, cm, lambda c: hT[:, c, :], lambda c: [hT.b[c]], NCH, HALF, sqp, B[6], rstd, D_MODEL)
        for c in range(NCH):
            cx.stt(dst[:, c, :], hT[:, c, :], gcol(gw, c), rstd[:, :], ALU.mult, ALU.mult,
                   [hT.b[c]] + gv.all + rstd.all, dst_bufs)

    memf = aT
    cx.load(memf[:, :, 0:NMEM], memT_d.rearrange("(c p) t -> p c t", p=128), memf.all)
    rms_stats(cx, cm, lambda c: memf[:, c, 0:NMEM], lambda c: [memf.b[c]], NCH, NMEM, sqp, B[6], rstd, D_MODEL)
    memn = hnb
    for c in range(NCH):
        cx.stt(memn[:, c, 0:NMEM], memf[:, c, 0:NMEM], gcol(GV_XAMEM, c), rstd[:, 0:NMEM], ALU.mult, ALU.mult,
               [memf.b[c]] + gv.all + rstd.all, memn.all)
    for hd in range(4):
        wv_, wb_ = ring.load(xk_d[hd], NCH, 128)
        bk = B[hd % 2]
        for c in range(NCH):
            cx.mm(bk[:, 0:NMEM], wv_[:, c, :], memn[:, c, 0:NMEM], c == 0, c == NCH - 1, wb_ + memn.all, bk.all)
        cx.copy(kx[:, hd, :], bk[:, 0:NMEM], bk.all, kx.all)
    wv0, wb0 = ring.load(xv_d[0], 8, 512)
    wv1, wb1 = ring.load(xv_d[1], 8, 512)
    for mt in range(2):
        bk = B[2 + mt]
        for c in range(NCH):
            wv_, wb_ = (wv0, wb0) if c < 8 else (wv1, wb1)
            cx.mm(bk[:, :], memn[:, c, mt * 128:(mt + 1) * 128], wv_[:, c % 8, :], c == 0, c == NCH - 1,
                  wb_ + memn.all, bk.all)
        cx.copy(vx[:, mt, :], bk[:, :], bk.all, vx.all, eng="act")

    for hf in range(TOK // HALF):
        tsl = slice(hf * HALF, (hf + 1) * HALF)
        for c in range(0, NCH, 4):
            cx.load(hT[:, c:c + 4, :], io.hT_in(hf, c, c + 4), hT.b[c:c + 4])
        for j in range(2):
            io.hn_load(hf, j, hnb[:, :, j * 256:(j + 1) * 256], hnb.all)
        io.ybr_load(ybr, hf, RA.all)
        it = 0
        for j in range(NCH):
            for n in range(4):
                wg_, wgb = ring.load(wg_d[j * 4 + n], NCH, 128)
                wb_, wbb = ring.load(wb_d[j * 4 + n], 4, 128)
                bg, bp = B[it % 2], B[2 + it % 2]
                it += 1
                for c in range(NCH):
                    cx.mm(bg[:, :], wg_[:, c, :], hnb[:, c, :], c == 0, c == NCH - 1, wgb + hnb.all, bg.all)
                for hh in range(4):
                    cx.mm(bp[:, :], wb_[:, hh, :], ybr[:, n * 4 + hh, :], hh == 0, hh == 3, wbb + RA.all, bp.all)
                sg_ = tmpA[it % 2]
                cx.activation(sg_[:, :], bg[:, :], AF.Sigmoid, bg.all, sg_.all)
                if n == 0:
                    cx.tt(acc[:, :], bp[:, :], sg_[:, :], ALU.mult, bp.all + sg_.all, acc.all)
                else:
                    t2 = tmpB[it % 2]
                    cx.tt(t2[:, :], bp[:, :], sg_[:, :], ALU.mult, bp.all + sg_.all, t2.all)
                    if n < 3:
                        cx.tt(acc[:, :], acc[:, :], t2[:, :], ALU.add, acc.all + t2.all, acc.all)
                    else:
                        cx.tt(sT[:, j, :], acc[:, :], t2[:, :], ALU.add, acc.all + t2.all, RA.all)
        if getattr(io, "after_gates", None) is not None:
            io.after_gates(hf)
        for j in range(NCH):
            w_, wbf = ring.load(wo_d[j], NCH, 128)
            bk = B[j % 2]
            for c in range(NCH):
                cx.mm(bk[:, :], w_[:, c, :], sT[:, c, :], c == 0, c == NCH - 1, wbf + RA.all, bk.all)
            cx.copy(aT[:, j, :], bk[:, :], bk.all, [aT.b[j]], eng="act" if j % 2 else "dve")
        norm_add(GV_MIXPOST)
        norm_to(GV_XAPRE, hnb, hnb.all)
        for hd in range(4):
            w_, wbf = ring.load(xq_d[hd], NCH, 128)
            bk = B[hd % 2]
            for c in range(NCH):
                cx.mm(bk[:, :], w_[:, c, :], hnb[:, c, :], c == 0, c == NCH - 1, wbf + hnb.all, bk.all)
            S.act(lambda e, hd=hd, bk=bk: e.mul(out=qx[:, hd, :], in_=bk[:, :], mul=SCALE), bk.all, qx.all)
        for hd in range(4):
            ob, db = B[2 + hd % 2], B[4 + hd % 2]
            for mt in range(2):
                sb_ = B[mt]
                cx.mm(sb_[:, :], kx[:, hd, mt * 128:(mt + 1) * 128], qx[:, hd, :], True, True, kx.all + qx.all, sb_.all)
                pT = pTs[mt]
                cx.activation(pT[:, :], sb_[:, :], AF.Exp, sb_.all, pT.all)
                cx.mm(ob[:, :], vx[:, mt, hd * 128:(hd + 1) * 128], pT[:, :], mt == 0, mt == 1, vx.all + pT.all, ob.all)
                cx.mm(db[:, :], cm.ones, pT[:, :], mt == 0, mt == 1, pT.all, db.all)
            S.dve(lambda e, db=db: e.reciprocal(out=rc[:, :], in_=db[:, :]), db.all, rc.all)
            cx.tt(ox[:, hd, :], ob[:, :], rc[:, :], ALU.mult, ob.all + rc.all, ox.all)
        for j in range(NCH):
            w_, wbf = ring.load(xo_d[j], 4, 128)
            bk = B[j % 2]
            for hd in range(4):
                cx.mm(bk[:, :], w_[:, hd, :], ox[:, hd, :], hd == 0, hd == 3, wbf + ox.all, bk.all)
            cx.copy(aT[:, j, :], bk[:, :], bk.all, [aT.b[j]], eng="act" if j % 2 else "dve")
        norm_add(GV_XAPOST)
        norm_to(GV_MLPPRE, hnb, hnb.all)
        uT = RA
        for fh in range(2):
            for fb in range(32):
                w_, wbf = ring.load(wup_d[fh * 32 + fb], NCH, 128)
                bk = B[fb % 4]
                for c in range(NCH):
                    cx.mm(bk[:, :], w_[:, c, :], hnb[:, c, :], c == 0, c == NCH - 1, wbf + hnb.all, bk.all)
                r_ = tmpA[fb % 2]
                cx.activation(r_[:, :], bk[:, :], AF.Relu, bk.all, r_.all)
                cx.tt(uT[:, fb, :], r_[:, :], r_[:, :], ALU.mult, r_.all, RA.all, eng="dve")
            for j in range(NCH):
                w_, wbf = ring.load(wdn_d[j * 2 + fh], 32, 128)
                bk = B[4 + j % 2]
                for fb in range(32):
                    cx.mm(bk[:, :], w_[:, fb, :], uT[:, fb, :], fb == 0, fb == 31, wbf + RA.all, bk.all)
                if fh == 0:
                    cx.copy(aT[:, j, :], bk[:, :], bk.all, [aT.b[j]], eng="act")
                else:
                    cx.tt(aT[:, j, :], bk[:, :], aT[:, j, :], ALU.add, bk.all + [aT.b[j]], [aT.b[j]])
        norm_add(GV_MLPPOST)
        for c in range(0, NCH, 4):
            cx.store(io.hT_out(hf, c, c + 4), hT[:, c:c + 4, :], hT.b[c:c + 4], is_output=io.h_is_output)
        if not last:
            norm_to(GV_NEXT, hnb, hnb.all)
            for j in range(2):
                io.hn_store(hf, j, hnb[:, :, j * 256:(j + 1) * 256], hnb.all)
        elif io.write_hn_when_last:
            for j in range(2):
                io.hn_store(hf, j, hnb[:, :, j * 256:(j + 1) * 256], hnb.all)


def tok_weights(inp, l):
    f32 = np.float32
    wg = inp["w_in"][l][:, IN_OFF["gl"]:]
    wg = wg.reshape(NCH, 128, 4, NCH, 128).transpose(3, 2, 1, 0, 4)
    wg = np.ascontiguousarray(wg).reshape(64, 128, NCH, 128)
    wb = inp["w_branch"][l].reshape(4, 4, 128, NCH, 128).transpose(3, 0, 2, 1, 4)
    wb = np.ascontiguousarray(wb).reshape(64, 128, 4, 128)

    def tiles(w, kc):
        K_, N_ = w.shape
        return np.ascontiguousarray(w.reshape(kc, 128, N_ // 128, 128).transpose(2, 1, 0, 3))
    wo = tiles(inp["w_mix_out"][l], NCH)
    xq = tiles(inp["xa_wq"][l], NCH)
    xk = tiles(inp["xa_wkv"][l][:, 0:512], NCH)
    wv = inp["xa_wkv"][l][:, 512:1024].reshape(2, 8, 128, 512).transpose(0, 2, 1, 3)
    xv = np.ascontiguousarray(wv)
    xo = tiles(inp["xa_wo"][l], 4)
    wup = tiles(inp["mlp_w_up"][l], NCH)
    wd = inp["mlp_w_down"][l].reshape(2, 32, 128, NCH, 128).transpose(3, 0, 2, 1, 4)
    wdn = np.ascontiguousarray(wd).reshape(32, 128, 32, 128)
    gv = np.zeros((128, 7 * NCH), f32)
    names = ["mix_norm_post", "xa_norm_pre", "xa_norm_mem", "xa_norm_post", "mlp_norm_pre", "mlp_norm_post"]
    for i, nme in enumerate(names):
        gv[:, i * NCH:(i + 1) * NCH] = inp[nme][l].reshape(NCH, 128).T
    if l + 1 < inp["mix_norm_pre"].shape[0]:
        gv[:, GV_NEXT * NCH:(GV_NEXT + 1) * NCH] = inp["mix_norm_pre"][l + 1].reshape(NCH, 128).T
    return {"wg": wg, "wb": wb, "wo": wo, "xq": xq, "xk": xk, "xv": xv, "xo": xo, "wup": wup, "wdn": wdn,
            "gv": gv, "consts": host_consts()["consts"]}


_PROGS = {}


def _prog(key, builder):
    if key not in _PROGS:
        _PROGS[key] = builder()
    return _PROGS[key]


def kernel_unfused(**inp):
    inp = {k: np.asarray(v) for k, v in inp.items()}
    depth = inp["w_in"].shape[0]
    cores = list(range(8))
    hc = host_consts()
    x = inp["x"]
    tsl = [slice((c % 4) * TOK, (c % 4 + 1) * TOK) for c in cores]
    hT = [np.ascontiguousarray(x[c // 4, tsl[c]].T) for c in cores]
    memT = [np.ascontiguousarray(inp["mem"][b].T) for b in range(BATCH)]
    g0 = np.ascontiguousarray(inp["mix_norm_pre"][0].reshape(NCH, 128).T)
    res = run_bass_kernel_spmd(_prog("pre", build_pre),
                               [{"xT": hT[c], "gpre": g0, "consts": hc["consts"]} for c in cores], core_ids=cores)
    hn = [np.asarray(res.results[c]["hn_out"]) for c in cores]
    for l in range(depth):
        hn_full = [np.ascontiguousarray(np.concatenate([hn[b * 4 + q] for q in range(4)], axis=1)) for b in range(BATCH)]
        res = run_bass_kernel_spmd(_prog(("mix", l), lambda: build_mix(l)),
                                   [mix_inputs(inp, l, c, hn_full[c // 4]) for c in cores], core_ids=cores)
        yT = [np.asarray(res.results[c]["yT"]) for c in cores]
        tw = tok_weights(inp, l)
        maps = []
        for c in cores:
            b = c // 4
            yb = np.stack([yT[b * 4 + h][:, :, tsl[c]] for h in range(4)], axis=1)
            m = dict(tw)
            m["hT"] = hT[c]
            m["hnT"] = hn[c]
            m["ybr"] = np.ascontiguousarray(yb.reshape(D_MODEL, TOK))
            m["memT"] = memT[b]
            maps.append(m)
        res = run_bass_kernel_spmd(_prog(("tok", l), lambda: build_tok(l, l == depth - 1)), maps, core_ids=cores)
        hT = [np.asarray(res.results[c]["hT_out"]) for c in cores]
        hn = [np.asarray(res.results[c]["hn_next"]) for c in cores]
    out = np.empty_like(x)
    for c in cores:
        out[c // 4, tsl[c]] = hT[c].T
    return out


def fused_inputs(inp, c):
    depth = inp["w_in"].shape[0]
    b, q = c // 4, c % 4
    hc = host_consts()
    m = {"consts": hc["consts"], "c32": hc["c32"], "cmask": hc["cmask"], "esel": hc["esel"], "hmask": hc["hmask"],
         "perm": hc["perm"], "ropeC": hc["ropeC"], "ropeS": hc["ropeS"], "poolM": hc[f"poolM{q}"],
         "xT": np.ascontiguousarray(inp["x"][b, q * TOK:(q + 1) * TOK].T),
         "memT": np.ascontiguousarray(inp["mem"][b].T),
         "gpre": np.ascontiguousarray(inp["mix_norm_pre"][0].reshape(NCH, 128).T)}
    for l in range(depth):
        mi = mix_inputs(inp, l, c, None)
        m[f"wmix_{l}"] = mi["wmix"]
        m[f"small_{l}"] = mi["small"]
        m[f"poolw_{l}"] = mi["poolw"]
    return m


def kernel(**inp):
    inp = {k: np.asarray(v) for k, v in inp.items()}
    depth = inp["w_in"].shape[0]
    cores = list(range(8))
    nc = _prog(("fused", depth), lambda: build_fused(depth))
    tws = []
    for l in range(depth):
        tw = tok_weights(inp, l)
        tw.pop("consts")
        tws.append({f"{k}_{l}": v for k, v in tw.items()})
    maps = []
    for c in cores:
        m = fused_inputs(inp, c)
        for tw in tws:
            m.update(tw)
        maps.append(m)
    res = run_bass_kernel_spmd(nc, maps, core_ids=cores)
    x = inp["x"]
    out = np.empty_like(x)
    for c in cores:
        out[c // 4, (c % 4) * TOK:(c % 4 + 1) * TOK] = np.asarray(res.results[c]["outT"]).T
    return out
```

```python
import contextlib
import numpy as np
import ml_dtypes
import concourse.bass as bass
import concourse.mybir as mybir
from concourse.bass_utils import run_bass_kernel_spmd

F32 = mybir.dt.float32
BF16 = mybir.dt.bfloat16
AF = mybir.ActivationFunctionType
ALU = mybir.AluOpType
AX = mybir.AxisListType
NPBF = ml_dtypes.bfloat16

D_MODEL = 2048
NCH = 16
BATCH = 2
SEQ = 4096
TOK = 1024
HD = 128
NEG = -30000.0
EPS = 1e-6
SCALE = HD ** -0.5
POOL_WINDOWS = (2, 4, 8, 16)
DEBUG_STOP = 0
ROPE_ADD_ENG = "dve"

COMPUTE = ("pe", "act", "dve", "pool")
N_DMA_SEMS = 12


class Buf:
    __slots__ = ("name", "w", "r", "excl")

    def __init__(self, name="", excl=False):
        self.name = name
        self.w = None
        self.r = []
        self.excl = excl


class Op:
    __slots__ = ("eng", "fn", "waits", "sig", "idx", "dma", "dma_ev")

    def __init__(self, eng, fn, sig, dma):
        self.eng = eng
        self.fn = fn
        self.waits = []
        self.sig = sig
        self.dma = dma
        self.dma_ev = None


class Sched:
    def __init__(self, nc):
        self.nc = nc
        self.ops = {e: [] for e in COMPUTE + ("sp",)}
        self.dma_count = {e: 0 for e in ("sp", "act", "pool")}
        self.out_events = []
        self.fence_waits = {e: [] for e in COMPUTE + ("sp",)}
        self.n_cc = 0

    def collective(self, fn, reads, writes):
        op = Op("pool", fn, False, False)
        op.dma = "cc"
        lst = self.ops["pool"]
        op.idx = len(lst)
        lst.append(op)
        ev = ("x", self.n_cc)
        op.dma_ev = ev
        self.n_cc += 1
        if self.fence_waits["pool"]:
            op.waits.extend(self.fence_waits["pool"])
            self.fence_waits["pool"] = []
        for b in reads:
            if b.w is not None:
                op.waits.append(b.w)
        for b in writes:
            if b.w is not None:
                op.waits.append(b.w)
            op.waits.extend(b.r)
        for d in op.waits:
            if d[0] == "c":
                self.ops[d[1]][d[2]].sig = True
        for b in reads:
            b.r.append(ev)
        for b in writes:
            b.w = ev
            b.r = []
        return ev

    def fence(self):
        evs = []
        for e in COMPUTE + ("sp",):
            lst = self.ops[e]
            for op in reversed(lst):
                if not op.dma:
                    evs.append(("c", e, op.idx))
                    op.sig = True
                    break
        for q, n in self.dma_count.items():
            for i in range(max(0, n - N_DMA_SEMS), n):
                evs.append(("d", q, i))
        for i in range(self.n_cc):
            evs.append(("x", i))
        for e in self.fence_waits:
            self.fence_waits[e] = list(evs)

    def _add(self, eng, fn, reads, writes, sig=False, dma=False):
        op = Op(eng, fn, sig, dma)
        lst = self.ops[eng]
        op.idx = len(lst)
        lst.append(op)
        if dma:
            n = self.dma_count[eng]
            self.dma_count[eng] += 1
            ev = ("d", eng, n)
            op.dma_ev = ev
            if n >= N_DMA_SEMS:
                op.waits.append(("d", eng, n - N_DMA_SEMS))
        else:
            ev = ("c", eng, op.idx)
        if self.fence_waits[eng]:
            op.waits.extend(d for d in self.fence_waits[eng] if d != ev)
            self.fence_waits[eng] = []
        excl_reads = [b for b in reads if b.excl]
        if excl_reads:
            reads = [b for b in reads if not b.excl]
            writes = list(writes) + excl_reads
        for b in reads:
            if b.w is not None and b.w != ev:
                op.waits.append(b.w)
        for b in writes:
            if b.w is not None and b.w != ev:
                op.waits.append(b.w)
            for d in b.r:
                if d != ev:
                    op.waits.append(d)
        if eng == "pe" and not dma:
            op.waits = [d for d in op.waits if not (d[0] == "c" and d[1] == "pe")]
        for d in op.waits:
            if d[0] == "c":
                self.ops[d[1]][d[2]].sig = True
        for b in reads:
            if not dma:
                b.r = [d for d in b.r if not (d[0] == "c" and d[1] == eng)]
            b.r.append(ev)
        for b in writes:
            b.w = ev
            b.r = []
        return ev

    def pe(self, fn, reads, writes, sig=False):
        return self._add("pe", fn, reads, writes, sig)

    def act(self, fn, reads, writes):
        return self._add("act", fn, reads, writes)

    def dve(self, fn, reads, writes):
        return self._add("dve", fn, reads, writes)

    def pool(self, fn, reads, writes):
        return self._add("pool", fn, reads, writes)

    def dma(self, q, fn, reads, writes, is_output=False):
        ev = self._add(q, fn, reads, writes, dma=True)
        if is_output:
            self.out_events.append(ev)
        return ev

    def emit(self):
        nc = self.nc
        with contextlib.ExitStack() as st:
            csem = {e: st.enter_context(nc.semaphore("c_" + e)) for e in COMPUTE}
            dsem = {q: [st.enter_context(nc.semaphore(f"d_{q}{i}")) for i in range(N_DMA_SEMS)]
                    for q in ("sp", "act", "pool")}
            xsem = [st.enter_context(nc.semaphore(f"x_{i}")) for i in range(self.n_cc)]
            block = st.enter_context(nc.Block())
            sigcount = {}
            for e in COMPUTE:
                lst = self.ops[e]
                last = None
                for op in lst:
                    if not op.dma:
                        last = op
                if last is not None:
                    last.sig = True
                c = 0
                arr = []
                for op in lst:
                    if (not op.dma) and op.sig:
                        c += 1
                    arr.append(c)
                sigcount[e] = arr
            self.stats = {e: (len(self.ops[e]), sigcount[e][-1] if sigcount[e] else 0) for e in COMPUTE}
            self.stats["dma"] = dict(self.dma_count)

            def resolve(ev):
                if ev[0] == "c":
                    _, e, i = ev
                    op = self.ops[e][i]
                    v = sigcount[e][i]
                    assert op.sig
                    return ("c", e), csem[e], v
                if ev[0] == "x":
                    return ev, xsem[ev[1]], 1
                _, q, n = ev
                return ("d", q, n % N_DMA_SEMS), dsem[q][n % N_DMA_SEMS], 16 * (n // N_DMA_SEMS + 1)

            def run_engine(ename, eng):
                known = {}
                for op in self.ops[ename]:
                    need = {}
                    for ev in op.waits:
                        key, sem, v = resolve(ev)
                        if known.get(key, 0) >= v:
                            continue
                        if need.get(key, (None, 0))[1] < v:
                            need[key] = (sem, v)
                    for key, (sem, v) in need.items():
                        eng.wait_ge(sem, v)
                        known[key] = v
                    ins = op.fn(eng)
                    if op.dma == "cc":
                        ins.then_inc(xsem[op.dma_ev[1]])
                    elif op.dma:
                        _, q, n = op.dma_ev
                        ins.then_inc(dsem[q][n % N_DMA_SEMS], 16)
                    elif op.sig:
                        ins.then_inc(csem[ename], 1)
                if ename == "sp":
                    for ev in self.out_events:
                        key, sem, v = resolve(ev)
                        if known.get(key, 0) >= v:
                            continue
                        eng.wait_ge(sem, v)
                        known[key] = v

            @block.sync
            def _(eng):
                if getattr(self, "sp_init", None) is not None:
                    self.sp_init(eng)
                run_engine("sp", eng)

            @block.tensor
            def _(eng):
                run_engine("pe", eng)

            @block.scalar
            def _(eng):
                run_engine("act", eng)

            @block.vector
            def _(eng):
                run_engine("dve", eng)

            @block.gpsimd
            def _(eng):
                run_engine("pool", eng)


class T:
    def __init__(self, t, nsub=1, name="", psum=False):
        self.t = t
        if psum:
            self.b = [Buf(name, excl=True)] * nsub
        else:
            self.b = [Buf(f"{name}{i}") for i in range(nsub)]

    def __getitem__(self, k):
        return self.t[k]

    @property
    def all(self):
        return list(self.b)


class Ctx:
    def __init__(self):
        self.nc = bass.Bass("TRN2", target_bir_lowering=False)
        self.S = Sched(self.nc)
        self._n = 0
        self.scope = None

    def sb(self, shape, dt, nsub=1, name=None):
        self._n += 1
        name = (name or "sb") + f"_{self._n}"
        if self.scope is not None:
            return T(self.scope.enter_context(self.nc.sbuf_tensor(name, list(shape), dt)), nsub, name)
        return T(self.nc.alloc_sbuf_tensor(name, list(shape), dt), nsub, name)

    def ps(self, shape, dt=F32, nsub=1, name=None):
        self._n += 1
        name = name or f"ps{self._n}"
        return T(self.nc.alloc_psum_tensor(name, list(shape), dt), nsub, name, psum=True)

    def din(self, name, shape, dt):
        return self.nc.dram_tensor(name, list(shape), dt, kind="ExternalInput").ap()

    def dout(self, name, shape, dt):
        return self.nc.dram_tensor(name, list(shape), dt, kind="ExternalOutput").ap()

    def load(self, dst_ap, src_ap, wbufs, q="sp", rbufs=()):
        return self.S.dma(q, lambda e: e.dma_start(out=dst_ap, in_=src_ap), list(rbufs), list(wbufs))

    def store(self, dst_ap, src_ap, rbufs, q="sp", is_output=True, wbufs=()):
        return self.S.dma(q, lambda e: e.dma_start(out=dst_ap, in_=src_ap), list(rbufs), list(wbufs), is_output=is_output)

    def mm(self, out_ap, lhsT, rhs, start, stop, reads, writes, sig=None):
        if sig is None:
            sig = False
        return self.S.pe(lambda e: e.matmul(out_ap, lhsT, rhs, start=start, stop=stop), reads, writes, sig=sig)

    def transpose(self, out_ap, in_ap, ident_ap, reads, writes):
        return self.S.pe(lambda e: e.transpose(out_ap, in_ap, ident_ap), reads, writes)

    def activation(self, out_ap, in_ap, func, reads, writes, bias=None, scale=None):
        kw = {}
        if bias is not None:
            kw["bias"] = bias
        if scale is not None:
            kw["scale"] = scale
        return self.S.act(lambda e: e.activation(out=out_ap, in_=in_ap, func=func, **kw), reads, writes)

    def tt(self, out_ap, in0, in1, op, reads, writes, eng="dve"):
        f = lambda e: e.tensor_tensor(out=out_ap, in0=in0, in1=in1, op=op)
        return (self.S.dve if eng == "dve" else self.S.pool)(f, reads, writes)

    def ts(self, out_ap, in0, s1, s2, op0, op1, reads, writes, eng="dve"):
        if op1 is None:
            f = lambda e: e.tensor_scalar(out=out_ap, in0=in0, scalar1=s1, scalar2=None, op0=op0)
        else:
            f = lambda e: e.tensor_scalar(out=out_ap, in0=in0, scalar1=s1, scalar2=s2, op0=op0, op1=op1)
        return (self.S.dve if eng == "dve" else self.S.pool)(f, reads, writes)

    def stt(self, out_ap, in0, scalar, in1, op0, op1, reads, writes):
        return self.S.dve(lambda e: e.scalar_tensor_tensor(out=out_ap, in0=in0, scalar=scalar, in1=in1,
                                                            op0=op0, op1=op1), reads, writes)

    def copy(self, out_ap, in_ap, reads, writes, eng="dve"):
        if eng == "act":
            return self.S.act(lambda e: e.copy(out=out_ap, in_=in_ap), reads, writes)
        f = lambda e: e.tensor_copy(out=out_ap, in_=in_ap)
        return (self.S.dve if eng == "dve" else self.S.pool)(f, reads, writes)

    def memset(self, ap, val, writes, eng="pool"):
        f = lambda e: e.memset(ap, val)
        return (self.S.dve if eng == "dve" else self.S.pool)(f, [], writes)


class Common:
    def __init__(self, cx, consts_ap):
        self.cx = cx
        self.cb = cx.sb([128, 256], BF16, name="cbf")
        cx.load(self.cb[:, :], consts_ap, self.cb.all)
        self.ident = self.cb[:, 0:128]
        self.ones = self.cb[:, 128:256]
        self.banks = [cx.ps([128, 512], F32, nsub=4, name=f"bank{i}") for i in range(7)]
        self.bankb = cx.ps([128, 1024], BF16, nsub=8, name="bankb")


def rms_stats(cx, cm, src_fn, src_bufs, nfeat_chunks, ncols, sq_pool, bank, rstd, denom):
    for c in range(nfeat_chunks):
        sq = sq_pool[c % len(sq_pool)]
        cx.activation(sq[:, 0:ncols], src_fn(c), AF.Square, src_bufs(c), sq.all)
        cx.mm(bank[:, 0:ncols], cm.ones, sq[:, 0:ncols], c == 0, c == nfeat_chunks - 1,
              [cm.cb.b[0]] + sq.all, bank.all)
    cx.activation(rstd[:, 0:ncols], bank[:, 0:ncols], AF.Ln, bank.all + cm.epst.all, rstd.all, bias=cm.eps_ap,
                  scale=1.0 / denom)
    cx.activation(rstd[:, 0:ncols], rstd[:, 0:ncols], AF.Exp, rstd.all, rstd.all, scale=-0.5)


def add_eps(cx, cm):
    cm.epst = cx.sb([128, 1], F32, name="epst")
    cx.memset(cm.epst[:, :], EPS, cm.epst.all)
    cm.eps_ap = cm.epst[:, 0:1]


def phase_pre(cx, cm, xT, gpre, hn_out_fn, is_output):
    g = cx.sb([128, NCH], F32, name="g")
    cx.load(g[:, :], gpre, g.all)
    xv = xT.rearrange("(c p) t -> p c t", p=128)
    xs = [cx.sb([128, NCH, 512], F32, nsub=NCH, name=f"xs{i}") for i in range(2)]
    hs = [cx.sb([128, NCH, 512], BF16, nsub=1, name=f"hs{i}") for i in range(2)]
    sqp = [cx.sb([128, 512], BF16, name=f"sq{i}") for i in range(3)]
    rstd = cx.sb([128, 512], F32, name="rstd")
    for n in range(TOK // 512):
        x = xs[n % 2]
        h = hs[n % 2]
        cx.load(x[:, :, :], xv[:, :, n * 512:(n + 1) * 512], x.all)
        rms_stats(cx, cm, lambda c: x[:, c, :], lambda c: [x.b[c]], NCH, 512, sqp, cm.banks[n % 2], rstd, D_MODEL)
        for c in range(NCH):
            cx.stt(h[:, c, :], x[:, c, :], g[:, c:c + 1], rstd[:, :], ALU.mult, ALU.mult,
                   [x.b[c]] + g.all + rstd.all, h.all)
        for j in range(2):
            hn_out_fn(n, j, h[:, :, j * 256:(j + 1) * 256], h.all)


def build_pre():
    cx = Ctx()
    xT = cx.din("xT", [D_MODEL, TOK], F32)
    gpre = cx.din("gpre", [128, NCH], F32)
    consts = cx.din("consts", [128, 256], BF16)
    hn_out = cx.dout("hn_out", [D_MODEL, TOK], BF16)
    cm = Common(cx, consts)
    add_eps(cx, cm)
    ov = hn_out.rearrange("(c p) t -> p c t", p=128)
    phase_pre(cx, cm, xT, gpre,
              lambda n, j, src, rb: cx.store(ov[:, :, n * 512 + j * 256:n * 512 + (j + 1) * 256], src, rb), True)
    cx.S.emit()
    return cx.nc


W_OFF = {"moba": (0, 384), "fox": (384, 384), "hgrn": (768, 512), "pool": (1280, 128)}
W_MIXCOLS = 1408
SM_WFF, SM_G0, SM_G1, SM_HNORM, SM_PSCALE, SM_FB = 0, 16, 17, 18, 19, 20
N_SMALL = 24


class MixEnv:
    pass


def project(cx, cm, env, w, fm_blocks, tm, row=None, pre_chunk=None):
    nb = 0
    for n in range(SEQ // 512):
        hs = env.hs[n % 2]
        for j in range(2):
            cx.load(hs[:, :, j * 256:(j + 1) * 256], env.hn_chunk(n, j), hs.all, rbufs=env.hn_rbufs(n, j))
        if pre_chunk is not None:
            pre_chunk(n)
        for (c0, handler) in fm_blocks:
            bank = cm.banks[nb % 4]
            nb += 1
            for c in range(NCH):
                cx.mm(bank[:, :], w[:, c, c0:c0 + 128], hs[:, c, :], c == 0, c == NCH - 1,
                      w.all + hs.all, bank.all)
            handler(n, bank)
        if tm is not None:
            c0, ncols, handler = tm
            for tl in range(4):
                bank = cm.banks[nb % 4]
                nb += 1
                for c in range(NCH):
                    cx.mm(bank[:, 0:ncols], hs[:, c, tl * 128:(tl + 1) * 128], w[:, c, c0:c0 + ncols],
                          c == 0, c == NCH - 1, w.all + hs.all, bank.all)
                handler(n * 4 + tl, bank)
        if row is not None:
            lfn, rreads, handler = row
            bank = cm.banks[nb % 4]
            nb += 1
            for c in range(NCH):
                cx.mm(bank[0:1, :], lfn(c), hs[:, c, :], c == 0, c == NCH - 1, rreads + hs.all, bank.all)
            handler(n, bank)


def load_w(cx, env, name, eng="pool"):
    c0, nc_ = W_OFF[name]
    w = cx.sb([128, NCH, nc_], BF16, name="w_" + name)
    for c in range(0, NCH, 4):
        cx.load(w[:, c:c + 4, :], env.wmix[:, c:c + 4, c0:c0 + nc_], w.all, q=eng)
    return w


def softmax_finish(cx, env, obank, dbank, ncols, out_ap_dram, k):
    rc = env.rc[k % 2]
    yt = env.yt[k % 2]
    cx.S.dve(lambda e: e.reciprocal(out=rc[:, 0:ncols], in_=dbank[:, 0:ncols]), dbank.all, rc.all)
    cx.tt(yt[:, 0:ncols], obank[:, 0:ncols], rc[:, 0:ncols], ALU.mult, obank.all + rc.all, yt.all)
    cx.store(out_ap_dram, yt[:, 0:ncols], yt.all, is_output=env.y_is_output)


def mix_fox(cx, cm, env):
    S = cx.S
    w = load_w(cx, env, "fox")
    fQ = cx.sb([128, SEQ], BF16, name="fQ")
    fK = cx.sb([128, SEQ], BF16, name="fK")
    fV = cx.sb([128, 32, 128], BF16, name="fV")
    wff = cx.sb([128, NCH], BF16, name="wff")
    cx.copy(wff[:, :], env.small[:, SM_WFF:SM_WFF + NCH], env.small.all, wff.all)
    nfb = cx.sb([128, 1], F32, name="nfb")
    cx.ts(nfb[:, :], env.small[:, SM_FB:SM_FB + 1], -1.0, None, ALU.mult, None, env.small.all, nfb.all)
    sprow = cx.sb([1, SEQ], F32, name="sprow")
    cprow = cx.sb([1, SEQ], F32, name="cprow")
    nrh = cx.sb([1, SEQ], BF16, name="nrh")
    nrl = cx.sb([1, SEQ], BF16, name="nrl")
    rowtmp = cx.sb([1, 512], F32, name="rowtmp")

    def h_q(n, bank):
        S.act(lambda e: e.mul(out=fQ[:, n * 512:(n + 1) * 512], in_=bank[:, :], mul=SCALE), bank.all, fQ.all)

    def h_k(n, bank):
        cx.copy(fK[:, n * 512:(n + 1) * 512], bank[:, :], bank.all, fK.all)

    def h_v(tl, bank):
        cx.copy(fV[:, tl, :], bank[:, 0:128], bank.all, fV.all, eng="act" if tl % 2 else "dve")

    def h_row(n, bank):
        cx.activation(rowtmp[0:1, :], bank[0:1, :], AF.Exp, bank.all + nfb.all, rowtmp.all, bias=nfb[0:1, 0:1], scale=-1.0)
        cx.activation(sprow[0:1, n * 512:(n + 1) * 512], rowtmp[0:1, :], AF.Ln, rowtmp.all + cm.c32.all, sprow.all,
                      bias=cm.one_ap[0:1, 0:1])

    project(cx, cm, env, w, [(0, h_q), (128, h_k)], (256, 128, h_v),
            row=(lambda c: wff[:, c:c + 1], wff.all, h_row))

    cx.memset(nrh[0:1, :], 1.0, nrh.all, eng="dve")
    S.dve(lambda e: e.tensor_tensor_scan(out=cprow[0:1, :], data0=nrh[0:1, :], data1=sprow[0:1, :], initial=0.0,
                                         op0=ALU.mult, op1=ALU.add), nrh.all + sprow.all, cprow.all)
    sm = cm.banks[6]
    cpv = cprow[0:1, :].rearrange("o (a b) -> o a b", b=512)
    cx.mm(sm[:, 0:8], cm.ones32[0:1, 0:128], cpv[:, :, 0], True, True, cm.c32.all + cprow.all, sm.all)
    for kt in range(32):
        cx.mm(sm[:, 8 + kt:9 + kt], cprow[0:1, kt * 128:(kt + 1) * 128], cm.ones32[0:1, 0:1], True, True,
              cm.c32.all + cprow.all, sm.all)
    rbcp = cx.sb([128, 40], F32, name="rbcp")
    cx.copy(rbcp[:, :], sm[:, 0:40], sm.all, rbcp.all)
    for qc in range(8):
        sl = slice(qc * 512, (qc + 1) * 512)
        cx.ts(sprow[0:1, sl], cprow[0:1, sl], cprow[0:1, qc * 512:qc * 512 + 1], -1.0, ALU.subtract, ALU.mult,
              cprow.all, sprow.all)
    cx.copy(nrh[0:1, :], sprow[0:1, :], sprow.all, nrh.all)
    cx.tt(nrl[0:1, :], sprow[0:1, :], nrh[0:1, :], ALU.subtract, sprow.all + nrh.all, nrl.all)

    biasq = [cx.sb([128, 32], F32, name=f"biasq{i}") for i in range(2)]
    pTs = [cx.sb([128, 512], BF16, name=f"fpT{i}") for i in range(3)]
    it = 0
    for qc in range(8):
        sl = slice(qc * 512, (qc + 1) * 512)
        bq = biasq[qc % 2]
        cx.ts(bq[:, :], rbcp[:, 8:40], rbcp[:, qc:qc + 1], None, ALU.subtract, None, rbcp.all, bq.all)
        obank = cm.banks[2 + qc % 2]
        dbank = cm.banks[4 + qc % 2]
        nkt = 4 * (qc + 1)

        def emit_S(kt):
            sbank = cm.banks[kt % 2]
            a = kt - 4 * qc
            cx.mm(sbank[:, :], fK[:, kt * 128:(kt + 1) * 128], fQ[:, sl], True, False, fK.all + fQ.all, sbank.all)
            cx.mm(sbank[:, :], cm.ones[0:1, 0:128], nrh[0:1, sl], False, False, nrh.all, sbank.all)
            cx.mm(sbank[:, :], cm.ones[0:1, 0:128], nrl[0:1, sl], False, a < 0, nrl.all, sbank.all)
            if a >= 0:
                cx.mm(sbank[:, :], cm.ident, env.cmask[:, a, :], False, True, env.cmask.all, sbank.all)
        emit_S(0)
        for kt in range(nkt):
            sbank = cm.banks[kt % 2]
            pT = pTs[it % 3]
            it += 1
            cx.activation(pT[:, :], sbank[:, :], AF.Exp, sbank.all + bq.all, pT.all, bias=bq[:, kt:kt + 1])
            if kt + 1 < nkt:
                emit_S(kt + 1)
            cx.mm(obank[:, :], fV[:, kt, :], pT[:, :], kt == 0, kt == nkt - 1, fV.all + pT.all, obank.all)
            cx.mm(dbank[:, :], cm.ones, pT[:, :], kt == 0, kt == nkt - 1, pT.all, dbank.all)
        softmax_finish(cx, env, obank, dbank, 512, env.y_out(1, qc * 512, 512), qc)


def mix_moba(cx, cm, env):
    S = cx.S
    w = load_w(cx, env, "moba")
    mQ = cx.sb([128, SEQ], BF16, name="mQ")
    mK = cx.sb([128, SEQ], BF16, name="mK")
    mV = cx.sb([128, 32, 128], BF16, name="mV")
    perm = cx.sb([128, 128], BF16, name="perm")
    cx.load(perm[:, :], env.perm, perm.all)
    rC = [cx.sb([128, 512], F32, name=f"rC{i}") for i in range(2)]
    rS = [cx.sb([128, 512], F32, name=f"rS{i}") for i in range(2)]
    xb = [cx.sb([128, 512], BF16, name=f"xb{i}") for i in range(2)]
    t1 = [cx.sb([128, 512], F32, name=f"t1{i}") for i in range(2)]
    t2 = [cx.sb([128, 512], F32, name=f"t2{i}") for i in range(2)]
    cnt = [0]

    def pre_chunk(n):
        cx.load(rC[n % 2][:, :], env.ropeC[:, n * 512:(n + 1) * 512], rC[n % 2].all)
        cx.load(rS[n % 2][:, :], env.ropeS[:, n * 512:(n + 1) * 512], rS[n % 2].all)

    def rope_handler(dst, sc):
        def h(n, bank):
            k = cnt[0] % 2
            cnt[0] += 1
            sl = slice(n * 512, (n + 1) * 512)
            if DEBUG_STOP == 10:
                cx.copy(dst[:, sl], bank[:, :], bank.all, dst.all)
                return
            cx.copy(xb[k][:, :], bank[:, :], bank.all, xb[k].all, eng="act")
            swb = cm.banks[4 + k]
            cx.mm(swb[:, :], perm[:, :], xb[k][:, :], True, True, perm.all + xb[k].all, swb.all)
            if DEBUG_STOP == 11:
                cx.copy(dst[:, sl], swb[:, :], swb.all, dst.all)
                return
            if DEBUG_STOP == 13:
                cx.stt(t1[k][:, :], bank[:, :], sc, t2[k][:, :], ALU.mult, ALU.mult, bank.all + t2[k].all, t1[k].all)
            else:
                cx.stt(t1[k][:, :], bank[:, :], sc, rC[n % 2][:, :], ALU.mult, ALU.mult, bank.all + rC[n % 2].all, t1[k].all)
            if DEBUG_STOP in (12, 13):
                cx.copy(dst[:, sl], t1[k][:, :], t1[k].all, dst.all)
                return
            cx.stt(t2[k][:, :], swb[:, :], sc, rS[n % 2][:, :], ALU.mult, ALU.mult, swb.all + rS[n % 2].all, t2[k].all)
            cx.tt(dst[:, sl], t1[k][:, :], t2[k][:, :], ALU.add, t1[k].all + t2[k].all, dst.all, eng=ROPE_ADD_ENG)
        return h

    def h_v(tl, bank):
        cx.copy(mV[:, tl, :], bank[:, 0:128], bank.all, mV.all, eng="act")

    project(cx, cm, env, w, [(0, rope_handler(mQ, SCALE)), (128, rope_handler(mK, 1.0))], (256, 128, h_v),
            pre_chunk=pre_chunk)

    if DEBUG_STOP in (1, 10, 11, 12, 13):
        return
    kb32 = cx.sb([128, 16], F32, name="kb32")
    kbT = cx.sb([128, 16], BF16, name="kbT")
    S.dve(lambda e: e.tensor_reduce(out=kb32[:, :], in_=mK[:, :].rearrange("p (j k) -> p j k", k=256),
                                    axis=AX.X, op=ALU.add), mK.all, kb32.all)
    cx.copy(kbT[:, :], kb32[:, :], kb32.all, kbT.all)
    gb = cm.banks[6]
    for qt in range(32):
        cx.mm(gb[:, qt * 16:(qt + 1) * 16], mQ[:, qt * 128:(qt + 1) * 128], kbT[:, :], True, True,
              mQ.all + kbT.all, gb.all)
    g_sb = cx.sb([128, 32, 16], F32, name="g_sb")
    cx.copy(g_sb[:, :, :], gb[:, :].rearrange("p (a b) -> p a b", b=16), gb.all, g_sb.all)
    if DEBUG_STOP == 2:
        return
    S.pool(lambda e: e.affine_select(out=g_sb[:, :, :], in_=g_sb[:, :, :], pattern=[[1, 16], [0, 2], [-1, 16]],
                                     compare_op=ALU.is_ge, fill=-1e30, base=-1, channel_multiplier=0),
           g_sb.all, g_sb.all)
    if DEBUG_STOP == 3:
        return
    m8 = cx.sb([128, 32, 8], F32, name="m8")
    for qt in range(32):
        S.dve(lambda e, qt=qt: e.max(out=m8[:, qt, :], in_=g_sb[:, qt, :]), g_sb.all, m8.all)
    thr = cx.sb([128, 32, 1], F32, name="thr")
    cx.ts(thr[:, :, :], m8[:, :, 2:3], -1e29, None, ALU.max, None, m8.all, thr.all)
    nm = cx.sb([128, 32, 16], F32, name="nm")
    cx.tt(nm[:, :, :], g_sb[:, :, :], thr[:, :, :].to_broadcast([128, 32, 16]), ALU.is_lt, g_sb.all + thr.all, nm.all)
    cx.ts(nm[:, :, :], nm[:, :, :], NEG, None, ALU.mult, None, nm.all, nm.all)
    if DEBUG_STOP == 4:
        return
    nmT = cx.sb([16, SEQ], BF16, name="nmT")
    for grp in range(8):
        tb = cm.banks[grp % 2]
        for i in range(4):
            qt = grp * 4 + i
            cx.transpose(tb[0:16, i * 128:(i + 1) * 128], nm[:, qt, :], cm.ident32, nm.all + cm.c32.all, tb.all)
        cx.copy(nmT[0:16, grp * 512:(grp + 1) * 512], tb[0:16, :], tb.all, nmT.all, eng="act" if grp % 2 else "dve")

    if DEBUG_STOP == 5:
        return
    pTs = [cx.sb([128, 256], BF16, name=f"mpT{i}") for i in range(3)]
    it = 0
    for qb in range(16):
        sl = slice(qb * 256, (qb + 1) * 256)
        obank = cm.banks[2 + qb % 2]
        dbank = cm.banks[4 + qb % 2]
        nkt = 2 * qb + 2

        def emit_S(kt):
            sbank = cm.banks[kt % 2]
            cx.mm(sbank[:, 0:256], mK[:, kt * 128:(kt + 1) * 128], mQ[:, sl], True, False, mK.all + mQ.all, sbank.all)
            if kt < 2 * qb:
                cx.mm(sbank[:, 0:256], env.esel[0:16, kt // 2, :], nmT[0:16, sl], False, True,
                      env.esel.all + nmT.all, sbank.all)
            else:
                cx.mm(sbank[:, 0:256], cm.ident, env.cmask[:, kt - 2 * qb, 0:256], False, True, env.cmask.all, sbank.all)
        emit_S(0)
        for kt in range(nkt):
            sbank = cm.banks[kt % 2]
            pT = pTs[it % 3]
            it += 1
            cx.activation(pT[:, :], sbank[:, 0:256], AF.Exp, sbank.all, pT.all)
            if kt + 1 < nkt:
                emit_S(kt + 1)
            cx.mm(obank[:, 0:256], mV[:, kt, :], pT[:, :], kt == 0, kt == nkt - 1, mV.all + pT.all, obank.all)
            cx.mm(dbank[:, 0:256], cm.ones, pT[:, :], kt == 0, kt == nkt - 1, pT.all, dbank.all)
        softmax_finish(cx, env, obank, dbank, 256, env.y_out(0, qb * 256, 256), qb)


def mix_hgrn(cx, cm, env, layer):
    S = cx.S
    w = load_w(cx, env, "hgrn")
    hq = cx.sb([128, SEQ], F32, name="hq")
    lf = cx.sb([128, SEQ], F32, name="lf")
    hk = cx.sb([128, SEQ], BF16, name="hk")
    sg = cx.sb([128, SEQ], BF16, name="sg")
    hv = cx.sb([128, 32, 128], BF16, name="hv")
    lb = cx.sb([128, 2], F32, name="lb")
    if layer == 0:
        cx.memset(lb[:, 0:1], 0.0, lb.all, eng="dve")
        cx.memset(lb[:, 1:2], 1.0, lb.all, eng="dve")
    else:
        ee = cx.sb([128, 4], F32, name="ee")
        cx.activation(ee[:, 0:2], env.small[:, SM_G0:SM_G0 + 2], AF.Exp, env.small.all, ee.all)
        cx.tt(ee[:, 2:3], ee[:, 0:1], ee[:, 1:2], ALU.add, ee.all, ee.all)
        S.dve(lambda e: e.reciprocal(out=ee[:, 3:4], in_=ee[:, 2:3]), ee.all, ee.all)
        cx.tt(lb[:, 0:1], ee[:, 1:2], ee[:, 3:4], ALU.mult, ee.all, lb.all)
        cx.ts(lb[:, 1:2], lb[:, 0:1], -1.0, 1.0, ALU.mult, ALU.add, lb.all, lb.all)
    sgm = [cx.sb([128, 512], F32, name=f"sgm{i}") for i in range(2)]
    ff_ = [cx.sb([128, 512], F32, name=f"ff{i}") for i in range(2)]

    def h_q(n, bank):
        cx.copy(hq[:, n * 512:(n + 1) * 512], bank[:, :], bank.all, hq.all)

    def h_f(n, bank):
        sl = slice(n * 512, (n + 1) * 512)
        a, f = sgm[n % 2], ff_[n % 2]
        cx.activation(a[:, :], bank[:, :], AF.Sigmoid, bank.all, a.all)
        cx.ts(f[:, :], a[:, :], lb[:, 1:2], lb[:, 0:1], ALU.mult, ALU.add, a.all + lb.all, f.all)
        cx.activation(lf[:, sl], f[:, :], AF.Ln, f.all, lf.all)
        cx.ts(hk[:, sl], f[:, :], -1.0, 1.0, ALU.mult, ALU.add, f.all, hk.all, eng="pool")

    def h_g(n, bank):
        sl = slice(n * 512, (n + 1) * 512)
        a = sgm[n % 2]
        cx.activation(a[:, :], bank[:, :], AF.Sigmoid, bank.all, a.all)
        cx.tt(sg[:, sl], bank[:, :], a[:, :], ALU.mult, bank.all + a.all, sg.all)

    def h_v(tl, bank):
        cx.copy(hv[:, tl, :], bank[:, 0:128], bank.all, hv.all, eng="act" if tl % 2 else "dve")

    project(cx, cm, env, w, [(0, h_q), (128, h_f), (256, h_g)], (384, 128, h_v))

    rm = cx.sb([128, SEQ], BF16, name="rm")
    cx.memset(rm[:, :], 1.0, rm.all)
    cx.memset(rm[:, :].rearrange("p (a b) -> p a b", b=64)[:, :, 0:1], 0.0, rm.all)
    bb = cx.sb([128, SEQ], F32, name="bb")
    S.dve(lambda e: e.tensor_tensor_scan(out=bb[:, :], data0=rm[:, :], data1=lf[:, :], initial=0.0,
                                         op0=ALU.mult, op1=ALU.add), rm.all + lf.all, bb.all)
    bv = bb[:, :].rearrange("p (a b) -> p a b", b=64)
    sm = cx.sb([128, 5, 64], F32, name="hsm")
    cx.copy(sm[:, 0, :], bv[:, :, 31], bb.all, sm.all)
    cx.activation(sm[:, 4, :], bv[:, :, 31], AF.Exp, bb.all, sm.all)
    cx.activation(sm[:, 1, :], bv[:, :, 63], AF.Exp, bb.all, sm.all)
    cx.tt(sm[:, 3, :], bv[:, :, 63], sm[:, 0, :], ALU.subtract, bb.all + sm.all, sm.all)
    cx.activation(sm[:, 2, :], sm[:, 3, :], AF.Exp, sm.all, sm.all)
    cx.tt(bv, bv, sm[:, 0, :].unsqueeze(2).to_broadcast([128, 64, 64]), ALU.subtract, bb.all + sm.all, bb.all)
    E = lf
    qt_ = cx.sb([128, SEQ], BF16, name="qtil")
    kt_ = cx.sb([128, SEQ], BF16, name="ktil")
    kh_ = cx.sb([128, SEQ], BF16, name="khat")
    cx.activation(E[:, :], bb[:, :], AF.Exp, bb.all, E.all)
    cx.tt(qt_[:, :], hq[:, :], E[:, :], ALU.mult, hq.all + E.all, qt_.all)
    cx.activation(E[:, :], bb[:, :], AF.Exp, bb.all, E.all, scale=-1.0)
    cx.tt(kt_[:, :], hk[:, :], E[:, :], ALU.mult, hk.all + E.all, kt_.all)
    cx.tt(kh_[:, :].rearrange("p (a b) -> p a b", b=64), kt_[:, :].rearrange("p (a b) -> p a b", b=64),
          sm[:, 2, :].unsqueeze(2).to_broadcast([128, 64, 64]), ALU.mult, kt_.all + sm.all, kh_.all, eng="pool")
    khtm = cx.sb([128, 32, 128], BF16, name="khtm")
    for grp in range(4):
        for i in range(8):
            tl = grp * 8 + i
            cx.transpose(cm.bankb[:, i * 128:(i + 1) * 128], kh_[:, tl * 128:(tl + 1) * 128], cm.ident,
                         kh_.all, cm.bankb.all)
        cx.copy(khtm[:, grp * 8:(grp + 1) * 8, :], cm.bankb[:, :].rearrange("p (a b) -> p a b", b=128),
                cm.bankb.all, khtm.all, eng="act" if grp % 2 else "dve")
    attT = cx.sb([128, 32, 128], BF16, nsub=8, name="attT")
    for grp in range(8):
        ab = cm.banks[grp % 2]
        for i in range(4):
            tl = grp * 4 + i
            ts_ = slice(tl * 128, (tl + 1) * 128)
            cx.mm(ab[:, i * 128:(i + 1) * 128], kt_[:, ts_], qt_[:, ts_], True, True, kt_.all + qt_.all, ab.all)
        cx.tt(attT[:, grp * 4:(grp + 1) * 4, :], ab[:, :].rearrange("p (a b) -> p a b", b=128),
              env.hmask[:, :].unsqueeze(1).to_broadcast([128, 4, 128]), ALU.mult, ab.all + env.hmask.all, [attT.b[grp]])
    S32 = cx.sb([128, 2, 128], F32, nsub=2, name="S32")
    Sb = cx.sb([128, 4, 128], BF16, nsub=4, name="Sb")
    cx.memset(S32[:, 0, :], 0.0, [S32.b[0]], eng="dve")
    cx.memset(Sb[:, 0, :], 0.0, [Sb.b[0]], eng="dve")
    oT = [cx.sb([128, 512], F32, name=f"oT{i}") for i in range(2)]
    sq = [cx.sb([128, 512], BF16, name=f"osq{i}") for i in range(2)]
    rs = [cx.sb([128, 512], F32, name=f"ors{i}") for i in range(2)]
    yo = [cx.sb([128, 512], BF16, name=f"oyo{i}") for i in range(2)]
    mbanks = [cm.banks[4], cm.banks[5], cm.banks[0], cm.banks[1]]

    def emit_M(c):
        ps_ = slice((c % 2) * 64, (c % 2) * 64 + 64)
        mb_ = mbanks[c % 4]
        cx.mm(mb_[:, 0:128], khtm[ps_, c // 2, :], hv[ps_, c // 2, :], True, True, khtm.all + hv.all, mb_.all)
    for c in range(3):
        emit_M(c)
    for tl in range(32):
        ob = cm.banks[2 + (tl // 4) % 2]
        for half in range(2):
            c = 2 * tl + half
            ps = slice(half * 64, half * 64 + 64)
            mslot = 0
            mb = mbanks[c % 4]
            oc = slice((tl % 4) * 128 + half * 64, (tl % 4) * 128 + half * 64 + 64)
            tsl = slice(c * 64, c * 64 + 64)
            cx.mm(ob[:, oc], hv[:, tl, :], attT[:, tl, half * 64:half * 64 + 64], True, False,
                  hv.all + [attT.b[tl // 4]], [ob.b[tl % 4]])
            cx.mm(ob[:, oc], Sb[:, c % 4, :], qt_[:, tsl], False, True, [Sb.b[c % 4]] + qt_.all, [ob.b[tl % 4]])
            cx.stt(S32[:, (c + 1) % 2, :], S32[:, c % 2, :], sm[:, 1, c:c + 1], mb[:, mslot * 128:(mslot + 1) * 128],
                   ALU.mult, ALU.add, [S32.b[c % 2], mb.b[mslot]] + sm.all, [S32.b[(c + 1) % 2]])
            if c + 1 < 64:
                cx.S.act(lambda e, c=c: e.mul(out=Sb[:, (c + 1) % 4, :], in_=S32[:, (c + 1) % 2, :], mul=sm[:, 4, c + 1:c + 2]),
                         [S32.b[(c + 1) % 2]] + sm.all, [Sb.b[(c + 1) % 4]])
            if c + 3 < 64:
                emit_M(c + 3)
        if tl % 4 == 3:
            n = tl // 4
            k = n % 2
            sl = slice(n * 512, (n + 1) * 512)
            cx.copy(oT[k][:, :], ob[:, :], ob.all, oT[k].all)
            cx.activation(sq[k][:, :], ob[:, :], AF.Square, ob.all, sq[k].all)
            nb_ = cm.banks[6]
            cx.mm(nb_[:, :], cm.ones, sq[k][:, :], True, True, sq[k].all, nb_.all)
            cx.activation(rs[k][:, :], nb_[:, :], AF.Ln, nb_.all + cm.epst.all, rs[k].all, bias=cm.eps_ap, scale=1.0 / HD)
            cx.activation(rs[k][:, :], rs[k][:, :], AF.Exp, rs[k].all, rs[k].all, scale=-0.5)
            cx.stt(oT[k][:, :], oT[k][:, :], env.small[:, SM_HNORM:SM_HNORM + 1], rs[k][:, :], ALU.mult, ALU.mult,
                   oT[k].all + rs[k].all + env.small.all, oT[k].all)
            cx.tt(yo[k][:, :], oT[k][:, :], sg[:, sl], ALU.mult, oT[k].all + sg.all, yo[k].all, eng="pool")
            cx.store(env.y_out(2, n * 512, 512), yo[k][:, :], yo[k].all, is_output=env.y_is_output)


def mix_pool(cx, cm, env):
    w = load_w(cx, env, "pool")
    pin = cx.sb([128, 32, 128], BF16, nsub=32, name="pin")
    pM = cx.sb([128, 3, 128], BF16, name="pM")
    cx.load(pM[:, :, :], env.poolM, pM.all)
    pw = cx.sb([128, 128], BF16, name="pw")
    cx.load(pw[:, :], env.poolw, pw.all, q="pool")

    def h_p(tl, bank):
        cx.copy(pin[:, tl, :], bank[:, 0:128], bank.all, [pin.b[tl]], eng="act" if tl % 2 else "dve")

    project(cx, cm, env, w, [], (0, 128, h_p))
    pt = [cx.sb([128, 512], BF16, name=f"ppt{i}") for i in range(2)]
    yo = [cx.sb([128, 512], BF16, name=f"pyo{i}") for i in range(2)]
    for n in range(8):
        pb = cm.banks[n % 2]
        for i in range(4):
            tl = n * 4 + i
            cs = slice(i * 128, (i + 1) * 128)
            cx.mm(pb[:, cs], pin[:, tl, :], pM[:, 0 if tl == 0 else 1, :], True, tl == 0, [pin.b[tl]] + pM.all, [pb.b[i]])
            if tl > 0:
                cx.mm(pb[:, cs], pin[:, tl - 1, :], pM[:, 2, :], False, True, [pin.b[tl - 1]] + pM.all, [pb.b[i]])
        k = n % 2
        cx.copy(pt[k][:, :], pb[:, :], pb.all, pt[k].all, eng="act")
        ob = cm.banks[2 + n % 2]
        cx.mm(ob[:, :], pw[:, :], pt[k][:, :], True, True, pw.all + pt[k].all, ob.all)
        cx.ts(yo[k][:, :], ob[:, :], env.small[:, SM_PSCALE:SM_PSCALE + 1], None, ALU.mult, None,
              ob.all + env.small.all, yo[k].all)
        cx.store(env.y_out(3, n * 512, 512), yo[k][:, :], yo[k].all, is_output=env.y_is_output)


def setup_common32(cx, cm, c32_ap):
    cm.c32 = cx.sb([128, 256], F32, name="c32")
    cx.load(cm.c32[:, :], c32_ap, cm.c32.all)
    cm.ident32 = cm.c32[:, 0:128]
    cm.ones32 = cm.c32[:, 128:256]
    cm.one_ap = cm.c32[:, 128:129]


def mix_globals(cx, env, cmask_d, esel_d, hmask_d):
    env.cmask = cx.sb([128, 4, 512], BF16, name="cmask")
    cx.load(env.cmask[:, :, :], cmask_d, env.cmask.all)
    env.esel = cx.sb([16, 16, 128], BF16, name="esel")
    cx.load(env.esel[:, :, :], esel_d, env.esel.all)
    env.hmask = cx.sb([128, 128], BF16, name="hmask")
    cx.load(env.hmask[:, :], hmask_d, env.hmask.all)


def phase_mix(cx, cm, env, layer, small_d, which=("fox", "moba", "hgrn", "pool")):
    with contextlib.ExitStack() as outer:
        cx.scope = outer
        env.small = cx.sb([128, N_SMALL], F32, name="small")
        cx.load(env.small[:, :], small_d, env.small.all)
        env.hs = [cx.sb([128, NCH, 512], BF16, name=f"hs{i}") for i in range(2)]
        env.rc = [cx.sb([128, 512], F32, name=f"rc{i}") for i in range(2)]
        env.yt = [cx.sb([128, 512], BF16, name=f"yt{i}") for i in range(2)]
        fns = {"fox": lambda: mix_fox(cx, cm, env), "moba": lambda: mix_moba(cx, cm, env),
               "hgrn": lambda: mix_hgrn(cx, cm, env, layer), "pool": lambda: mix_pool(cx, cm, env)}
        for name in which:
            with contextlib.ExitStack() as sc:
                cx.scope = sc
                fns[name]()
                cx.S.fence()
                if getattr(env, "after_mixer", None) is not None:
                    env.after_mixer(name)
            cx.scope = outer
    cx.scope = None


def build_mix(layer, which=("fox", "moba", "hgrn", "pool")):
    cx = Ctx()
    env = MixEnv()
    env.hnT = cx.din("hnT", [D_MODEL, SEQ], BF16)
    env.wmix = cx.din("wmix", [128, NCH, W_MIXCOLS], F32)
    small_d = cx.din("small", [128, N_SMALL], F32)
    env.poolw = cx.din("poolw", [128, 128], F32)
    consts = cx.din("consts", [128, 256], BF16)
    c32 = cx.din("c32", [128, 256], F32)
    cmask_d = cx.din("cmask", [128, 4, 512], BF16)
    env.perm = cx.din("perm", [128, 128], BF16)
    env.ropeC = cx.din("ropeC", [128, SEQ], F32)
    env.ropeS = cx.din("ropeS", [128, SEQ], F32)
    esel_d = cx.din("esel", [16, 16, 128], BF16)
    hmask_d = cx.din("hmask", [128, 128], BF16)
    env.poolM = cx.din("poolM", [128, 3, 128], BF16)
    env.yT = cx.dout("yT", [4, 128, SEQ], BF16)
    hview_ = env.hnT.rearrange("(c p) t -> p c t", p=128)
    env.hn_chunk = lambda n, j: hview_[:, :, n * 512 + j * 256:n * 512 + (j + 1) * 256]
    env.hn_rbufs = lambda n, j: []
    env.y_out = lambda bi, t0, ncols: env.yT[bi, :, t0:t0 + ncols]
    env.y_is_output = True
    cm = Common(cx, consts)
    add_eps(cx, cm)
    setup_common32(cx, cm, c32)
    mix_globals(cx, env, cmask_d, esel_d, hmask_d)
    phase_mix(cx, cm, env, layer, small_d, which)
    cx.S.emit()
    return cx.nc


CC_GROUPS = [[0, 1, 2, 3], [4, 5, 6, 7]]


def _core_quarter(cx, e):
    if getattr(cx, "_q", None) is None:
        cx._q = e.snap(e.partition_id() % 4, min_val=0, max_val=3)
    return cx._q


def build_fused(depth):
    cx = Ctx()
    nc = cx.nc
    S = cx.S
    consts = cx.din("consts", [128, 256], BF16)
    c32 = cx.din("c32", [128, 256], F32)
    xT = cx.din("xT", [D_MODEL, TOK], F32)
    memT = cx.din("memT", [D_MODEL, NMEM], F32)
    gpre = cx.din("gpre", [128, NCH], F32)
    cmask_d = cx.din("cmask", [128, 4, 512], BF16)
    esel_d = cx.din("esel", [16, 16, 128], BF16)
    hmask_d = cx.din("hmask", [128, 128], BF16)
    env = MixEnv()
    env.perm = cx.din("perm", [128, 128], BF16)
    env.ropeC = cx.din("ropeC", [128, SEQ], F32)
    env.ropeS = cx.din("ropeS", [128, SEQ], F32)
    env.poolM = cx.din("poolM", [128, 3, 128], BF16)
    wmix_d = [cx.din(f"wmix_{l}", [128, NCH, W_MIXCOLS], F32) for l in range(depth)]
    small_d = [cx.din(f"small_{l}", [128, N_SMALL], F32) for l in range(depth)]
    poolw_d = [cx.din(f"poolw_{l}", [128, 128], F32) for l in range(depth)]
    ios = []
    for l in range(depth):
        io = TokIO()
        tok_weight_inputs(cx, io, f"_{l}")
        io.memT = memT
        ios.append(io)
    outT = cx.dout("outT", [D_MODEL, TOK], F32)
    hn_own = [nc.dram_tensor(f"hn_own{k}", [D_MODEL, 256], BF16, kind="Internal").ap() for k in range(4)]
    hn_all = [nc.dram_tensor(f"hn_all{k}", [4 * D_MODEL, 256], BF16, kind="Internal").ap() for k in range(4)]
    hnp = [Buf(f"hnp{k}") for k in range(4)]
    hna = [Buf(f"hna{k}") for k in range(4)]

    def hn_store(piece, src, rb):
        cx.store(hn_own_v[piece], src, rb, is_output=False, wbufs=[hnp[piece]])

    def hn_gather(piece):
        S.collective(lambda e: e.collective_compute("AllGather", ALU.bypass, replica_groups=CC_GROUPS,
                                                    ins=[hn_own[piece].opt()], outs=[hn_all[piece].opt()]),
                     [hnp[piece]], [hna[piece]])
    y_own = [nc.dram_tensor(f"y_own{n}", [512, TOK], BF16, kind="Internal").ap() for n in range(4)]
    y_all = [nc.dram_tensor(f"y_all{n}", [4 * 512, TOK], BF16, kind="Internal").ap() for n in range(4)]
    h_res = nc.dram_tensor("h_res", [D_MODEL, TOK], F32, kind="Internal").ap()

    cm = Common(cx, consts)
    add_eps(cx, cm)
    setup_common32(cx, cm, c32)
    mix_globals(cx, env, cmask_d, esel_d, hmask_d)
    S.sp_init = lambda e: _core_quarter(cx, e)

    with contextlib.ExitStack() as sc:
        cx.scope = sc
        hn_own_v = [a.rearrange("(c p) t -> p c t", p=128) for a in hn_own]

        def pre_out(n, j, src, rb):
            hn_store(2 * n + j, src, rb)
            hn_gather(2 * n + j)
        phase_pre(cx, cm, xT, gpre, pre_out, False)
        S.fence()
    cx.scope = None

    hn_all_v = [a.rearrange("(r c p) t -> r p c t", r=4, p=128) for a in hn_all]
    y_own_v = [a.rearrange("(tq d) t -> tq d t", tq=4) for a in y_own]
    y_all_v = [a.rearrange("(hh tq d) t -> tq d hh t", hh=4, tq=4) for a in y_all]
    env.hn_chunk = lambda n, j: hn_all_v[2 * (n % 2) + j][n // 2]
    env.hn_rbufs = lambda n, j: [hna[2 * (n % 2) + j]]
    env.y_out = lambda bi, t0, ncols: y_own_v[bi][t0 // TOK][:, t0 % TOK:t0 % TOK + ncols]
    env.y_is_output = False
    h_res_v = h_res.rearrange("(c p) t -> p c t", p=128)
    x_v = xT.rearrange("(c p) t -> p c t", p=128)
    out_v = outT.rearrange("(c p) t -> p c t", p=128)

    for l in range(depth):
        last = l == depth - 1
        S.fence()
        env.wmix = wmix_d[l]
        env.poolw = poolw_d[l]
        def after_mixer(name):
            n = {"moba": 0, "fox": 1, "hgrn": 2, "pool": 3}[name]
            S.collective(lambda e, n=n: e.collective_compute("AllGather", ALU.bypass, replica_groups=CC_GROUPS,
                                                             ins=[y_own[n].opt()], outs=[y_all[n].opt()]), [], [])
        env.after_mixer = after_mixer
        phase_mix(cx, cm, env, l, small_d[l])
        S.fence()
        io = ios[l]
        src_v = x_v if l == 0 else h_res_v
        dst_v = out_v if last else h_res_v
        io.hT_in = lambda hf, c0, c1, src_v=src_v: src_v[:, c0:c1, hf * HALF:(hf + 1) * HALF]
        io.hn_load = lambda hf, j, dst, wb: cx.load(dst, hn_own_v[2 * hf + j], wb, rbufs=[hnp[2 * hf + j]])

        def ybr_load(dst, hf, wb):
            for n in range(4):
                def f(e, n=n):
                    q = _core_quarter(cx, e)
                    src = y_all_v[n][bass.ds(q, 1)][0][:, :, hf * HALF:(hf + 1) * HALF]
                    return e.dma_start(out=dst[:, n * 4:(n + 1) * 4, :], in_=src)
                S.dma("sp", f, [], list(wb))
        io.ybr_load = ybr_load
        io.hT_out = lambda hf, c0, c1, dst_v=dst_v: dst_v[:, c0:c1, hf * HALF:(hf + 1) * HALF]
        pending = []

        def hn_store_tok(hf, j, src, rb):
            hn_store(2 * hf + j, src, rb)
            if hf == 0:
                pending.append(2 * hf + j)
            else:
                hn_gather(2 * hf + j)

        def after_gates(hf):
            while pending:
                hn_gather(pending.pop(0))
        io.hn_store = hn_store_tok
        io.after_gates = after_gates
        io.h_is_output = last
        io.hn_is_output = False
        io.write_hn_when_last = False
        with contextlib.ExitStack() as sc:
            cx.scope = sc
            phase_tok(cx, cm, io, l, last)
            S.fence()
        cx.scope = None
    S.emit()
    return nc


IN_OFF = {"mq": 0, "mk": 512, "mv": 1024, "fq": 1536, "fk": 2048, "fv": 2560, "ff": 3072,
          "hq": 3076, "hf": 3588, "hi": 4100, "hg": 4612, "pin": 5124, "gl": 5636}
_HC = {}


def host_consts():
    if _HC:
        return _HC
    f32 = np.float32
    c = np.zeros((128, 256), f32)
    c[:, :128] = np.eye(128)
    c[:, 128:] = 1
    _HC["c32"] = c
    _HC["consts"] = c.astype(NPBF)
    k = np.arange(128)[:, None, None]
    a = np.arange(4)[None, :, None]
    q = np.arange(512)[None, None, :]
    _HC["cmask"] = np.where(q >= 128 * a + k, 0.0, NEG).astype(NPBF)
    perm = np.zeros((128, 128), f32)
    d = np.arange(128)
    perm[(d + 64) % 128, d] = 1
    _HC["perm"] = perm.astype(NPBF)
    inv_freq = (f32(10000.0) ** (-np.arange(64, dtype=f32) * f32(2.0) / f32(128))).astype(f32)
    ang = (np.arange(SEQ, dtype=f32)[None, :] * inv_freq[:, None]).astype(f32)
    cos, sin = np.cos(ang).astype(f32), np.sin(ang).astype(f32)
    _HC["ropeC"] = np.ascontiguousarray(np.concatenate([cos, cos], 0))
    _HC["ropeS"] = np.ascontiguousarray(np.concatenate([-sin, sin], 0))
    es = np.zeros((16, 16, 128), f32)
    for j in range(16):
        es[j, j, :] = 1
    _HC["esel"] = es.astype(NPBF)
    s = np.arange(128)[:, None]
    t = np.arange(128)[None, :]
    _HC["hmask"] = ((s // 64 == t // 64) & (s <= t)).astype(f32).astype(NPBF)
    for h, w in enumerate(POOL_WINDOWS):
        M = np.zeros((128, 3, 128), f32)
        eye = (s == t).astype(f32)
        band = ((s <= t) & (s > t - w)).astype(f32)
        M[:, 0, :] = band / np.minimum(w, t + 1).astype(f32) - eye
        M[:, 1, :] = band / f32(w) - eye
        M[:, 2, :] = ((s - 128) > (t - w)).astype(f32) / f32(w)
        _HC[f"poolM{h}"] = M.astype(NPBF)
    return _HC


def fm_layout(w):
    K, N = w.shape
    return np.ascontiguousarray(w.reshape(K // 128, 128, N).transpose(1, 0, 2))


def mix_inputs(inp, l, c, hnT_b):
    b, h = c // 4, c % 4
    hc = host_consts()
    w_in = inp["w_in"][l]
    hs = slice(h * 128, (h + 1) * 128)

    def col(name):
        return w_in[:, IN_OFF[name] + h * 128: IN_OFF[name] + (h + 1) * 128]
    wm = np.concatenate([col(n) for n in ("mq", "mk", "mv", "fq", "fk", "fv", "hq", "hf", "hg", "hi", "pin")], axis=1)
    small = np.zeros((128, N_SMALL), np.float32)
    small[:, SM_WFF:SM_WFF + NCH] = w_in[:, IN_OFF["ff"] + h].reshape(NCH, 128).T
    small[:, SM_G0] = inp["hgrn_lb_logits"][0, hs]
    small[:, SM_G1] = inp["hgrn_lb_logits"][1, hs]
    small[:, SM_HNORM] = inp["hgrn_out_norm"][l, hs]
    small[:, SM_PSCALE] = inp["pool_scale"][l, hs]
    small[:, SM_FB] = inp["fox_f_bias"][l, h]
    return {"hnT": hnT_b, "wmix": fm_layout(wm), "small": small,
            "poolw": np.ascontiguousarray(inp["pool_w"][l, h]),
            "consts": hc["consts"], "c32": hc["c32"], "cmask": hc["cmask"], "perm": hc["perm"],
            "ropeC": hc["ropeC"], "ropeS": hc["ropeS"], "esel": hc["esel"], "hmask": hc["hmask"],
            "poolM": hc[f"poolM{h}"]}


GV_MIXPOST, GV_XAPRE, GV_XAMEM, GV_XAPOST, GV_MLPPRE, GV_MLPPOST, GV_NEXT = range(7)
HALF = 512
NMEM = 256
RING_N = 6
RING_ELEMS = 4096


class Ring:
    def __init__(self, cx, n=RING_N, elems=RING_ELEMS):
        self.cx = cx
        self.bufs = [cx.sb([128, elems], BF16, name=f"ring{i}") for i in range(n)]
        self.i = 0

    def load(self, src_ap, a, b):
        t = self.bufs[self.i % len(self.bufs)]
        self.i += 1
        view = t[:, 0:a * b].rearrange("p (a b) -> p a b", b=b)
        self.cx.load(view, src_ap, t.all, q="pool")
        return view, t.all


class TokIO:
    pass


def tok_io_external(cx):
    io = TokIO()
    hT_d = cx.din("hT", [D_MODEL, TOK], F32)
    hnT_d = cx.din("hnT", [D_MODEL, TOK], BF16)
    ybr_d = cx.din("ybr", [D_MODEL, TOK], BF16)
    tok_weight_inputs(cx, io, "")
    io.memT = cx.din("memT", [D_MODEL, NMEM], F32)
    hT_o = cx.dout("hT_out", [D_MODEL, TOK], F32)
    hn_o = cx.dout("hn_next", [D_MODEL, TOK], BF16)
    hview = hT_d.rearrange("(c p) t -> p c t", p=128)
    hnview = hnT_d.rearrange("(c p) t -> p c t", p=128)
    ybview = ybr_d.rearrange("(c p) t -> p c t", p=128)
    hoview = hT_o.rearrange("(c p) t -> p c t", p=128)
    hnoview = hn_o.rearrange("(c p) t -> p c t", p=128)
    io.hT_in = lambda hf, c0, c1: hview[:, c0:c1, hf * HALF:(hf + 1) * HALF]
    io.hn_load = lambda hf, j, dst, wb: cx.load(dst, hnview[:, :, hf * HALF + j * 256:hf * HALF + (j + 1) * 256], wb)
    io.ybr_load = lambda dst, hf, wb: cx.load(dst, ybview[:, :, hf * HALF:(hf + 1) * HALF], wb)
    io.hT_out = lambda hf, c0, c1: hoview[:, c0:c1, hf * HALF:(hf + 1) * HALF]
    io.hn_store = lambda hf, j, src, rb: cx.store(hnoview[:, :, hf * HALF + j * 256:hf * HALF + (j + 1) * 256], src, rb)
    io.h_is_output = True
    io.hn_is_output = True
    io.write_hn_when_last = True
    return io


def tok_weight_inputs(cx, io, sfx):
    io.wg_d = cx.din("wg" + sfx, [64, 128, NCH, 128], F32)
    io.wb_d = cx.din("wb" + sfx, [64, 128, 4, 128], F32)
    io.wo_d = cx.din("wo" + sfx, [16, 128, NCH, 128], F32)
    io.xq_d = cx.din("xq" + sfx, [4, 128, NCH, 128], F32)
    io.xk_d = cx.din("xk" + sfx, [4, 128, NCH, 128], F32)
    io.xv_d = cx.din("xv" + sfx, [2, 128, 8, 512], F32)
    io.xo_d = cx.din("xo" + sfx, [16, 128, 4, 128], F32)
    io.wup_d = cx.din("wup" + sfx, [64, 128, NCH, 128], F32)
    io.wdn_d = cx.din("wdn" + sfx, [32, 128, 32, 128], F32)
    io.gv_d = cx.din("gv" + sfx, [128, 7 * NCH], F32)


def build_tok(layer, last):
    cx = Ctx()
    io = tok_io_external(cx)
    consts = cx.din("consts", [128, 256], BF16)
    cm = Common(cx, consts)
    add_eps(cx, cm)
    phase_tok(cx, cm, io, layer, last)
    cx.S.emit()
    return cx.nc


def phase_tok(cx, cm, io, layer, last):
    S = cx.S
    wg_d, wb_d, wo_d, xq_d, xk_d, xv_d, xo_d, wup_d, wdn_d = (io.wg_d, io.wb_d, io.wo_d, io.xq_d, io.xk_d, io.xv_d,
                                                              io.xo_d, io.wup_d, io.wdn_d)
    memT_d, gv_d = io.memT, io.gv_d
    gv = cx.sb([128, 7 * NCH], F32, name="gv")
    cx.load(gv[:, :], gv_d, gv.all)

    def gcol(which, c):
        return gv[:, which * NCH + c: which * NCH + c + 1]

    ring = Ring(cx)
    hT = cx.sb([128, NCH, HALF], F32, nsub=NCH, name="hT")
    hnb = cx.sb([128, NCH, HALF], BF16, name="hnb")
    RA = cx.sb([128, 32, HALF], BF16, name="RA")
    ybr = RA[:, 0:NCH, :]
    sT = RA[:, NCH:2 * NCH, :]
    aT = cx.sb([128, NCH, HALF], F32, nsub=NCH, name="aT")
    sqp = [cx.sb([128, HALF], BF16, name=f"sq{i}") for i in range(3)]
    rstd = cx.sb([128, HALF], F32, name="rstd")
    tmpA = [cx.sb([128, HALF], F32, name=f"tmpA{i}") for i in range(2)]
    tmpB = [cx.sb([128, HALF], F32, name=f"tmpB{i}") for i in range(2)]
    acc = cx.sb([128, HALF], F32, name="acc")
    kx = cx.sb([128, 4, NMEM], BF16, name="kx")
    vx = cx.sb([128, 2, 512], BF16, name="vx")
    qx = cx.sb([128, 4, HALF], BF16, name="qx")
    ox = cx.sb([128, 4, HALF], BF16, name="ox")
    pTs = [cx.sb([128, HALF], BF16, name=f"xpT{i}") for i in range(2)]
    rc = cx.sb([128, HALF], F32, name="xrc")
    B = cm.banks

    def norm_add(gw):
        rms_stats(cx, cm, lambda c: aT[:, c, :], lambda c: [aT.b[c]], NCH, HALF, sqp, B[6], rstd, D_MODEL)
        for c in range(NCH):
            t = tmpA[c % 2]
            cx.stt(t[:, :], aT[:, c, :], gcol(gw, c), rstd[:, :], ALU.mult, ALU.mult,
                   [aT.b[c]] + gv.all + rstd.all, t.all)
            cx.tt(hT[:, c, :], hT[:, c, :], t[:, :], ALU.add, [hT.b[c]] + t.all, [hT.b[c]])

    def norm_to(gw, dst, dst_bufs):
        rms_stats(cx, cm, lambda c: hT[:, c, :], lambda c: [hT.b[c]], NCH, HALF, sqp, B[6], rstd, D_MODEL)
        for c in range(NCH):
            cx.stt(dst[:, c, :], hT[:, c, :], gcol(gw, c), rstd[:, :], ALU.mult, ALU.mult,
                   [hT.b[c]] + gv.all + rstd.all, dst_bufs)

    memf = aT
    cx.load(memf[:, :, 0:NMEM], memT_d.rearrange("(c p) t -> p c t", p=128), memf.all)
    rms_stats(cx, cm, lambda c: memf[:, c, 0:NMEM], lambda c: [memf.b[c]], NCH, NMEM, sqp, B[6], rstd, D_MODEL)
    memn = hnb
    for c in range(NCH):
        cx.stt(memn[:, c, 0:NMEM], memf[:, c, 0:NMEM], gcol(GV_XAMEM, c), rstd[:, 0:NMEM], ALU.mult, ALU.mult,
               [memf.b[c]] + gv.all + rstd.all, memn.all)
    for hd in range(4):
        wv_, wb_ = ring.load(xk_d[hd], NCH, 128)
        bk = B[hd % 2]
        for c in range(NCH):
            cx.mm(bk[:, 0:NMEM], wv_[:, c, :], memn[:, c, 0:NMEM], c == 0, c == NCH - 1, wb_ + memn.all, bk.all)
        cx.copy(kx[:, hd, :], bk[:, 0:NMEM], bk.all, kx.all)
    wv0, wb0 = ring.load(xv_d[0], 8, 512)
    wv1, wb1 = ring.load(xv_d[1], 8, 512)
    for mt in range(2):
        bk = B[2 + mt]
        for c in range(NCH):
            wv_, wb_ = (wv0, wb0) if c < 8 else (wv1, wb1)
            cx.mm(bk[:, :], memn[:, c, mt * 128:(mt + 1) * 128], wv_[:, c % 8, :], c == 0, c == NCH - 1,
                  wb_ + memn.all, bk.all)
        cx.copy(vx[:, mt, :], bk[:, :], bk.all, vx.all, eng="act")

    for hf in range(TOK // HALF):
        tsl = slice(hf * HALF, (hf + 1) * HALF)
        for c in range(0, NCH, 4):
            cx.load(hT[:, c:c + 4, :], io.hT_in(hf, c, c + 4), hT.b[c:c + 4])
        for j in range(2):
            io.hn_load(hf, j, hnb[:, :, j * 256:(j + 1) * 256], hnb.all)
        io.ybr_load(ybr, hf, RA.all)
        it = 0
        for j in range(NCH):
            for n in range(4):
                wg_, wgb = ring.load(wg_d[j * 4 + n], NCH, 128)
                wb_, wbb = ring.load(wb_d[j * 4 + n], 4, 128)
                bg, bp = B[it % 2], B[2 + it % 2]
                it += 1
                for c in range(NCH):
                    cx.mm(bg[:, :], wg_[:, c, :], hnb[:, c, :], c == 0, c == NCH - 1, wgb + hnb.all, bg.all)
                for hh in range(4):
                    cx.mm(bp[:, :], wb_[:, hh, :], ybr[:, n * 4 + hh, :], hh == 0, hh == 3, wbb + RA.all, bp.all)
                sg_ = tmpA[it % 2]
                cx.activation(sg_[:, :], bg[:, :], AF.Sigmoid, bg.all, sg_.all)
                if n == 0:
                    cx.tt(acc[:, :], bp[:, :], sg_[:, :], ALU.mult, bp.all + sg_.all, acc.all)
                else:
                    t2 = tmpB[it % 2]
                    cx.tt(t2[:, :], bp[:, :], sg_[:, :], ALU.mult, bp.all + sg_.all, t2.all)
                    if n < 3:
                        cx.tt(acc[:, :], acc[:, :], t2[:, :], ALU.add, acc.all + t2.all, acc.all)
                    else:
                        cx.tt(sT[:, j, :], acc[:, :], t2[:, :], ALU.add, acc.all + t2.all, RA.all)
        if getattr(io, "after_gates", None) is not None:
            io.after_gates(hf)
        for j in range(NCH):
            w_, wbf = ring.load(wo_d[j], NCH, 128)
            bk = B[j % 2]
            for c in range(NCH):
                cx.mm(bk[:, :], w_[:, c, :], sT[:, c, :], c == 0, c == NCH - 1, wbf + RA.all, bk.all)
            cx.copy(aT[:, j, :], bk[:, :], bk.all, [aT.b[j]], eng="act" if j % 2 else "dve")
        norm_add(GV_MIXPOST)
        norm_to(GV_XAPRE, hnb, hnb.all)
        for hd in range(4):
            w_, wbf = ring.load(xq_d[hd], NCH, 128)
            bk = B[hd % 2]
            for c in range(NCH):
                cx.mm(bk[:, :], w_[:, c, :], hnb[:, c, :], c == 0, c == NCH - 1, wbf + hnb.all, bk.all)
            S.act(lambda e, hd=hd, bk=bk: e.mul(out=qx[:, hd, :], in_=bk[:, :], mul=SCALE), bk.all, qx.all)
        for hd in range(4):
            ob, db = B[2 + hd % 2], B[4 + hd % 2]
            for mt in range(2):
                sb_ = B[mt]
                cx.mm(sb_[:, :], kx[:, hd, mt * 128:(mt + 1) * 128], qx[:, hd, :], True, True, kx.all + qx.all, sb_.all)
                pT = pTs[mt]
                cx.activation(pT[:, :], sb_[:, :], AF.Exp, sb_.all, pT.all)
                cx.mm(ob[:, :], vx[:, mt, hd * 128:(hd + 1) * 128], pT[:, :], mt == 0, mt == 1, vx.all + pT.all, ob.all)
                cx.mm(db[:, :], cm.ones, pT[:, :], mt == 0, mt == 1, pT.all, db.all)
            S.dve(lambda e, db=db: e.reciprocal(out=rc[:, :], in_=db[:, :]), db.all, rc.all)
            cx.tt(ox[:, hd, :], ob[:, :], rc[:, :], ALU.mult, ob.all + rc.all, ox.all)
        for j in range(NCH):
            w_, wbf = ring.load(xo_d[j], 4, 128)
            bk = B[j % 2]
            for hd in range(4):
                cx.mm(bk[:, :], w_[:, hd, :], ox[:, hd, :], hd == 0, hd == 3, wbf + ox.all, bk.all)
            cx.copy(aT[:, j, :], bk[:, :], bk.all, [aT.b[j]], eng="act" if j % 2 else "dve")
        norm_add(GV_XAPOST)
        norm_to(GV_MLPPRE, hnb, hnb.all)
        uT = RA
        for fh in range(2):
            for fb in range(32):
                w_, wbf = ring.load(wup_d[fh * 32 + fb], NCH, 128)
                bk = B[fb % 4]
                for c in range(NCH):
                    cx.mm(bk[:, :], w_[:, c, :], hnb[:, c, :], c == 0, c == NCH - 1, wbf + hnb.all, bk.all)
                r_ = tmpA[fb % 2]
                cx.activation(r_[:, :], bk[:, :], AF.Relu, bk.all, r_.all)
                cx.tt(uT[:, fb, :], r_[:, :], r_[:, :], ALU.mult, r_.all, RA.all, eng="dve")
            for j in range(NCH):
                w_, wbf = ring.load(wdn_d[j * 2 + fh], 32, 128)
                bk = B[4 + j % 2]
                for fb in range(32):
                    cx.mm(bk[:, :], w_[:, fb, :], uT[:, fb, :], fb == 0, fb == 31, wbf + RA.all, bk.all)
                if fh == 0:
                    cx.copy(aT[:, j, :], bk[:, :], bk.all, [aT.b[j]], eng="act")
                else:
                    cx.tt(aT[:, j, :], bk[:, :], aT[:, j, :], ALU.add, bk.all + [aT.b[j]], [aT.b[j]])
        norm_add(GV_MLPPOST)
        for c in range(0, NCH, 4):
            cx.store(io.hT_out(hf, c, c + 4), hT[:, c:c + 4, :], hT.b[c:c + 4], is_output=io.h_is_output)
        if not last:
            norm_to(GV_NEXT, hnb, hnb.all)
            for j in range(2):
                io.hn_store(hf, j, hnb[:, :, j * 256:(j + 1) * 256], hnb.all)
        elif io.write_hn_when_last:
            for j in range(2):
                io.hn_store(hf, j, hnb[:, :, j * 256:(j + 1) * 256], hnb.all)


def tok_weights(inp, l):
    f32 = np.float32
    wg = inp["w_in"][l][:, IN_OFF["gl"]:]
    wg = wg.reshape(NCH, 128, 4, NCH, 128).transpose(3, 2, 1, 0, 4)
    wg = np.ascontiguousarray(wg).reshape(64, 128, NCH, 128)
    wb = inp["w_branch"][l].reshape(4, 4, 128, NCH, 128).transpose(3, 0, 2, 1, 4)
    wb = np.ascontiguousarray(wb).reshape(64, 128, 4, 128)

    def tiles(w, kc):
        K_, N_ = w.shape
        return np.ascontiguousarray(w.reshape(kc, 128, N_ // 128, 128).transpose(2, 1, 0, 3))
    wo = tiles(inp["w_mix_out"][l], NCH)
    xq = tiles(inp["xa_wq"][l], NCH)
    xk = tiles(inp["xa_wkv"][l][:, 0:512], NCH)
    wv = inp["xa_wkv"][l][:, 512:1024].reshape(2, 8, 128, 512).transpose(0, 2, 1, 3)
    xv = np.ascontiguousarray(wv)
    xo = tiles(inp["xa_wo"][l], 4)
    wup = tiles(inp["mlp_w_up"][l], NCH)
    wd = inp["mlp_w_down"][l].reshape(2, 32, 128, NCH, 128).transpose(3, 0, 2, 1, 4)
    wdn = np.ascontiguousarray(wd).reshape(32, 128, 32, 128)
    gv = np.zeros((128, 7 * NCH), f32)
    names = ["mix_norm_post", "xa_norm_pre", "xa_norm_mem", "xa_norm_post", "mlp_norm_pre", "mlp_norm_post"]
    for i, nme in enumerate(names):
        gv[:, i * NCH:(i + 1) * NCH] = inp[nme][l].reshape(NCH, 128).T
    if l + 1 < inp["mix_norm_pre"].shape[0]:
        gv[:, GV_NEXT * NCH:(GV_NEXT + 1) * NCH] = inp["mix_norm_pre"][l + 1].reshape(NCH, 128).T
    return {"wg": wg, "wb": wb, "wo": wo, "xq": xq, "xk": xk, "xv": xv, "xo": xo, "wup": wup, "wdn": wdn,
            "gv": gv, "consts": host_consts()["consts"]}


_PROGS = {}


def _prog(key, builder):
    if key not in _PROGS:
        _PROGS[key] = builder()
    return _PROGS[key]


def kernel_unfused(**inp):
    inp = {k: np.asarray(v) for k, v in inp.items()}
    depth = inp["w_in"].shape[0]
    cores = list(range(8))
    hc = host_consts()
    x = inp["x"]
    tsl = [slice((c % 4) * TOK, (c % 4 + 1) * TOK) for c in cores]
    hT = [np.ascontiguousarray(x[c // 4, tsl[c]].T) for c in cores]
    memT = [np.ascontiguousarray(inp["mem"][b].T) for b in range(BATCH)]
    g0 = np.ascontiguousarray(inp["mix_norm_pre"][0].reshape(NCH, 128).T)
    res = run_bass_kernel_spmd(_prog("pre", build_pre),
                               [{"xT": hT[c], "gpre": g0, "consts": hc["consts"]} for c in cores], core_ids=cores)
    hn = [np.asarray(res.results[c]["hn_out"]) for c in cores]
    for l in range(depth):
        hn_full = [np.ascontiguousarray(np.concatenate([hn[b * 4 + q] for q in range(4)], axis=1)) for b in range(BATCH)]
        res = run_bass_kernel_spmd(_prog(("mix", l), lambda: build_mix(l)),
                                   [mix_inputs(inp, l, c, hn_full[c // 4]) for c in cores], core_ids=cores)
        yT = [np.asarray(res.results[c]["yT"]) for c in cores]
        tw = tok_weights(inp, l)
        maps = []
        for c in cores:
            b = c // 4
            yb = np.stack([yT[b * 4 + h][:, :, tsl[c]] for h in range(4)], axis=1)
            m = dict(tw)
            m["hT"] = hT[c]
            m["hnT"] = hn[c]
            m["ybr"] = np.ascontiguousarray(yb.reshape(D_MODEL, TOK))
            m["memT"] = memT[b]
            maps.append(m)
        res = run_bass_kernel_spmd(_prog(("tok", l), lambda: build_tok(l, l == depth - 1)), maps, core_ids=cores)
        hT = [np.asarray(res.results[c]["hT_out"]) for c in cores]
        hn = [np.asarray(res.results[c]["hn_next"]) for c in cores]
    out = np.empty_like(x)
    for c in cores:
        out[c // 4, tsl[c]] = hT[c].T
    return out


def fused_inputs(inp, c):
    depth = inp["w_in"].shape[0]
    b, q = c // 4, c % 4
    hc = host_consts()
    m = {"consts": hc["consts"], "c32": hc["c32"], "cmask": hc["cmask"], "esel": hc["esel"], "hmask": hc["hmask"],
         "perm": hc["perm"], "ropeC": hc["ropeC"], "ropeS": hc["ropeS"], "poolM": hc[f"poolM{q}"],
         "xT": np.ascontiguousarray(inp["x"][b, q * TOK:(q + 1) * TOK].T),
         "memT": np.ascontiguousarray(inp["mem"][b].T),
         "gpre": np.ascontiguousarray(inp["mix_norm_pre"][0].reshape(NCH, 128).T)}
    for l in range(depth):
        mi = mix_inputs(inp, l, c, None)
        m[f"wmix_{l}"] = mi["wmix"]
        m[f"small_{l}"] = mi["small"]
        m[f"poolw_{l}"] = mi["poolw"]
    return m


def kernel(**inp):
    inp = {k: np.asarray(v) for k, v in inp.items()}
    depth = inp["w_in"].shape[0]
    cores = list(range(8))
    nc = _prog(("fused", depth), lambda: build_fused(depth))
    tws = []
    for l in range(depth):
        tw = tok_weights(inp, l)
        tw.pop("consts")
        tws.append({f"{k}_{l}": v for k, v in tw.items()})
    maps = []
    for c in cores:
        m = fused_inputs(inp, c)
        for tw in tws:
            m.update(tw)
        maps.append(m)
    res = run_bass_kernel_spmd(nc, maps, core_ids=cores)
    x = inp["x"]
    out = np.empty_like(x)
    for c in cores:
        out[c // 4, (c % 4) * TOK:(c % 4 + 1) * TOK] = np.asarray(res.results[c]["outT"]).T
    return out
```

```python
import contextlib
import numpy as np
import ml_dtypes
import concourse.bass as bass
import concourse.mybir as mybir
from concourse.bass_utils import run_bass_kernel_spmd

F32 = mybir.dt.float32
BF16 = mybir.dt.bfloat16
AF = mybir.ActivationFunctionType
ALU = mybir.AluOpType
AX = mybir.AxisListType
NPBF = ml_dtypes.bfloat16

D_MODEL = 2048
NCH = 16
BATCH = 2
SEQ = 4096
TOK = 1024
HD = 128
NEG = -30000.0
EPS = 1e-6
SCALE = HD ** -0.5
POOL_WINDOWS = (2, 4, 8, 16)
DEBUG_STOP = 0
ROPE_ADD_ENG = "dve"

COMPUTE = ("pe", "act", "dve", "pool")
N_DMA_SEMS = 12


class Buf:
    __slots__ = ("name", "w", "r", "excl")

    def __init__(self, name="", excl=False):
        self.name = name
        self.w = None
        self.r = []
        self.excl = excl


class Op:
    __slots__ = ("eng", "fn", "waits", "sig", "idx", "dma", "dma_ev")

    def __init__(self, eng, fn, sig, dma):
        self.eng = eng
        self.fn = fn
        self.waits = []
        self.sig = sig
        self.dma = dma
        self.dma_ev = None


class Sched:
    def __init__(self, nc):
        self.nc = nc
        self.ops = {e: [] for e in COMPUTE + ("sp",)}
        self.dma_count = {e: 0 for e in ("sp", "act", "pool")}
        self.out_events = []
        self.fence_waits = {e: [] for e in COMPUTE + ("sp",)}
        self.n_cc = 0

    def collective(self, fn, reads, writes):
        op = Op("pool", fn, False, False)
        op.dma = "cc"
        lst = self.ops["pool"]
        op.idx = len(lst)
        lst.append(op)
        ev = ("x", self.n_cc)
        op.dma_ev = ev
        self.n_cc += 1
        if self.fence_waits["pool"]:
            op.waits.extend(self.fence_waits["pool"])
            self.fence_waits["pool"] = []
        for b in reads:
            if b.w is not None:
                op.waits.append(b.w)
        for b in writes:
            if b.w is not None:
                op.waits.append(b.w)
            op.waits.extend(b.r)
        for d in op.waits:
            if d[0] == "c":
                self.ops[d[1]][d[2]].sig = True
        for b in reads:
            b.r.append(ev)
        for b in writes:
            b.w = ev
            b.r = []
        return ev

    def fence(self):
        evs = []
        for e in COMPUTE + ("sp",):
            lst = self.ops[e]
            for op in reversed(lst):
                if not op.dma:
                    evs.append(("c", e, op.idx))
                    op.sig = True
                    break
        for q, n in self.dma_count.items():
            for i in range(max(0, n - N_DMA_SEMS), n):
                evs.append(("d", q, i))
        for i in range(self.n_cc):
            evs.append(("x", i))
        for e in self.fence_waits:
            self.fence_waits[e] = list(evs)

    def _add(self, eng, fn, reads, writes, sig=False, dma=False):
        op = Op(eng, fn, sig, dma)
        lst = self.ops[eng]
        op.idx = len(lst)
        lst.append(op)
        if dma:
            n = self.dma_count[eng]
            self.dma_count[eng] += 1
            ev = ("d", eng, n)
            op.dma_ev = ev
            if n >= N_DMA_SEMS:
                op.waits.append(("d", eng, n - N_DMA_SEMS))
        else:
            ev = ("c", eng, op.idx)
        if self.fence_waits[eng]:
            op.waits.extend(d for d in self.fence_waits[eng] if d != ev)
            self.fence_waits[eng] = []
        excl_reads = [b for b in reads if b.excl]
        if excl_reads:
            reads = [b for b in reads if not b.excl]
            writes = list(writes) + excl_reads
        for b in reads:
            if b.w is not None and b.w != ev:
                op.waits.append(b.w)
        for b in writes:
            if b.w is not None and b.w != ev:
                op.waits.append(b.w)
            for d in b.r:
                if d != ev:
                    op.waits.append(d)
        if eng == "pe" and not dma:
            op.waits = [d for d in op.waits if not (d[0] == "c" and d[1] == "pe")]
        for d in op.waits:
            if d[0] == "c":
                self.ops[d[1]][d[2]].sig = True
        for b in reads:
            if not dma:
                b.r = [d for d in b.r if not (d[0] == "c" and d[1] == eng)]
            b.r.append(ev)
        for b in writes:
            b.w = ev
            b.r = []
        return ev

    def pe(self, fn, reads, writes, sig=False):
        return self._add("pe", fn, reads, writes, sig)

    def act(self, fn, reads, writes):
        return self._add("act", fn, reads, writes)

    def dve(self, fn, reads, writes):
        return self._add("dve", fn, reads, writes)

    def pool(self, fn, reads, writes):
        return self._add("pool", fn, reads, writes)

    def dma(self, q, fn, reads, writes, is_output=False):
        ev = self._add(q, fn, reads, writes, dma=True)
        if is_output:
            self.out_events.append(ev)
        return ev

    def emit(self):
        nc = self.nc
        with contextlib.ExitStack() as st:
            csem = {e: st.enter_context(nc.semaphore("c_" + e)) for e in COMPUTE}
            dsem = {q: [st.enter_context(nc.semaphore(f"d_{q}{i}")) for i in range(N_DMA_SEMS)]
                    for q in ("sp", "act", "pool")}
            xsem = [st.enter_context(nc.semaphore(f"x_{i}")) for i in range(self.n_cc)]
            block = st.enter_context(nc.Block())
            sigcount = {}
            for e in COMPUTE:
                lst = self.ops[e]
                last = None
                for op in lst:
                    if not op.dma:
                        last = op
                if last is not None:
                    last.sig = True
                c = 0
                arr = []
                for op in lst:
                    if (not op.dma) and op.sig:
                        c += 1
                    arr.append(c)
                sigcount[e] = arr
            self.stats = {e: (len(self.ops[e]), sigcount[e][-1] if sigcount[e] else 0) for e in COMPUTE}
            self.stats["dma"] = dict(self.dma_count)

            def resolve(ev):
                if ev[0] == "c":
                    _, e, i = ev
                    op = self.ops[e][i]
                    v = sigcount[e][i]
                    assert op.sig
                    return ("c", e), csem[e], v
                if ev[0] == "x":
                    return ev, xsem[ev[1]], 1
                _, q, n = ev
                return ("d", q, n % N_DMA_SEMS), dsem[q][n % N_DMA_SEMS], 16 * (n // N_DMA_SEMS + 1)

            def run_engine(ename, eng):
                known = {}
                for op in self.ops[ename]:
                    need = {}
                    for ev in op.waits:
                        key, sem, v = resolve(ev)
                        if known.get(key, 0) >= v:
                            continue
                        if need.get(key, (None, 0))[1] < v:
                            need[key] = (sem, v)
                    for key, (sem, v) in need.items():
                        eng.wait_ge(sem, v)
                        known[key] = v
                    ins = op.fn(eng)
                    if op.dma == "cc":
                        ins.then_inc(xsem[op.dma_ev[1]])
                    elif op.dma:
                        _, q, n = op.dma_ev
                        ins.then_inc(dsem[q][n % N_DMA_SEMS], 16)
                    elif op.sig:
                        ins.then_inc(csem[ename], 1)
                if ename == "sp":
                    for ev in self.out_events:
                        key, sem, v = resolve(ev)
                        if known.get(key, 0) >= v:
                            continue
                        eng.wait_ge(sem, v)
                        known[key] = v

            @block.sync
            def _(eng):
                if getattr(self, "sp_init", None) is not None:
                    self.sp_init(eng)
                run_engine("sp", eng)

            @block.tensor
            def _(eng):
                run_engine("pe", eng)

            @block.scalar
            def _(eng):
                run_engine("act", eng)

            @block.vector
            def _(eng):
                run_engine("dve", eng)

            @block.gpsimd
            def _(eng):
                run_engine("pool", eng)


class T:
    def __init__(self, t, nsub=1, name="", psum=False):
        self.t = t
        if psum:
            self.b = [Buf(name, excl=True)] * nsub
        else:
            self.b = [Buf(f"{name}{i}") for i in range(nsub)]

    def __getitem__(self, k):
        return self.t[k]

    @property
    def all(self):
        return list(self.b)


class Ctx:
    def __init__(self):
        self.nc = bass.Bass("TRN2", target_bir_lowering=False)
        self.S = Sched(self.nc)
        self._n = 0
        self.scope = None

    def sb(self, shape, dt, nsub=1, name=None):
        self._n += 1
        name = (name or "sb") + f"_{self._n}"
        if self.scope is not None:
            return T(self.scope.enter_context(self.nc.sbuf_tensor(name, list(shape), dt)), nsub, name)
        return T(self.nc.alloc_sbuf_tensor(name, list(shape), dt), nsub, name)

    def ps(self, shape, dt=F32, nsub=1, name=None):
        self._n += 1
        name = name or f"ps{self._n}"
        return T(self.nc.alloc_psum_tensor(name, list(shape), dt), nsub, name, psum=True)

    def din(self, name, shape, dt):
        return self.nc.dram_tensor(name, list(shape), dt, kind="ExternalInput").ap()

    def dout(self, name, shape, dt):
        return self.nc.dram_tensor(name, list(shape), dt, kind="ExternalOutput").ap()

    def load(self, dst_ap, src_ap, wbufs, q="sp", rbufs=()):
        return self.S.dma(q, lambda e: e.dma_start(out=dst_ap, in_=src_ap), list(rbufs), list(wbufs))

    def store(self, dst_ap, src_ap, rbufs, q="sp", is_output=True, wbufs=()):
        return self.S.dma(q, lambda e: e.dma_start(out=dst_ap, in_=src_ap), list(rbufs), list(wbufs), is_output=is_output)

    def mm(self, out_ap, lhsT, rhs, start, stop, reads, writes, sig=None):
        if sig is None:
            sig = False
        return self.S.pe(lambda e: e.matmul(out_ap, lhsT, rhs, start=start, stop=stop), reads, writes, sig=sig)

    def transpose(self, out_ap, in_ap, ident_ap, reads, writes):
        return self.S.pe(lambda e: e.transpose(out_ap, in_ap, ident_ap), reads, writes)

    def activation(self, out_ap, in_ap, func, reads, writes, bias=None, scale=None):
        kw = {}
        if bias is not None:
            kw["bias"] = bias
        if scale is not None:
            kw["scale"] = scale
        return self.S.act(lambda e: e.activation(out=out_ap, in_=in_ap, func=func, **kw), reads, writes)

    def tt(self, out_ap, in0, in1, op, reads, writes, eng="dve"):
        f = lambda e: e.tensor_tensor(out=out_ap, in0=in0, in1=in1, op=op)
        return (self.S.dve if eng == "dve" else self.S.pool)(f, reads, writes)

    def ts(self, out_ap, in0, s1, s2, op0, op1, reads, writes, eng="dve"):
        if op1 is None:
            f = lambda e: e.tensor_scalar(out=out_ap, in0=in0, scalar1=s1, scalar2=None, op0=op0)
        else:
            f = lambda e: e.tensor_scalar(out=out_ap, in0=in0, scalar1=s1, scalar2=s2, op0=op0, op1=op1)
        return (self.S.dve if eng == "dve" else self.S.pool)(f, reads, writes)

    def stt(self, out_ap, in0, scalar, in1, op0, op1, reads, writes):
        return self.S.dve(lambda e: e.scalar_tensor_tensor(out=out_ap, in0=in0, scalar=scalar, in1=in1,
                                                            op0=op0, op1=op1), reads, writes)

    def copy(self, out_ap, in_ap, reads, writes, eng="dve"):
        if eng == "act":
            return self.S.act(lambda e: e.copy(out=out_ap, in_=in_ap), reads, writes)
        f = lambda e: e.tensor_copy(out=out_ap, in_=in_ap)
        return (self.S.dve if eng == "dve" else self.S.pool)(f, reads, writes)

    def memset(self, ap, val, writes, eng="pool"):
        f = lambda e: e.memset(ap, val)
        return (self.S.dve if eng == "dve" else self.S.pool)(f, [], writes)


class Common:
    def __init__(self, cx, consts_ap):
        self.cx = cx
        self.cb = cx.sb([128, 256], BF16, name="cbf")
        cx.load(self.cb[:, :], consts_ap, self.cb.all)
        self.ident = self.cb[:, 0:128]
        self.ones = self.cb[:, 128:256]
        self.banks = [cx.ps([128, 512], F32, nsub=4, name=f"bank{i}") for i in range(7)]
        self.bankb = cx.ps([128, 1024], BF16, nsub=8, name="bankb")


def rms_stats(cx, cm, src_fn, src_bufs, nfeat_chunks, ncols, sq_pool, bank, rstd, denom):
    for c in range(nfeat_chunks):
        sq = sq_pool[c % len(sq_pool)]
        cx.activation(sq[:, 0:ncols], src_fn(c), AF.Square, src_bufs(c), sq.all)
        cx.mm(bank[:, 0:ncols], cm.ones, sq[:, 0:ncols], c == 0, c == nfeat_chunks - 1,
              [cm.cb.b[0]] + sq.all, bank.all)
    cx.activation(rstd[:, 0:ncols], bank[:, 0:ncols], AF.Ln, bank.all + cm.epst.all, rstd.all, bias=cm.eps_ap,
                  scale=1.0 / denom)
    cx.activation(rstd[:, 0:ncols], rstd[:, 0:ncols], AF.Exp, rstd.all, rstd.all, scale=-0.5)


def add_eps(cx, cm):
    cm.epst = cx.sb([128, 1], F32, name="epst")
    cx.memset(cm.epst[:, :], EPS, cm.epst.all)
    cm.eps_ap = cm.epst[:, 0:1]


def phase_pre(cx, cm, xT, gpre, hn_out_fn, is_output):
    g = cx.sb([128, NCH], F32, name="g")
    cx.load(g[:, :], gpre, g.all)
    xv = xT.rearrange("(c p) t -> p c t", p=128)
    xs = [cx.sb([128, NCH, 512], F32, nsub=NCH, name=f"xs{i}") for i in range(2)]
    hs = [cx.sb([128, NCH, 512], BF16, nsub=1, name=f"hs{i}") for i in range(2)]
    sqp = [cx.sb([128, 512], BF16, name=f"sq{i}") for i in range(3)]
    rstd = cx.sb([128, 512], F32, name="rstd")
    for n in range(TOK // 512):
        x = xs[n % 2]
        h = hs[n % 2]
        cx.load(x[:, :, :], xv[:, :, n * 512:(n + 1) * 512], x.all)
        rms_stats(cx, cm, lambda c: x[:, c, :], lambda c: [x.b[c]], NCH, 512, sqp, cm.banks[n % 2], rstd, D_MODEL)
        for c in range(NCH):
            cx.stt(h[:, c, :], x[:, c, :], g[:, c:c + 1], rstd[:, :], ALU.mult, ALU.mult,
                   [x.b[c]] + g.all + rstd.all, h.all)
        for j in range(2):
            hn_out_fn(n, j, h[:, :, j * 256:(j + 1) * 256], h.all)


def build_pre():
    cx = Ctx()
    xT = cx.din("xT", [D_MODEL, TOK], F32)
    gpre = cx.din("gpre", [128, NCH], F32)
    consts = cx.din("consts", [128, 256], BF16)
    hn_out = cx.dout("hn_out", [D_MODEL, TOK], BF16)
    cm = Common(cx, consts)
    add_eps(cx, cm)
    ov = hn_out.rearrange("(c p) t -> p c t", p=128)
    phase_pre(cx, cm, xT, gpre,
              lambda n, j, src, rb: cx.store(ov[:, :, n * 512 + j * 256:n * 512 + (j + 1) * 256], src, rb), True)
    cx.S.emit()
    return cx.nc


W_OFF = {"moba": (0, 384), "fox": (384, 384), "hgrn": (768, 512), "pool": (1280, 128)}
W_MIXCOLS = 1408
SM_WFF, SM_G0, SM_G1, SM_HNORM, SM_PSCALE, SM_FB = 0, 16, 17, 18, 19, 20
N_SMALL = 24


class MixEnv:
    pass


def project(cx, cm, env, w, fm_blocks, tm, row=None, pre_chunk=None):
    nb = 0
    if getattr(env, "flush_cc", None) is not None:
        env.flush_cc()
    for n in range(SEQ // 512):
        hs = env.hs[n % 2]
        for j in range(2):
            cx.load(hs[:, :, j * 256:(j + 1) * 256], env.hn_chunk(n, j), hs.all, rbufs=env.hn_rbufs(n, j))
        if pre_chunk is not None:
            pre_chunk(n)
        for (c0, handler) in fm_blocks:
            bank = cm.banks[nb % 4]
            nb += 1
            for c in range(NCH):
                cx.mm(bank[:, :], w[:, c, c0:c0 + 128], hs[:, c, :], c == 0, c == NCH - 1,
                      w.all + hs.all, bank.all)
            handler(n, bank)
        if tm is not None:
            c0, ncols, handler = tm
            for tl in range(4):
                bank = cm.banks[nb % 4]
                nb += 1
                for c in range(NCH):
                    cx.mm(bank[:, 0:ncols], hs[:, c, tl * 128:(tl + 1) * 128], w[:, c, c0:c0 + ncols],
                          c == 0, c == NCH - 1, w.all + hs.all, bank.all)
                handler(n * 4 + tl, bank)
        if row is not None:
            lfn, rreads, handler = row
            bank = cm.banks[nb % 4]
            nb += 1
            for c in range(NCH):
                cx.mm(bank[0:1, :], lfn(c), hs[:, c, :], c == 0, c == NCH - 1, rreads + hs.all, bank.all)
            handler(n, bank)


def load_w(cx, env, name, eng="pool"):
    c0, nc_ = W_OFF[name]
    w = cx.sb([128, NCH, nc_], BF16, name="w_" + name)
    for c in range(0, NCH, 4):
        cx.load(w[:, c:c + 4, :], env.wmix[:, c:c + 4, c0:c0 + nc_], w.all, q=eng)
    return w


def softmax_finish(cx, env, obank, dbank, ncols, out_ap_dram, k):
    rc = env.rc[k % 2]
    yt = env.yt[k % 2]
    cx.S.dve(lambda e: e.reciprocal(out=rc[:, 0:ncols], in_=dbank[:, 0:ncols]), dbank.all, rc.all)
    cx.tt(yt[:, 0:ncols], obank[:, 0:ncols], rc[:, 0:ncols], ALU.mult, obank.all + rc.all, yt.all)
    cx.store(out_ap_dram, yt[:, 0:ncols], yt.all, is_output=env.y_is_output)


def mix_fox(cx, cm, env):
    S = cx.S
    w = load_w(cx, env, "fox")
    fQ = cx.sb([128, SEQ], BF16, name="fQ")
    fK = cx.sb([128, SEQ], BF16, name="fK")
    fV = cx.sb([128, 32, 128], BF16, name="fV")
    wff = cx.sb([128, NCH], BF16, name="wff")
    cx.copy(wff[:, :], env.small[:, SM_WFF:SM_WFF + NCH], env.small.all, wff.all)
    nfb = cx.sb([128, 1], F32, name="nfb")
    cx.ts(nfb[:, :], env.small[:, SM_FB:SM_FB + 1], -1.0, None, ALU.mult, None, env.small.all, nfb.all)
    sprow = cx.sb([1, SEQ], F32, name="sprow")
    cprow = cx.sb([1, SEQ], F32, name="cprow")
    nrh = cx.sb([1, SEQ], BF16, name="nrh")
    nrl = cx.sb([1, SEQ], BF16, name="nrl")
    rowtmp = cx.sb([1, 512], F32, name="rowtmp")

    def h_q(n, bank):
        S.act(lambda e: e.mul(out=fQ[:, n * 512:(n + 1) * 512], in_=bank[:, :], mul=SCALE), bank.all, fQ.all)

    def h_k(n, bank):
        cx.copy(fK[:, n * 512:(n + 1) * 512], bank[:, :], bank.all, fK.all)

    def h_v(tl, bank):
        cx.copy(fV[:, tl, :], bank[:, 0:128], bank.all, fV.all, eng="act" if tl % 2 else "dve")

    def h_row(n, bank):
        cx.activation(rowtmp[0:1, :], bank[0:1, :], AF.Exp, bank.all + nfb.all, rowtmp.all, bias=nfb[0:1, 0:1], scale=-1.0)
        cx.activation(sprow[0:1, n * 512:(n + 1) * 512], rowtmp[0:1, :], AF.Ln, rowtmp.all + cm.c32.all, sprow.all,
                      bias=cm.one_ap[0:1, 0:1])

    project(cx, cm, env, w, [(0, h_q), (128, h_k)], (256, 128, h_v),
            row=(lambda c: wff[:, c:c + 1], wff.all, h_row))

    cx.memset(nrh[0:1, :], 1.0, nrh.all, eng="dve")
    S.dve(lambda e: e.tensor_tensor_scan(out=cprow[0:1, :], data0=nrh[0:1, :], data1=sprow[0:1, :], initial=0.0,
                                         op0=ALU.mult, op1=ALU.add), nrh.all + sprow.all, cprow.all)
    sm = cm.banks[6]
    cpv = cprow[0:1, :].rearrange("o (a b) -> o a b", b=512)
    cx.mm(sm[:, 0:8], cm.ones32[0:1, 0:128], cpv[:, :, 0], True, True, cm.c32.all + cprow.all, sm.all)
    for kt in range(32):
        cx.mm(sm[:, 8 + kt:9 + kt], cprow[0:1, kt * 128:(kt + 1) * 128], cm.ones32[0:1, 0:1], True, True,
              cm.c32.all + cprow.all, sm.all)
    rbcp = cx.sb([128, 40], F32, name="rbcp")
    cx.copy(rbcp[:, :], sm[:, 0:40], sm.all, rbcp.all)
    for qc in range(8):
        sl = slice(qc * 512, (qc + 1) * 512)
        cx.ts(sprow[0:1, sl], cprow[0:1, sl], cprow[0:1, qc * 512:qc * 512 + 1], -1.0, ALU.subtract, ALU.mult,
              cprow.all, sprow.all)
    cx.copy(nrh[0:1, :], sprow[0:1, :], sprow.all, nrh.all)
    cx.tt(nrl[0:1, :], sprow[0:1, :], nrh[0:1, :], ALU.subtract, sprow.all + nrh.all, nrl.all)

    biasq = [cx.sb([128, 32], F32, name=f"biasq{i}") for i in range(2)]
    pTs = [cx.sb([128, 512], BF16, name=f"fpT{i}") for i in range(3)]
    it = 0
    for qc in range(8):
        sl = slice(qc * 512, (qc + 1) * 512)
        bq = biasq[qc % 2]
        cx.ts(bq[:, :], rbcp[:, 8:40], rbcp[:, qc:qc + 1], None, ALU.subtract, None, rbcp.all, bq.all)
        obank = cm.banks[2 + qc % 2]
        dbank = cm.banks[4 + qc % 2]
        nkt = 4 * (qc + 1)

        def emit_S(kt):
            sbank = cm.banks[kt % 2]
            a = kt - 4 * qc
            cx.mm(sbank[:, :], fK[:, kt * 128:(kt + 1) * 128], fQ[:, sl], True, False, fK.all + fQ.all, sbank.all)
            cx.mm(sbank[:, :], cm.ones[0:1, 0:128], nrh[0:1, sl], False, False, nrh.all, sbank.all)
            cx.mm(sbank[:, :], cm.ones[0:1, 0:128], nrl[0:1, sl], False, a < 0, nrl.all, sbank.all)
            if a >= 0:
                cx.mm(sbank[:, :], cm.ident, env.cmask[:, a, :], False, True, env.cmask.all, sbank.all)
        emit_S(0)
        for kt in range(nkt):
            sbank = cm.banks[kt % 2]
            pT = pTs[it % 3]
            it += 1
            cx.activation(pT[:, :], sbank[:, :], AF.Exp, sbank.all + bq.all, pT.all, bias=bq[:, kt:kt + 1])
            if kt + 1 < nkt:
                emit_S(kt + 1)
            cx.mm(obank[:, :], fV[:, kt, :], pT[:, :], kt == 0, kt == nkt - 1, fV.all + pT.all, obank.all)
            cx.mm(dbank[:, :], cm.ones, pT[:, :], kt == 0, kt == nkt - 1, pT.all, dbank.all)
        softmax_finish(cx, env, obank, dbank, 512, env.y_out(1, qc * 512, 512), qc)


def mix_moba(cx, cm, env):
    S = cx.S
    w = load_w(cx, env, "moba")
    mQ = cx.sb([128, SEQ], BF16, name="mQ")
    mK = cx.sb([128, SEQ], BF16, name="mK")
    mV = cx.sb([128, 32, 128], BF16, name="mV")
    perm = cx.sb([128, 128], BF16, name="perm")
    cx.load(perm[:, :], env.perm, perm.all)
    rC = [cx.sb([128, 512], F32, name=f"rC{i}") for i in range(2)]
    rS = [cx.sb([128, 512], F32, name=f"rS{i}") for i in range(2)]
    xb = [cx.sb([128, 512], BF16, name=f"xb{i}") for i in range(2)]
    t1 = [cx.sb([128, 512], F32, name=f"t1{i}") for i in range(2)]
    t2 = [cx.sb([128, 512], F32, name=f"t2{i}") for i in range(2)]
    cnt = [0]

    def pre_chunk(n):
        cx.load(rC[n % 2][:, :], env.ropeC[:, n * 512:(n + 1) * 512], rC[n % 2].all)
        cx.load(rS[n % 2][:, :], env.ropeS[:, n * 512:(n + 1) * 512], rS[n % 2].all)

    def rope_handler(dst, sc):
        def h(n, bank):
            k = cnt[0] % 2
            cnt[0] += 1
            sl = slice(n * 512, (n + 1) * 512)
            if DEBUG_STOP == 10:
                cx.copy(dst[:, sl], bank[:, :], bank.all, dst.all)
                return
            cx.copy(xb[k][:, :], bank[:, :], bank.all, xb[k].all, eng="act")
            swb = cm.banks[4 + k]
            cx.mm(swb[:, :], perm[:, :], xb[k][:, :], True, True, perm.all + xb[k].all, swb.all)
            if DEBUG_STOP == 11:
                cx.copy(dst[:, sl], swb[:, :], swb.all, dst.all)
                return
            if DEBUG_STOP == 13:
                cx.stt(t1[k][:, :], bank[:, :], sc, t2[k][:, :], ALU.mult, ALU.mult, bank.all + t2[k].all, t1[k].all)
            else:
                cx.stt(t1[k][:, :], bank[:, :], sc, rC[n % 2][:, :], ALU.mult, ALU.mult, bank.all + rC[n % 2].all, t1[k].all)
            if DEBUG_STOP in (12, 13):
                cx.copy(dst[:, sl], t1[k][:, :], t1[k].all, dst.all)
                return
            cx.stt(t2[k][:, :], swb[:, :], sc, rS[n % 2][:, :], ALU.mult, ALU.mult, swb.all + rS[n % 2].all, t2[k].all)
            cx.tt(dst[:, sl], t1[k][:, :], t2[k][:, :], ALU.add, t1[k].all + t2[k].all, dst.all, eng=ROPE_ADD_ENG)
        return h

    def h_v(tl, bank):
        cx.copy(mV[:, tl, :], bank[:, 0:128], bank.all, mV.all, eng="act")

    project(cx, cm, env, w, [(0, rope_handler(mQ, SCALE)), (128, rope_handler(mK, 1.0))], (256, 128, h_v),
            pre_chunk=pre_chunk)

    if DEBUG_STOP in (1, 10, 11, 12, 13):
        return
    kb32 = cx.sb([128, 16], F32, name="kb32")
    kbT = cx.sb([128, 16], BF16, name="kbT")
    S.dve(lambda e: e.tensor_reduce(out=kb32[:, :], in_=mK[:, :].rearrange("p (j k) -> p j k", k=256),
                                    axis=AX.X, op=ALU.add), mK.all, kb32.all)
    cx.copy(kbT[:, :], kb32[:, :], kb32.all, kbT.all)
    gb = cm.banks[6]
    for qt in range(32):
        cx.mm(gb[:, qt * 16:(qt + 1) * 16], mQ[:, qt * 128:(qt + 1) * 128], kbT[:, :], True, True,
              mQ.all + kbT.all, gb.all)
    g_sb = cx.sb([128, 32, 16], F32, name="g_sb")
    cx.copy(g_sb[:, :, :], gb[:, :].rearrange("p (a b) -> p a b", b=16), gb.all, g_sb.all)
    if DEBUG_STOP == 2:
        return
    S.pool(lambda e: e.affine_select(out=g_sb[:, :, :], in_=g_sb[:, :, :], pattern=[[1, 16], [0, 2], [-1, 16]],
                                     compare_op=ALU.is_ge, fill=-1e30, base=-1, channel_multiplier=0),
           g_sb.all, g_sb.all)
    if DEBUG_STOP == 3:
        return
    m8 = cx.sb([128, 32, 8], F32, name="m8")
    for qt in range(32):
        S.dve(lambda e, qt=qt: e.max(out=m8[:, qt, :], in_=g_sb[:, qt, :]), g_sb.all, m8.all)
    thr = cx.sb([128, 32, 1], F32, name="thr")
    cx.ts(thr[:, :, :], m8[:, :, 2:3], -1e29, None, ALU.max, None, m8.all, thr.all)
    nm = cx.sb([128, 32, 16], F32, name="nm")
    cx.tt(nm[:, :, :], g_sb[:, :, :], thr[:, :, :].to_broadcast([128, 32, 16]), ALU.is_lt, g_sb.all + thr.all, nm.all)
    cx.ts(nm[:, :, :], nm[:, :, :], NEG, None, ALU.mult, None, nm.all, nm.all)
    if DEBUG_STOP == 4:
        return
    nmT = cx.sb([16, SEQ], BF16, name="nmT")
    for grp in range(8):
        tb = cm.banks[grp % 2]
        for i in range(4):
            qt = grp * 4 + i
            cx.transpose(tb[0:16, i * 128:(i + 1) * 128], nm[:, qt, :], cm.ident32, nm.all + cm.c32.all, tb.all)
        cx.copy(nmT[0:16, grp * 512:(grp + 1) * 512], tb[0:16, :], tb.all, nmT.all, eng="act" if grp % 2 else "dve")

    if DEBUG_STOP == 5:
        return
    pTs = [cx.sb([128, 256], BF16, name=f"mpT{i}") for i in range(3)]
    it = 0
    for qb in range(16):
        sl = slice(qb * 256, (qb + 1) * 256)
        obank = cm.banks[2 + qb % 2]
        dbank = cm.banks[4 + qb % 2]
        nkt = 2 * qb + 2

        def emit_S(kt):
            sbank = cm.banks[kt % 2]
            cx.mm(sbank[:, 0:256], mK[:, kt * 128:(kt + 1) * 128], mQ[:, sl], True, False, mK.all + mQ.all, sbank.all)
            if kt < 2 * qb:
                cx.mm(sbank[:, 0:256], env.esel[0:16, kt // 2, :], nmT[0:16, sl], False, True,
                      env.esel.all + nmT.all, sbank.all)
            else:
                cx.mm(sbank[:, 0:256], cm.ident, env.cmask[:, kt - 2 * qb, 0:256], False, True, env.cmask.all, sbank.all)
        emit_S(0)
        for kt in range(nkt):
            sbank = cm.banks[kt % 2]
            pT = pTs[it % 3]
            it += 1
            cx.activation(pT[:, :], sbank[:, 0:256], AF.Exp, sbank.all, pT.all)
            if kt + 1 < nkt:
                emit_S(kt + 1)
            cx.mm(obank[:, 0:256], mV[:, kt, :], pT[:, :], kt == 0, kt == nkt - 1, mV.all + pT.all, obank.all)
            cx.mm(dbank[:, 0:256], cm.ones, pT[:, :], kt == 0, kt == nkt - 1, pT.all, dbank.all)
        softmax_finish(cx, env, obank, dbank, 256, env.y_out(0, qb * 256, 256), qb)


def mix_hgrn(cx, cm, env, layer):
    S = cx.S
    w = load_w(cx, env, "hgrn")
    hq = cx.sb([128, SEQ], F32, name="hq")
    lf = cx.sb([128, SEQ], F32, name="lf")
    hk = cx.sb([128, SEQ], BF16, name="hk")
    sg = cx.sb([128, SEQ], BF16, name="sg")
    hv = cx.sb([128, 32, 128], BF16, name="hv")
    lb = cx.sb([128, 2], F32, name="lb")
    if layer == 0:
        cx.memset(lb[:, 0:1], 0.0, lb.all, eng="dve")
        cx.memset(lb[:, 1:2], 1.0, lb.all, eng="dve")
    else:
        ee = cx.sb([128, 4], F32, name="ee")
        cx.activation(ee[:, 0:2], env.small[:, SM_G0:SM_G0 + 2], AF.Exp, env.small.all, ee.all)
        cx.tt(ee[:, 2:3], ee[:, 0:1], ee[:, 1:2], ALU.add, ee.all, ee.all)
        S.dve(lambda e: e.reciprocal(out=ee[:, 3:4], in_=ee[:, 2:3]), ee.all, ee.all)
        cx.tt(lb[:, 0:1], ee[:, 1:2], ee[:, 3:4], ALU.mult, ee.all, lb.all)
        cx.ts(lb[:, 1:2], lb[:, 0:1], -1.0, 1.0, ALU.mult, ALU.add, lb.all, lb.all)
    sgm = [cx.sb([128, 512], F32, name=f"sgm{i}") for i in range(2)]
    ff_ = [cx.sb([128, 512], F32, name=f"ff{i}") for i in range(2)]

    def h_q(n, bank):
        cx.copy(hq[:, n * 512:(n + 1) * 512], bank[:, :], bank.all, hq.all)

    def h_f(n, bank):
        sl = slice(n * 512, (n + 1) * 512)
        a, f = sgm[n % 2], ff_[n % 2]
        cx.activation(a[:, :], bank[:, :], AF.Sigmoid, bank.all, a.all)
        cx.ts(f[:, :], a[:, :], lb[:, 1:2], lb[:, 0:1], ALU.mult, ALU.add, a.all + lb.all, f.all)
        cx.activation(lf[:, sl], f[:, :], AF.Ln, f.all, lf.all)
        cx.ts(hk[:, sl], f[:, :], -1.0, 1.0, ALU.mult, ALU.add, f.all, hk.all, eng="pool")

    def h_g(n, bank):
        sl = slice(n * 512, (n + 1) * 512)
        a = sgm[n % 2]
        cx.activation(a[:, :], bank[:, :], AF.Sigmoid, bank.all, a.all)
        cx.tt(sg[:, sl], bank[:, :], a[:, :], ALU.mult, bank.all + a.all, sg.all)

    def h_v(tl, bank):
        cx.copy(hv[:, tl, :], bank[:, 0:128], bank.all, hv.all, eng="act" if tl % 2 else "dve")

    project(cx, cm, env, w, [(0, h_q), (128, h_f), (256, h_g)], (384, 128, h_v))

    rm = cx.sb([128, SEQ], BF16, name="rm")
    cx.memset(rm[:, :], 1.0, rm.all)
    cx.memset(rm[:, :].rearrange("p (a b) -> p a b", b=64)[:, :, 0:1], 0.0, rm.all)
    bb = cx.sb([128, SEQ], F32, name="bb")
    S.dve(lambda e: e.tensor_tensor_scan(out=bb[:, :], data0=rm[:, :], data1=lf[:, :], initial=0.0,
                                         op0=ALU.mult, op1=ALU.add), rm.all + lf.all, bb.all)
    bv = bb[:, :].rearrange("p (a b) -> p a b", b=64)
    sm = cx.sb([128, 5, 64], F32, name="hsm")
    cx.copy(sm[:, 0, :], bv[:, :, 31], bb.all, sm.all)
    cx.activation(sm[:, 4, :], bv[:, :, 31], AF.Exp, bb.all, sm.all)
    cx.activation(sm[:, 1, :], bv[:, :, 63], AF.Exp, bb.all, sm.all)
    cx.tt(sm[:, 3, :], bv[:, :, 63], sm[:, 0, :], ALU.subtract, bb.all + sm.all, sm.all)
    cx.activation(sm[:, 2, :], sm[:, 3, :], AF.Exp, sm.all, sm.all)
    cx.tt(bv, bv, sm[:, 0, :].unsqueeze(2).to_broadcast([128, 64, 64]), ALU.subtract, bb.all + sm.all, bb.all)
    E = lf
    qt_ = cx.sb([128, SEQ], BF16, name="qtil")
    kt_ = cx.sb([128, SEQ], BF16, name="ktil")
    kh_ = cx.sb([128, SEQ], BF16, name="khat")
    cx.activation(E[:, :], bb[:, :], AF.Exp, bb.all, E.all)
    cx.tt(qt_[:, :], hq[:, :], E[:, :], ALU.mult, hq.all + E.all, qt_.all)
    cx.activation(E[:, :], bb[:, :], AF.Exp, bb.all, E.all, scale=-1.0)
    cx.tt(kt_[:, :], hk[:, :], E[:, :], ALU.mult, hk.all + E.all, kt_.all)
    cx.tt(kh_[:, :].rearrange("p (a b) -> p a b", b=64), kt_[:, :].rearrange("p (a b) -> p a b", b=64),
          sm[:, 2, :].unsqueeze(2).to_broadcast([128, 64, 64]), ALU.mult, kt_.all + sm.all, kh_.all, eng="pool")
    khtm = cx.sb([128, 32, 128], BF16, name="khtm")
    for grp in range(4):
        for i in range(8):
            tl = grp * 8 + i
            cx.transpose(cm.bankb[:, i * 128:(i + 1) * 128], kh_[:, tl * 128:(tl + 1) * 128], cm.ident,
                         kh_.all, cm.bankb.all)
        cx.copy(khtm[:, grp * 8:(grp + 1) * 8, :], cm.bankb[:, :].rearrange("p (a b) -> p a b", b=128),
                cm.bankb.all, khtm.all, eng="act" if grp % 2 else "dve")
    attT = cx.sb([128, 32, 128], BF16, nsub=8, name="attT")
    for grp in range(8):
        ab = cm.banks[grp % 2]
        for i in range(4):
            tl = grp * 4 + i
            ts_ = slice(tl * 128, (tl + 1) * 128)
            cx.mm(ab[:, i * 128:(i + 1) * 128], kt_[:, ts_], qt_[:, ts_], True, True, kt_.all + qt_.all, ab.all)
        cx.tt(attT[:, grp * 4:(grp + 1) * 4, :], ab[:, :].rearrange("p (a b) -> p a b", b=128),
              env.hmask[:, :].unsqueeze(1).to_broadcast([128, 4, 128]), ALU.mult, ab.all + env.hmask.all, [attT.b[grp]])
    S32 = cx.sb([128, 2, 128], F32, nsub=2, name="S32")
    Sb = cx.sb([128, 4, 128], BF16, nsub=4, name="Sb")
    cx.memset(S32[:, 0, :], 0.0, [S32.b[0]], eng="dve")
    cx.memset(Sb[:, 0, :], 0.0, [Sb.b[0]], eng="dve")
    oT = [cx.sb([128, 512], F32, name=f"oT{i}") for i in range(2)]
    sq = [cx.sb([128, 512], BF16, name=f"osq{i}") for i in range(2)]
    rs = [cx.sb([128, 512], F32, name=f"ors{i}") for i in range(2)]
    yo = [cx.sb([128, 512], BF16, name=f"oyo{i}") for i in range(2)]
    mbanks = [cm.banks[4], cm.banks[5], cm.banks[0], cm.banks[1]]

    def emit_M(c):
        ps_ = slice((c % 2) * 64, (c % 2) * 64 + 64)
        mb_ = mbanks[c % 4]
        cx.mm(mb_[:, 0:128], khtm[ps_, c // 2, :], hv[ps_, c // 2, :], True, True, khtm.all + hv.all, mb_.all)
    for c in range(3):
        emit_M(c)
    for tl in range(32):
        ob = cm.banks[2 + (tl // 4) % 2]
        for half in range(2):
            c = 2 * tl + half
            ps = slice(half * 64, half * 64 + 64)
            mslot = 0
            mb = mbanks[c % 4]
            oc = slice((tl % 4) * 128 + half * 64, (tl % 4) * 128 + half * 64 + 64)
            tsl = slice(c * 64, c * 64 + 64)
            cx.mm(ob[:, oc], hv[:, tl, :], attT[:, tl, half * 64:half * 64 + 64], True, False,
                  hv.all + [attT.b[tl // 4]], [ob.b[tl % 4]])
            cx.mm(ob[:, oc], Sb[:, c % 4, :], qt_[:, tsl], False, True, [Sb.b[c % 4]] + qt_.all, [ob.b[tl % 4]])
            cx.stt(S32[:, (c + 1) % 2, :], S32[:, c % 2, :], sm[:, 1, c:c + 1], mb[:, mslot * 128:(mslot + 1) * 128],
                   ALU.mult, ALU.add, [S32.b[c % 2], mb.b[mslot]] + sm.all, [S32.b[(c + 1) % 2]])
            if c + 1 < 64:
                cx.S.act(lambda e, c=c: e.mul(out=Sb[:, (c + 1) % 4, :], in_=S32[:, (c + 1) % 2, :], mul=sm[:, 4, c + 1:c + 2]),
                         [S32.b[(c + 1) % 2]] + sm.all, [Sb.b[(c + 1) % 4]])
            if c + 3 < 64:
                emit_M(c + 3)
        if tl % 4 == 3:
            n = tl // 4
            k = n % 2
            sl = slice(n * 512, (n + 1) * 512)
            cx.copy(oT[k][:, :], ob[:, :], ob.all, oT[k].all)
            cx.activation(sq[k][:, :], ob[:, :], AF.Square, ob.all, sq[k].all)
            nb_ = cm.banks[6]
            cx.mm(nb_[:, :], cm.ones, sq[k][:, :], True, True, sq[k].all, nb_.all)
            cx.activation(rs[k][:, :], nb_[:, :], AF.Ln, nb_.all + cm.epst.all, rs[k].all, bias=cm.eps_ap, scale=1.0 / HD)
            cx.activation(rs[k][:, :], rs[k][:, :], AF.Exp, rs[k].all, rs[k].all, scale=-0.5)
            cx.stt(oT[k][:, :], oT[k][:, :], env.small[:, SM_HNORM:SM_HNORM + 1], rs[k][:, :], ALU.mult, ALU.mult,
                   oT[k].all + rs[k].all + env.small.all, oT[k].all)
            cx.tt(yo[k][:, :], oT[k][:, :], sg[:, sl], ALU.mult, oT[k].all + sg.all, yo[k].all, eng="pool")
            cx.store(env.y_out(2, n * 512, 512), yo[k][:, :], yo[k].all, is_output=env.y_is_output)


def mix_pool(cx, cm, env):
    w = load_w(cx, env, "pool")
    pin = cx.sb([128, 32, 128], BF16, nsub=32, name="pin")
    pM = cx.sb([128, 3, 128], BF16, name="pM")
    cx.load(pM[:, :, :], env.poolM, pM.all)
    pw = cx.sb([128, 128], BF16, name="pw")
    cx.load(pw[:, :], env.poolw, pw.all, q="pool")

    def h_p(tl, bank):
        cx.copy(pin[:, tl, :], bank[:, 0:128], bank.all, [pin.b[tl]], eng="act" if tl % 2 else "dve")

    project(cx, cm, env, w, [], (0, 128, h_p))
    pt = [cx.sb([128, 512], BF16, name=f"ppt{i}") for i in range(2)]
    yo = [cx.sb([128, 512], BF16, name=f"pyo{i}") for i in range(2)]
    for n in range(8):
        pb = cm.banks[n % 2]
        for i in range(4):
            tl = n * 4 + i
            cs = slice(i * 128, (i + 1) * 128)
            cx.mm(pb[:, cs], pin[:, tl, :], pM[:, 0 if tl == 0 else 1, :], True, tl == 0, [pin.b[tl]] + pM.all, [pb.b[i]])
            if tl > 0:
                cx.mm(pb[:, cs], pin[:, tl - 1, :], pM[:, 2, :], False, True, [pin.b[tl - 1]] + pM.all, [pb.b[i]])
        k = n % 2
        cx.copy(pt[k][:, :], pb[:, :], pb.all, pt[k].all, eng="act")
        ob = cm.banks[2 + n % 2]
        cx.mm(ob[:, :], pw[:, :], pt[k][:, :], True, True, pw.all + pt[k].all, ob.all)
        cx.ts(yo[k][:, :], ob[:, :], env.small[:, SM_PSCALE:SM_PSCALE + 1], None, ALU.mult, None,
              ob.all + env.small.all, yo[k].all)
        cx.store(env.y_out(3, n * 512, 512), yo[k][:, :], yo[k].all, is_output=env.y_is_output)


def setup_common32(cx, cm, c32_ap):
    cm.c32 = cx.sb([128, 256], F32, name="c32")
    cx.load(cm.c32[:, :], c32_ap, cm.c32.all)
    cm.ident32 = cm.c32[:, 0:128]
    cm.ones32 = cm.c32[:, 128:256]
    cm.one_ap = cm.c32[:, 128:129]


def mix_globals(cx, env, cmask_d, esel_d, hmask_d):
    env.cmask = cx.sb([128, 4, 512], BF16, name="cmask")
    cx.load(env.cmask[:, :, :], cmask_d, env.cmask.all)
    env.esel = cx.sb([16, 16, 128], BF16, name="esel")
    cx.load(env.esel[:, :, :], esel_d, env.esel.all)
    env.hmask = cx.sb([128, 128], BF16, name="hmask")
    cx.load(env.hmask[:, :], hmask_d, env.hmask.all)


def phase_mix(cx, cm, env, layer, small_d, which=("fox", "moba", "hgrn", "pool")):
    with contextlib.ExitStack() as outer:
        cx.scope = outer
        env.small = cx.sb([128, N_SMALL], F32, name="small")
        cx.load(env.small[:, :], small_d, env.small.all)
        env.hs = [cx.sb([128, NCH, 512], BF16, name=f"hs{i}") for i in range(2)]
        env.rc = [cx.sb([128, 512], F32, name=f"rc{i}") for i in range(2)]
        env.yt = [cx.sb([128, 512], BF16, name=f"yt{i}") for i in range(2)]
        fns = {"fox": lambda: mix_fox(cx, cm, env), "moba": lambda: mix_moba(cx, cm, env),
               "hgrn": lambda: mix_hgrn(cx, cm, env, layer), "pool": lambda: mix_pool(cx, cm, env)}
        for name in which:
            with contextlib.ExitStack() as sc:
                cx.scope = sc
                fns[name]()
                cx.S.fence()
                if getattr(env, "after_mixer", None) is not None:
                    env.after_mixer(name)
            cx.scope = outer
    cx.scope = None


def build_mix(layer, which=("fox", "moba", "hgrn", "pool")):
    cx = Ctx()
    env = MixEnv()
    env.hnT = cx.din("hnT", [D_MODEL, SEQ], BF16)
    env.wmix = cx.din("wmix", [128, NCH, W_MIXCOLS], F32)
    small_d = cx.din("small", [128, N_SMALL], F32)
    env.poolw = cx.din("poolw", [128, 128], F32)
    consts = cx.din("consts", [128, 256], BF16)
    c32 = cx.din("c32", [128, 256], F32)
    cmask_d = cx.din("cmask", [128, 4, 512], BF16)
    env.perm = cx.din("perm", [128, 128], BF16)
    env.ropeC = cx.din("ropeC", [128, SEQ], F32)
    env.ropeS = cx.din("ropeS", [128, SEQ], F32)
    esel_d = cx.din("esel", [16, 16, 128], BF16)
    hmask_d = cx.din("hmask", [128, 128], BF16)
    env.poolM = cx.din("poolM", [128, 3, 128], BF16)
    env.yT = cx.dout("yT", [4, 128, SEQ], BF16)
    hview_ = env.hnT.rearrange("(c p) t -> p c t", p=128)
    env.hn_chunk = lambda n, j: hview_[:, :, n * 512 + j * 256:n * 512 + (j + 1) * 256]
    env.hn_rbufs = lambda n, j: []
    env.y_out = lambda bi, t0, ncols: env.yT[bi, :, t0:t0 + ncols]
    env.y_is_output = True
    cm = Common(cx, consts)
    add_eps(cx, cm)
    setup_common32(cx, cm, c32)
    mix_globals(cx, env, cmask_d, esel_d, hmask_d)
    phase_mix(cx, cm, env, layer, small_d, which)
    cx.S.emit()
    return cx.nc


CC_GROUPS = [[0, 1, 2, 3], [4, 5, 6, 7]]


def _core_quarter(cx, e):
    if getattr(cx, "_q", None) is None:
        cx._q = e.snap(e.partition_id() % 4, min_val=0, max_val=3)
    return cx._q


def build_fused(depth):
    cx = Ctx()
    nc = cx.nc
    S = cx.S
    consts = cx.din("consts", [128, 256], BF16)
    c32 = cx.din("c32", [128, 256], F32)
    xT = cx.din("xT", [D_MODEL, TOK], F32)
    memT = cx.din("memT", [D_MODEL, NMEM], F32)
    gpre = cx.din("gpre", [128, NCH], F32)
    cmask_d = cx.din("cmask", [128, 4, 512], BF16)
    esel_d = cx.din("esel", [16, 16, 128], BF16)
    hmask_d = cx.din("hmask", [128, 128], BF16)
    env = MixEnv()
    env.perm = cx.din("perm", [128, 128], BF16)
    env.ropeC = cx.din("ropeC", [128, SEQ], F32)
    env.ropeS = cx.din("ropeS", [128, SEQ], F32)
    env.poolM = cx.din("poolM", [128, 3, 128], BF16)
    wmix_d = [cx.din(f"wmix_{l}", [128, NCH, W_MIXCOLS], F32) for l in range(depth)]
    small_d = [cx.din(f"small_{l}", [128, N_SMALL], F32) for l in range(depth)]
    poolw_d = [cx.din(f"poolw_{l}", [128, 128], F32) for l in range(depth)]
    ios = []
    for l in range(depth):
        io = TokIO()
        tok_weight_inputs(cx, io, f"_{l}")
        io.memT = memT
        ios.append(io)
    outT = cx.dout("outT", [D_MODEL, TOK], F32)
    hn_own = [nc.dram_tensor(f"hn_own{k}", [D_MODEL, 256], BF16, kind="Internal").ap() for k in range(4)]
    hn_all = [nc.dram_tensor(f"hn_all{k}", [4 * D_MODEL, 256], BF16, kind="Internal").ap() for k in range(4)]
    hnp = [Buf(f"hnp{k}") for k in range(4)]
    hna = [Buf(f"hna{k}") for k in range(4)]

    def hn_store(piece, src, rb):
        cx.store(hn_own_v[piece], src, rb, is_output=False, wbufs=[hnp[piece]])

    def hn_gather(piece):
        S.collective(lambda e: e.collective_compute("AllGather", ALU.bypass, replica_groups=CC_GROUPS,
                                                    ins=[hn_own[piece].opt()], outs=[hn_all[piece].opt()]),
                     [hnp[piece]], [hna[piece]])
    y_own = [nc.dram_tensor(f"y_own{n}", [512, TOK], BF16, kind="Internal").ap() for n in range(4)]
    y_all = [nc.dram_tensor(f"y_all{n}", [4 * 512, TOK], BF16, kind="Internal").ap() for n in range(4)]
    h_res = nc.dram_tensor("h_res", [D_MODEL, TOK], F32, kind="Internal").ap()

    cm = Common(cx, consts)
    add_eps(cx, cm)
    setup_common32(cx, cm, c32)
    mix_globals(cx, env, cmask_d, esel_d, hmask_d)
    S.sp_init = lambda e: _core_quarter(cx, e)

    with contextlib.ExitStack() as sc:
        cx.scope = sc
        hn_own_v = [a.rearrange("(c p) t -> p c t", p=128) for a in hn_own]

        def pre_out(n, j, src, rb):
            hn_store(2 * n + j, src, rb)
            hn_gather(2 * n + j)
        phase_pre(cx, cm, xT, gpre, pre_out, False)
        S.fence()
    cx.scope = None

    hn_all_v = [a.rearrange("(r c p) t -> r p c t", r=4, p=128) for a in hn_all]
    y_own_v = [a.rearrange("(tq d) t -> tq d t", tq=4) for a in y_own]
    y_all_v = [a.rearrange("(hh tq d) t -> tq d hh t", hh=4, tq=4) for a in y_all]
    env.hn_chunk = lambda n, j: hn_all_v[2 * (n % 2) + j][n // 2]
    env.hn_rbufs = lambda n, j: [hna[2 * (n % 2) + j]]
    env.y_out = lambda bi, t0, ncols: y_own_v[bi][t0 // TOK][:, t0 % TOK:t0 % TOK + ncols]
    env.y_is_output = False
    h_res_v = h_res.rearrange("(c p) t -> p c t", p=128)
    x_v = xT.rearrange("(c p) t -> p c t", p=128)
    out_v = outT.rearrange("(c p) t -> p c t", p=128)

    for l in range(depth):
        last = l == depth - 1
        S.fence()
        env.wmix = wmix_d[l]
        env.poolw = poolw_d[l]
        pend_y = []

        def y_gather(n):
            S.collective(lambda e, n=n: e.collective_compute("AllGather", ALU.bypass, replica_groups=CC_GROUPS,
                                                             ins=[y_own[n].opt()], outs=[y_all[n].opt()]), [], [])

        def flush_cc():
            while pend_y:
                y_gather(pend_y.pop(0))

        def after_mixer(name):
            n = {"moba": 0, "fox": 1, "hgrn": 2, "pool": 3}[name]
            pend_y.append(n)
            if name == "pool":
                flush_cc()
        env.after_mixer = after_mixer
        env.flush_cc = flush_cc
        phase_mix(cx, cm, env, l, small_d[l])
        S.fence()
        io = ios[l]
        src_v = x_v if l == 0 else h_res_v
        dst_v = out_v if last else h_res_v
        io.hT_in = lambda hf, c0, c1, src_v=src_v: src_v[:, c0:c1, hf * HALF:(hf + 1) * HALF]
        io.hn_load = lambda hf, j, dst, wb: cx.load(dst, hn_own_v[2 * hf + j], wb, rbufs=[hnp[2 * hf + j]])

        def ybr_load(dst, hf, wb):
            for n in range(4):
                def f(e, n=n):
                    q = _core_quarter(cx, e)
                    src = y_all_v[n][bass.ds(q, 1)][0][:, :, hf * HALF:(hf + 1) * HALF]
                    return e.dma_start(out=dst[:, n * 4:(n + 1) * 4, :], in_=src)
                S.dma("sp", f, [], list(wb))
        io.ybr_load = ybr_load
        io.hT_out = lambda hf, c0, c1, dst_v=dst_v: dst_v[:, c0:c1, hf * HALF:(hf + 1) * HALF]
        pending = []

        def hn_store_tok(hf, j, src, rb):
            hn_store(2 * hf + j, src, rb)
            if hf == 0:
                pending.append(2 * hf + j)
            else:
                hn_gather(2 * hf + j)

        def after_gates(hf):
            while pending:
                hn_gather(pending.pop(0))
        io.hn_store = hn_store_tok
        io.after_gates = after_gates
        io.h_is_output = last
        io.hn_is_output = False
        io.write_hn_when_last = False
        with contextlib.ExitStack() as sc:
            cx.scope = sc
            phase_tok(cx, cm, io, l, last)
            S.fence()
        cx.scope = None
    S.emit()
    return nc


IN_OFF = {"mq": 0, "mk": 512, "mv": 1024, "fq": 1536, "fk": 2048, "fv": 2560, "ff": 3072,
          "hq": 3076, "hf": 3588, "hi": 4100, "hg": 4612, "pin": 5124, "gl": 5636}
_HC = {}


def host_consts():
    if _HC:
        return _HC
    f32 = np.float32
    c = np.zeros((128, 256), f32)
    c[:, :128] = np.eye(128)
    c[:, 128:] = 1
    _HC["c32"] = c
    _HC["consts"] = c.astype(NPBF)
    k = np.arange(128)[:, None, None]
    a = np.arange(4)[None, :, None]
    q = np.arange(512)[None, None, :]
    _HC["cmask"] = np.where(q >= 128 * a + k, 0.0, NEG).astype(NPBF)
    perm = np.zeros((128, 128), f32)
    d = np.arange(128)
    perm[(d + 64) % 128, d] = 1
    _HC["perm"] = perm.astype(NPBF)
    inv_freq = (f32(10000.0) ** (-np.arange(64, dtype=f32) * f32(2.0) / f32(128))).astype(f32)
    ang = (np.arange(SEQ, dtype=f32)[None, :] * inv_freq[:, None]).astype(f32)
    cos, sin = np.cos(ang).astype(f32), np.sin(ang).astype(f32)
    _HC["ropeC"] = np.ascontiguousarray(np.concatenate([cos, cos], 0))
    _HC["ropeS"] = np.ascontiguousarray(np.concatenate([-sin, sin], 0))
    es = np.zeros((16, 16, 128), f32)
    for j in range(16):
        es[j, j, :] = 1
    _HC["esel"] = es.astype(NPBF)
    s = np.arange(128)[:, None]
    t = np.arange(128)[None, :]
    _HC["hmask"] = ((s // 64 == t // 64) & (s <= t)).astype(f32).astype(NPBF)
    for h, w in enumerate(POOL_WINDOWS):
        M = np.zeros((128, 3, 128), f32)
        eye = (s == t).astype(f32)
        band = ((s <= t) & (s > t - w)).astype(f32)
        M[:, 0, :] = band / np.minimum(w, t + 1).astype(f32) - eye
        M[:, 1, :] = band / f32(w) - eye
        M[:, 2, :] = ((s - 128) > (t - w)).astype(f32) / f32(w)
        _HC[f"poolM{h}"] = M.astype(NPBF)
    return _HC


def fm_layout(w):
    K, N = w.shape
    return np.ascontiguousarray(w.reshape(K // 128, 128, N).transpose(1, 0, 2))


def mix_inputs(inp, l, c, hnT_b):
    b, h = c // 4, c % 4
    hc = host_consts()
    w_in = inp["w_in"][l]
    hs = slice(h * 128, (h + 1) * 128)

    def col(name):
        return w_in[:, IN_OFF[name] + h * 128: IN_OFF[name] + (h + 1) * 128]
    wm = np.concatenate([col(n) for n in ("mq", "mk", "mv", "fq", "fk", "fv", "hq", "hf", "hg", "hi", "pin")], axis=1)
    small = np.zeros((128, N_SMALL), np.float32)
    small[:, SM_WFF:SM_WFF + NCH] = w_in[:, IN_OFF["ff"] + h].reshape(NCH, 128).T
    small[:, SM_G0] = inp["hgrn_lb_logits"][0, hs]
    small[:, SM_G1] = inp["hgrn_lb_logits"][1, hs]
    small[:, SM_HNORM] = inp["hgrn_out_norm"][l, hs]
    small[:, SM_PSCALE] = inp["pool_scale"][l, hs]
    small[:, SM_FB] = inp["fox_f_bias"][l, h]
    return {"hnT": hnT_b, "wmix": fm_layout(wm), "small": small,
            "poolw": np.ascontiguousarray(inp["pool_w"][l, h]),
            "consts": hc["consts"], "c32": hc["c32"], "cmask": hc["cmask"], "perm": hc["perm"],
            "ropeC": hc["ropeC"], "ropeS": hc["ropeS"], "esel": hc["esel"], "hmask": hc["hmask"],
            "poolM": hc[f"poolM{h}"]}


GV_MIXPOST, GV_XAPRE, GV_XAMEM, GV_XAPOST, GV_MLPPRE, GV_MLPPOST, GV_NEXT = range(7)
HALF = 512
NMEM = 256
RING_N = 6
RING_ELEMS = 4096


class Ring:
    def __init__(self, cx, n=RING_N, elems=RING_ELEMS):
        self.cx = cx
        self.bufs = [cx.sb([128, elems], BF16, name=f"ring{i}") for i in range(n)]
        self.i = 0

    def load(self, src_ap, a, b):
        t = self.bufs[self.i % len(self.bufs)]
        self.i += 1
        view = t[:, 0:a * b].rearrange("p (a b) -> p a b", b=b)
        self.cx.load(view, src_ap, t.all, q="pool")
        return view, t.all


class TokIO:
    pass


def tok_io_external(cx):
    io = TokIO()
    hT_d = cx.din("hT", [D_MODEL, TOK], F32)
    hnT_d = cx.din("hnT", [D_MODEL, TOK], BF16)
    ybr_d = cx.din("ybr", [D_MODEL, TOK], BF16)
    tok_weight_inputs(cx, io, "")
    io.memT = cx.din("memT", [D_MODEL, NMEM], F32)
    hT_o = cx.dout("hT_out", [D_MODEL, TOK], F32)
    hn_o = cx.dout("hn_next", [D_MODEL, TOK], BF16)
    hview = hT_d.rearrange("(c p) t -> p c t", p=128)
    hnview = hnT_d.rearrange("(c p) t -> p c t", p=128)
    ybview = ybr_d.rearrange("(c p) t -> p c t", p=128)
    hoview = hT_o.rearrange("(c p) t -> p c t", p=128)
    hnoview = hn_o.rearrange("(c p) t -> p c t", p=128)
    io.hT_in = lambda hf, c0, c1: hview[:, c0:c1, hf * HALF:(hf + 1) * HALF]
    io.hn_load = lambda hf, j, dst, wb: cx.load(dst, hnview[:, :, hf * HALF + j * 256:hf * HALF + (j + 1) * 256], wb)
    io.ybr_load = lambda dst, hf, wb: cx.load(dst, ybview[:, :, hf * HALF:(hf + 1) * HALF], wb)
    io.hT_out = lambda hf, c0, c1: hoview[:, c0:c1, hf * HALF:(hf + 1) * HALF]
    io.hn_store = lambda hf, j, src, rb: cx.store(hnoview[:, :, hf * HALF + j * 256:hf * HALF + (j + 1) * 256], src, rb)
    io.h_is_output = True
    io.hn_is_output = True
    io.write_hn_when_last = True
    return io


def tok_weight_inputs(cx, io, sfx):
    io.wg_d = cx.din("wg" + sfx, [64, 128, NCH, 128], F32)
    io.wb_d = cx.din("wb" + sfx, [64, 128, 4, 128], F32)
    io.wo_d = cx.din("wo" + sfx, [16, 128, NCH, 128], F32)
    io.xq_d = cx.din("xq" + sfx, [4, 128, NCH, 128], F32)
    io.xk_d = cx.din("xk" + sfx, [4, 128, NCH, 128], F32)
    io.xv_d = cx.din("xv" + sfx, [2, 128, 8, 512], F32)
    io.xo_d = cx.din("xo" + sfx, [16, 128, 4, 128], F32)
    io.wup_d = cx.din("wup" + sfx, [64, 128, NCH, 128], F32)
    io.wdn_d = cx.din("wdn" + sfx, [32, 128, 32, 128], F32)
    io.gv_d = cx.din("gv" + sfx, [128, 7 * NCH], F32)


def build_tok(layer, last):
    cx = Ctx()
    io = tok_io_external(cx)
    consts = cx.din("consts", [128, 256], BF16)
    cm = Common(cx, consts)
    add_eps(cx, cm)
    phase_tok(cx, cm, io, layer, last)
    cx.S.emit()
    return cx.nc


def phase_tok(cx, cm, io, layer, last):
    S = cx.S
    wg_d, wb_d, wo_d, xq_d, xk_d, xv_d, xo_d, wup_d, wdn_d = (io.wg_d, io.wb_d, io.wo_d, io.xq_d, io.xk_d, io.xv_d,
                                                              io.xo_d, io.wup_d, io.wdn_d)
    memT_d, gv_d = io.memT, io.gv_d
    gv = cx.sb([128, 7 * NCH], F32, name="gv")
    cx.load(gv[:, :], gv_d, gv.all)

    def gcol(which, c):
        return gv[:, which * NCH + c: which * NCH + c + 1]

    ring = Ring(cx)
    hT = cx.sb([128, NCH, HALF], F32, nsub=NCH, name="hT")
    hnb = cx.sb([128, NCH, HALF], BF16, name="hnb")
    RA = cx.sb([128, 32, HALF], BF16, name="RA")
    ybr = RA[:, 0:NCH, :]
    sT = RA[:, NCH:2 * NCH, :]
    aT = cx.sb([128, NCH, HALF], F32, nsub=NCH, name="aT")
    sqp = [cx.sb([128, HALF], BF16, name=f"sq{i}") for i in range(3)]
    rstd = cx.sb([128, HALF], F32, name="rstd")
    tmpA = [cx.sb([128, HALF], F32, name=f"tmpA{i}") for i in range(2)]
    tmpB = [cx.sb([128, HALF], F32, name=f"tmpB{i}") for i in range(2)]
    acc = cx.sb([128, HALF], F32, name="acc")
    kx = cx.sb([128, 4, NMEM], BF16, name="kx")
    vx = cx.sb([128, 2, 512], BF16, name="vx")
    qx = cx.sb([128, 4, HALF], BF16, name="qx")
    ox = cx.sb([128, 4, HALF], BF16, name="ox")
    pTs = [cx.sb([128, HALF], BF16, name=f"xpT{i}") for i in range(2)]
    rc = cx.sb([128, HALF], F32, name="xrc")
    B = cm.banks

    def norm_add(gw):
        rms_stats(cx, cm, lambda c: aT[:, c, :], lambda c: [aT.b[c]], NCH, HALF, sqp, B[6], rstd, D_MODEL)
        for c in range(NCH):
            t = tmpA[c % 2]
            cx.stt(t[:, :], aT[:, c, :], gcol(gw, c), rstd[:, :], ALU.mult, ALU.mult,
                   [aT.b[c]] + gv.all + rstd.all, t.all)
            cx.tt(hT[:, c, :], hT[:, c, :], t[:, :], ALU.add, [hT.b[c]] + t.all, [hT.b[c]])

    def norm_to(gw, dst, dst_bufs):
        rms_stats(cx, cm, lambda c: hT[:, c, :], lambda c: [hT.b[c]], NCH, HALF, sqp, B[6], rstd, D_MODEL)
        for c in range(NCH):
            cx.stt(dst[:, c, :], hT[:, c, :], gcol(gw, c), rstd[:, :], ALU.mult, ALU.mult,
                   [hT.b[c]] + gv.all + rstd.all, dst_bufs)

    memf = aT
    cx.load(memf[:, :, 0:NMEM], memT_d.rearrange("(c p) t -> p c t", p=128), memf.all)
    rms_stats(cx, cm, lambda c: memf[:, c, 0:NMEM], lambda c: [memf.b[c]], NCH, NMEM, sqp, B[6], rstd, D_MODEL)
    memn = hnb
    for c in range(NCH):
        cx.stt(memn[:, c, 0:NMEM], memf[:, c, 0:NMEM], gcol(GV_XAMEM, c), rstd[:, 0:NMEM], ALU.mult, ALU.mult,
               [memf.b[c]] + gv.all + rstd.all, memn.all)
    for hd in range(4):
        wv_, wb_ = ring.load(xk_d[hd], NCH, 128)
        bk = B[hd % 2]
        for c in range(NCH):
            cx.mm(bk[:, 0:NMEM], wv_[:, c, :], memn[:, c, 0:NMEM], c == 0, c == NCH - 1, wb_ + memn.all, bk.all)
        cx.copy(kx[:, hd, :], bk[:, 0:NMEM], bk.all, kx.all)
    wv0, wb0 = ring.load(xv_d[0], 8, 512)
    wv1, wb1 = ring.load(xv_d[1], 8, 512)
    for mt in range(2):
        bk = B[2 + mt]
        for c in range(NCH):
            wv_, wb_ = (wv0, wb0) if c < 8 else (wv1, wb1)
            cx.mm(bk[:, :], memn[:, c, mt * 128:(mt + 1) * 128], wv_[:, c % 8, :], c == 0, c == NCH - 1,
                  wb_ + memn.all, bk.all)
        cx.copy(vx[:, mt, :], bk[:, :], bk.all, vx.all, eng="act")

    for hf in range(TOK // HALF):
        tsl = slice(hf * HALF, (hf + 1) * HALF)
        for c in range(0, NCH, 4):
            cx.load(hT[:, c:c + 4, :], io.hT_in(hf, c, c + 4), hT.b[c:c + 4])
        for j in range(2):
            io.hn_load(hf, j, hnb[:, :, j * 256:(j + 1) * 256], hnb.all)
        io.ybr_load(ybr, hf, RA.all)
        it = 0
        for j in range(NCH):
            for n in range(4):
                wg_, wgb = ring.load(wg_d[j * 4 + n], NCH, 128)
                wb_, wbb = ring.load(wb_d[j * 4 + n], 4, 128)
                bg, bp = B[it % 2], B[2 + it % 2]
                it += 1
                for c in range(NCH):
                    cx.mm(bg[:, :], wg_[:, c, :], hnb[:, c, :], c == 0, c == NCH - 1, wgb + hnb.all, bg.all)
                for hh in range(4):
                    cx.mm(bp[:, :], wb_[:, hh, :], ybr[:, n * 4 + hh, :], hh == 0, hh == 3, wbb + RA.all, bp.all)
                sg_ = tmpA[it % 2]
                cx.activation(sg_[:, :], bg[:, :], AF.Sigmoid, bg.all, sg_.all)
                if n == 0:
                    cx.tt(acc[:, :], bp[:, :], sg_[:, :], ALU.mult, bp.all + sg_.all, acc.all)
                else:
                    t2 = tmpB[it % 2]
                    cx.tt(t2[:, :], bp[:, :], sg_[:, :], ALU.mult, bp.all + sg_.all, t2.all)
                    if n < 3:
                        cx.tt(acc[:, :], acc[:, :], t2[:, :], ALU.add, acc.all + t2.all, acc.all)
                    else:
                        cx.tt(sT[:, j, :], acc[:, :], t2[:, :], ALU.add, acc.all + t2.all, RA.all)
        if getattr(io, "after_gates", None) is not None:
            io.after_gates(hf)
        for j in range(NCH):
            w_, wbf = ring.load(wo_d[j], NCH, 128)
            bk = B[j % 2]
            for c in range(NCH):
                cx.mm(bk[:, :], w_[:, c, :], sT[:, c, :], c == 0, c == NCH - 1, wbf + RA.all, bk.all)
            cx.copy(aT[:, j, :], bk[:, :], bk.all, [aT.b[j]], eng="act" if j % 2 else "dve")
        norm_add(GV_MIXPOST)
        norm_to(GV_XAPRE, hnb, hnb.all)
        for hd in range(4):
            w_, wbf = ring.load(xq_d[hd], NCH, 128)
            bk = B[hd % 2]
            for c in range(NCH):
                cx.mm(bk[:, :], w_[:, c, :], hnb[:, c, :], c == 0, c == NCH - 1, wbf + hnb.all, bk.all)
            S.act(lambda e, hd=hd, bk=bk: e.mul(out=qx[:, hd, :], in_=bk[:, :], mul=SCALE), bk.all, qx.all)
        for hd in range(4):
            ob, db = B[2 + hd % 2], B[4 + hd % 2]
            for mt in range(2):
                sb_ = B[mt]
                cx.mm(sb_[:, :], kx[:, hd, mt * 128:(mt + 1) * 128], qx[:, hd, :], True, True, kx.all + qx.all, sb_.all)
                pT = pTs[mt]
                cx.activation(pT[:, :], sb_[:, :], AF.Exp, sb_.all, pT.all)
                cx.mm(ob[:, :], vx[:, mt, hd * 128:(hd + 1) * 128], pT[:, :], mt == 0, mt == 1, vx.all + pT.all, ob.all)
                cx.mm(db[:, :], cm.ones, pT[:, :], mt == 0, mt == 1, pT.all, db.all)
            S.dve(lambda e, db=db: e.reciprocal(out=rc[:, :], in_=db[:, :]), db.all, rc.all)
            cx.tt(ox[:, hd, :], ob[:, :], rc[:, :], ALU.mult, ob.all + rc.all, ox.all)
        for j in range(NCH):
            w_, wbf = ring.load(xo_d[j], 4, 128)
            bk = B[j % 2]
            for hd in range(4):
                cx.mm(bk[:, :], w_[:, hd, :], ox[:, hd, :], hd == 0, hd == 3, wbf + ox.all, bk.all)
            cx.copy(aT[:, j, :], bk[:, :], bk.all, [aT.b[j]], eng="act" if j % 2 else "dve")
        norm_add(GV_XAPOST)
        norm_to(GV_MLPPRE, hnb, hnb.all)
        uT = RA
        for fh in range(2):
            for fb in range(32):
                w_, wbf = ring.load(wup_d[fh * 32 + fb], NCH, 128)
                bk = B[fb % 4]
                for c in range(NCH):
                    cx.mm(bk[:, :], w_[:, c, :], hnb[:, c, :], c == 0, c == NCH - 1, wbf + hnb.all, bk.all)
                r_ = tmpA[fb % 2]
                cx.activation(r_[:, :], bk[:, :], AF.Relu, bk.all, r_.all)
                cx.tt(uT[:, fb, :], r_[:, :], r_[:, :], ALU.mult, r_.all, RA.all, eng="dve")
            for j in range(NCH):
                w_, wbf = ring.load(wdn_d[j * 2 + fh], 32, 128)
                bk = B[4 + j % 2]
                for fb in range(32):
                    cx.mm(bk[:, :], w_[:, fb, :], uT[:, fb, :], fb == 0, fb == 31, wbf + RA.all, bk.all)
                if fh == 0:
                    cx.copy(aT[:, j, :], bk[:, :], bk.all, [aT.b[j]], eng="act")
                else:
                    cx.tt(aT[:, j, :], bk[:, :], aT[:, j, :], ALU.add, bk.all + [aT.b[j]], [aT.b[j]])
        norm_add(GV_MLPPOST)
        for c in range(0, NCH, 4):
            cx.store(io.hT_out(hf, c, c + 4), hT[:, c:c + 4, :], hT.b[c:c + 4], is_output=io.h_is_output)
        if not last:
            norm_to(GV_NEXT, hnb, hnb.all)
            for j in range(2):
                io.hn_store(hf, j, hnb[:, :, j * 256:(j + 1) * 256], hnb.all)
        elif io.write_hn_when_last:
            for j in range(2):
                io.hn_store(hf, j, hnb[:, :, j * 256:(j + 1) * 256], hnb.all)


def tok_weights(inp, l):
    f32 = np.float32
    wg = inp["w_in"][l][:, IN_OFF["gl"]:]
    wg = wg.reshape(NCH, 128, 4, NCH, 128).transpose(3, 2, 1, 0, 4)
    wg = np.ascontiguousarray(wg).reshape(64, 128, NCH, 128)
    wb = inp["w_branch"][l].reshape(4, 4, 128, NCH, 128).transpose(3, 0, 2, 1, 4)
    wb = np.ascontiguousarray(wb).reshape(64, 128, 4, 128)

    def tiles(w, kc):
        K_, N_ = w.shape
        return np.ascontiguousarray(w.reshape(kc, 128, N_ // 128, 128).transpose(2, 1, 0, 3))
    wo = tiles(inp["w_mix_out"][l], NCH)
    xq = tiles(inp["xa_wq"][l], NCH)
    xk = tiles(inp["xa_wkv"][l][:, 0:512], NCH)
    wv = inp["xa_wkv"][l][:, 512:1024].reshape(2, 8, 128, 512).transpose(0, 2, 1, 3)
    xv = np.ascontiguousarray(wv)
    xo = tiles(inp["xa_wo"][l], 4)
    wup = tiles(inp["mlp_w_up"][l], NCH)
    wd = inp["mlp_w_down"][l].reshape(2, 32, 128, NCH, 128).transpose(3, 0, 2, 1, 4)
    wdn = np.ascontiguousarray(wd).reshape(32, 128, 32, 128)
    gv = np.zeros((128, 7 * NCH), f32)
    names = ["mix_norm_post", "xa_norm_pre", "xa_norm_mem", "xa_norm_post", "mlp_norm_pre", "mlp_norm_post"]
    for i, nme in enumerate(names):
        gv[:, i * NCH:(i + 1) * NCH] = inp[nme][l].reshape(NCH, 128).T
    if l + 1 < inp["mix_norm_pre"].shape[0]:
        gv[:, GV_NEXT * NCH:(GV_NEXT + 1) * NCH] = inp["mix_norm_pre"][l + 1].reshape(NCH, 128).T
    return {"wg": wg, "wb": wb, "wo": wo, "xq": xq, "xk": xk, "xv": xv, "xo": xo, "wup": wup, "wdn": wdn,
            "gv": gv, "consts": host_consts()["consts"]}


_PROGS = {}


def _prog(key, builder):
    if key not in _PROGS:
        _PROGS[key] = builder()
    return _PROGS[key]


def kernel_unfused(**inp):
    inp = {k: np.asarray(v) for k, v in inp.items()}
    depth = inp["w_in"].shape[0]
    cores = list(range(8))
    hc = host_consts()
    x = inp["x"]
    tsl = [slice((c % 4) * TOK, (c % 4 + 1) * TOK) for c in cores]
    hT = [np.ascontiguousarray(x[c // 4, tsl[c]].T) for c in cores]
    memT = [np.ascontiguousarray(inp["mem"][b].T) for b in range(BATCH)]
    g0 = np.ascontiguousarray(inp["mix_norm_pre"][0].reshape(NCH, 128).T)
    res = run_bass_kernel_spmd(_prog("pre", build_pre),
                               [{"xT": hT[c], "gpre": g0, "consts": hc["consts"]} for c in cores], core_ids=cores)
    hn = [np.asarray(res.results[c]["hn_out"]) for c in cores]
    for l in range(depth):
        hn_full = [np.ascontiguousarray(np.concatenate([hn[b * 4 + q] for q in range(4)], axis=1)) for b in range(BATCH)]
        res = run_bass_kernel_spmd(_prog(("mix", l), lambda: build_mix(l)),
                                   [mix_inputs(inp, l, c, hn_full[c // 4]) for c in cores], core_ids=cores)
        yT = [np.asarray(res.results[c]["yT"]) for c in cores]
        tw = tok_weights(inp, l)
        maps = []
        for c in cores:
            b = c // 4
            yb = np.stack([yT[b * 4 + h][:, :, tsl[c]] for h in range(4)], axis=1)
            m = dict(tw)
            m["hT"] = hT[c]
            m["hnT"] = hn[c]
            m["ybr"] = np.ascontiguousarray(yb.reshape(D_MODEL, TOK))
            m["memT"] = memT[b]
            maps.append(m)
        res = run_bass_kernel_spmd(_prog(("tok", l), lambda: build_tok(l, l == depth - 1)), maps, core_ids=cores)
        hT = [np.asarray(res.results[c]["hT_out"]) for c in cores]
        hn = [np.asarray(res.results[c]["hn_next"]) for c in cores]
    out = np.empty_like(x)
    for c in cores:
        out[c // 4, tsl[c]] = hT[c].T
    return out


def fused_inputs(inp, c):
    depth = inp["w_in"].shape[0]
    b, q = c // 4, c % 4
    hc = host_consts()
    m = {"consts": hc["consts"], "c32": hc["c32"], "cmask": hc["cmask"], "esel": hc["esel"], "hmask": hc["hmask"],
         "perm": hc["perm"], "ropeC": hc["ropeC"], "ropeS": hc["ropeS"], "poolM": hc[f"poolM{q}"],
         "xT": np.ascontiguousarray(inp["x"][b, q * TOK:(q + 1) * TOK].T),
         "memT": np.ascontiguousarray(inp["mem"][b].T),
         "gpre": np.ascontiguousarray(inp["mix_norm_pre"][0].reshape(NCH, 128).T)}
    for l in range(depth):
        mi = mix_inputs(inp, l, c, None)
        m[f"wmix_{l}"] = mi["wmix"]
        m[f"small_{l}"] = mi["small"]
        m[f"poolw_{l}"] = mi["poolw"]
    return m


def kernel(**inp):
    inp = {k: np.asarray(v) for k, v in inp.items()}
    depth = inp["w_in"].shape[0]
    cores = list(range(8))
    nc = _prog(("fused", depth), lambda: build_fused(depth))
    tws = []
    for l in range(depth):
        tw = tok_weights(inp, l)
        tw.pop("consts")
        tws.append({f"{k}_{l}": v for k, v in tw.items()})
    maps = []
    for c in cores:
        m = fused_inputs(inp, c)
        for tw in tws:
            m.update(tw)
        maps.append(m)
    res = run_bass_kernel_spmd(nc, maps, core_ids=cores)
    x = inp["x"]
    out = np.empty_like(x)
    for c in cores:
        out[c // 4, (c % 4) * TOK:(c % 4 + 1) * TOK] = np.asarray(res.results[c]["outT"]).T
    return out
```

```python
import contextlib
import numpy as np
import ml_dtypes
import concourse.bass as bass
import concourse.mybir as mybir
from concourse.bass_utils import run_bass_kernel_spmd

F32 = mybir.dt.float32
BF16 = mybir.dt.bfloat16
AF = mybir.ActivationFunctionType
ALU = mybir.AluOpType
AX = mybir.AxisListType
NPBF = ml_dtypes.bfloat16

D_MODEL = 2048
NCH = 16
BATCH = 2
SEQ = 4096
TOK = 1024
HD = 128
NEG = -30000.0
EPS = 1e-6
SCALE = HD ** -0.5
POOL_WINDOWS = (2, 4, 8, 16)
DEBUG_STOP = 0
ROPE_ADD_ENG = "dve"

COMPUTE = ("pe", "act", "dve", "pool")
N_DMA_SEMS = 12


class Buf:
    __slots__ = ("name", "w", "r", "excl")

    def __init__(self, name="", excl=False):
        self.name = name
        self.w = None
        self.r = []
        self.excl = excl


class Op:
    __slots__ = ("eng", "fn", "waits", "sig", "idx", "dma", "dma_ev")

    def __init__(self, eng, fn, sig, dma):
        self.eng = eng
        self.fn = fn
        self.waits = []
        self.sig = sig
        self.dma = dma
        self.dma_ev = None


class Sched:
    def __init__(self, nc):
        self.nc = nc
        self.ops = {e: [] for e in COMPUTE + ("sp",)}
        self.dma_count = {e: 0 for e in ("sp", "act", "pool")}
        self.out_events = []
        self.fence_waits = {e: [] for e in COMPUTE + ("sp",)}
        self.n_cc = 0

    def collective(self, fn, reads, writes):
        op = Op("pool", fn, False, False)
        op.dma = "cc"
        lst = self.ops["pool"]
        op.idx = len(lst)
        lst.append(op)
        ev = ("x", self.n_cc)
        op.dma_ev = ev
        self.n_cc += 1
        if self.fence_waits["pool"]:
            op.waits.extend(self.fence_waits["pool"])
            self.fence_waits["pool"] = []
        for b in reads:
            if b.w is not None:
                op.waits.append(b.w)
        for b in writes:
            if b.w is not None:
                op.waits.append(b.w)
            op.waits.extend(b.r)
        for d in op.waits:
            if d[0] == "c":
                self.ops[d[1]][d[2]].sig = True
        for b in reads:
            b.r.append(ev)
        for b in writes:
            b.w = ev
            b.r = []
        return ev

    def fence(self, include_cc=True):
        evs = []
        for e in COMPUTE + ("sp",):
            lst = self.ops[e]
            for op in reversed(lst):
                if not op.dma:
                    evs.append(("c", e, op.idx))
                    op.sig = True
                    break
        for q, n in self.dma_count.items():
            for i in range(max(0, n - N_DMA_SEMS), n):
                evs.append(("d", q, i))
        if include_cc:
            for i in range(self.n_cc):
                evs.append(("x", i))
        for e in self.fence_waits:
            self.fence_waits[e] = list(evs)

    def _add(self, eng, fn, reads, writes, sig=False, dma=False):
        op = Op(eng, fn, sig, dma)
        lst = self.ops[eng]
        op.idx = len(lst)
        lst.append(op)
        if dma:
            n = self.dma_count[eng]
            self.dma_count[eng] += 1
            ev = ("d", eng, n)
            op.dma_ev = ev
            if n >= N_DMA_SEMS:
                op.waits.append(("d", eng, n - N_DMA_SEMS))
        else:
            ev = ("c", eng, op.idx)
        if self.fence_waits[eng]:
            op.waits.extend(d for d in self.fence_waits[eng] if d != ev)
            self.fence_waits[eng] = []
        excl_reads = [b for b in reads if b.excl]
        if excl_reads:
            reads = [b for b in reads if not b.excl]
            writes = list(writes) + excl_reads
        for b in reads:
            if b.w is not None and b.w != ev:
                op.waits.append(b.w)
        for b in writes:
            if b.w is not None and b.w != ev:
                op.waits.append(b.w)
            for d in b.r:
                if d != ev:
                    op.waits.append(d)
        if eng == "pe" and not dma:
            op.waits = [d for d in op.waits if not (d[0] == "c" and d[1] == "pe")]
        for d in op.waits:
            if d[0] == "c":
                self.ops[d[1]][d[2]].sig = True
        for b in reads:
            if not dma:
                b.r = [d for d in b.r if not (d[0] == "c" and d[1] == eng)]
            b.r.append(ev)
        for b in writes:
            b.w = ev
            b.r = []
        return ev

    def pe(self, fn, reads, writes, sig=False):
        return self._add("pe", fn, reads, writes, sig)

    def act(self, fn, reads, writes):
        return self._add("act", fn, reads, writes)

    def dve(self, fn, reads, writes):
        return self._add("dve", fn, reads, writes)

    def pool(self, fn, reads, writes):
        return self._add("pool", fn, reads, writes)

    def dma(self, q, fn, reads, writes, is_output=False):
        ev = self._add(q, fn, reads, writes, dma=True)
        if is_output:
            self.out_events.append(ev)
        return ev

    def emit(self):
        nc = self.nc
        with contextlib.ExitStack() as st:
            csem = {e: st.enter_context(nc.semaphore("c_" + e)) for e in COMPUTE}
            dsem = {q: [st.enter_context(nc.semaphore(f"d_{q}{i}")) for i in range(N_DMA_SEMS)]
                    for q in ("sp", "act", "pool")}
            xsem = [st.enter_context(nc.semaphore(f"x_{i}")) for i in range(self.n_cc)]
            block = st.enter_context(nc.Block())
            sigcount = {}
            for e in COMPUTE:
                lst = self.ops[e]
                last = None
                for op in lst:
                    if not op.dma:
                        last = op
                if last is not None:
                    last.sig = True
                c = 0
                arr = []
                for op in lst:
                    if (not op.dma) and op.sig:
                        c += 1
                    arr.append(c)
                sigcount[e] = arr
            self.stats = {e: (len(self.ops[e]), sigcount[e][-1] if sigcount[e] else 0) for e in COMPUTE}
            self.stats["dma"] = dict(self.dma_count)

            def resolve(ev):
                if ev[0] == "c":
                    _, e, i = ev
                    op = self.ops[e][i]
                    v = sigcount[e][i]
                    assert op.sig
                    return ("c", e), csem[e], v
                if ev[0] == "x":
                    return ev, xsem[ev[1]], 1
                _, q, n = ev
                return ("d", q, n % N_DMA_SEMS), dsem[q][n % N_DMA_SEMS], 16 * (n // N_DMA_SEMS + 1)

            def run_engine(ename, eng):
                known = {}
                for op in self.ops[ename]:
                    need = {}
                    for ev in op.waits:
                        key, sem, v = resolve(ev)
                        if known.get(key, 0) >= v:
                            continue
                        if need.get(key, (None, 0))[1] < v:
                            need[key] = (sem, v)
                    for key, (sem, v) in need.items():
                        eng.wait_ge(sem, v)
                        known[key] = v
                    ins = op.fn(eng)
                    if op.dma == "cc":
                        ins.then_inc(xsem[op.dma_ev[1]])
                    elif op.dma:
                        _, q, n = op.dma_ev
                        ins.then_inc(dsem[q][n % N_DMA_SEMS], 16)
                    elif op.sig:
                        ins.then_inc(csem[ename], 1)
                if ename == "sp":
                    for ev in self.out_events:
                        key, sem, v = resolve(ev)
                        if known.get(key, 0) >= v:
                            continue
                        eng.wait_ge(sem, v)
                        known[key] = v

            @block.sync
            def _(eng):
                if getattr(self, "sp_init", None) is not None:
                    self.sp_init(eng)
                run_engine("sp", eng)

            @block.tensor
            def _(eng):
                run_engine("pe", eng)

            @block.scalar
            def _(eng):
                run_engine("act", eng)

            @block.vector
            def _(eng):
                run_engine("dve", eng)

            @block.gpsimd
            def _(eng):
                run_engine("pool", eng)


class T:
    def __init__(self, t, nsub=1, name="", psum=False):
        self.t = t
        if psum:
            self.b = [Buf(name, excl=True)] * nsub
        else:
            self.b = [Buf(f"{name}{i}") for i in range(nsub)]

    def __getitem__(self, k):
        return self.t[k]

    @property
    def all(self):
        return list(self.b)


class Ctx:
    def __init__(self):
        self.nc = bass.Bass("TRN2", target_bir_lowering=False)
        self.S = Sched(self.nc)
        self._n = 0
        self.scope = None

    def sb(self, shape, dt, nsub=1, name=None):
        self._n += 1
        name = (name or "sb") + f"_{self._n}"
        if self.scope is not None:
            return T(self.scope.enter_context(self.nc.sbuf_tensor(name, list(shape), dt)), nsub, name)
        return T(self.nc.alloc_sbuf_tensor(name, list(shape), dt), nsub, name)

    def ps(self, shape, dt=F32, nsub=1, name=None):
        self._n += 1
        name = name or f"ps{self._n}"
        return T(self.nc.alloc_psum_tensor(name, list(shape), dt), nsub, name, psum=True)

    def din(self, name, shape, dt):
        return self.nc.dram_tensor(name, list(shape), dt, kind="ExternalInput").ap()

    def dout(self, name, shape, dt):
        return self.nc.dram_tensor(name, list(shape), dt, kind="ExternalOutput").ap()

    def load(self, dst_ap, src_ap, wbufs, q="sp", rbufs=()):
        return self.S.dma(q, lambda e: e.dma_start(out=dst_ap, in_=src_ap), list(rbufs), list(wbufs))

    def store(self, dst_ap, src_ap, rbufs, q="sp", is_output=True, wbufs=()):
        return self.S.dma(q, lambda e: e.dma_start(out=dst_ap, in_=src_ap), list(rbufs), list(wbufs), is_output=is_output)

    def mm(self, out_ap, lhsT, rhs, start, stop, reads, writes, sig=None):
        if sig is None:
            sig = False
        return self.S.pe(lambda e: e.matmul(out_ap, lhsT, rhs, start=start, stop=stop), reads, writes, sig=sig)

    def transpose(self, out_ap, in_ap, ident_ap, reads, writes):
        return self.S.pe(lambda e: e.transpose(out_ap, in_ap, ident_ap), reads, writes)

    def activation(self, out_ap, in_ap, func, reads, writes, bias=None, scale=None):
        kw = {}
        if bias is not None:
            kw["bias"] = bias
        if scale is not None:
            kw["scale"] = scale
        return self.S.act(lambda e: e.activation(out=out_ap, in_=in_ap, func=func, **kw), reads, writes)

    def tt(self, out_ap, in0, in1, op, reads, writes, eng="dve"):
        f = lambda e: e.tensor_tensor(out=out_ap, in0=in0, in1=in1, op=op)
        return (self.S.dve if eng == "dve" else self.S.pool)(f, reads, writes)

    def ts(self, out_ap, in0, s1, s2, op0, op1, reads, writes, eng="dve"):
        if op1 is None:
            f = lambda e: e.tensor_scalar(out=out_ap, in0=in0, scalar1=s1, scalar2=None, op0=op0)
        else:
            f = lambda e: e.tensor_scalar(out=out_ap, in0=in0, scalar1=s1, scalar2=s2, op0=op0, op1=op1)
        return (self.S.dve if eng == "dve" else self.S.pool)(f, reads, writes)

    def stt(self, out_ap, in0, scalar, in1, op0, op1, reads, writes):
        return self.S.dve(lambda e: e.scalar_tensor_tensor(out=out_ap, in0=in0, scalar=scalar, in1=in1,
                                                            op0=op0, op1=op1), reads, writes)

    def copy(self, out_ap, in_ap, reads, writes, eng="dve"):
        if eng == "act":
            return self.S.act(lambda e: e.copy(out=out_ap, in_=in_ap), reads, writes)
        f = lambda e: e.tensor_copy(out=out_ap, in_=in_ap)
        return (self.S.dve if eng == "dve" else self.S.pool)(f, reads, writes)

    def memset(self, ap, val, writes, eng="pool"):
        f = lambda e: e.memset(ap, val)
        return (self.S.dve if eng == "dve" else self.S.pool)(f, [], writes)


class Common:
    def __init__(self, cx, consts_ap):
        self.cx = cx
        self.cb = cx.sb([128, 256], BF16, name="cbf")
        cx.load(self.cb[:, :], consts_ap, self.cb.all)
        self.ident = self.cb[:, 0:128]
        self.ones = self.cb[:, 128:256]
        self.banks = [cx.ps([128, 512], F32, nsub=4, name=f"bank{i}") for i in range(7)]
        self.bankb = cx.ps([128, 1024], BF16, nsub=8, name="bankb")


def rms_stats(cx, cm, src_fn, src_bufs, nfeat_chunks, ncols, sq_pool, bank, rstd, denom):
    for c in range(nfeat_chunks):
        sq = sq_pool[c % len(sq_pool)]
        cx.activation(sq[:, 0:ncols], src_fn(c), AF.Square, src_bufs(c), sq.all)
        cx.mm(bank[:, 0:ncols], cm.ones, sq[:, 0:ncols], c == 0, c == nfeat_chunks - 1,
              [cm.cb.b[0]] + sq.all, bank.all)
    cx.activation(rstd[:, 0:ncols], bank[:, 0:ncols], AF.Ln, bank.all + cm.epst.all, rstd.all, bias=cm.eps_ap,
                  scale=1.0 / denom)
    cx.activation(rstd[:, 0:ncols], rstd[:, 0:ncols], AF.Exp, rstd.all, rstd.all, scale=-0.5)


def add_eps(cx, cm):
    cm.epst = cx.sb([128, 1], F32, name="epst")
    cx.memset(cm.epst[:, :], EPS, cm.epst.all)
    cm.eps_ap = cm.epst[:, 0:1]


def phase_pre(cx, cm, xT, gpre, hn_out_fn, is_output):
    g = cx.sb([128, NCH], F32, name="g")
    cx.load(g[:, :], gpre, g.all)
    xv = xT.rearrange("(c p) t -> p c t", p=128)
    xs = [cx.sb([128, NCH, 512], F32, nsub=NCH, name=f"xs{i}") for i in range(2)]
    hs = [cx.sb([128, NCH, 512], BF16, nsub=1, name=f"hs{i}") for i in range(2)]
    sqp = [cx.sb([128, 512], BF16, name=f"sq{i}") for i in range(3)]
    rstd = cx.sb([128, 512], F32, name="rstd")
    for n in range(TOK // 512):
        x = xs[n % 2]
        h = hs[n % 2]
        cx.load(x[:, :, :], xv[:, :, n * 512:(n + 1) * 512], x.all)
        rms_stats(cx, cm, lambda c: x[:, c, :], lambda c: [x.b[c]], NCH, 512, sqp, cm.banks[n % 2], rstd, D_MODEL)
        for c in range(NCH):
            cx.stt(h[:, c, :], x[:, c, :], g[:, c:c + 1], rstd[:, :], ALU.mult, ALU.mult,
                   [x.b[c]] + g.all + rstd.all, h.all)
        for j in range(2):
            hn_out_fn(n, j, h[:, :, j * 256:(j + 1) * 256], h.all)


def build_pre():
    cx = Ctx()
    xT = cx.din("xT", [D_MODEL, TOK], F32)
    gpre = cx.din("gpre", [128, NCH], F32)
    consts = cx.din("consts", [128, 256], BF16)
    hn_out = cx.dout("hn_out", [D_MODEL, TOK], BF16)
    cm = Common(cx, consts)
    add_eps(cx, cm)
    ov = hn_out.rearrange("(c p) t -> p c t", p=128)
    phase_pre(cx, cm, xT, gpre,
              lambda n, j, src, rb: cx.store(ov[:, :, n * 512 + j * 256:n * 512 + (j + 1) * 256], src, rb), True)
    cx.S.emit()
    return cx.nc


W_OFF = {"moba": (0, 384), "fox": (384, 384), "hgrn": (768, 512), "pool": (1280, 128)}
W_MIXCOLS = 1408
SM_WFF, SM_G0, SM_G1, SM_HNORM, SM_PSCALE, SM_FB = 0, 16, 17, 18, 19, 20
N_SMALL = 24


class MixEnv:
    pass


def project(cx, cm, env, w, fm_blocks, tm, row=None, pre_chunk=None):
    nb = 0
    if getattr(env, "flush_cc", None) is not None:
        env.flush_cc()
    for n in range(SEQ // 512):
        hs = env.hs[n % 2]
        for j in range(2):
            cx.load(hs[:, :, j * 256:(j + 1) * 256], env.hn_chunk(n, j), hs.all, rbufs=env.hn_rbufs(n, j))
        if pre_chunk is not None:
            pre_chunk(n)
        for (c0, handler) in fm_blocks:
            bank = cm.banks[nb % 4]
            nb += 1
            for c in range(NCH):
                cx.mm(bank[:, :], w[:, c, c0:c0 + 128], hs[:, c, :], c == 0, c == NCH - 1,
                      w.all + hs.all, bank.all)
            handler(n, bank)
        if tm is not None:
            c0, ncols, handler = tm
            for tl in range(4):
                bank = cm.banks[nb % 4]
                nb += 1
                for c in range(NCH):
                    cx.mm(bank[:, 0:ncols], hs[:, c, tl * 128:(tl + 1) * 128], w[:, c, c0:c0 + ncols],
                          c == 0, c == NCH - 1, w.all + hs.all, bank.all)
                handler(n * 4 + tl, bank)
        if row is not None:
            lfn, rreads, handler = row
            bank = cm.banks[nb % 4]
            nb += 1
            for c in range(NCH):
                cx.mm(bank[0:1, :], lfn(c), hs[:, c, :], c == 0, c == NCH - 1, rreads + hs.all, bank.all)
            handler(n, bank)


def load_w(cx, env, name, eng="pool"):
    c0, nc_ = W_OFF[name]
    w = cx.sb([128, NCH, nc_], BF16, name="w_" + name)
    for c in range(0, NCH, 4):
        cx.load(w[:, c:c + 4, :], env.wmix[:, c:c + 4, c0:c0 + nc_], w.all, q=eng)
    return w


def softmax_finish(cx, env, obank, dbank, ncols, out_ap_dram, k):
    rc = env.rc[k % 2]
    yt = env.yt[k % 2]
    cx.S.dve(lambda e: e.reciprocal(out=rc[:, 0:ncols], in_=dbank[:, 0:ncols]), dbank.all, rc.all)
    cx.tt(yt[:, 0:ncols], obank[:, 0:ncols], rc[:, 0:ncols], ALU.mult, obank.all + rc.all, yt.all)
    cx.store(out_ap_dram, yt[:, 0:ncols], yt.all, is_output=env.y_is_output)


def mix_fox(cx, cm, env):
    S = cx.S
    w = load_w(cx, env, "fox")
    fQ = cx.sb([128, SEQ], BF16, name="fQ")
    fK = cx.sb([128, SEQ], BF16, name="fK")
    fV = cx.sb([128, 32, 128], BF16, name="fV")
    wff = cx.sb([128, NCH], BF16, name="wff")
    cx.copy(wff[:, :], env.small[:, SM_WFF:SM_WFF + NCH], env.small.all, wff.all)
    nfb = cx.sb([128, 1], F32, name="nfb")
    cx.ts(nfb[:, :], env.small[:, SM_FB:SM_FB + 1], -1.0, None, ALU.mult, None, env.small.all, nfb.all)
    sprow = cx.sb([1, SEQ], F32, name="sprow")
    cprow = cx.sb([1, SEQ], F32, name="cprow")
    nrh = cx.sb([1, SEQ], BF16, name="nrh")
    nrl = cx.sb([1, SEQ], BF16, name="nrl")
    rowtmp = cx.sb([1, 512], F32, name="rowtmp")

    def h_q(n, bank):
        S.act(lambda e: e.mul(out=fQ[:, n * 512:(n + 1) * 512], in_=bank[:, :], mul=SCALE), bank.all, fQ.all)

    def h_k(n, bank):
        cx.copy(fK[:, n * 512:(n + 1) * 512], bank[:, :], bank.all, fK.all)

    def h_v(tl, bank):
        cx.copy(fV[:, tl, :], bank[:, 0:128], bank.all, fV.all, eng="act" if tl % 2 else "dve")

    def h_row(n, bank):
        cx.activation(rowtmp[0:1, :], bank[0:1, :], AF.Exp, bank.all + nfb.all, rowtmp.all, bias=nfb[0:1, 0:1], scale=-1.0)
        cx.activation(sprow[0:1, n * 512:(n + 1) * 512], rowtmp[0:1, :], AF.Ln, rowtmp.all + cm.c32.all, sprow.all,
                      bias=cm.one_ap[0:1, 0:1])

    project(cx, cm, env, w, [(0, h_q), (128, h_k)], (256, 128, h_v),
            row=(lambda c: wff[:, c:c + 1], wff.all, h_row))

    cx.memset(nrh[0:1, :], 1.0, nrh.all, eng="dve")
    S.dve(lambda e: e.tensor_tensor_scan(out=cprow[0:1, :], data0=nrh[0:1, :], data1=sprow[0:1, :], initial=0.0,
                                         op0=ALU.mult, op1=ALU.add), nrh.all + sprow.all, cprow.all)
    sm = cm.banks[6]
    cpv = cprow[0:1, :].rearrange("o (a b) -> o a b", b=512)
    cx.mm(sm[:, 0:8], cm.ones32[0:1, 0:128], cpv[:, :, 0], True, True, cm.c32.all + cprow.all, sm.all)
    for kt in range(32):
        cx.mm(sm[:, 8 + kt:9 + kt], cprow[0:1, kt * 128:(kt + 1) * 128], cm.ones32[0:1, 0:1], True, True,
              cm.c32.all + cprow.all, sm.all)
    rbcp = cx.sb([128, 40], F32, name="rbcp")
    cx.copy(rbcp[:, :], sm[:, 0:40], sm.all, rbcp.all)
    for qc in range(8):
        sl = slice(qc * 512, (qc + 1) * 512)
        cx.ts(sprow[0:1, sl], cprow[0:1, sl], cprow[0:1, qc * 512:qc * 512 + 1], -1.0, ALU.subtract, ALU.mult,
              cprow.all, sprow.all)
    cx.copy(nrh[0:1, :], sprow[0:1, :], sprow.all, nrh.all)
    cx.tt(nrl[0:1, :], sprow[0:1, :], nrh[0:1, :], ALU.subtract, sprow.all + nrh.all, nrl.all)

    biasq = [cx.sb([128, 32], F32, name=f"biasq{i}") for i in range(2)]
    pTs = [cx.sb([128, 512], BF16, name=f"fpT{i}") for i in range(3)]
    it = 0
    for qc in range(8):
        sl = slice(qc * 512, (qc + 1) * 512)
        bq = biasq[qc % 2]
        cx.ts(bq[:, :], rbcp[:, 8:40], rbcp[:, qc:qc + 1], None, ALU.subtract, None, rbcp.all, bq.all)
        obank = cm.banks[2 + qc % 2]
        dbank = cm.banks[4 + qc % 2]
        nkt = 4 * (qc + 1)

        def emit_S(kt):
            sbank = cm.banks[kt % 2]
            a = kt - 4 * qc
            cx.mm(sbank[:, :], fK[:, kt * 128:(kt + 1) * 128], fQ[:, sl], True, False, fK.all + fQ.all, sbank.all)
            cx.mm(sbank[:, :], cm.ones[0:1, 0:128], nrh[0:1, sl], False, False, nrh.all, sbank.all)
            cx.mm(sbank[:, :], cm.ones[0:1, 0:128], nrl[0:1, sl], False, a < 0, nrl.all, sbank.all)
            if a >= 0:
                cx.mm(sbank[:, :], cm.ident, env.cmask[:, a, :], False, True, env.cmask.all, sbank.all)
        emit_S(0)
        for kt in range(nkt):
            sbank = cm.banks[kt % 2]
            pT = pTs[it % 3]
            it += 1
            cx.activation(pT[:, :], sbank[:, :], AF.Exp, sbank.all + bq.all, pT.all, bias=bq[:, kt:kt + 1])
            if kt + 1 < nkt:
                emit_S(kt + 1)
            cx.mm(obank[:, :], fV[:, kt, :], pT[:, :], kt == 0, kt == nkt - 1, fV.all + pT.all, obank.all)
            cx.mm(dbank[:, :], cm.ones, pT[:, :], kt == 0, kt == nkt - 1, pT.all, dbank.all)
        softmax_finish(cx, env, obank, dbank, 512, env.y_out(1, qc * 512, 512), qc)


def mix_moba(cx, cm, env):
    S = cx.S
    w = load_w(cx, env, "moba")
    mQ = cx.sb([128, SEQ], BF16, name="mQ")
    mK = cx.sb([128, SEQ], BF16, name="mK")
    mV = cx.sb([128, 32, 128], BF16, name="mV")
    perm = cx.sb([128, 128], BF16, name="perm")
    cx.load(perm[:, :], env.perm, perm.all)
    rC = [cx.sb([128, 512], F32, name=f"rC{i}") for i in range(2)]
    rS = [cx.sb([128, 512], F32, name=f"rS{i}") for i in range(2)]
    xb = [cx.sb([128, 512], BF16, name=f"xb{i}") for i in range(2)]
    t1 = [cx.sb([128, 512], F32, name=f"t1{i}") for i in range(2)]
    t2 = [cx.sb([128, 512], F32, name=f"t2{i}") for i in range(2)]
    cnt = [0]

    def pre_chunk(n):
        cx.load(rC[n % 2][:, :], env.ropeC[:, n * 512:(n + 1) * 512], rC[n % 2].all)
        cx.load(rS[n % 2][:, :], env.ropeS[:, n * 512:(n + 1) * 512], rS[n % 2].all)

    def rope_handler(dst, sc):
        def h(n, bank):
            k = cnt[0] % 2
            cnt[0] += 1
            sl = slice(n * 512, (n + 1) * 512)
            if DEBUG_STOP == 10:
                cx.copy(dst[:, sl], bank[:, :], bank.all, dst.all)
                return
            cx.copy(xb[k][:, :], bank[:, :], bank.all, xb[k].all, eng="act")
            swb = cm.banks[4 + k]
            cx.mm(swb[:, :], perm[:, :], xb[k][:, :], True, True, perm.all + xb[k].all, swb.all)
            if DEBUG_STOP == 11:
                cx.copy(dst[:, sl], swb[:, :], swb.all, dst.all)
                return
            if DEBUG_STOP == 13:
                cx.stt(t1[k][:, :], bank[:, :], sc, t2[k][:, :], ALU.mult, ALU.mult, bank.all + t2[k].all, t1[k].all)
            else:
                cx.stt(t1[k][:, :], bank[:, :], sc, rC[n % 2][:, :], ALU.mult, ALU.mult, bank.all + rC[n % 2].all, t1[k].all)
            if DEBUG_STOP in (12, 13):
                cx.copy(dst[:, sl], t1[k][:, :], t1[k].all, dst.all)
                return
            cx.stt(t2[k][:, :], swb[:, :], sc, rS[n % 2][:, :], ALU.mult, ALU.mult, swb.all + rS[n % 2].all, t2[k].all)
            cx.tt(dst[:, sl], t1[k][:, :], t2[k][:, :], ALU.add, t1[k].all + t2[k].all, dst.all, eng=ROPE_ADD_ENG)
        return h

    def h_v(tl, bank):
        cx.copy(mV[:, tl, :], bank[:, 0:128], bank.all, mV.all, eng="act")

    project(cx, cm, env, w, [(0, rope_handler(mQ, SCALE)), (128, rope_handler(mK, 1.0))], (256, 128, h_v),
            pre_chunk=pre_chunk)

    if DEBUG_STOP in (1, 10, 11, 12, 13):
        return
    kb32 = cx.sb([128, 16], F32, name="kb32")
    kbT = cx.sb([128, 16], BF16, name="kbT")
    S.dve(lambda e: e.tensor_reduce(out=kb32[:, :], in_=mK[:, :].rearrange("p (j k) -> p j k", k=256),
                                    axis=AX.X, op=ALU.add), mK.all, kb32.all)
    cx.copy(kbT[:, :], kb32[:, :], kb32.all, kbT.all)
    gb = cm.banks[6]
    for qt in range(32):
        cx.mm(gb[:, qt * 16:(qt + 1) * 16], mQ[:, qt * 128:(qt + 1) * 128], kbT[:, :], True, True,
              mQ.all + kbT.all, gb.all)
    g_sb = cx.sb([128, 32, 16], F32, name="g_sb")
    cx.copy(g_sb[:, :, :], gb[:, :].rearrange("p (a b) -> p a b", b=16), gb.all, g_sb.all)
    if DEBUG_STOP == 2:
        return
    S.pool(lambda e: e.affine_select(out=g_sb[:, :, :], in_=g_sb[:, :, :], pattern=[[1, 16], [0, 2], [-1, 16]],
                                     compare_op=ALU.is_ge, fill=-1e30, base=-1, channel_multiplier=0),
           g_sb.all, g_sb.all)
    if DEBUG_STOP == 3:
        return
    m8 = cx.sb([128, 32, 8], F32, name="m8")
    for qt in range(32):
        S.dve(lambda e, qt=qt: e.max(out=m8[:, qt, :], in_=g_sb[:, qt, :]), g_sb.all, m8.all)
    thr = cx.sb([128, 32, 1], F32, name="thr")
    cx.ts(thr[:, :, :], m8[:, :, 2:3], -1e29, None, ALU.max, None, m8.all, thr.all)
    nm = cx.sb([128, 32, 16], F32, name="nm")
    cx.tt(nm[:, :, :], g_sb[:, :, :], thr[:, :, :].to_broadcast([128, 32, 16]), ALU.is_lt, g_sb.all + thr.all, nm.all)
    cx.ts(nm[:, :, :], nm[:, :, :], NEG, None, ALU.mult, None, nm.all, nm.all)
    if DEBUG_STOP == 4:
        return
    nmT = cx.sb([16, SEQ], BF16, name="nmT")
    for grp in range(8):
        tb = cm.banks[grp % 2]
        for i in range(4):
            qt = grp * 4 + i
            cx.transpose(tb[0:16, i * 128:(i + 1) * 128], nm[:, qt, :], cm.ident32, nm.all + cm.c32.all, tb.all)
        cx.copy(nmT[0:16, grp * 512:(grp + 1) * 512], tb[0:16, :], tb.all, nmT.all, eng="act" if grp % 2 else "dve")

    if DEBUG_STOP == 5:
        return
    pTs = [cx.sb([128, 256], BF16, name=f"mpT{i}") for i in range(3)]
    it = 0
    for qb in range(16):
        sl = slice(qb * 256, (qb + 1) * 256)
        obank = cm.banks[2 + qb % 2]
        dbank = cm.banks[4 + qb % 2]
        nkt = 2 * qb + 2

        def emit_S(kt):
            sbank = cm.banks[kt % 2]
            cx.mm(sbank[:, 0:256], mK[:, kt * 128:(kt + 1) * 128], mQ[:, sl], True, False, mK.all + mQ.all, sbank.all)
            if kt < 2 * qb:
                cx.mm(sbank[:, 0:256], env.esel[0:16, kt // 2, :], nmT[0:16, sl], False, True,
                      env.esel.all + nmT.all, sbank.all)
            else:
                cx.mm(sbank[:, 0:256], cm.ident, env.cmask[:, kt - 2 * qb, 0:256], False, True, env.cmask.all, sbank.all)
        emit_S(0)
        for kt in range(nkt):
            sbank = cm.banks[kt % 2]
            pT = pTs[it % 3]
            it += 1
            cx.activation(pT[:, :], sbank[:, 0:256], AF.Exp, sbank.all, pT.all)
            if kt + 1 < nkt:
                emit_S(kt + 1)
            cx.mm(obank[:, 0:256], mV[:, kt, :], pT[:, :], kt == 0, kt == nkt - 1, mV.all + pT.all, obank.all)
            cx.mm(dbank[:, 0:256], cm.ones, pT[:, :], kt == 0, kt == nkt - 1, pT.all, dbank.all)
        softmax_finish(cx, env, obank, dbank, 256, env.y_out(0, qb * 256, 256), qb)


def mix_hgrn(cx, cm, env, layer):
    S = cx.S
    w = load_w(cx, env, "hgrn")
    hq = cx.sb([128, SEQ], F32, name="hq")
    lf = cx.sb([128, SEQ], F32, name="lf")
    hk = cx.sb([128, SEQ], BF16, name="hk")
    sg = cx.sb([128, SEQ], BF16, name="sg")
    hv = cx.sb([128, 32, 128], BF16, name="hv")
    lb = cx.sb([128, 2], F32, name="lb")
    if layer == 0:
        cx.memset(lb[:, 0:1], 0.0, lb.all, eng="dve")
        cx.memset(lb[:, 1:2], 1.0, lb.all, eng="dve")
    else:
        ee = cx.sb([128, 4], F32, name="ee")
        cx.activation(ee[:, 0:2], env.small[:, SM_G0:SM_G0 + 2], AF.Exp, env.small.all, ee.all)
        cx.tt(ee[:, 2:3], ee[:, 0:1], ee[:, 1:2], ALU.add, ee.all, ee.all)
        S.dve(lambda e: e.reciprocal(out=ee[:, 3:4], in_=ee[:, 2:3]), ee.all, ee.all)
        cx.tt(lb[:, 0:1], ee[:, 1:2], ee[:, 3:4], ALU.mult, ee.all, lb.all)
        cx.ts(lb[:, 1:2], lb[:, 0:1], -1.0, 1.0, ALU.mult, ALU.add, lb.all, lb.all)
    sgm = [cx.sb([128, 512], F32, name=f"sgm{i}") for i in range(2)]
    ff_ = [cx.sb([128, 512], F32, name=f"ff{i}") for i in range(2)]

    def h_q(n, bank):
        cx.copy(hq[:, n * 512:(n + 1) * 512], bank[:, :], bank.all, hq.all)

    def h_f(n, bank):
        sl = slice(n * 512, (n + 1) * 512)
        a, f = sgm[n % 2], ff_[n % 2]
        cx.activation(a[:, :], bank[:, :], AF.Sigmoid, bank.all, a.all)
        cx.ts(f[:, :], a[:, :], lb[:, 1:2], lb[:, 0:1], ALU.mult, ALU.add, a.all + lb.all, f.all)
        cx.activation(lf[:, sl], f[:, :], AF.Ln, f.all, lf.all)
        cx.ts(hk[:, sl], f[:, :], -1.0, 1.0, ALU.mult, ALU.add, f.all, hk.all, eng="pool")

    def h_g(n, bank):
        sl = slice(n * 512, (n + 1) * 512)
        a = sgm[n % 2]
        cx.activation(a[:, :], bank[:, :], AF.Sigmoid, bank.all, a.all)
        cx.tt(sg[:, sl], bank[:, :], a[:, :], ALU.mult, bank.all + a.all, sg.all)

    def h_v(tl, bank):
        cx.copy(hv[:, tl, :], bank[:, 0:128], bank.all, hv.all, eng="act" if tl % 2 else "dve")

    project(cx, cm, env, w, [(0, h_q), (128, h_f), (256, h_g)], (384, 128, h_v))

    rm = cx.sb([128, SEQ], BF16, name="rm")
    cx.memset(rm[:, :], 1.0, rm.all)
    cx.memset(rm[:, :].rearrange("p (a b) -> p a b", b=64)[:, :, 0:1], 0.0, rm.all)
    bb = cx.sb([128, SEQ], F32, name="bb")
    S.dve(lambda e: e.tensor_tensor_scan(out=bb[:, :], data0=rm[:, :], data1=lf[:, :], initial=0.0,
                                         op0=ALU.mult, op1=ALU.add), rm.all + lf.all, bb.all)
    bv = bb[:, :].rearrange("p (a b) -> p a b", b=64)
    sm = cx.sb([128, 5, 64], F32, name="hsm")
    cx.copy(sm[:, 0, :], bv[:, :, 31], bb.all, sm.all)
    cx.activation(sm[:, 4, :], bv[:, :, 31], AF.Exp, bb.all, sm.all)
    cx.activation(sm[:, 1, :], bv[:, :, 63], AF.Exp, bb.all, sm.all)
    cx.tt(sm[:, 3, :], bv[:, :, 63], sm[:, 0, :], ALU.subtract, bb.all + sm.all, sm.all)
    cx.activation(sm[:, 2, :], sm[:, 3, :], AF.Exp, sm.all, sm.all)
    cx.tt(bv, bv, sm[:, 0, :].unsqueeze(2).to_broadcast([128, 64, 64]), ALU.subtract, bb.all + sm.all, bb.all)
    E = lf
    qt_ = cx.sb([128, SEQ], BF16, name="qtil")
    kt_ = cx.sb([128, SEQ], BF16, name="ktil")
    kh_ = cx.sb([128, SEQ], BF16, name="khat")
    cx.activation(E[:, :], bb[:, :], AF.Exp, bb.all, E.all)
    cx.tt(qt_[:, :], hq[:, :], E[:, :], ALU.mult, hq.all + E.all, qt_.all)
    cx.activation(E[:, :], bb[:, :], AF.Exp, bb.all, E.all, scale=-1.0)
    cx.tt(kt_[:, :], hk[:, :], E[:, :], ALU.mult, hk.all + E.all, kt_.all)
    cx.tt(kh_[:, :].rearrange("p (a b) -> p a b", b=64), kt_[:, :].rearrange("p (a b) -> p a b", b=64),
          sm[:, 2, :].unsqueeze(2).to_broadcast([128, 64, 64]), ALU.mult, kt_.all + sm.all, kh_.all, eng="pool")
    khtm = cx.sb([128, 32, 128], BF16, name="khtm")
    for grp in range(4):
        for i in range(8):
            tl = grp * 8 + i
            cx.transpose(cm.bankb[:, i * 128:(i + 1) * 128], kh_[:, tl * 128:(tl + 1) * 128], cm.ident,
                         kh_.all, cm.bankb.all)
        cx.copy(khtm[:, grp * 8:(grp + 1) * 8, :], cm.bankb[:, :].rearrange("p (a b) -> p a b", b=128),
                cm.bankb.all, khtm.all, eng="act" if grp % 2 else "dve")
    attT = cx.sb([128, 32, 128], BF16, nsub=8, name="attT")
    for grp in range(8):
        ab = cm.banks[grp % 2]
        for i in range(4):
            tl = grp * 4 + i
            ts_ = slice(tl * 128, (tl + 1) * 128)
            cx.mm(ab[:, i * 128:(i + 1) * 128], kt_[:, ts_], qt_[:, ts_], True, True, kt_.all + qt_.all, ab.all)
        cx.tt(attT[:, grp * 4:(grp + 1) * 4, :], ab[:, :].rearrange("p (a b) -> p a b", b=128),
              env.hmask[:, :].unsqueeze(1).to_broadcast([128, 4, 128]), ALU.mult, ab.all + env.hmask.all, [attT.b[grp]])
    S32 = cx.sb([128, 2, 128], F32, nsub=2, name="S32")
    Sb = cx.sb([128, 4, 128], BF16, nsub=4, name="Sb")
    cx.memset(S32[:, 0, :], 0.0, [S32.b[0]], eng="dve")
    cx.memset(Sb[:, 0, :], 0.0, [Sb.b[0]], eng="dve")
    oT = [cx.sb([128, 512], F32, name=f"oT{i}") for i in range(2)]
    sq = [cx.sb([128, 512], BF16, name=f"osq{i}") for i in range(2)]
    rs = [cx.sb([128, 512], F32, name=f"ors{i}") for i in range(2)]
    yo = [cx.sb([128, 512], BF16, name=f"oyo{i}") for i in range(2)]
    mbanks = [cm.banks[4], cm.banks[5], cm.banks[0], cm.banks[1]]

    def emit_M(c):
        ps_ = slice((c % 2) * 64, (c % 2) * 64 + 64)
        mb_ = mbanks[c % 4]
        cx.mm(mb_[:, 0:128], khtm[ps_, c // 2, :], hv[ps_, c // 2, :], True, True, khtm.all + hv.all, mb_.all)
    for c in range(3):
        emit_M(c)
    for tl in range(32):
        ob = cm.banks[2 + (tl // 4) % 2]
        for half in range(2):
            c = 2 * tl + half
            ps = slice(half * 64, half * 64 + 64)
            mslot = 0
            mb = mbanks[c % 4]
            oc = slice((tl % 4) * 128 + half * 64, (tl % 4) * 128 + half * 64 + 64)
            tsl = slice(c * 64, c * 64 + 64)
            cx.mm(ob[:, oc], hv[:, tl, :], attT[:, tl, half * 64:half * 64 + 64], True, False,
                  hv.all + [attT.b[tl // 4]], [ob.b[tl % 4]])
            cx.mm(ob[:, oc], Sb[:, c % 4, :], qt_[:, tsl], False, True, [Sb.b[c % 4]] + qt_.all, [ob.b[tl % 4]])
            cx.stt(S32[:, (c + 1) % 2, :], S32[:, c % 2, :], sm[:, 1, c:c + 1], mb[:, mslot * 128:(mslot + 1) * 128],
                   ALU.mult, ALU.add, [S32.b[c % 2], mb.b[mslot]] + sm.all, [S32.b[(c + 1) % 2]])
            if c + 1 < 64:
                cx.S.act(lambda e, c=c: e.mul(out=Sb[:, (c + 1) % 4, :], in_=S32[:, (c + 1) % 2, :], mul=sm[:, 4, c + 1:c + 2]),
                         [S32.b[(c + 1) % 2]] + sm.all, [Sb.b[(c + 1) % 4]])
            if c + 3 < 64:
                emit_M(c + 3)
        if tl % 4 == 3:
            n = tl // 4
            k = n % 2
            sl = slice(n * 512, (n + 1) * 512)
            cx.copy(oT[k][:, :], ob[:, :], ob.all, oT[k].all)
            cx.activation(sq[k][:, :], ob[:, :], AF.Square, ob.all, sq[k].all)
            nb_ = cm.banks[6]
            cx.mm(nb_[:, :], cm.ones, sq[k][:, :], True, True, sq[k].all, nb_.all)
            cx.activation(rs[k][:, :], nb_[:, :], AF.Ln, nb_.all + cm.epst.all, rs[k].all, bias=cm.eps_ap, scale=1.0 / HD)
            cx.activation(rs[k][:, :], rs[k][:, :], AF.Exp, rs[k].all, rs[k].all, scale=-0.5)
            cx.stt(oT[k][:, :], oT[k][:, :], env.small[:, SM_HNORM:SM_HNORM + 1], rs[k][:, :], ALU.mult, ALU.mult,
                   oT[k].all + rs[k].all + env.small.all, oT[k].all)
            cx.tt(yo[k][:, :], oT[k][:, :], sg[:, sl], ALU.mult, oT[k].all + sg.all, yo[k].all, eng="pool")
            cx.store(env.y_out(2, n * 512, 512), yo[k][:, :], yo[k].all, is_output=env.y_is_output)


def mix_pool(cx, cm, env):
    w = load_w(cx, env, "pool")
    pin = cx.sb([128, 32, 128], BF16, nsub=32, name="pin")
    pM = cx.sb([128, 3, 128], BF16, name="pM")
    cx.load(pM[:, :, :], env.poolM, pM.all)
    pw = cx.sb([128, 128], BF16, name="pw")
    cx.load(pw[:, :], env.poolw, pw.all, q="pool")

    def h_p(tl, bank):
        cx.copy(pin[:, tl, :], bank[:, 0:128], bank.all, [pin.b[tl]], eng="act" if tl % 2 else "dve")

    project(cx, cm, env, w, [], (0, 128, h_p))
    pt = [cx.sb([128, 512], BF16, name=f"ppt{i}") for i in range(2)]
    yo = [cx.sb([128, 512], BF16, name=f"pyo{i}") for i in range(2)]
    for n in range(8):
        pb = cm.banks[n % 2]
        for i in range(4):
            tl = n * 4 + i
            cs = slice(i * 128, (i + 1) * 128)
            cx.mm(pb[:, cs], pin[:, tl, :], pM[:, 0 if tl == 0 else 1, :], True, tl == 0, [pin.b[tl]] + pM.all, [pb.b[i]])
            if tl > 0:
                cx.mm(pb[:, cs], pin[:, tl - 1, :], pM[:, 2, :], False, True, [pin.b[tl - 1]] + pM.all, [pb.b[i]])
        k = n % 2
        cx.copy(pt[k][:, :], pb[:, :], pb.all, pt[k].all, eng="act")
        ob = cm.banks[2 + n % 2]
        cx.mm(ob[:, :], pw[:, :], pt[k][:, :], True, True, pw.all + pt[k].all, ob.all)
        cx.ts(yo[k][:, :], ob[:, :], env.small[:, SM_PSCALE:SM_PSCALE + 1], None, ALU.mult, None,
              ob.all + env.small.all, yo[k].all)
        cx.store(env.y_out(3, n * 512, 512), yo[k][:, :], yo[k].all, is_output=env.y_is_output)


def setup_common32(cx, cm, c32_ap):
    cm.c32 = cx.sb([128, 256], F32, name="c32")
    cx.load(cm.c32[:, :], c32_ap, cm.c32.all)
    cm.ident32 = cm.c32[:, 0:128]
    cm.ones32 = cm.c32[:, 128:256]
    cm.one_ap = cm.c32[:, 128:129]


def mix_globals(cx, env, cmask_d, esel_d, hmask_d):
    env.cmask = cx.sb([128, 4, 512], BF16, name="cmask")
    cx.load(env.cmask[:, :, :], cmask_d, env.cmask.all)
    env.esel = cx.sb([16, 16, 128], BF16, name="esel")
    cx.load(env.esel[:, :, :], esel_d, env.esel.all)
    env.hmask = cx.sb([128, 128], BF16, name="hmask")
    cx.load(env.hmask[:, :], hmask_d, env.hmask.all)


def phase_mix(cx, cm, env, layer, small_d, which=("fox", "moba", "hgrn", "pool")):
    with contextlib.ExitStack() as outer:
        cx.scope = outer
        env.small = cx.sb([128, N_SMALL], F32, name="small")
        cx.load(env.small[:, :], small_d, env.small.all)
        env.hs = [cx.sb([128, NCH, 512], BF16, name=f"hs{i}") for i in range(2)]
        env.rc = [cx.sb([128, 512], F32, name=f"rc{i}") for i in range(2)]
        env.yt = [cx.sb([128, 512], BF16, name=f"yt{i}") for i in range(2)]
        fns = {"fox": lambda: mix_fox(cx, cm, env), "moba": lambda: mix_moba(cx, cm, env),
               "hgrn": lambda: mix_hgrn(cx, cm, env, layer), "pool": lambda: mix_pool(cx, cm, env)}
        for name in which:
            with contextlib.ExitStack() as sc:
                cx.scope = sc
                fns[name]()
                cx.S.fence()
                if getattr(env, "after_mixer", None) is not None:
                    env.after_mixer(name)
            cx.scope = outer
    cx.scope = None


def build_mix(layer, which=("fox", "moba", "hgrn", "pool")):
    cx = Ctx()
    env = MixEnv()
    env.hnT = cx.din("hnT", [D_MODEL, SEQ], BF16)
    env.wmix = cx.din("wmix", [128, NCH, W_MIXCOLS], F32)
    small_d = cx.din("small", [128, N_SMALL], F32)
    env.poolw = cx.din("poolw", [128, 128], F32)
    consts = cx.din("consts", [128, 256], BF16)
    c32 = cx.din("c32", [128, 256], F32)
    cmask_d = cx.din("cmask", [128, 4, 512], BF16)
    env.perm = cx.din("perm", [128, 128], BF16)
    env.ropeC = cx.din("ropeC", [128, SEQ], F32)
    env.ropeS = cx.din("ropeS", [128, SEQ], F32)
    esel_d = cx.din("esel", [16, 16, 128], BF16)
    hmask_d = cx.din("hmask", [128, 128], BF16)
    env.poolM = cx.din("poolM", [128, 3, 128], BF16)
    env.yT = cx.dout("yT", [4, 128, SEQ], BF16)
    hview_ = env.hnT.rearrange("(c p) t -> p c t", p=128)
    env.hn_chunk = lambda n, j: hview_[:, :, n * 512 + j * 256:n * 512 + (j + 1) * 256]
    env.hn_rbufs = lambda n, j: []
    env.y_out = lambda bi, t0, ncols: env.yT[bi, :, t0:t0 + ncols]
    env.y_is_output = True
    cm = Common(cx, consts)
    add_eps(cx, cm)
    setup_common32(cx, cm, c32)
    mix_globals(cx, env, cmask_d, esel_d, hmask_d)
    phase_mix(cx, cm, env, layer, small_d, which)
    cx.S.emit()
    return cx.nc


CC_GROUPS = [[0, 1, 2, 3], [4, 5, 6, 7]]


def _core_quarter(cx, e):
    if getattr(cx, "_q", None) is None:
        cx._q = e.snap(e.partition_id() % 4, min_val=0, max_val=3)
    return cx._q


def build_fused(depth):
    cx = Ctx()
    nc = cx.nc
    S = cx.S
    consts = cx.din("consts", [128, 256], BF16)
    c32 = cx.din("c32", [128, 256], F32)
    xT = cx.din("xT", [D_MODEL, TOK], F32)
    memT = cx.din("memT", [D_MODEL, NMEM], F32)
    gpre = cx.din("gpre", [128, NCH], F32)
    cmask_d = cx.din("cmask", [128, 4, 512], BF16)
    esel_d = cx.din("esel", [16, 16, 128], BF16)
    hmask_d = cx.din("hmask", [128, 128], BF16)
    env = MixEnv()
    env.perm = cx.din("perm", [128, 128], BF16)
    env.ropeC = cx.din("ropeC", [128, SEQ], F32)
    env.ropeS = cx.din("ropeS", [128, SEQ], F32)
    env.poolM = cx.din("poolM", [128, 3, 128], BF16)
    wmix_d = [cx.din(f"wmix_{l}", [128, NCH, W_MIXCOLS], F32) for l in range(depth)]
    small_d = [cx.din(f"small_{l}", [128, N_SMALL], F32) for l in range(depth)]
    poolw_d = [cx.din(f"poolw_{l}", [128, 128], F32) for l in range(depth)]
    ios = []
    for l in range(depth):
        io = TokIO()
        tok_weight_inputs(cx, io, f"_{l}")
        io.memT = memT
        ios.append(io)
    outT = cx.dout("outT", [D_MODEL, TOK], F32)
    hn_own = [nc.dram_tensor(f"hn_own{k}", [D_MODEL, 256], BF16, kind="Internal").ap() for k in range(4)]
    hn_all = [nc.dram_tensor(f"hn_all{k}", [4 * D_MODEL, 256], BF16, kind="Internal").ap() for k in range(4)]
    hnp = [Buf(f"hnp{k}") for k in range(4)]
    ya = [Buf(f"ya{k}") for k in range(4)]
    hna = [Buf(f"hna{k}") for k in range(4)]

    def hn_store(piece, src, rb):
        cx.store(hn_own_v[piece], src, rb, is_output=False, wbufs=[hnp[piece]])

    def hn_gather(piece):
        S.collective(lambda e: e.collective_compute("AllGather", ALU.bypass, replica_groups=CC_GROUPS,
                                                    ins=[hn_own[piece].opt()], outs=[hn_all[piece].opt()]),
                     [hnp[piece]], [hna[piece]])
    y_own = [nc.dram_tensor(f"y_own{n}", [512, TOK], BF16, kind="Internal").ap() for n in range(4)]
    y_all = [nc.dram_tensor(f"y_all{n}", [4 * 512, TOK], BF16, kind="Internal").ap() for n in range(4)]
    h_res = nc.dram_tensor("h_res", [D_MODEL, TOK], F32, kind="Internal").ap()

    cm = Common(cx, consts)
    add_eps(cx, cm)
    setup_common32(cx, cm, c32)
    mix_globals(cx, env, cmask_d, esel_d, hmask_d)
    S.sp_init = lambda e: _core_quarter(cx, e)

    with contextlib.ExitStack() as sc:
        cx.scope = sc
        hn_own_v = [a.rearrange("(c p) t -> p c t", p=128) for a in hn_own]

        def pre_out(n, j, src, rb):
            hn_store(2 * n + j, src, rb)
            hn_gather(2 * n + j)
        phase_pre(cx, cm, xT, gpre, pre_out, False)
        S.fence(include_cc=False)
    cx.scope = None

    hn_all_v = [a.rearrange("(r c p) t -> r p c t", r=4, p=128) for a in hn_all]
    y_own_v = [a.rearrange("(tq d) t -> tq d t", tq=4) for a in y_own]
    y_all_v = [a.rearrange("(hh tq d) t -> tq d hh t", hh=4, tq=4) for a in y_all]
    env.hn_chunk = lambda n, j: hn_all_v[2 * (n % 2) + j][n // 2]
    env.hn_rbufs = lambda n, j: [hna[2 * (n % 2) + j]]
    env.y_out = lambda bi, t0, ncols: y_own_v[bi][t0 // TOK][:, t0 % TOK:t0 % TOK + ncols]
    env.y_is_output = False
    h_res_v = h_res.rearrange("(c p) t -> p c t", p=128)
    x_v = xT.rearrange("(c p) t -> p c t", p=128)
    out_v = outT.rearrange("(c p) t -> p c t", p=128)

    for l in range(depth):
        last = l == depth - 1
        S.fence(include_cc=False)
        env.wmix = wmix_d[l]
        env.poolw = poolw_d[l]
        pend_y = []

        def y_gather(n):
            S.collective(lambda e, n=n: e.collective_compute("AllGather", ALU.bypass, replica_groups=CC_GROUPS,
                                                             ins=[y_own[n].opt()], outs=[y_all[n].opt()]), [], [ya[n]])

        def flush_cc():
            while pend_y:
                y_gather(pend_y.pop(0))

        def after_mixer(name):
            n = {"moba": 0, "fox": 1, "hgrn": 2, "pool": 3}[name]
            pend_y.append(n)
            if name == "pool":
                flush_cc()
        env.after_mixer = after_mixer
        env.flush_cc = flush_cc
        phase_mix(cx, cm, env, l, small_d[l])
        S.fence(include_cc=False)
        io = ios[l]
        src_v = x_v if l == 0 else h_res_v
        dst_v = out_v if last else h_res_v
        io.hT_in = lambda hf, c0, c1, src_v=src_v: src_v[:, c0:c1, hf * HALF:(hf + 1) * HALF]
        io.hn_load = lambda hf, j, dst, wb: cx.load(dst, hn_own_v[2 * hf + j], wb, rbufs=[hnp[2 * hf + j]])

        def ybr_load(dst, hf, wb):
            for n in range(4):
                def f(e, n=n):
                    q = _core_quarter(cx, e)
                    src = y_all_v[n][bass.ds(q, 1)][0][:, :, hf * HALF:(hf + 1) * HALF]
                    return e.dma_start(out=dst[:, n * 4:(n + 1) * 4, :], in_=src)
                S.dma("sp", f, [ya[n]], list(wb))
        io.ybr_load = ybr_load
        io.hT_out = lambda hf, c0, c1, dst_v=dst_v: dst_v[:, c0:c1, hf * HALF:(hf + 1) * HALF]
        pending = []

        def hn_store_tok(hf, j, src, rb):
            hn_store(2 * hf + j, src, rb)
            if hf == 0:
                pending.append(2 * hf + j)
            else:
                hn_gather(2 * hf + j)

        def after_gates(hf):
            while pending:
                hn_gather(pending.pop(0))
        io.hn_store = hn_store_tok
        io.after_gates = after_gates
        io.h_is_output = last
        io.hn_is_output = False
        io.write_hn_when_last = False
        with contextlib.ExitStack() as sc:
            cx.scope = sc
            phase_tok(cx, cm, io, l, last)
            S.fence()
        cx.scope = None
    S.emit()
    return nc


IN_OFF = {"mq": 0, "mk": 512, "mv": 1024, "fq": 1536, "fk": 2048, "fv": 2560, "ff": 3072,
          "hq": 3076, "hf": 3588, "hi": 4100, "hg": 4612, "pin": 5124, "gl": 5636}
_HC = {}


def host_consts():
    if _HC:
        return _HC
    f32 = np.float32
    c = np.zeros((128, 256), f32)
    c[:, :128] = np.eye(128)
    c[:, 128:] = 1
    _HC["c32"] = c
    _HC["consts"] = c.astype(NPBF)
    k = np.arange(128)[:, None, None]
    a = np.arange(4)[None, :, None]
    q = np.arange(512)[None, None, :]
    _HC["cmask"] = np.where(q >= 128 * a + k, 0.0, NEG).astype(NPBF)
    perm = np.zeros((128, 128), f32)
    d = np.arange(128)
    perm[(d + 64) % 128, d] = 1
    _HC["perm"] = perm.astype(NPBF)
    inv_freq = (f32(10000.0) ** (-np.arange(64, dtype=f32) * f32(2.0) / f32(128))).astype(f32)
    ang = (np.arange(SEQ, dtype=f32)[None, :] * inv_freq[:, None]).astype(f32)
    cos, sin = np.cos(ang).astype(f32), np.sin(ang).astype(f32)
    _HC["ropeC"] = np.ascontiguousarray(np.concatenate([cos, cos], 0))
    _HC["ropeS"] = np.ascontiguousarray(np.concatenate([-sin, sin], 0))
    es = np.zeros((16, 16, 128), f32)
    for j in range(16):
        es[j, j, :] = 1
    _HC["esel"] = es.astype(NPBF)
    s = np.arange(128)[:, None]
    t = np.arange(128)[None, :]
    _HC["hmask"] = ((s // 64 == t // 64) & (s <= t)).astype(f32).astype(NPBF)
    for h, w in enumerate(POOL_WINDOWS):
        M = np.zeros((128, 3, 128), f32)
        eye = (s == t).astype(f32)
        band = ((s <= t) & (s > t - w)).astype(f32)
        M[:, 0, :] = band / np.minimum(w, t + 1).astype(f32) - eye
        M[:, 1, :] = band / f32(w) - eye
        M[:, 2, :] = ((s - 128) > (t - w)).astype(f32) / f32(w)
        _HC[f"poolM{h}"] = M.astype(NPBF)
    return _HC


def fm_layout(w):
    K, N = w.shape
    return np.ascontiguousarray(w.reshape(K // 128, 128, N).transpose(1, 0, 2))


def mix_inputs(inp, l, c, hnT_b):
    b, h = c // 4, c % 4
    hc = host_consts()
    w_in = inp["w_in"][l]
    hs = slice(h * 128, (h + 1) * 128)

    def col(name):
        return w_in[:, IN_OFF[name] + h * 128: IN_OFF[name] + (h + 1) * 128]
    wm = np.concatenate([col(n) for n in ("mq", "mk", "mv", "fq", "fk", "fv", "hq", "hf", "hg", "hi", "pin")], axis=1)
    small = np.zeros((128, N_SMALL), np.float32)
    small[:, SM_WFF:SM_WFF + NCH] = w_in[:, IN_OFF["ff"] + h].reshape(NCH, 128).T
    small[:, SM_G0] = inp["hgrn_lb_logits"][0, hs]
    small[:, SM_G1] = inp["hgrn_lb_logits"][1, hs]
    small[:, SM_HNORM] = inp["hgrn_out_norm"][l, hs]
    small[:, SM_PSCALE] = inp["pool_scale"][l, hs]
    small[:, SM_FB] = inp["fox_f_bias"][l, h]
    return {"hnT": hnT_b, "wmix": fm_layout(wm), "small": small,
            "poolw": np.ascontiguousarray(inp["pool_w"][l, h]),
            "consts": hc["consts"], "c32": hc["c32"], "cmask": hc["cmask"], "perm": hc["perm"],
            "ropeC": hc["ropeC"], "ropeS": hc["ropeS"], "esel": hc["esel"], "hmask": hc["hmask"],
            "poolM": hc[f"poolM{h}"]}


GV_MIXPOST, GV_XAPRE, GV_XAMEM, GV_XAPOST, GV_MLPPRE, GV_MLPPOST, GV_NEXT = range(7)
HALF = 512
NMEM = 256
RING_N = 6
RING_ELEMS = 4096


class Ring:
    def __init__(self, cx, n=RING_N, elems=RING_ELEMS):
        self.cx = cx
        self.bufs = [cx.sb([128, elems], BF16, name=f"ring{i}") for i in range(n)]
        self.i = 0

    def load(self, src_ap, a, b):
        t = self.bufs[self.i % len(self.bufs)]
        self.i += 1
        view = t[:, 0:a * b].rearrange("p (a b) -> p a b", b=b)
        self.cx.load(view, src_ap, t.all, q="pool")
        return view, t.all


class TokIO:
    pass


def tok_io_external(cx):
    io = TokIO()
    hT_d = cx.din("hT", [D_MODEL, TOK], F32)
    hnT_d = cx.din("hnT", [D_MODEL, TOK], BF16)
    ybr_d = cx.din("ybr", [D_MODEL, TOK], BF16)
    tok_weight_inputs(cx, io, "")
    io.memT = cx.din("memT", [D_MODEL, NMEM], F32)
    hT_o = cx.dout("hT_out", [D_MODEL, TOK], F32)
    hn_o = cx.dout("hn_next", [D_MODEL, TOK], BF16)
    hview = hT_d.rearrange("(c p) t -> p c t", p=128)
    hnview = hnT_d.rearrange("(c p) t -> p c t", p=128)
    ybview = ybr_d.rearrange("(c p) t -> p c t", p=128)
    hoview = hT_o.rearrange("(c p) t -> p c t", p=128)
    hnoview = hn_o.rearrange("(c p) t -> p c t", p=128)
    io.hT_in = lambda hf, c0, c1: hview[:, c0:c1, hf * HALF:(hf + 1) * HALF]
    io.hn_load = lambda hf, j, dst, wb: cx.load(dst, hnview[:, :, hf * HALF + j * 256:hf * HALF + (j + 1) * 256], wb)
    io.ybr_load = lambda dst, hf, wb: cx.load(dst, ybview[:, :, hf * HALF:(hf + 1) * HALF], wb)
    io.hT_out = lambda hf, c0, c1: hoview[:, c0:c1, hf * HALF:(hf + 1) * HALF]
    io.hn_store = lambda hf, j, src, rb: cx.store(hnoview[:, :, hf * HALF + j * 256:hf * HALF + (j + 1) * 256], src, rb)
    io.h_is_output = True
    io.hn_is_output = True
    io.write_hn_when_last = True
    return io


def tok_weight_inputs(cx, io, sfx):
    io.wg_d = cx.din("wg" + sfx, [64, 128, NCH, 128], F32)
    io.wb_d = cx.din("wb" + sfx, [64, 128, 4, 128], F32)
    io.wo_d = cx.din("wo" + sfx, [16, 128, NCH, 128], F32)
    io.xq_d = cx.din("xq" + sfx, [4, 128, NCH, 128], F32)
    io.xk_d = cx.din("xk" + sfx, [4, 128, NCH, 128], F32)
    io.xv_d = cx.din("xv" + sfx, [2, 128, 8, 512], F32)
    io.xo_d = cx.din("xo" + sfx, [16, 128, 4, 128], F32)
    io.wup_d = cx.din("wup" + sfx, [64, 128, NCH, 128], F32)
    io.wdn_d = cx.din("wdn" + sfx, [32, 128, 32, 128], F32)
    io.gv_d = cx.din("gv" + sfx, [128, 7 * NCH], F32)


def build_tok(layer, last):
    cx = Ctx()
    io = tok_io_external(cx)
    consts = cx.din("consts", [128, 256], BF16)
    cm = Common(cx, consts)
    add_eps(cx, cm)
    phase_tok(cx, cm, io, layer, last)
    cx.S.emit()
    return cx.nc


def phase_tok(cx, cm, io, layer, last):
    S = cx.S
    wg_d, wb_d, wo_d, xq_d, xk_d, xv_d, xo_d, wup_d, wdn_d = (io.wg_d, io.wb_d, io.wo_d, io.xq_d, io.xk_d, io.xv_d,
                                                              io.xo_d, io.wup_d, io.wdn_d)
    memT_d, gv_d = io.memT, io.gv_d
    gv = cx.sb([128, 7 * NCH], F32, name="gv")
    cx.load(gv[:, :], gv_d, gv.all)

    def gcol(which, c):
        return gv[:, which * NCH + c: which * NCH + c + 1]

    ring = Ring(cx)
    hT = cx.sb([128, NCH, HALF], F32, nsub=NCH, name="hT")
    hnb = cx.sb([128, NCH, HALF], BF16, name="hnb")
    RA = cx.sb([128, 32, HALF], BF16, name="RA")
    ybr = RA[:, 0:NCH, :]
    sT = RA[:, NCH:2 * NCH, :]
    aT = cx.sb([128, NCH, HALF], F32, nsub=NCH, name="aT")
    sqp = [cx.sb([128, HALF], BF16, name=f"sq{i}") for i in range(3)]
    rstd = cx.sb([128, HALF], F32, name="rstd")
    tmpA = [cx.sb([128, HALF], F32, name=f"tmpA{i}") for i in range(2)]
    tmpB = [cx.sb([128, HALF], F32, name=f"tmpB{i}") for i in range(2)]
    acc = cx.sb([128, HALF], F32, name="acc")
    kx = cx.sb([128, 4, NMEM], BF16, name="kx")
    vx = cx.sb([128, 2, 512], BF16, name="vx")
    qx = cx.sb([128, 4, HALF], BF16, name="qx")
    ox = cx.sb([128, 4, HALF], BF16, name="ox")
    pTs = [cx.sb([128, HALF], BF16, name=f"xpT{i}") for i in range(2)]
    rc = cx.sb([128, HALF], F32, name="xrc")
    B = cm.banks

    def norm_add(gw):
        rms_stats(cx, cm, lambda c: aT[:, c, :], lambda c: [aT.b[c]], NCH, HALF, sqp, B[6], rstd, D_MODEL)
        for c in range(NCH):
            t = tmpA[c % 2]
            cx.stt(t[:, :], aT[:, c, :], gcol(gw, c), rstd[:, :], ALU.mult, ALU.mult,
                   [aT.b[c]] + gv.all + rstd.all, t.all)
            cx.tt(hT[:, c, :], hT[:, c, :], t[:, :], ALU.add, [hT.b[c]] + t.all, [hT.b[c]])

    def norm_to(gw, dst, dst_bufs):
        rms_stats(cx, cm, lambda c: hT[:, c, :], lambda c: [hT.b[c]], NCH, HALF, sqp, B[6], rstd, D_MODEL)
        for c in range(NCH):
            cx.stt(dst[:, c, :], hT[:, c, :], gcol(gw, c), rstd[:, :], ALU.mult, ALU.mult,
                   [hT.b[c]] + gv.all + rstd.all, dst_bufs)

    memf = aT
    cx.load(memf[:, :, 0:NMEM], memT_d.rearrange("(c p) t -> p c t", p=128), memf.all)
    rms_stats(cx, cm, lambda c: memf[:, c, 0:NMEM], lambda c: [memf.b[c]], NCH, NMEM, sqp, B[6], rstd, D_MODEL)
    memn = hnb
    for c in range(NCH):
        cx.stt(memn[:, c, 0:NMEM], memf[:, c, 0:NMEM], gcol(GV_XAMEM, c), rstd[:, 0:NMEM], ALU.mult, ALU.mult,
               [memf.b[c]] + gv.all + rstd.all, memn.all)
    for hd in range(4):
        wv_, wb_ = ring.load(xk_d[hd], NCH, 128)
        bk = B[hd % 2]
        for c in range(NCH):
            cx.mm(bk[:, 0:NMEM], wv_[:, c, :], memn[:, c, 0:NMEM], c == 0, c == NCH - 1, wb_ + memn.all, bk.all)
        cx.copy(kx[:, hd, :], bk[:, 0:NMEM], bk.all, kx.all)
    wv0, wb0 = ring.load(xv_d[0], 8, 512)
    wv1, wb1 = ring.load(xv_d[1], 8, 512)
    for mt in range(2):
        bk = B[2 + mt]
        for c in range(NCH):
            wv_, wb_ = (wv0, wb0) if c < 8 else (wv1, wb1)
            cx.mm(bk[:, :], memn[:, c, mt * 128:(mt + 1) * 128], wv_[:, c % 8, :], c == 0, c == NCH - 1,
                  wb_ + memn.all, bk.all)
        cx.copy(vx[:, mt, :], bk[:, :], bk.all, vx.all, eng="act")

    for hf in range(TOK // HALF):
        tsl = slice(hf * HALF, (hf + 1) * HALF)
        for c in range(0, NCH, 4):
            cx.load(hT[:, c:c + 4, :], io.hT_in(hf, c, c + 4), hT.b[c:c + 4])
        for j in range(2):
            io.hn_load(hf, j, hnb[:, :, j * 256:(j + 1) * 256], hnb.all)
        io.ybr_load(ybr, hf, RA.all)
        it = 0
        for j in range(NCH):
            for n in range(4):
                wg_, wgb = ring.load(wg_d[j * 4 + n], NCH, 128)
                wb_, wbb = ring.load(wb_d[j * 4 + n], 4, 128)
                bg, bp = B[it % 2], B[2 + it % 2]
                it += 1
                for c in range(NCH):
                    cx.mm(bg[:, :], wg_[:, c, :], hnb[:, c, :], c == 0, c == NCH - 1, wgb + hnb.all, bg.all)
                for hh in range(4):
                    cx.mm(bp[:, :], wb_[:, hh, :], ybr[:, n * 4 + hh, :], hh == 0, hh == 3, wbb + RA.all, bp.all)
                sg_ = tmpA[it % 2]
                cx.activation(sg_[:, :], bg[:, :], AF.Sigmoid, bg.all, sg_.all)
                if n == 0:
                    cx.tt(acc[:, :], bp[:, :], sg_[:, :], ALU.mult, bp.all + sg_.all, acc.all)
                else:
                    t2 = tmpB[it % 2]
                    cx.tt(t2[:, :], bp[:, :], sg_[:, :], ALU.mult, bp.all + sg_.all, t2.all)
                    if n < 3:
                        cx.tt(acc[:, :], acc[:, :], t2[:, :], ALU.add, acc.all + t2.all, acc.all)
                    else:
                        cx.tt(sT[:, j, :], acc[:, :], t2[:, :], ALU.add, acc.all + t2.all, RA.all)
        if getattr(io, "after_gates", None) is not None:
            io.after_gates(hf)
        for j in range(NCH):
            w_, wbf = ring.load(wo_d[j], NCH, 128)
            bk = B[j % 2]
            for c in range(NCH):
                cx.mm(bk[:, :], w_[:, c, :], sT[:, c, :], c == 0, c == NCH - 1, wbf + RA.all, bk.all)
            cx.copy(aT[:, j, :], bk[:, :], bk.all, [aT.b[j]], eng="act" if j % 2 else "dve")
        norm_add(GV_MIXPOST)
        norm_to(GV_XAPRE, hnb, hnb.all)
        for hd in range(4):
            w_, wbf = ring.load(xq_d[hd], NCH, 128)
            bk = B[hd % 2]
            for c in range(NCH):
                cx.mm(bk[:, :], w_[:, c, :], hnb[:, c, :], c == 0, c == NCH - 1, wbf + hnb.all, bk.all)
            S.act(lambda e, hd=hd, bk=bk: e.mul(out=qx[:, hd, :], in_=bk[:, :], mul=SCALE), bk.all, qx.all)
        for hd in range(4):
            ob, db = B[2 + hd % 2], B[4 + hd % 2]
            for mt in range(2):
                sb_ = B[mt]
                cx.mm(sb_[:, :], kx[:, hd, mt * 128:(mt + 1) * 128], qx[:, hd, :], True, True, kx.all + qx.all, sb_.all)
                pT = pTs[mt]
                cx.activation(pT[:, :], sb_[:, :], AF.Exp, sb_.all, pT.all)
                cx.mm(ob[:, :], vx[:, mt, hd * 128:(hd + 1) * 128], pT[:, :], mt == 0, mt == 1, vx.all + pT.all, ob.all)
                cx.mm(db[:, :], cm.ones, pT[:, :], mt == 0, mt == 1, pT.all, db.all)
            S.dve(lambda e, db=db: e.reciprocal(out=rc[:, :], in_=db[:, :]), db.all, rc.all)
            cx.tt(ox[:, hd, :], ob[:, :], rc[:, :], ALU.mult, ob.all + rc.all, ox.all)
        for j in range(NCH):
            w_, wbf = ring.load(xo_d[j], 4, 128)
            bk = B[j % 2]
            for hd in range(4):
                cx.mm(bk[:, :], w_[:, hd, :], ox[:, hd, :], hd == 0, hd == 3, wbf + ox.all, bk.all)
            cx.copy(aT[:, j, :], bk[:, :], bk.all, [aT.b[j]], eng="act" if j % 2 else "dve")
        norm_add(GV_XAPOST)
        norm_to(GV_MLPPRE, hnb, hnb.all)
        uT = RA
        for fh in range(2):
            for fb in range(32):
                w_, wbf = ring.load(wup_d[fh * 32 + fb], NCH, 128)
                bk = B[fb % 4]
                for c in range(NCH):
                    cx.mm(bk[:, :], w_[:, c, :], hnb[:, c, :], c == 0, c == NCH - 1, wbf + hnb.all, bk.all)
                r_ = tmpA[fb % 2]
                cx.activation(r_[:, :], bk[:, :], AF.Relu, bk.all, r_.all)
                cx.tt(uT[:, fb, :], r_[:, :], r_[:, :], ALU.mult, r_.all, RA.all, eng="dve")
            for j in range(NCH):
                w_, wbf = ring.load(wdn_d[j * 2 + fh], 32, 128)
                bk = B[4 + j % 2]
                for fb in range(32):
                    cx.mm(bk[:, :], w_[:, fb, :], uT[:, fb, :], fb == 0, fb == 31, wbf + RA.all, bk.all)
                if fh == 0:
                    cx.copy(aT[:, j, :], bk[:, :], bk.all, [aT.b[j]], eng="act")
                else:
                    cx.tt(aT[:, j, :], bk[:, :], aT[:, j, :], ALU.add, bk.all + [aT.b[j]], [aT.b[j]])
        norm_add(GV_MLPPOST)
        for c in range(0, NCH, 4):
            cx.store(io.hT_out(hf, c, c + 4), hT[:, c:c + 4, :], hT.b[c:c + 4], is_output=io.h_is_output)
        if not last:
            norm_to(GV_NEXT, hnb, hnb.all)
            for j in range(2):
                io.hn_store(hf, j, hnb[:, :, j * 256:(j + 1) * 256], hnb.all)
        elif io.write_hn_when_last:
            for j in range(2):
                io.hn_store(hf, j, hnb[:, :, j * 256:(j + 1) * 256], hnb.all)


def tok_weights(inp, l):
    f32 = np.float32
    wg = inp["w_in"][l][:, IN_OFF["gl"]:]
    wg = wg.reshape(NCH, 128, 4, NCH, 128).transpose(3, 2, 1, 0, 4)
    wg = np.ascontiguousarray(wg).reshape(64, 128, NCH, 128)
    wb = inp["w_branch"][l].reshape(4, 4, 128, NCH, 128).transpose(3, 0, 2, 1, 4)
    wb = np.ascontiguousarray(wb).reshape(64, 128, 4, 128)

    def tiles(w, kc):
        K_, N_ = w.shape
        return np.ascontiguousarray(w.reshape(kc, 128, N_ // 128, 128).transpose(2, 1, 0, 3))
    wo = tiles(inp["w_mix_out"][l], NCH)
    xq = tiles(inp["xa_wq"][l], NCH)
    xk = tiles(inp["xa_wkv"][l][:, 0:512], NCH)
    wv = inp["xa_wkv"][l][:, 512:1024].reshape(2, 8, 128, 512).transpose(0, 2, 1, 3)
    xv = np.ascontiguousarray(wv)
    xo = tiles(inp["xa_wo"][l], 4)
    wup = tiles(inp["mlp_w_up"][l], NCH)
    wd = inp["mlp_w_down"][l].reshape(2, 32, 128, NCH, 128).transpose(3, 0, 2, 1, 4)
    wdn = np.ascontiguousarray(wd).reshape(32, 128, 32, 128)
    gv = np.zeros((128, 7 * NCH), f32)
    names = ["mix_norm_post", "xa_norm_pre", "xa_norm_mem", "xa_norm_post", "mlp_norm_pre", "mlp_norm_post"]
    for i, nme in enumerate(names):
        gv[:, i * NCH:(i + 1) * NCH] = inp[nme][l].reshape(NCH, 128).T
    if l + 1 < inp["mix_norm_pre"].shape[0]:
        gv[:, GV_NEXT * NCH:(GV_NEXT + 1) * NCH] = inp["mix_norm_pre"][l + 1].reshape(NCH, 128).T
    return {"wg": wg, "wb": wb, "wo": wo, "xq": xq, "xk": xk, "xv": xv, "xo": xo, "wup": wup, "wdn": wdn,
            "gv": gv, "consts": host_consts()["consts"]}


_PROGS = {}


def _prog(key, builder):
    if key not in _PROGS:
        _PROGS[key] = builder()
    return _PROGS[key]


def kernel_unfused(**inp):
    inp = {k: np.asarray(v) for k, v in inp.items()}
    depth = inp["w_in"].shape[0]
    cores = list(range(8))
    hc = host_consts()
    x = inp["x"]
    tsl = [slice((c % 4) * TOK, (c % 4 + 1) * TOK) for c in cores]
    hT = [np.ascontiguousarray(x[c // 4, tsl[c]].T) for c in cores]
    memT = [np.ascontiguousarray(inp["mem"][b].T) for b in range(BATCH)]
    g0 = np.ascontiguousarray(inp["mix_norm_pre"][0].reshape(NCH, 128).T)
    res = run_bass_kernel_spmd(_prog("pre", build_pre),
                               [{"xT": hT[c], "gpre": g0, "consts": hc["consts"]} for c in cores], core_ids=cores)
    hn = [np.asarray(res.results[c]["hn_out"]) for c in cores]
    for l in range(depth):
        hn_full = [np.ascontiguousarray(np.concatenate([hn[b * 4 + q] for q in range(4)], axis=1)) for b in range(BATCH)]
        res = run_bass_kernel_spmd(_prog(("mix", l), lambda: build_mix(l)),
                                   [mix_inputs(inp, l, c, hn_full[c // 4]) for c in cores], core_ids=cores)
        yT = [np.asarray(res.results[c]["yT"]) for c in cores]
        tw = tok_weights(inp, l)
        maps = []
        for c in cores:
            b = c // 4
            yb = np.stack([yT[b * 4 + h][:, :, tsl[c]] for h in range(4)], axis=1)
            m = dict(tw)
            m["hT"] = hT[c]
            m["hnT"] = hn[c]
            m["ybr"] = np.ascontiguousarray(yb.reshape(D_MODEL, TOK))
            m["memT"] = memT[b]
            maps.append(m)
        res = run_bass_kernel_spmd(_prog(("tok", l), lambda: build_tok(l, l == depth - 1)), maps, core_ids=cores)
        hT = [np.asarray(res.results[c]["hT_out"]) for c in cores]
        hn = [np.asarray(res.results[c]["hn_next"]) for c in cores]
    out = np.empty_like(x)
    for c in cores:
        out[c // 4, tsl[c]] = hT[c].T
    return out


def fused_inputs(inp, c):
    depth = inp["w_in"].shape[0]
    b, q = c // 4, c % 4
    hc = host_consts()
    m = {"consts": hc["consts"], "c32": hc["c32"], "cmask": hc["cmask"], "esel": hc["esel"], "hmask": hc["hmask"],
         "perm": hc["perm"], "ropeC": hc["ropeC"], "ropeS": hc["ropeS"], "poolM": hc[f"poolM{q}"],
         "xT": np.ascontiguousarray(inp["x"][b, q * TOK:(q + 1) * TOK].T),
         "memT": np.ascontiguousarray(inp["mem"][b].T),
         "gpre": np.ascontiguousarray(inp["mix_norm_pre"][0].reshape(NCH, 128).T)}
    for l in range(depth):
        mi = mix_inputs(inp, l, c, None)
        m[f"wmix_{l}"] = mi["wmix"]
        m[f"small_{l}"] = mi["small"]
        m[f"poolw_{l}"] = mi["poolw"]
    return m


def kernel(**inp):
    inp = {k: np.asarray(v) for k, v in inp.items()}
    depth = inp["w_in"].shape[0]
    cores = list(range(8))
    nc = _prog(("fused", depth), lambda: build_fused(depth))
    tws = []
    for l in range(depth):
        tw = tok_weights(inp, l)
        tw.pop("consts")
        tws.append({f"{k}_{l}": v for k, v in tw.items()})
    maps = []
    for c in cores:
        m = fused_inputs(inp, c)
        for tw in tws:
            m.update(tw)
        maps.append(m)
    res = run_bass_kernel_spmd(nc, maps, core_ids=cores)
    x = inp["x"]
    out = np.empty_like(x)
    for c in cores:
        out[c // 4, (c % 4) * TOK:(c % 4 + 1) * TOK] = np.asarray(res.results[c]["outT"]).T
    return out
```
